# Optimizing a Trainium2 kernel written in Bass

```python
import math
import jax, jax.numpy as jnp
from jax import lax
import numpy as np

D_MODEL = 1024
BATCH = 8
SEQ = 2048
DEPTH = 4

HEAD_DIM = 64
FOX_HEADS = 6
DIFF_HEADS = 4
DIFF_QK_DIM = 32
DIFF_V_DIM = 64
DSA_HEADS = 6
DSA_KV_DIM = 64
IDX_HEADS = 4
IDX_DIM = 64
DSA_TOPK = 256
ROPE_THETA = 500000.0
ROPE_FRACTION = 4
Q_BLOCK = 128
NORM_EPS = 1e-6
FFN_HIDDEN = -(-8 * D_MODEL // (3 * 256)) * 256

MIX_WIDTH = FOX_HEADS * HEAD_DIM + DIFF_HEADS * DIFF_V_DIM + DSA_HEADS * HEAD_DIM

SPLIT_SIZES = (
    FOX_HEADS * HEAD_DIM,
    FOX_HEADS * HEAD_DIM,
    FOX_HEADS * HEAD_DIM,
    FOX_HEADS,
    DIFF_HEADS * 2 * DIFF_QK_DIM,
    DIFF_HEADS * 2 * DIFF_QK_DIM,
    DIFF_HEADS * DIFF_V_DIM,
    DSA_HEADS * HEAD_DIM,
    DSA_KV_DIM,
    DSA_KV_DIM,
    IDX_HEADS * IDX_DIM,
    IDX_DIM,
    IDX_HEADS,
)
IN_WIDTH = sum(SPLIT_SIZES)

kernel_name = "hybrid_fox_diff_dsa_trunk"


def rms_norm(x, g):
    xf = x.astype(jnp.float32)
    y = xf * lax.rsqrt(jnp.mean(xf * xf, axis=-1, keepdims=True) + NORM_EPS)
    return (y * g.astype(jnp.float32)).astype(x.dtype)


def rope_tables(seq_len, head_dim):
    rot = head_dim // ROPE_FRACTION
    inv_freq = 1.0 / (ROPE_THETA ** (jnp.arange(0, rot, 2, dtype=jnp.float32) / rot))
    ang = jnp.arange(seq_len, dtype=jnp.float32)[:, None] * inv_freq[None, :]
    return jnp.cos(ang), jnp.sin(ang)


def partial_rope(x, cos, sin):
    rot = x.shape[-1] // ROPE_FRACTION
    half = rot // 2
    c = cos.astype(x.dtype)
    s = sin.astype(x.dtype)
    x1 = x[..., :half]
    x2 = x[..., half:rot]
    return jnp.concatenate([x1 * c - x2 * s, x2 * c + x1 * s, x[..., rot:]], axis=-1)


def blocks_to_seq(out):
    out = jnp.moveaxis(out, 0, 1)
    return out.reshape((out.shape[0], out.shape[1] * out.shape[2]) + out.shape[3:])


def causal_mask(start, seq_len):
    qpos = start + jnp.arange(Q_BLOCK)
    kpos = jnp.arange(seq_len)
    return kpos[None, :] <= qpos[:, None], qpos


def fox_attention(q, k, v, cum_logf):
    S = q.shape[2]
    scale = q.shape[-1] ** -0.5

    def block(i):
        start = i * Q_BLOCK
        qb = lax.dynamic_slice_in_dim(q, start, Q_BLOCK, axis=2)
        cb = lax.dynamic_slice_in_dim(cum_logf, start, Q_BLOCK, axis=2)
        logits = (jnp.einsum('bhqd,bhkd->bhqk', qb, k).astype(jnp.float32) * scale
                  + cb[..., :, None] - cum_logf[..., None, :])
        mask, _ = causal_mask(start, S)
        p = jax.nn.softmax(jnp.where(mask, logits, -jnp.inf), axis=-1).astype(v.dtype)
        return jnp.einsum('bhqk,bhkd->bqhd', p, v)

    return blocks_to_seq(lax.map(block, jnp.arange(S // Q_BLOCK)))


def diff_attention(q, k, v, lam):
    S = q.shape[3]
    scale = q.shape[-1] ** -0.5

    def block(i):
        start = i * Q_BLOCK
        qb = lax.dynamic_slice_in_dim(q, start, Q_BLOCK, axis=3)
        logits = jnp.einsum('bhmqd,bhmkd->bhmqk', qb, k).astype(jnp.float32) * scale
        mask, _ = causal_mask(start, S)
        p = jax.nn.softmax(jnp.where(mask, logits, -jnp.inf), axis=-1)
        a = (p[:, :, 0] - lam * p[:, :, 1]).astype(v.dtype)
        return jnp.einsum('bhqk,bhkd->bqhd', a, v)

    return blocks_to_seq(lax.map(block, jnp.arange(S // Q_BLOCK)))


def dsa_attention(q, k, v, q_idx, k_idx, w_idx, topk):
    S = q.shape[2]
    scale = q.shape[-1] ** -0.5
    idx_scale = q_idx.shape[-1] ** -0.5
    gather = jax.vmap(lambda table, ix: table[ix])

    def block(i):
        start = i * Q_BLOCK
        qib = lax.dynamic_slice_in_dim(q_idx, start, Q_BLOCK, axis=2)
        wib = lax.dynamic_slice_in_dim(w_idx, start, Q_BLOCK, axis=1).astype(jnp.float32)
        s_idx = jnp.einsum('bhqd,bkd->bhqk', qib, k_idx).astype(jnp.float32) * idx_scale
        score = jnp.einsum('bqh,bhqk->bqk', wib, jax.nn.relu(s_idx))
        mask, qpos = causal_mask(start, S)
        score = jnp.where(mask[None], score, -jnp.inf)
        _, sel = lax.top_k(score, topk)
        valid = sel <= qpos[None, :, None]
        kg = gather(k, sel)
        vg = gather(v, sel)
        qb = lax.dynamic_slice_in_dim(q, start, Q_BLOCK, axis=2)
        logits = jnp.einsum('bhqd,bqjd->bhqj', qb, kg).astype(jnp.float32) * scale
        p = jax.nn.softmax(jnp.where(valid[:, None], logits, -jnp.inf), axis=-1).astype(vg.dtype)
        return jnp.einsum('bhqj,bqjd->bqhd', p, vg)

    return blocks_to_seq(lax.map(block, jnp.arange(S // Q_BLOCK)))


def setup_inputs(seed: int = 0) -> dict:
    key = jax.random.key(seed)
    ks = jax.random.split(key, 19)
    f32 = jnp.float32

    def nrm(k, shape, scale):
        return jax.random.normal(k, shape, f32) * scale

    res_scale = 1.0 / math.sqrt(2 * DEPTH)
    return {
        "x": nrm(ks[0], (BATCH, SEQ, D_MODEL), 1.0),
        "attn_norm": 1.0 + nrm(ks[1], (DEPTH, D_MODEL), 0.02),
        "w_in": nrm(ks[2], (DEPTH, D_MODEL, IN_WIDTH), D_MODEL ** -0.5),
        "fox_fb": jax.random.uniform(ks[3], (DEPTH, FOX_HEADS), f32, 1.0, 4.0),
        "fox_qn": 1.0 + nrm(ks[4], (DEPTH, HEAD_DIM), 0.02),
        "fox_kn": 1.0 + nrm(ks[5], (DEPTH, HEAD_DIM), 0.02),
        "diff_qn": 1.0 + nrm(ks[6], (DEPTH, DIFF_QK_DIM), 0.02),
        "diff_kn": 1.0 + nrm(ks[7], (DEPTH, DIFF_QK_DIM), 0.02),
        "diff_lq1": nrm(ks[8], (DEPTH, DIFF_QK_DIM), 0.1),
        "diff_lk1": nrm(ks[9], (DEPTH, DIFF_QK_DIM), 0.1),
        "diff_lq2": nrm(ks[10], (DEPTH, DIFF_QK_DIM), 0.1),
        "diff_lk2": nrm(ks[11], (DEPTH, DIFF_QK_DIM), 0.1),
        "diff_subln": 1.0 + nrm(ks[12], (DEPTH, DIFF_V_DIM), 0.02),
        "dsa_qn": 1.0 + nrm(ks[13], (DEPTH, HEAD_DIM), 0.02),
        "dsa_kn": 1.0 + nrm(ks[14], (DEPTH, DSA_KV_DIM), 0.02),
        "w_out": nrm(ks[15], (DEPTH, MIX_WIDTH, D_MODEL), MIX_WIDTH ** -0.5 * res_scale),
        "ffn_norm": 1.0 + nrm(ks[16], (DEPTH, D_MODEL), 0.02),
        "w_gate_up": nrm(ks[17], (DEPTH, D_MODEL, 2 * FFN_HIDDEN), D_MODEL ** -0.5),
        "w_down": nrm(ks[18], (DEPTH, FFN_HIDDEN, D_MODEL), FFN_HIDDEN ** -0.5 * res_scale),
    }


def reference(x, attn_norm, w_in, fox_fb, fox_qn, fox_kn, diff_qn, diff_kn,
              diff_lq1, diff_lk1, diff_lq2, diff_lk2, diff_subln, dsa_qn, dsa_kn,
              w_out, ffn_norm, w_gate_up, w_down):
    B, S, _ = x.shape
    topk = min(DSA_TOPK, S // 4)
    cos64, sin64 = rope_tables(S, HEAD_DIM)
    cos32, sin32 = rope_tables(S, DIFF_QK_DIM)
    split_points = [int(v) for v in np.cumsum(SPLIT_SIZES)[:-1]]

    for l in range(DEPTH):
        h = rms_norm(x, attn_norm[l])
        proj = h @ w_in[l]
        (fq, fk, fv, ff, dq, dk, dv, sq, sk, sv, iq, ik, iw) = jnp.split(proj, split_points, axis=-1)

        fq = rms_norm(fq.reshape(B, S, FOX_HEADS, HEAD_DIM), fox_qn[l]).transpose(0, 2, 1, 3)
        fk = rms_norm(fk.reshape(B, S, FOX_HEADS, HEAD_DIM), fox_kn[l]).transpose(0, 2, 1, 3)
        fv = fv.reshape(B, S, FOX_HEADS, HEAD_DIM).transpose(0, 2, 1, 3)
        log_f = jax.nn.log_sigmoid(ff.astype(jnp.float32) + fox_fb[l].astype(jnp.float32))
        cum_logf = jnp.cumsum(log_f, axis=1).transpose(0, 2, 1)
        o_fox = fox_attention(fq, fk, fv, cum_logf).reshape(B, S, FOX_HEADS * HEAD_DIM)

        lam_init = 0.8 - 0.6 * math.exp(-0.3 * l)
        lam = (jnp.exp(jnp.sum(diff_lq1[l].astype(jnp.float32) * diff_lk1[l].astype(jnp.float32)))
               - jnp.exp(jnp.sum(diff_lq2[l].astype(jnp.float32) * diff_lk2[l].astype(jnp.float32)))
               + lam_init)
        dq = rms_norm(dq.reshape(B, S, DIFF_HEADS, 2, DIFF_QK_DIM), diff_qn[l]).transpose(0, 2, 3, 1, 4)
        dk = rms_norm(dk.reshape(B, S, DIFF_HEADS, 2, DIFF_QK_DIM), diff_kn[l]).transpose(0, 2, 3, 1, 4)
        dq = partial_rope(dq, cos32, sin32)
        dk = partial_rope(dk, cos32, sin32)
        dv = dv.reshape(B, S, DIFF_HEADS, DIFF_V_DIM).transpose(0, 2, 1, 3)
        o_diff = diff_attention(dq, dk, dv, lam)
        o_diff = (rms_norm(o_diff, diff_subln[l]) * (1.0 - lam_init)).reshape(B, S, DIFF_HEADS * DIFF_V_DIM)

        sq = partial_rope(rms_norm(sq.reshape(B, S, DSA_HEADS, HEAD_DIM), dsa_qn[l]).transpose(0, 2, 1, 3), cos64, sin64)
        sk = partial_rope(rms_norm(sk, dsa_kn[l]), cos64, sin64)
        iq = partial_rope(iq.reshape(B, S, IDX_HEADS, IDX_DIM).transpose(0, 2, 1, 3), cos64, sin64)
        ik = partial_rope(ik, cos64, sin64)
        iw = iw * (IDX_HEADS ** -0.5)
        o_dsa = dsa_attention(sq, sk, sv, iq, ik, iw, topk).reshape(B, S, DSA_HEADS * HEAD_DIM)

        x = x + jnp.concatenate([o_fox, o_diff, o_dsa], axis=-1) @ w_out[l]

        h = rms_norm(x, ffn_norm[l])
        gate, up = jnp.split(h @ w_gate_up[l], 2, axis=-1)
        x = x + (jax.nn.silu(gate) * up) @ w_down[l]
    return x
```

```python
import math
from contextlib import ExitStack
import numpy as np
import concourse.bass as bass
import concourse.mybir as mybir
from concourse.bass_utils import run_bass_kernel_spmd

F32 = mybir.dt.float32
BF16 = mybir.dt.bfloat16
ALU = mybir.AluOpType
AF = mybir.ActivationFunctionType
AX = mybir.AxisListType

D = 1024
S = 2048
DEPTH = 4
NCH = 4
FFN_H = 2816
INW = 2762
EPS = 1e-6
NEG = -30000.0
KBIS = 18
TOPK = 256
PERT_EPS = 2.0 ** -20
NPP = 157

O_FQ, O_FK, O_FV, O_FF = 0, 384, 768, 1152
O_DQ, O_DK, O_DV = 1158, 1414, 1670
O_SQ, O_SK, O_SV, O_IQ, O_IK, O_IW = 1926, 2310, 2374, 2438, 2694, 2758


class Prog:
    ENGS = ("pe", "act", "dve", "pool", "sp")
    NDMA = 24

    def __init__(self):
        self.ops = []

    def add(self, eng, fn, R=(), W=(), dma=False):
        R = tuple(R)
        W = tuple(W) + tuple(k for k in R if k[0] == "ps" and k not in W)
        R = tuple(k for k in R if k[0] != "ps")
        self.ops.append((eng, fn, R, W, dma))

    def _deps(self):
        lastw = {}
        readers = {}
        desc = {}
        deps_all = []

        def related(k):
            out = []
            for i in range(1, len(k)):
                p = k[:i]
                if p in lastw or p in readers:
                    out.append(p)
            out.extend(desc.get(k, ()))
            return out

        def register(k):
            if k in lastw or k in readers:
                return
            for i in range(1, len(k) + 1):
                desc.setdefault(k[:i], set()).add(k)

        for i, (eng, fn, R, W, dma) in enumerate(self.ops):
            d = set()
            for k in R:
                register(k)
                lastw.setdefault(k, None)
                for r in related(k):
                    w = lastw.get(r)
                    if w is not None:
                        d.add(w)
            for k in W:
                register(k)
                lastw.setdefault(k, None)
                for r in related(k):
                    w = lastw.get(r)
                    if w is not None:
                        d.add(w)
                    d.update(readers.get(r, ()))
            d.discard(i)
            for k in R:
                readers.setdefault(k, []).append(i)
            for k in W:
                lastw[k] = i
                for r in desc.get(k, ()):
                    if r in readers:
                        readers[r] = []
                    lastw[r] = i
            deps_all.append(d)
        return deps_all

    def emit(self, nc, stack):
        ops = self.ops
        deps_all = self._deps()
        n = len(ops)
        sig = [False] * n
        for i, d in enumerate(deps_all):
            e_i = ops[i][0]
            for j in d:
                if ops[j][0] == "pe" and e_i == "pe":
                    continue
                sig[j] = True
        dma_prev = {}
        dma_slot = {}
        nd = 0
        for i, op in enumerate(ops):
            if op[4]:
                s = nd % self.NDMA
                nd += 1
                dma_slot[i] = s
                if s in dma_prev:
                    deps_all[i].add(dma_prev[s])
                dma_prev[s] = i
                sig[i] = True
        EPOCH = 1000
        esems = {e: [] for e in ("pe", "act", "dve", "pool")}
        dsem = [stack.enter_context(nc.semaphore("dsem%d" % k)) for k in range(min(self.NDMA, max(nd, 1)))]
        count = {e: 0 for e in esems}
        dcount = [0] * self.NDMA
        known = {e: {} for e in self.ENGS}
        event = [None] * n
        vc = [None] * n
        plan = {e: [] for e in self.ENGS}
        nwaits = 0
        Z = (0, 0)

        def sem_of(src, ep):
            if isinstance(src, tuple):
                return dsem[src[1]]
            lst = esems[src]
            while len(lst) <= ep:
                lst.append(stack.enter_context(nc.semaphore("sem_%s_%d" % (src, len(lst)))))
            return lst[ep]

        for i, (eng, fn, R, W, dma) in enumerate(ops):
            kn = known[eng]
            wm = {}
            for j in sorted(deps_all[i]):
                if ops[j][0] == "pe" and eng == "pe":
                    continue
                src, val = event[j]
                if kn.get(src, Z) >= val:
                    continue
                if wm.get(src, Z) < val:
                    wm[src] = val
                for s2, v2 in vc[j].items():
                    if kn.get(s2, Z) < v2:
                        kn[s2] = v2
            nwaits += len(wm)
            inc = None
            if sig[i]:
                if dma:
                    s = dma_slot[i]
                    dcount[s] += 16
                    event[i] = (("d", s), (0, dcount[s]))
                    inc = (dsem[s], 16)
                else:
                    ep, cn = divmod(count[eng], EPOCH)
                    count[eng] += 1
                    event[i] = (eng, (ep, cn + 1))
                    inc = (sem_of(eng, ep), 1)
                v = dict(kn)
                v[event[i][0]] = event[i][1]
                vc[i] = v
            plan[eng].append((fn, [(sem_of(src, val[0]), val[1]) for src, val in wm.items()], inc))
        self.stats = dict(n_ops=n, n_waits=nwaits, counts=dict(count), n_dma=nd,
                          n_sems=len(dsem) + sum(len(v) for v in esems.values()))

        block = stack.enter_context(nc.Block())

        def run(engine, items):
            for fn, waits, inc in items:
                for sem, val in waits:
                    engine.wait_ge(sem, val)
                ins = fn(engine)
                if inc is not None:
                    ins.then_inc(inc[0], inc[1])

        @block.tensor
        def _(e):
            run(e, plan["pe"])

        @block.scalar
        def _(e):
            run(e, plan["act"])

        @block.vector
        def _(e):
            run(e, plan["dve"])

        @block.gpsimd
        def _(e):
            run(e, plan["pool"])

        @block.sync
        def _(e):
            run(e, plan["sp"])
            for s in range(len(dsem)):
                if dcount[s] > 0:
                    e.wait_ge(dsem[s], dcount[s])


class Skew:
    def __init__(self, lag):
        self.q = []
        self.lag = lag

    def push(self, fn):
        self.q.append(fn)
        while len(self.q) > self.lag:
            self.q.pop(0)()

    def flush(self):
        while self.q:
            self.q.pop(0)()


def _rope_tab(head_dim, rows_rep):
    rot = head_dim // 4
    half = rot // 2
    inv = (1.0 / (np.float32(500000.0) ** (np.arange(0, rot, 2, dtype=np.float32) / np.float32(rot)))).astype(np.float32)
    ang = np.arange(S, dtype=np.float32)[:, None] * inv[None, :]
    cos = np.cos(ang).astype(np.float32).T
    sin = np.sin(ang).astype(np.float32).T
    C = np.ones((128, S), np.float32)
    Sn = np.zeros((128, S), np.float32)
    for b in range(128 // head_dim):
        o = b * head_dim
        C[o:o + half] = cos
        C[o + half:o + rot] = cos
        Sn[o:o + half] = sin
        Sn[o + half:o + rot] = sin
    P = np.zeros((128, 128), np.float32)
    for b in range(128 // head_dim):
        o = b * head_dim
        for r in range(half):
            P[o + r + half, o + r] = -1.0
            P[o + r, o + r + half] = 1.0
    return C, Sn, P


def _consts():
    c = {}
    eye = np.eye(128, dtype=np.float32)
    idx = np.arange(128)
    c["ident"] = eye
    c["tri"] = -(idx[:, None] <= idx[None, :]).astype(np.float32)
    c["negones"] = -np.ones((128, 128), np.float32)
    c["ones"] = np.ones((128, 128), np.float32)
    b64 = np.zeros((128, 128), np.float32)
    b64[:64, :64] = 1
    b64[64:, 64:] = 1
    c["bones64"] = b64
    b32 = np.zeros((128, 128), np.float32)
    for b in range(4):
        b32[32 * b:32 * b + 32, 32 * b:32 * b + 32] = 1
    c["bones32"] = b32
    sel = np.zeros((128, 6, 128), np.float32)
    for h in range(6):
        sel[h, h, :] = 1
        sel[32 + h, h, :] = 1
        sel[64 + h, h, :] = 1
    c["sel"] = sel.reshape(128, 768)
    c["cb"] = np.where(idx[:, None] <= idx[None, :], 0.0, NEG).astype(np.float32)
    c["cbt"] = np.where(idx[None, :] <= idx[:, None], 0.0, NEG).astype(np.float32)
    c["irep"] = np.tile(eye, (1, 4))
    C64, S64, P64 = _rope_tab(64, 2)
    C32, S32, P32 = _rope_tab(32, 4)
    c["prot64"] = P64
    c["prot32"] = P32
    c["rope"] = np.stack([C32, S32, C64, S64]).astype(np.float32)
    c["pert"] = np.tile((-PERT_EPS * np.arange(S, dtype=np.float32))[None, :], (128, 1)).astype(np.float32)
    p2 = np.zeros((128, 64), np.float32)
    for k in range(32):
        p2[:, k] = 2.0 ** (-k)
        p2[:, 32 + k] = 2.0 ** (1 - k)
    c["pow2"] = p2
    return c


_CONST_SHAPES = [("ident", 128), ("tri", 128), ("negones", 128), ("ones", 128), ("bones64", 128), ("bones32", 128),
                 ("sel", 768), ("cb", 128), ("cbt", 128), ("irep", 512), ("prot64", 128), ("prot32", 128), ("pow2", 64)]


def _pack_pp(inp, l):
    pp = np.zeros((128, NPP), np.float32)
    p = np.arange(128)
    pp[:, 0] = inp["fox_qn"][l][p % 64]
    pp[:, 1] = inp["fox_kn"][l][p % 64]
    pp[:, 2] = inp["diff_qn"][l][p % 32]
    pp[:, 3] = inp["diff_kn"][l][p % 32]
    pp[:, 4] = inp["dsa_qn"][l][p % 64]
    pp[:, 5] = inp["dsa_kn"][l][p % 64]
    pp[:, 6] = inp["diff_subln"][l][p % 64]
    pp[:, 7:15] = inp["attn_norm"][l].reshape(8, 128).T
    pp[:, 15:23] = inp["ffn_norm"][l].reshape(8, 128).T
    pp[:, 23:29] = inp["fox_fb"][l][None, :]
    pp[:, 29:61] = inp["diff_lq1"][l][None, :]
    pp[:, 61:93] = inp["diff_lk1"][l][None, :]
    pp[:, 93:125] = inp["diff_lq2"][l][None, :]
    pp[:, 125:157] = inp["diff_lk2"][l][None, :]
    return pp


def build_nc(layer_ids, debug=None):
    NL = len(layer_ids)
    nc = bass.Bass("TRN2", target_bir_lowering=False)
    stack = ExitStack()
    P = Prog()

    def dram(name, shape, dt=F32, kind="ExternalInput"):
        return nc.dram_tensor(name, list(shape), dt, kind=kind).ap()

    x_d = dram("xT", [D, S])
    y_d = dram("yT", [D, S], kind="ExternalOutput")
    w_in_d = dram("w_in", [NL, D, INW])
    w_out_d = dram("w_out", [NL, D, D])
    w_gu_d = dram("w_gu", [NL, D, 2 * FFN_H])
    w_dn_d = dram("w_dn", [NL, FFN_H, D])
    pp_d = dram("pp", [NL, 128, NPP])
    cst_d = {nm: dram("c_" + nm, [128, w]) for nm, w in _CONST_SHAPES}
    rope_d = dram("c_rope", [4, 128, S])
    pert_d = dram("c_pert", [128, S])
    sc_fm = dram("sc_fm", [17, 128, S], BF16, kind="Internal")
    sc_v = dram("sc_v", [5, 128, 16, 128], BF16, kind="Internal")
    sc_sv = dram("sc_sv", [128, 16, 64], BF16, kind="Internal")
    dbg = {}
    if debug:
        for nm, shape, dt in debug:
            dbg[nm] = dram("dbg_" + nm, shape, dt, kind="ExternalOutput")

    def sb(name, shape, dt=F32):
        return stack.enter_context(nc.sbuf_tensor(name, list(shape), dt))

    def ps(name, shape, dt=F32):
        return stack.enter_context(nc.psum_tensor(name, list(shape), dt))

    xT = sb("xT_sb", [128, 8, S])
    hT = sb("hT_sb", [128, 8, S], BF16)
    RW = 15 * 1024
    Rg = sb("R_sb", [128, RW])
    stg = sb("stg_sb", [128, 2, 1024])
    banks = [ps("bank%d" % b, [128, 512]) for b in range(8)]

    def rv(off_kb, size_kb, dt=F32):
        a = Rg[:, off_kb * 256:(off_kb + size_kb) * 256]
        if dt == BF16:
            a = a.bitcast(BF16)
        return a

    def rk(off_kb, size_kb):
        return [("R", pg) for pg in range(off_kb // 4, (off_kb + size_kb + 3) // 4)]

    sqt = sb("sqt", [128, 2, 512], BF16)
    rs = sb("rs", [128, 2, 512])
    qn = sb("qn", [128, 2, 512], BF16)
    t1 = sb("t1", [128, 2, 512])
    t2 = sb("t2", [128, 2, 512])
    pt = sb("pt", [128, 4, 512], BF16)
    rct = sb("rct", [128, 2, 512])
    ppt = sb("ppt", [128, NPP])
    der = sb("der", [128, 16])
    FFt = sb("FFt", [128, 16, 6])
    IWs = sb("IWs", [128, 16, 4])
    Lt = sb("Lt", [128, 16, 6])
    negc = sb("negc", [128, 16, 6])
    chb = sb("chb", [128, 16, 6], BF16)
    r1t = sb("r1t", [128, 16, 6])
    bis = sb("bis", [128, 8])
    STt = sb("STt", [128, 64])
    lam4 = sb("lam4", [128, 4, 32])
    CST = {}
    for nm, w in _CONST_SHAPES:
        f32c = nm in ("ident", "tri", "negones", "cbt", "pow2")
        CST[nm] = sb("k_" + nm, [128, w], F32 if f32c else BF16)
    epsc = sb("epsc", [128, 2])

    bank_rr = {}

    def bank(pool):
        i = bank_rr.get(pool, 0)
        bank_rr[pool] = i + 1
        b = pool[i % len(pool)]
        return banks[b], ("ps", b)

    slot_rr = {}

    def slot(name, nslots):
        i = slot_rr.get(name, 0)
        slot_rr[name] = i + 1
        return i % nslots

    def dma(out, in_, R, W):
        P.add("sp", lambda e: e.dma_start(out=out, in_=in_), R=R, W=W, dma=True)

    def mm(out, lhsT, rhs, start, stop, R, W, **kw):
        P.add("pe", lambda e: e.matmul(out, lhsT=lhsT, rhs=rhs, start=start, stop=stop, **kw), R=R, W=W)

    def act(out, in_, func, R, W, bias=0.0, scale=1.0):
        P.add("act", lambda e: e.activation(out=out, in_=in_, func=func, bias=bias, scale=scale), R=R, W=W)

    def load_w(dst, dst_keys, src_ap):
        s = slot("stg", 2)
        n = 1
        for d_ in src_ap.shape[1:]:
            n *= d_
        sv = stg[:, s, 0:n]
        if len(src_ap.shape) == 3:
            sv = sv.rearrange("p (a b) -> p a b", a=src_ap.shape[1])
        dma(sv, src_ap, R=[], W=[("stg", s)])
        P.add("pool", lambda e: e.tensor_copy(out=dst, in_=sv), R=[("stg", s)], W=dst_keys)

    for kc in range(8):
        dma(xT[:, kc, :], x_d[kc * 128:(kc + 1) * 128, :], R=[], W=[("xT", kc)])
    for nm, w in _CONST_SHAPES:
        if CST[nm].dtype == F32:
            dma(CST[nm][:, :], cst_d[nm][:, :], R=[], W=[("k", nm)])
        else:
            for o in range(0, w, 512):
                ww = min(512, w - o)
                s = slot("t1", 2)
                dma(t1[:, s, 0:ww], cst_d[nm][:, o:o + ww], R=[], W=[("t1", s)])
                P.add("pool", lambda e, nm=nm, o=o, ww=ww, s=s: e.tensor_copy(out=CST[nm][:, o:o + ww], in_=t1[:, s, 0:ww]),
                      R=[("t1", s)], W=[("k", nm)])
    P.add("pool", lambda e: e.memset(epsc[:, 0:1], EPS), W=[("epsc",)])
    P.add("pool", lambda e: e.memset(epsc[:, 1:2], 1.0), W=[("epsc",)])
    KEPS = [("epsc",)]
    eps_ap = epsc[:, 0:1]
    one_ap = epsc[:, 1:2]

    def norm_phase(gcol0):
        for c in range(NCH):
            cs = slice(c * 512, (c + 1) * 512)
            bk, bkk = bank((0, 1))
            for kc in range(8):
                s = slot("sqt", 2)
                act(sqt[:, s, :], xT[:, kc, cs], AF.Square, R=[("xT", kc, c)], W=[("sqt", s)])
                mm(bk[:, :], CST["ones"][:, :], sqt[:, s, :], kc == 0, kc == 7, R=[("sqt", s), ("k", "ones")], W=[bkk])
            s = slot("rs", 2)
            act(rs[:, s, :], bk[:, :], AF.Ln, R=[bkk] + KEPS, W=[("rs", s)], bias=eps_ap, scale=1.0 / D)
            act(rs[:, s, :], rs[:, s, :], AF.Exp, R=[("rs", s)], W=[("rs", s)], scale=-0.5)
            for kc in range(8):
                P.add("dve", lambda e, kc=kc, cs=cs, s=s: e.scalar_tensor_tensor(
                    out=hT[:, kc, cs], in0=xT[:, kc, cs], scalar=ppt[:, gcol0 + kc:gcol0 + kc + 1], in1=rs[:, s, :],
                    op0=ALU.mult, op1=ALU.mult), R=[("xT", kc, c), ("ppt",), ("rs", s)], W=[("hT", kc, c)])

    def ffn_phase(li):
        groups = [(g * 512, 4) for g in range(5)] + [(2560, 2)]
        Wgu = [rv(0, 16, BF16).rearrange("p (k n) -> p k n", k=8), rv(16, 16, BF16).rearrange("p (k n) -> p k n", k=8)]
        Wgu_k = [rk(0, 16), rk(16, 16)]
        Wd = [rv(32, 8, BF16).rearrange("p (k n) -> p k n", k=4), rv(40, 8, BF16).rearrange("p (k n) -> p k n", k=4)]
        Wd_k = [rk(32, 8), rk(40, 8)]
        actT = rv(48, 8, BF16).rearrange("p (s k n) -> p s k n", s=2, k=4)
        actT_k = [rk(48, 4), rk(52, 4)]
        win = w_gu_d[li].rearrange("(kc p) n -> p kc n", p=128)

        def load_group(gi):
            h0, nt = groups[gi]
            sl = gi % 2
            for t in range(nt):
                load_w(Wgu[sl][:, :, t * 128:(t + 1) * 128], Wgu_k[sl], win[:, :, h0 + t * 128:h0 + (t + 1) * 128])
                load_w(Wgu[sl][:, :, 512 + t * 128:512 + (t + 1) * 128], Wgu_k[sl],
                       win[:, :, FFN_H + h0 + t * 128:FFN_H + h0 + (t + 1) * 128])
                load_w(Wd[sl][:, t, :], Wd_k[sl], w_dn_d[li, h0 + t * 128:h0 + (t + 1) * 128, :])

        load_group(0)
        for gi in range(len(groups)):
            if gi + 1 < len(groups):
                load_group(gi + 1)
            h0, nt = groups[gi]
            sl = gi % 2
            for c in range(NCH):
                cs = slice(c * 512, (c + 1) * 512)
                asl = slot("actT", 2)
                for t in range(nt):
                    gb, gbk = bank((0, 1, 2, 3))
                    ub, ubk = bank((0, 1, 2, 3))
                    for kc in range(8):
                        mm(gb[:, :], Wgu[sl][:, kc, t * 128:(t + 1) * 128], hT[:, kc, cs], kc == 0, kc == 7,
                           R=Wgu_k[sl] + [("hT", kc, c)], W=[gbk])
                    for kc in range(8):
                        mm(ub[:, :], Wgu[sl][:, kc, 512 + t * 128:512 + (t + 1) * 128], hT[:, kc, cs], kc == 0, kc == 7,
                           R=Wgu_k[sl] + [("hT", kc, c)], W=[ubk])
                    s = slot("t1", 2)
                    act(t1[:, s, :], gb[:, :], AF.Silu, R=[gbk], W=[("t1", s)])
                    P.add("dve", lambda e, s=s, ub=ub, asl=asl, t=t: e.tensor_tensor(
                        out=actT[:, asl, t, :], in0=ub[:, :], in1=t1[:, s, :], op=ALU.mult),
                        R=[ubk, ("t1", s)], W=actT_k[asl])
                for d_ in range(8):
                    db, dbk = bank((4, 5, 6, 7))
                    for t in range(nt):
                        mm(db[:, :], Wd[sl][:, t, d_ * 128:(d_ + 1) * 128], actT[:, asl, t, :], t == 0, t == nt - 1,
                           R=Wd_k[sl] + actT_k[asl], W=[dbk])
                    P.add("dve", lambda e, d_=d_, cs=cs, db=db: e.tensor_tensor(
                        out=xT[:, d_, cs], in0=db[:, :], in1=xT[:, d_, cs], op=ALU.add),
                        R=[dbk, ("xT", d_, c)], W=[("xT", d_, c)])


    def apb(base_ap, mid):
        a = base_ap.ap
        return bass.AP(base_ap.tensor, base_ap.offset, [list(a[0]), [0, mid], list(a[-1])])

    def tap(name, src, R):
        if name in dbg:
            dma(dbg[name], src, R=R, W=[("dbg", name)])

    def derive_phase(labs):
        lam_init = 0.8 - 0.6 * math.exp(-0.3 * labs)
        for col, src, mul in ((0, 0, 0.125), (1, 2, 32.0 ** -0.5), (2, 4, 0.125), (3, 6, 1.0 - lam_init)):
            P.add("dve", lambda e, col=col, src=src, mul=mul: e.tensor_scalar(
                out=der[:, col:col + 1], in0=ppt[:, src:src + 1], scalar1=mul, scalar2=None, op0=ALU.mult),
                R=[("ppt",)], W=[("der", col)])
        for q, (a, b) in enumerate(((29, 61), (93, 125))):
            P.add("dve", lambda e, q=q, a=a, b=b: e.tensor_tensor(
                out=lam4[:, q, :], in0=ppt[:, a:a + 32], in1=ppt[:, b:b + 32], op=ALU.mult), R=[("ppt",)], W=[("lam4", q)])
            P.add("dve", lambda e, q=q: e.tensor_reduce(out=der[:, 5 + q:6 + q], in_=lam4[:, q, :], axis=AX.X, op=ALU.add),
                  R=[("lam4", q)], W=[("der", 5 + q)])
            act(der[:, 5 + q:6 + q], der[:, 5 + q:6 + q], AF.Exp, R=[("der", 5 + q)], W=[("der", 5 + q)])
        P.add("dve", lambda e: e.tensor_tensor(out=der[:, 4:5], in0=der[:, 6:7], in1=der[:, 5:6], op=ALU.subtract),
              R=[("der", 5), ("der", 6)], W=[("der", 4)])
        P.add("dve", lambda e: e.tensor_scalar(out=der[:, 4:5], in0=der[:, 4:5], scalar1=-lam_init, scalar2=None, op0=ALU.add),
              R=[("der", 4)], W=[("der", 4)])

    def proj_phase(li):
        win = w_in_d[li].rearrange("(kc p) n -> p kc n", p=128)
        WT = [rv(0, 2, BF16).rearrange("p (k n) -> p k n", k=8), rv(2, 2, BF16).rearrange("p (k n) -> p k n", k=8)]
        WT_k = [[("R", 0, 0)], [("R", 0, 1)]]
        Wv = rv(4, 6, BF16).rearrange("p (k n) -> p k n", k=8)
        Wv_k = rk(4, 6)
        ost = rv(12, 4, BF16).rearrange("p (s n) -> p s n", s=4)
        ost_k = [[("R", 3, s_)] for s_ in range(4)]
        Ct = rv(36, 8)
        St = rv(44, 8)
        Ct_k, St_k = rk(36, 8), rk(44, 8)
        FM = []
        for p_ in range(3):
            FM.append(dict(sc=p_, segs=[(O_FQ + 128 * p_, 128)], norm=(64, "bones64", der, 0, ("der", 0)), rope=None))
        for p_ in range(3):
            FM.append(dict(sc=3 + p_, segs=[(O_FK + 128 * p_, 128)], norm=(64, "bones64", ppt, 1, ("ppt",)), rope=None))
        for p_ in range(2):
            FM.append(dict(sc=6 + p_, segs=[(O_DQ + 128 * p_, 128)], norm=(32, "bones32", der, 1, ("der", 1)), rope=32))
        for p_ in range(2):
            FM.append(dict(sc=8 + p_, segs=[(O_DK + 128 * p_, 128)], norm=(32, "bones32", ppt, 3, ("ppt",)), rope=32))
        for p_ in range(3):
            FM.append(dict(sc=10 + p_, segs=[(O_SQ + 128 * p_, 128)], norm=(64, "bones64", der, 2, ("der", 2)), rope=64))
        FM.append(dict(sc=13, segs=[(O_SK, 64), (O_SK, 64)], norm=(64, "bones64", ppt, 5, ("ppt",)), rope=64))
        FM.append(dict(sc=14, segs=[(O_IK, 64), (O_IK, 64)], norm=None, rope=64))
        for p_ in range(2):
            FM.append(dict(sc=15 + p_, segs=[(O_IQ + 128 * p_, 128)], norm=None, rope=64))

        def load_tile(ti):
            sl = ti % 2
            o = 0
            for col0, n_ in FM[ti]["segs"]:
                load_w(WT[sl][:, :, o:o + n_], WT_k[sl], win[:, :, col0:col0 + n_])
                o += n_

        cur_rope = None
        SKB = Skew(1)
        SKC = Skew(1)
        load_tile(0)
        for ti, T in enumerate(FM):
            if ti + 1 < len(FM):
                load_tile(ti + 1)
            sl = ti % 2
            if T["rope"] is not None and T["rope"] != cur_rope:
                SKB.flush()
                SKC.flush()
                cur_rope = T["rope"]
                ro = 0 if cur_rope == 32 else 2
                dma(Ct, rope_d[ro], R=[], W=Ct_k)
                dma(St, rope_d[ro + 1], R=[], W=St_k)
            for c in range(NCH):
                cs = slice(c * 512, (c + 1) * 512)
                pj, pjk = bank((0, 1, 2))
                for kc in range(8):
                    mm(pj[:, :], WT[sl][:, kc, :], hT[:, kc, cs], kc == 0, kc == 7, R=WT_k[sl] + [("hT", kc, c)], W=[pjk])
                sq_ = None
                if T["norm"] is not None:
                    sq_ = slot("sqt", 2)
                    act(sqt[:, sq_, :], pj[:, :], AF.Square, R=[pjk], W=[("sqt", sq_)])

                def stageB(T=T, c=c, cs=cs, pj=pj, pjk=pjk, sq_=sq_):
                    os_ = slot("ost", 4)
                    qs_ = None
                    if T["rope"] is not None:
                        qs_ = slot("qn", 2)
                        tgt, tgtk = qn[:, qs_, :], [("qn", qs_)]
                    else:
                        tgt, tgtk = ost[:, os_, :], ost_k[os_]
                    if T["norm"] is not None:
                        bs_, bname, gt, gcol, gkey = T["norm"]
                        sb_, sbk = bank((3, 4))
                        mm(sb_[:, :], CST[bname][:, :], sqt[:, sq_, :], True, True, R=[("sqt", sq_), ("k", bname)], W=[sbk])
                        r_ = slot("rs", 2)
                        act(rs[:, r_, :], sb_[:, :], AF.Ln, R=[sbk] + KEPS, W=[("rs", r_)], bias=eps_ap, scale=1.0 / bs_)
                        act(rs[:, r_, :], rs[:, r_, :], AF.Exp, R=[("rs", r_)], W=[("rs", r_)], scale=-0.5)
                        P.add("dve", lambda e: e.scalar_tensor_tensor(
                            out=tgt, in0=pj[:, :], scalar=gt[:, gcol:gcol + 1], in1=rs[:, r_, :], op0=ALU.mult, op1=ALU.mult),
                            R=[pjk, gkey, ("rs", r_)], W=tgtk)
                    else:
                        act(tgt, pj[:, :], AF.Copy, R=[pjk], W=tgtk)

                    def stageC():
                        if T["rope"] is not None:
                            pname = "prot%d" % T["rope"]
                            rp, rpk = bank((5, 6))
                            mm(rp[:, :], CST[pname][:, :], qn[:, qs_, :], True, True, R=[("qn", qs_), ("k", pname)], W=[rpk])
                            a_ = slot("t1", 2)
                            b_ = slot("t2", 2)
                            P.add("dve", lambda e: e.tensor_tensor(out=t1[:, a_, :], in0=rp[:, :], in1=St[:, cs], op=ALU.mult),
                                  R=[rpk] + St_k, W=[("t1", a_)])
                            P.add("pool", lambda e: e.tensor_tensor(out=t2[:, b_, :], in0=qn[:, qs_, :], in1=Ct[:, cs], op=ALU.mult),
                                  R=[("qn", qs_)] + Ct_k, W=[("t2", b_)])
                            P.add("pool", lambda e: e.tensor_tensor(out=ost[:, os_, :], in0=t1[:, a_, :], in1=t2[:, b_, :], op=ALU.add),
                                  R=[("t1", a_), ("t2", b_)], W=ost_k[os_])
                        dma(sc_fm[T["sc"], :, cs], ost[:, os_, :], R=ost_k[os_], W=[("scfm", T["sc"], c)])
                    SKC.push(stageC)
                SKB.push(stageB)
        SKB.flush()
        SKC.flush()

        def tm_group(col_segs, ncols, handler):
            o = 0
            for col0, n_ in col_segs:
                load_w(Wv[:, :, o:o + n_], Wv_k, win[:, :, col0:col0 + n_])
                o += n_
            handler()

        def v_pairs(npairs, sc0):
            for a in range(npairs):
                for jg in range(4):
                    tv, tvk = bank((6, 7))
                    for jj in range(4):
                        j = jg * 4 + jj
                        for kc in range(8):
                            mm(tv[:, jj * 128:(jj + 1) * 128], hT[:, kc, j * 128:(j + 1) * 128], Wv[:, kc, a * 128:(a + 1) * 128],
                               kc == 0, kc == 7, R=Wv_k + [("hT", kc, jg)], W=[tvk])
                    os_ = slot("ost", 4)
                    act(ost[:, os_, :], tv[:, :], AF.Copy, R=[tvk], W=ost_k[os_])
                    dma(sc_v[sc0 + a, :, jg * 4:(jg + 1) * 4, :], ost[:, os_, :].rearrange("p (j n) -> p j n", j=4),
                        R=ost_k[os_], W=[("scv", sc0 + a, jg)])

        tm_group([(O_FV, 128), (O_FV + 128, 128), (O_FV + 256, 128)], 384, lambda: v_pairs(3, 0))
        tm_group([(O_DV, 128), (O_DV + 128, 128)], 256, lambda: v_pairs(2, 3))

        def small():
            for jg in range(4):
                tv, tvk = bank((6, 7))
                for jj in range(4):
                    j = jg * 4 + jj
                    for kc in range(8):
                        mm(tv[:, jj * 74:(jj + 1) * 74], hT[:, kc, j * 128:(j + 1) * 128], Wv[:, kc, 0:74],
                           kc == 0, kc == 7, R=Wv_k + [("hT", kc, jg)], W=[tvk])
                tv3 = tv[:, 0:296].rearrange("p (j n) -> p j n", j=4)
                os_ = slot("ost", 4)
                act(ost[:, os_, 0:256].rearrange("p (j n) -> p j n", j=4), tv3[:, :, 0:64], AF.Copy, R=[tvk], W=ost_k[os_])
                dma(sc_sv[:, jg * 4:(jg + 1) * 4, :], ost[:, os_, 0:256].rearrange("p (j n) -> p j n", j=4),
                    R=ost_k[os_], W=[("scsv", jg)])
                P.add("dve", lambda e, jg=jg, tv3=tv3: e.tensor_copy(out=FFt[:, jg * 4:(jg + 1) * 4, :], in_=tv3[:, :, 64:70]),
                      R=[tvk], W=[("FFt", jg)])
                P.add("dve", lambda e, jg=jg, tv3=tv3: e.tensor_scalar(
                    out=IWs[:, jg * 4:(jg + 1) * 4, :], in0=tv3[:, :, 70:74], scalar1=0.0625, scalar2=None, op0=ALU.mult),
                    R=[tvk], W=[("IWs", jg)])

        tm_group([(O_SV, 64), (O_FF, 6), (O_IW, 4)], 74, small)

    Qp = [rv(0, 4, BF16), rv(16, 4, BF16)]
    Kp = [rv(4, 4, BF16), rv(20, 4, BF16)]
    Vp = [rv(8, 8, BF16).rearrange("p (j s d) -> p j s d", j=16, s=4), rv(24, 8, BF16).rearrange("p (j s d) -> p j s d", j=16, s=4)]
    Qp_k, Kp_k, Vp_k = [rk(0, 4), rk(16, 4)], [rk(4, 4), rk(20, 4)], [rk(8, 8), rk(24, 8)]
    identb = CST["irep"][:, 0:128]

    def load_pair(sl, qi, ki, vi):
        dma(Qp[sl], sc_fm[qi], R=[("scfm", qi)], W=Qp_k[sl])
        dma(Kp[sl], sc_fm[ki], R=[("scfm", ki)], W=Kp_k[sl])
        dma(Vp[sl][:, :, 0, :], sc_v[vi][:, :, 0:64], R=[("scv", vi)], W=Vp_k[sl])
        dma(Vp[sl][:, :, 3, :], sc_v[vi][:, :, 64:128], R=[("scv", vi)], W=Vp_k[sl])
        P.add("pool", lambda e: e.memset(Vp[sl][:, :, 1:3, :], 1.0), W=Vp_k[sl])

    SKA = Skew(2)

    def attn_map(sl, e_, base, kdim, c, h, fox, acc, acck):
        nj = 4 * c + 4
        for j in range(nj):
            n0 = max(j * 128, c * 512)
            wN = (c + 1) * 512 - n0
            off = n0 - c * 512
            diag = j >= 4 * c
            st, stk = bank((4, 5, 6, 7))
            kw = {}
            if kdim == 32:
                kw = dict(tile_position=(base, 0))
            mm(st[:, off:off + wN], Kp[sl][base:base + kdim, j * 128:(j + 1) * 128], Qp[sl][base:base + kdim, n0:n0 + wN],
               True, (not fox) and (not diag), R=Kp_k[sl] + Qp_k[sl], W=[stk], **kw)
            if fox:
                mm(st[:, off:off + wN], selv[0:96, h, :], CT[0:96, n0:n0 + wN], False, not diag,
                   R=[("k", "sel")] + CT_k, W=[stk])
            if diag:
                mm(st[:, off:off + 128], identb, CST["cb"][:, :], False, True, R=[("k", "irep"), ("k", "cb")], W=[stk])
            ps_ = slot("pt", 4)
            if fox:
                act(pt[:, ps_, off:off + wN], st[:, off:off + wN], AF.Exp, R=[stk, ("negc",)], W=[("pt", ps_)],
                    bias=negc[:, j, h:h + 1])
            else:
                act(pt[:, ps_, off:off + wN], st[:, off:off + wN], AF.Exp, R=[stk], W=[("pt", ps_)])
            SKA.push(lambda j=j, off=off, wN=wN, ps_=ps_: mm(
                acc[:, off:off + wN], Vp[sl][:, j, 2 * e_:2 * e_ + 2, :].rearrange("p s d -> p (s d)"), pt[:, ps_, off:off + wN],
                j == 0, j == nj - 1, R=Vp_k[sl] + [("pt", ps_)], W=[acck]))

    CT = rv(36, 4, BF16)
    CT_k = rk(36, 4)
    CS = rv(40, 6).rearrange("p (j n) -> p j n", j=16)
    CS_k = rk(40, 6)
    selv = CST["sel"][:, :].rearrange("p (h n) -> p h n", h=6)

    def fox_phase():
        P.add("dve", lambda e: e.tensor_tensor(out=Lt[:, :, :], in0=FFt[:, :, :], in1=apb(ppt[:, 23:29], 16), op=ALU.add),
              R=[("FFt",), ("ppt",)], W=[("Lt",)])
        act(Lt[:, :, :], Lt[:, :, :], AF.Exp, R=[("Lt",)], W=[("Lt",)], scale=-1.0)
        act(Lt[:, :, :], Lt[:, :, :], AF.Ln, R=[("Lt",)] + KEPS, W=[("Lt",)], bias=one_ap)
        cps, cpsk = bank((0,))
        for i in range(16):
            for j in range(i + 1):
                mm(cps[:, i * 6:(i + 1) * 6], CST["tri" if j == i else "negones"][:, :], Lt[:, j, :], j == 0, j == i,
                   R=[("Lt",), ("k", "tri"), ("k", "negones")], W=[cpsk])
        cps3 = cps[:, 0:96].rearrange("p (j n) -> p j n", j=16)
        P.add("dve", lambda e: e.tensor_scalar(out=negc[:, :, :], in0=cps3, scalar1=-1.0, scalar2=None, op0=ALU.mult),
              R=[cpsk], W=[("negc",)])
        P.add("pool", lambda e: e.memset(CS[:, :, :], 0.0), W=CS_k)
        P.add("dve", lambda e: e.tensor_copy(out=chb[:, :, :], in_=cps3), R=[cpsk], W=[("chb",)])
        P.add("dve", lambda e: e.tensor_copy(out=CS[:, :, 0:6], in_=chb[:, :, :]), R=[("chb",)], W=CS_k)
        P.add("dve", lambda e: e.tensor_tensor(out=r1t[:, :, :], in0=cps3, in1=chb[:, :, :], op=ALU.subtract),
              R=[cpsk, ("chb",)], W=[("r1t",)])
        P.add("dve", lambda e: e.tensor_copy(out=chb[:, :, :], in_=r1t[:, :, :]), R=[("r1t",)], W=[("chb",)])
        P.add("dve", lambda e: e.tensor_copy(out=CS[:, :, 32:38], in_=chb[:, :, :]), R=[("chb",)], W=CS_k)
        P.add("dve", lambda e: e.tensor_tensor(out=r1t[:, :, :], in0=r1t[:, :, :], in1=chb[:, :, :], op=ALU.subtract),
              R=[("r1t",), ("chb",)], W=[("r1t",)])
        P.add("dve", lambda e: e.tensor_copy(out=chb[:, :, :], in_=r1t[:, :, :]), R=[("r1t",)], W=[("chb",)])
        P.add("dve", lambda e: e.tensor_copy(out=CS[:, :, 64:70], in_=chb[:, :, :]), R=[("chb",)], W=CS_k)
        for q_ in range(4):
            ctp, ctpk = bank((1, 2, 3))
            for jj in range(4):
                j = q_ * 4 + jj
                P.add("pe", lambda e, ctp=ctp, jj=jj, j=j: e.transpose(
                    out=ctp[0:96, jj * 128:(jj + 1) * 128], in_=CS[:, j, :], identity=CST["ident"][:, :]),
                    R=CS_k + [("k", "ident")], W=[ctpk])
            act(CT[0:96, q_ * 512:(q_ + 1) * 512], ctp[0:96, :], AF.Copy, R=[ctpk], W=CT_k)
        tap("negc", negc[:, :, :], [("negc",)])
        load_pair(0, 0, 3, 0)
        for p_ in range(3):
            sl = p_ % 2
            SKA.flush()
            if p_ + 1 < 3:
                load_pair((p_ + 1) % 2, p_ + 1, 3 + p_ + 1, p_ + 1)
            for c in range(NCH):
                cs = slice(c * 512, (c + 1) * 512)
                for e_ in range(2):
                    base = 64 * e_
                    acc, acck = bank((0, 1, 2))
                    attn_map(sl, e_, base, 64, c, 2 * p_ + e_, True, acc, acck)
                    def fin(base=base, acc=acc, acck=acck, p_=p_, cs=cs, c=c):
                        O = slice(base, base + 64)
                        Dn = slice(64 - base, 128 - base)
                        rc = slot("rct", 2)
                        P.add("dve", lambda e: e.reciprocal(out=rct[O, rc, :], in_=acc[Dn, :]), R=[acck], W=[("rct", rc)])
                        P.add("dve", lambda e: e.tensor_tensor(out=hT[O, p_, cs], in0=acc[O, :], in1=rct[O, rc, :], op=ALU.mult),
                              R=[acck, ("rct", rc)], W=[("hT", p_, c)])
                    SKA.push(fin)
        SKA.flush()

    def diff_phase():
        load_pair(0, 6, 8, 3)
        for d_ in range(2):
            sl = d_ % 2
            SKA.flush()
            if d_ == 0:
                load_pair(1, 7, 9, 4)
            for c in range(NCH):
                cs = slice(c * 512, (c + 1) * 512)
                od = slot("t1", 2)
                for e_ in range(2):
                    accs = []
                    for m_ in range(2):
                        base = 64 * e_ + 32 * m_
                        acc, acck = bank((0, 1, 2))
                        attn_map(sl, e_, base, 32, c, 0, False, acc, acck)
                        accs.append((acc, acck))
                    def comb(e_=e_, accs=accs, od=od):
                        O = slice(64 * e_, 64 * e_ + 64)
                        Dn = slice(64 - 64 * e_, 128 - 64 * e_)
                        (a1, a1k), (a2, a2k) = accs
                        ra = slot("rct", 2)
                        rb = slot("rct", 2)
                        ob = slot("t2", 2)
                        P.add("dve", lambda e: e.reciprocal(out=rct[O, ra, :], in_=a1[Dn, :]), R=[a1k], W=[("rct", ra)])
                        P.add("dve", lambda e: e.reciprocal(out=rct[O, rb, :], in_=a2[Dn, :]), R=[a2k], W=[("rct", rb)])
                        P.add("dve", lambda e: e.tensor_scalar(out=rct[O, rb, :], in0=rct[O, rb, :], scalar1=der[O, 4:5],
                                                               scalar2=None, op0=ALU.mult),
                              R=[("rct", rb), ("der", 4)], W=[("rct", rb)])
                        P.add("dve", lambda e: e.tensor_tensor(out=t1[O, od, :], in0=a1[O, :], in1=rct[O, ra, :], op=ALU.mult),
                              R=[a1k, ("rct", ra)], W=[("t1", od)])
                        P.add("dve", lambda e: e.tensor_tensor(out=t2[O, ob, :], in0=a2[O, :], in1=rct[O, rb, :], op=ALU.mult),
                              R=[a2k, ("rct", rb)], W=[("t2", ob)])
                        P.add("pool", lambda e: e.tensor_tensor(out=t1[O, od, :], in0=t1[O, od, :], in1=t2[O, ob, :], op=ALU.add),
                              R=[("t1", od), ("t2", ob)], W=[("t1", od)])
                    SKA.push(comb)

                def subln(od=od, d_=d_, cs=cs, c=c):
                    sq_ = slot("sqt", 2)
                    act(sqt[:, sq_, :], t1[:, od, :], AF.Square, R=[("t1", od)], W=[("sqt", sq_)])
                    sb_, sbk = bank((3,))
                    mm(sb_[:, :], CST["bones64"][:, :], sqt[:, sq_, :], True, True, R=[("sqt", sq_), ("k", "bones64")], W=[sbk])
                    r_ = slot("rs", 2)
                    act(rs[:, r_, :], sb_[:, :], AF.Ln, R=[sbk] + KEPS, W=[("rs", r_)], bias=eps_ap, scale=1.0 / 64)
                    act(rs[:, r_, :], rs[:, r_, :], AF.Exp, R=[("rs", r_)], W=[("rs", r_)], scale=-0.5)
                    P.add("dve", lambda e: e.scalar_tensor_tensor(
                        out=hT[:, 3 + d_, cs], in0=t1[:, od, :], scalar=der[:, 3:4], in1=rs[:, r_, :], op0=ALU.mult, op1=ALU.mult),
                        R=[("t1", od), ("der", 3), ("rs", r_)], W=[("hT", 3 + d_, c)])
                SKA.push(subln)
        SKA.flush()

    def dsa_phase():
        Qs = rv(0, 12, BF16).rearrange("p (t n) -> p t n", t=3)
        Qs_k = rk(0, 12)
        SKK, SKK_k = rv(12, 4, BF16), rk(12, 4)
        IKK, IKK_k = rv(16, 4, BF16), rk(16, 4)
        IQ = rv(20, 8, BF16).rearrange("p (t n) -> p t n", t=2)
        IQ_k = rk(20, 8)
        SVa = rv(28, 6, BF16).rearrange("p (j s d) -> p j s d", j=16, s=3)
        SVa_k = rk(28, 6)
        accb, acc_k = rv(36, 8), rk(36, 8)
        MBs = [(rv(44, 4, BF16), rk(44, 4)), (rv(56, 4, BF16), rk(56, 4))]
        PERT, PERT_k = rv(48, 8), rk(48, 8)
        junk = t2[:, :, :].rearrange("p a n -> p (a n)").bitcast(BF16)
        junk_k = [("t2",)]
        for t_ in range(3):
            dma(Qs[:, t_, :], sc_fm[10 + t_], R=[("scfm", 10 + t_)], W=Qs_k)
        dma(SKK, sc_fm[13], R=[("scfm", 13)], W=SKK_k)
        dma(IKK, sc_fm[14], R=[("scfm", 14)], W=IKK_k)
        for t_ in range(2):
            dma(IQ[:, t_, :], sc_fm[15 + t_], R=[("scfm", 15 + t_)], W=IQ_k)
        dma(SVa[:, :, 0, :], sc_sv, R=[("scsv",)], W=SVa_k)
        dma(SVa[:, :, 2, :], sc_sv, R=[("scsv",)], W=SVa_k)
        P.add("pool", lambda e: e.memset(SVa[:, :, 1, :], 1.0), W=SVa_k)
        dma(PERT, pert_d, R=[], W=PERT_k)

        def index_block(i):
            MB, MB_k = MBs[i % 2]
            nk = (i + 1) * 128
            nch = (nk + 511) // 512
            for hh in range(4):
                tl, base = hh // 2, 64 * (hh % 2)
                for m_ in range(nch):
                    w_ = min(512, nk - 512 * m_)
                    dp, dpk = bank((0, 1))
                    mm(dp[:, 0:w_], IQ[base:base + 64, tl, i * 128:(i + 1) * 128], IKK[base:base + 64, 512 * m_:512 * m_ + w_],
                       True, True, R=IQ_k + IKK_k, W=[dpk])
                    act(dp[:, 0:w_], dp[:, 0:w_], AF.Relu, R=[dpk], W=[dpk])
                    src = PERT if hh == 0 else accb
                    srck = PERT_k if hh == 0 else acc_k
                    P.add("dve", lambda e, dp=dp, w_=w_, m_=m_, hh=hh, src=src: e.scalar_tensor_tensor(
                        out=accb[:, 512 * m_:512 * m_ + w_], in0=dp[:, 0:w_], scalar=IWs[:, i, hh:hh + 1],
                        in1=src[:, 512 * m_:512 * m_ + w_], op0=ALU.mult, op1=ALU.add),
                        R=[dpk, ("IWs",)] + srck, W=acc_k)
            P.add("dve", lambda e: e.tensor_reduce(out=bis[:, 0:1], in_=accb[:, 0:nk], axis=AX.X, op=ALU.max,
                                                   apply_absolute_value=True), R=acc_k, W=[("bis", 0)])
            P.add("dve", lambda e: e.tensor_tensor(out=accb[:, i * 128:(i + 1) * 128], in0=accb[:, i * 128:(i + 1) * 128],
                                                   in1=CST["cbt"][:, :], op=ALU.add), R=acc_k + [("k", "cbt")], W=acc_k)
            P.add("dve", lambda e: e.tensor_scalar(out=STt[:, :], in0=CST["pow2"][:, :], scalar1=bis[:, 0:1], scalar2=None,
                                                   op0=ALU.mult), R=[("bis", 0), ("k", "pow2")], W=[("STt",)])
            P.add("dve", lambda e: e.memset(bis[:, 1:2], 0.0), W=[("bis", 1)])
            for k in range(KBIS):
                P.add("dve", lambda e: e.tensor_scalar(out=junk[:, 0:nk], in0=accb[:, 0:nk], scalar1=bis[:, 1:2], scalar2=None,
                                                       op0=ALU.is_gt, op1=ALU.add, accum_out=bis[:, 2:3]),
                      R=acc_k + [("bis", 1)], W=junk_k + [("bis", 2)])
                P.add("dve", lambda e, k=k: e.tensor_scalar(out=bis[:, 3:4], in0=bis[:, 2:3], scalar1=TOPK - 0.5,
                                                            scalar2=STt[:, 32 + k:33 + k], op0=ALU.is_gt, op1=ALU.mult),
                      R=[("bis", 2), ("STt",)], W=[("bis", 3)])
                P.add("dve", lambda e, k=k: e.scalar_tensor_tensor(out=bis[:, 1:2], in0=bis[:, 3:4], scalar=STt[:, k:k + 1],
                                                                   in1=bis[:, 1:2], op0=ALU.subtract, op1=ALU.add),
                      R=[("bis", 3), ("bis", 1), ("STt",)], W=[("bis", 1)])
            P.add("dve", lambda e: e.tensor_scalar(out=MB[:, 0:nk], in0=accb[:, 0:nk], scalar1=bis[:, 1:2], scalar2=NEG,
                                                   op0=ALU.is_le, op1=ALU.mult), R=acc_k + [("bis", 1)], W=MB_k)

        SKD = Skew(1)

        def attend_block(i):
            MB, MB_k = MBs[i % 2]
            c = i // 4
            qs_ = slice(i * 128, (i + 1) * 128)
            accE, accEk = bank((6,))
            accO, accOk = bank((7,))
            for j in range(i + 1):
                ks_ = slice(j * 128, (j + 1) * 128)
                sts = []
                for par in range(2):
                    st, stk = bank((2, 3, 4, 5))
                    pr = slice(64 * par, 64 * par + 64)
                    for hi in range(3):
                        mm(st[:, hi * 128:(hi + 1) * 128], SKK[pr, ks_], Qs[pr, hi, qs_], hi == 0, False,
                           R=SKK_k + Qs_k, W=[stk])
                    mm(st[:, 0:384], MB[:, ks_], CST["irep"][:, 0:384], False, True, R=MB_k + [("k", "irep")], W=[stk])
                    sts.append((st, stk))
                pss = []
                for par in range(2):
                    ps_ = slot("pt", 4)
                    act(pt[:, ps_, 0:384], sts[par][0][:, 0:384], AF.Exp, R=[sts[par][1]], W=[("pt", ps_)])
                    pss.append(ps_)
                def pv(j=j, pss=pss):
                    mm(accE[:, 0:384], SVa[:, j, 0:2, :].rearrange("p s d -> p (s d)"), pt[:, pss[0], 0:384], j == 0, j == i,
                       R=SVa_k + [("pt", pss[0])], W=[accEk])
                    mm(accO[:, 0:384], SVa[:, j, 1:3, :].rearrange("p s d -> p (s d)"), pt[:, pss[1], 0:384], j == 0, j == i,
                       R=SVa_k + [("pt", pss[1])], W=[accOk])
                SKD.push(pv)

            def fin():
                for par, (acc, acck) in enumerate(((accE, accEk), (accO, accOk))):
                    O = slice(64 * par, 64 * par + 64)
                    Dn = slice(64 - 64 * par, 128 - 64 * par)
                    rc = slot("rct", 2)
                    P.add("dve", lambda e, O=O, Dn=Dn, rc=rc, acc=acc: e.reciprocal(out=rct[O, rc, 0:384], in_=acc[Dn, 0:384]),
                          R=[acck], W=[("rct", rc)])
                    P.add("dve", lambda e, O=O, rc=rc, acc=acc: e.tensor_tensor(
                        out=hT[O, 5:8, qs_], in0=acc[O, 0:384].rearrange("p (h n) -> p h n", h=3),
                        in1=rct[O, rc, 0:384].rearrange("p (h n) -> p h n", h=3), op=ALU.mult),
                        R=[acck, ("rct", rc)], W=[("hT", 5, c), ("hT", 6, c), ("hT", 7, c)])
            SKD.push(fin)

        index_block(0)
        for i in range(16):
            if i + 1 < 16:
                index_block(i + 1)
            attend_block(i)
        SKD.flush()
        tap("acc15", accb, acc_k)
        tap("MB15", MBs[1][0], MBs[1][1])
        tap("bis15", bis[:, 0:4], [("bis",)])
        tap("STt", STt[:, :], [("STt",)])

    def wout_phase(li):
        Wo = [rv(56, 2, BF16).rearrange("p (k n) -> p k n", k=8), rv(58, 2, BF16).rearrange("p (k n) -> p k n", k=8)]
        Wo_k = [[("R", 14, 0)], [("R", 14, 1)]]
        wsrc = w_out_d[li].rearrange("(kt p) n -> p kt n", p=128)
        load_w(Wo[0], Wo_k[0], wsrc[:, :, 0:128])
        for d_ in range(8):
            sl = d_ % 2
            if d_ + 1 < 8:
                load_w(Wo[1 - sl], Wo_k[1 - sl], wsrc[:, :, (d_ + 1) * 128:(d_ + 2) * 128])
            for c in range(NCH):
                cs = slice(c * 512, (c + 1) * 512)
                bk, bkk = bank((0, 1, 2, 3, 4, 5, 6, 7))
                for kt in range(8):
                    mm(bk[:, :], Wo[sl][:, kt, :], hT[:, kt, cs], kt == 0, kt == 7, R=Wo_k[sl] + [("hT", kt, c)], W=[bkk])
                P.add("dve", lambda e, d_=d_, cs=cs, bk=bk: e.tensor_tensor(
                    out=xT[:, d_, cs], in0=bk[:, :], in1=xT[:, d_, cs], op=ALU.add),
                    R=[bkk, ("xT", d_, c)], W=[("xT", d_, c)])

    for li, labs in enumerate(layer_ids):
        dma(ppt[:, :], pp_d[li], R=[], W=[("ppt",)])
        derive_phase(labs)
        norm_phase(7)
        proj_phase(li)
        if li == 0:
            tap("sc_fm", sc_fm, [("scfm",)])
            tap("sc_v", sc_v, [("scv",)])
            tap("sc_sv", sc_sv, [("scsv",)])
            tap("FFt", FFt[:, :, :], [("FFt",)])
            tap("IWs", IWs[:, :, :], [("IWs",)])
        fox_phase()
        diff_phase()
        dsa_phase()
        if li == 0:
            tap("cat", hT[:, :, :], [("hT",)])
        wout_phase(li)
        if li == 0:
            tap("x1", xT[:, :, :], [("xT",)])
        norm_phase(15)
        ffn_phase(li)

    for kc in range(8):
        dma(y_d[kc * 128:(kc + 1) * 128, :], xT[:, kc, :], R=[("xT", kc)], W=[("y", kc)])

    P.emit(nc, stack)
    stack.close()
    return nc, P.stats


_NC_CACHE = {}


def _get_nc(layer_ids):
    key = tuple(layer_ids)
    if key not in _NC_CACHE:
        _NC_CACHE[key] = build_nc(list(layer_ids))[0]
    return _NC_CACHE[key]


def kernel(**inputs):
    inp = {k: np.asarray(v) for k, v in inputs.items()}
    x = inp["x"].astype(np.float32, copy=False)
    B = x.shape[0]
    cst = _consts()
    base = {}
    for nm, w in _CONST_SHAPES:
        base["c_" + nm] = np.ascontiguousarray(cst[nm], dtype=np.float32)
    base["c_rope"] = cst["rope"]
    base["c_pert"] = cst["pert"]
    layer_ids = list(range(DEPTH))
    nc = _get_nc(layer_ids)
    base["w_in"] = np.ascontiguousarray(inp["w_in"], dtype=np.float32)
    base["w_out"] = np.ascontiguousarray(inp["w_out"], dtype=np.float32)
    base["w_gu"] = np.ascontiguousarray(inp["w_gate_up"], dtype=np.float32)
    base["w_dn"] = np.ascontiguousarray(inp["w_down"], dtype=np.float32)
    base["pp"] = np.stack([_pack_pp(inp, l) for l in layer_ids]).astype(np.float32)
    in_maps = []
    for b in range(B):
        m = dict(base)
        m["xT"] = np.ascontiguousarray(x[b].T)
        in_maps.append(m)
    res = run_bass_kernel_spmd(nc, in_maps, core_ids=list(range(B)))
    out = np.stack([np.asarray(r["yT"]).T for r in res.results]).astype(np.float32)
    return out
```

```python
import math
from contextlib import ExitStack
import numpy as np
import concourse.bass as bass
import concourse.mybir as mybir
from concourse.bass_utils import run_bass_kernel_spmd

F32 = mybir.dt.float32
BF16 = mybir.dt.bfloat16
ALU = mybir.AluOpType
AF = mybir.ActivationFunctionType
AX = mybir.AxisListType

D = 1024
S = 2048
DEPTH = 4
NCH = 4
FFN_H = 2816
INW = 2762
EPS = 1e-6
NEG = -30000.0
KBIS = 18
TOPK = 256
PERT_EPS = 2.0 ** -20
NPP = 157

O_FQ, O_FK, O_FV, O_FF = 0, 384, 768, 1152
O_DQ, O_DK, O_DV = 1158, 1414, 1670
O_SQ, O_SK, O_SV, O_IQ, O_IK, O_IW = 1926, 2310, 2374, 2438, 2694, 2758


class Prog:
    ENGS = ("pe", "act", "dve", "pool", "sp")
    NDMA = 24

    def __init__(self):
        self.ops = []

    def add(self, eng, fn, R=(), W=(), dma=False):
        R = tuple(R)
        W = tuple(W) + tuple(k for k in R if k[0] == "ps" and k not in W)
        R = tuple(k for k in R if k[0] != "ps")
        self.ops.append((eng, fn, R, W, dma))

    def _deps(self):
        lastw = {}
        readers = {}
        desc = {}
        deps_all = []

        def related(k):
            out = []
            for i in range(1, len(k)):
                p = k[:i]
                if p in lastw or p in readers:
                    out.append(p)
            out.extend(desc.get(k, ()))
            return out

        def register(k):
            if k in lastw or k in readers:
                return
            for i in range(1, len(k) + 1):
                desc.setdefault(k[:i], set()).add(k)

        for i, (eng, fn, R, W, dma) in enumerate(self.ops):
            d = set()
            for k in R:
                register(k)
                lastw.setdefault(k, None)
                for r in related(k):
                    w = lastw.get(r)
                    if w is not None:
                        d.add(w)
            for k in W:
                register(k)
                lastw.setdefault(k, None)
                for r in related(k):
                    w = lastw.get(r)
                    if w is not None:
                        d.add(w)
                    d.update(readers.get(r, ()))
            d.discard(i)
            for k in R:
                readers.setdefault(k, []).append(i)
            for k in W:
                lastw[k] = i
                for r in desc.get(k, ()):
                    if r in readers:
                        readers[r] = []
                    lastw[r] = i
            deps_all.append(d)
        return deps_all

    def emit(self, nc, stack):
        ops = self.ops
        deps_all = self._deps()
        n = len(ops)
        sig = [False] * n
        for i, d in enumerate(deps_all):
            e_i = ops[i][0]
            for j in d:
                if ops[j][0] == "pe" and e_i == "pe":
                    continue
                sig[j] = True
        dma_prev = {}
        dma_slot = {}
        nd = 0
        for i, op in enumerate(ops):
            if op[4]:
                s = nd % self.NDMA
                nd += 1
                dma_slot[i] = s
                if s in dma_prev:
                    deps_all[i].add(dma_prev[s])
                dma_prev[s] = i
                sig[i] = True
        EPOCH = 1000
        esems = {e: [] for e in ("pe", "act", "dve", "pool")}
        dsem = [stack.enter_context(nc.semaphore("dsem%d" % k)) for k in range(min(self.NDMA, max(nd, 1)))]
        count = {e: 0 for e in esems}
        dcount = [0] * self.NDMA
        known = {e: {} for e in self.ENGS}
        event = [None] * n
        vc = [None] * n
        plan = {e: [] for e in self.ENGS}
        nwaits = 0
        Z = (0, 0)

        def sem_of(src, ep):
            if isinstance(src, tuple):
                return dsem[src[1]]
            lst = esems[src]
            while len(lst) <= ep:
                lst.append(stack.enter_context(nc.semaphore("sem_%s_%d" % (src, len(lst)))))
            return lst[ep]

        for i, (eng, fn, R, W, dma) in enumerate(ops):
            kn = known[eng]
            wm = {}
            for j in sorted(deps_all[i]):
                if ops[j][0] == "pe" and eng == "pe":
                    continue
                src, val = event[j]
                if kn.get(src, Z) >= val:
                    continue
                if wm.get(src, Z) < val:
                    wm[src] = val
                for s2, v2 in vc[j].items():
                    if kn.get(s2, Z) < v2:
                        kn[s2] = v2
            nwaits += len(wm)
            inc = None
            if sig[i]:
                if dma:
                    s = dma_slot[i]
                    dcount[s] += 16
                    event[i] = (("d", s), (0, dcount[s]))
                    inc = (dsem[s], 16)
                else:
                    ep, cn = divmod(count[eng], EPOCH)
                    count[eng] += 1
                    event[i] = (eng, (ep, cn + 1))
                    inc = (sem_of(eng, ep), 1)
                v = dict(kn)
                v[event[i][0]] = event[i][1]
                vc[i] = v
            plan[eng].append((fn, [(sem_of(src, val[0]), val[1]) for src, val in wm.items()], inc))
        self.stats = dict(n_ops=n, n_waits=nwaits, counts=dict(count), n_dma=nd,
                          n_sems=len(dsem) + sum(len(v) for v in esems.values()))

        block = stack.enter_context(nc.Block())

        def run(engine, items):
            for fn, waits, inc in items:
                for sem, val in waits:
                    engine.wait_ge(sem, val)
                ins = fn(engine)
                if inc is not None:
                    ins.then_inc(inc[0], inc[1])

        @block.tensor
        def _(e):
            run(e, plan["pe"])

        @block.scalar
        def _(e):
            run(e, plan["act"])

        @block.vector
        def _(e):
            run(e, plan["dve"])

        @block.gpsimd
        def _(e):
            run(e, plan["pool"])

        @block.sync
        def _(e):
            run(e, plan["sp"])
            for s in range(len(dsem)):
                if dcount[s] > 0:
                    e.wait_ge(dsem[s], dcount[s])


class Skew:
    def __init__(self, lag):
        self.q = []
        self.lag = lag

    def push(self, fn):
        self.q.append(fn)
        while len(self.q) > self.lag:
            self.q.pop(0)()

    def flush(self):
        while self.q:
            self.q.pop(0)()


def _rope_tab(head_dim, rows_rep):
    rot = head_dim // 4
    half = rot // 2
    inv = (1.0 / (np.float32(500000.0) ** (np.arange(0, rot, 2, dtype=np.float32) / np.float32(rot)))).astype(np.float32)
    ang = np.arange(S, dtype=np.float32)[:, None] * inv[None, :]
    cos = np.cos(ang).astype(np.float32).T
    sin = np.sin(ang).astype(np.float32).T
    C = np.ones((128, S), np.float32)
    Sn = np.zeros((128, S), np.float32)
    for b in range(128 // head_dim):
        o = b * head_dim
        C[o:o + half] = cos
        C[o + half:o + rot] = cos
        Sn[o:o + half] = sin
        Sn[o + half:o + rot] = sin
    P = np.zeros((128, 128), np.float32)
    for b in range(128 // head_dim):
        o = b * head_dim
        for r in range(half):
            P[o + r + half, o + r] = -1.0
            P[o + r, o + r + half] = 1.0
    return C, Sn, P


def _consts():
    c = {}
    eye = np.eye(128, dtype=np.float32)
    idx = np.arange(128)
    c["ident"] = eye
    c["tri"] = -(idx[:, None] <= idx[None, :]).astype(np.float32)
    c["negones"] = -np.ones((128, 128), np.float32)
    c["ones"] = np.ones((128, 128), np.float32)
    b64 = np.zeros((128, 128), np.float32)
    b64[:64, :64] = 1
    b64[64:, 64:] = 1
    c["bones64"] = b64
    b32 = np.zeros((128, 128), np.float32)
    for b in range(4):
        b32[32 * b:32 * b + 32, 32 * b:32 * b + 32] = 1
    c["bones32"] = b32
    sel = np.zeros((128, 6, 128), np.float32)
    for h in range(6):
        sel[h, h, :] = 1
        sel[32 + h, h, :] = 1
        sel[64 + h, h, :] = 1
    c["sel"] = sel.reshape(128, 768)
    c["cb"] = np.where(idx[:, None] <= idx[None, :], 0.0, NEG).astype(np.float32)
    c["cbt"] = np.where(idx[None, :] <= idx[:, None], 0.0, NEG).astype(np.float32)
    c["irep"] = np.tile(eye, (1, 4))
    C64, S64, P64 = _rope_tab(64, 2)
    C32, S32, P32 = _rope_tab(32, 4)
    c["prot64"] = P64
    c["prot32"] = P32
    c["rope"] = np.stack([C32, S32, C64, S64]).astype(np.float32)
    c["pert"] = np.tile((-PERT_EPS * np.arange(S, dtype=np.float32))[None, :], (128, 1)).astype(np.float32)
    p2 = np.zeros((128, 64), np.float32)
    for k in range(32):
        p2[:, k] = 2.0 ** (-k)
        p2[:, 32 + k] = 2.0 ** (1 - k)
    c["pow2"] = p2
    return c


_CONST_SHAPES = [("ident", 128), ("tri", 128), ("negones", 128), ("ones", 128), ("bones64", 128), ("bones32", 128),
                 ("sel", 768), ("cb", 128), ("cbt", 128), ("irep", 512), ("prot64", 128), ("prot32", 128), ("pow2", 64)]


def _pack_pp(inp, l):
    pp = np.zeros((128, NPP), np.float32)
    p = np.arange(128)
    pp[:, 0] = inp["fox_qn"][l][p % 64]
    pp[:, 1] = inp["fox_kn"][l][p % 64]
    pp[:, 2] = inp["diff_qn"][l][p % 32]
    pp[:, 3] = inp["diff_kn"][l][p % 32]
    pp[:, 4] = inp["dsa_qn"][l][p % 64]
    pp[:, 5] = inp["dsa_kn"][l][p % 64]
    pp[:, 6] = inp["diff_subln"][l][p % 64]
    pp[:, 7:15] = inp["attn_norm"][l].reshape(8, 128).T
    pp[:, 15:23] = inp["ffn_norm"][l].reshape(8, 128).T
    pp[:, 23:29] = inp["fox_fb"][l][None, :]
    pp[:, 29:61] = inp["diff_lq1"][l][None, :]
    pp[:, 61:93] = inp["diff_lk1"][l][None, :]
    pp[:, 93:125] = inp["diff_lq2"][l][None, :]
    pp[:, 125:157] = inp["diff_lk2"][l][None, :]
    return pp


def build_nc(layer_ids, debug=None):
    NL = len(layer_ids)
    nc = bass.Bass("TRN2", target_bir_lowering=False)
    stack = ExitStack()
    P = Prog()

    def dram(name, shape, dt=F32, kind="ExternalInput"):
        return nc.dram_tensor(name, list(shape), dt, kind=kind).ap()

    x_d = dram("xT", [D, S])
    y_d = dram("yT", [D, S], kind="ExternalOutput")
    w_in_d = dram("w_in", [NL, D, INW])
    w_out_d = dram("w_out", [NL, D, D])
    w_gu_d = dram("w_gu", [NL, D, 2 * FFN_H])
    w_dn_d = dram("w_dn", [NL, FFN_H, D])
    pp_d = dram("pp", [NL, 128, NPP])
    cst_d = {nm: dram("c_" + nm, [128, w]) for nm, w in _CONST_SHAPES}
    rope_d = dram("c_rope", [4, 128, S])
    pert_d = dram("c_pert", [128, S])
    sc_fm = dram("sc_fm", [17, 128, S], BF16, kind="Internal")
    sc_v = dram("sc_v", [5, 128, 16, 128], BF16, kind="Internal")
    sc_sv = dram("sc_sv", [128, 16, 64], BF16, kind="Internal")
    dbg = {}
    if debug:
        for nm, shape, dt in debug:
            dbg[nm] = dram("dbg_" + nm, shape, dt, kind="ExternalOutput")

    def sb(name, shape, dt=F32):
        return stack.enter_context(nc.sbuf_tensor(name, list(shape), dt))

    def ps(name, shape, dt=F32):
        return stack.enter_context(nc.psum_tensor(name, list(shape), dt))

    xT = sb("xT_sb", [128, 8, S])
    hT = sb("hT_sb", [128, 8, S], BF16)
    RW = 15 * 1024
    Rg = sb("R_sb", [128, RW])
    stg = sb("stg_sb", [128, 2, 1024])
    banks = [ps("bank%d" % b, [128, 512]) for b in range(8)]

    def rv(off_kb, size_kb, dt=F32):
        a = Rg[:, off_kb * 256:(off_kb + size_kb) * 256]
        if dt == BF16:
            a = a.bitcast(BF16)
        return a

    def rk(off_kb, size_kb):
        return [("R", pg) for pg in range(off_kb // 4, (off_kb + size_kb + 3) // 4)]

    sqt = sb("sqt", [128, 2, 512], BF16)
    rs = sb("rs", [128, 2, 512])
    qn = sb("qn", [128, 2, 512], BF16)
    t1 = sb("t1", [128, 2, 512])
    t2 = sb("t2", [128, 2, 512])
    pt = sb("pt", [128, 4, 512], BF16)
    rct = sb("rct", [128, 2, 512])
    ppt = sb("ppt", [128, NPP])
    der = sb("der", [128, 16])
    FFt = sb("FFt", [128, 16, 6])
    IWs = sb("IWs", [128, 16, 4])
    Lt = sb("Lt", [128, 16, 6])
    negc = sb("negc", [128, 16, 6])
    chb = sb("chb", [128, 16, 6], BF16)
    r1t = sb("r1t", [128, 16, 6])
    bis = sb("bis", [128, 8])
    STt = sb("STt", [128, 64])
    lam4 = sb("lam4", [128, 4, 32])
    CST = {}
    for nm, w in _CONST_SHAPES:
        f32c = nm in ("ident", "tri", "negones", "cbt", "pow2")
        CST[nm] = sb("k_" + nm, [128, w], F32 if f32c else BF16)
    epsc = sb("epsc", [128, 2])
    accB_t = sb("accB_t", [128, 2048])
    bis2 = sb("bis2", [128, 2, 4])
    STt2 = sb("STt2", [128, 2, 64])

    bank_rr = {}

    def bank(pool):
        i = bank_rr.get(pool, 0)
        bank_rr[pool] = i + 1
        b = pool[i % len(pool)]
        return banks[b], ("ps", b)

    slot_rr = {}

    def slot(name, nslots):
        i = slot_rr.get(name, 0)
        slot_rr[name] = i + 1
        return i % nslots

    def dma(out, in_, R, W):
        P.add("sp", lambda e: e.dma_start(out=out, in_=in_), R=R, W=W, dma=True)

    def mm(out, lhsT, rhs, start, stop, R, W, **kw):
        P.add("pe", lambda e: e.matmul(out, lhsT=lhsT, rhs=rhs, start=start, stop=stop, **kw), R=R, W=W)

    def act(out, in_, func, R, W, bias=0.0, scale=1.0, accum_out=None):
        if accum_out is None:
            P.add("act", lambda e: e.activation(out=out, in_=in_, func=func, bias=bias, scale=scale), R=R, W=W)
        else:
            P.add("act", lambda e: e.activation(out=out, in_=in_, func=func, bias=bias, scale=scale, accum_out=accum_out),
                  R=R, W=W)

    def load_w(dst, dst_keys, src_ap):
        s = slot("stg", 2)
        n = 1
        for d_ in src_ap.shape[1:]:
            n *= d_
        sv = stg[:, s, 0:n]
        if len(src_ap.shape) == 3:
            sv = sv.rearrange("p (a b) -> p a b", a=src_ap.shape[1])
        dma(sv, src_ap, R=[], W=[("stg", s)])
        P.add("pool", lambda e: e.tensor_copy(out=dst, in_=sv), R=[("stg", s)], W=dst_keys)

    for kc in range(8):
        dma(xT[:, kc, :], x_d[kc * 128:(kc + 1) * 128, :], R=[], W=[("xT", kc)])
    for nm, w in _CONST_SHAPES:
        if CST[nm].dtype == F32:
            dma(CST[nm][:, :], cst_d[nm][:, :], R=[], W=[("k", nm)])
        else:
            for o in range(0, w, 512):
                ww = min(512, w - o)
                s = slot("t1", 2)
                dma(t1[:, s, 0:ww], cst_d[nm][:, o:o + ww], R=[], W=[("t1", s)])
                P.add("pool", lambda e, nm=nm, o=o, ww=ww, s=s: e.tensor_copy(out=CST[nm][:, o:o + ww], in_=t1[:, s, 0:ww]),
                      R=[("t1", s)], W=[("k", nm)])
    P.add("pool", lambda e: e.memset(epsc[:, 0:1], EPS), W=[("epsc",)])
    P.add("pool", lambda e: e.memset(epsc[:, 1:2], 1.0), W=[("epsc",)])
    KEPS = [("epsc",)]
    eps_ap = epsc[:, 0:1]
    one_ap = epsc[:, 1:2]

    def norm_phase(gcol0):
        for c in range(NCH):
            cs = slice(c * 512, (c + 1) * 512)
            bk, bkk = bank((0, 1))
            for kc in range(8):
                s = slot("sqt", 2)
                act(sqt[:, s, :], xT[:, kc, cs], AF.Square, R=[("xT", kc, c)], W=[("sqt", s)])
                mm(bk[:, :], CST["ones"][:, :], sqt[:, s, :], kc == 0, kc == 7, R=[("sqt", s), ("k", "ones")], W=[bkk])
            s = slot("rs", 2)
            act(rs[:, s, :], bk[:, :], AF.Ln, R=[bkk] + KEPS, W=[("rs", s)], bias=eps_ap, scale=1.0 / D)
            act(rs[:, s, :], rs[:, s, :], AF.Exp, R=[("rs", s)], W=[("rs", s)], scale=-0.5)
            for kc in range(8):
                P.add("dve", lambda e, kc=kc, cs=cs, s=s: e.scalar_tensor_tensor(
                    out=hT[:, kc, cs], in0=xT[:, kc, cs], scalar=ppt[:, gcol0 + kc:gcol0 + kc + 1], in1=rs[:, s, :],
                    op0=ALU.mult, op1=ALU.mult), R=[("xT", kc, c), ("ppt",), ("rs", s)], W=[("hT", kc, c)])

    def ffn_phase(li):
        groups = [(g * 512, 4) for g in range(5)] + [(2560, 2)]
        Wgu = [rv(0, 16, BF16).rearrange("p (k n) -> p k n", k=8), rv(16, 16, BF16).rearrange("p (k n) -> p k n", k=8)]
        Wgu_k = [rk(0, 16), rk(16, 16)]
        Wd = [rv(32, 8, BF16).rearrange("p (k n) -> p k n", k=4), rv(40, 8, BF16).rearrange("p (k n) -> p k n", k=4)]
        Wd_k = [rk(32, 8), rk(40, 8)]
        actT = rv(48, 8, BF16).rearrange("p (s k n) -> p s k n", s=2, k=4)
        actT_k = [rk(48, 4), rk(52, 4)]
        win = w_gu_d[li].rearrange("(kc p) n -> p kc n", p=128)

        def load_group(gi):
            h0, nt = groups[gi]
            sl = gi % 2
            for t in range(nt):
                load_w(Wgu[sl][:, :, t * 128:(t + 1) * 128], Wgu_k[sl], win[:, :, h0 + t * 128:h0 + (t + 1) * 128])
                load_w(Wgu[sl][:, :, 512 + t * 128:512 + (t + 1) * 128], Wgu_k[sl],
                       win[:, :, FFN_H + h0 + t * 128:FFN_H + h0 + (t + 1) * 128])
                load_w(Wd[sl][:, t, :], Wd_k[sl], w_dn_d[li, h0 + t * 128:h0 + (t + 1) * 128, :])

        load_group(0)
        for gi in range(len(groups)):
            if gi + 1 < len(groups):
                load_group(gi + 1)
            h0, nt = groups[gi]
            sl = gi % 2
            for c in range(NCH):
                cs = slice(c * 512, (c + 1) * 512)
                asl = slot("actT", 2)
                for t in range(nt):
                    gb, gbk = bank((0, 1, 2, 3))
                    ub, ubk = bank((0, 1, 2, 3))
                    for kc in range(8):
                        mm(gb[:, :], Wgu[sl][:, kc, t * 128:(t + 1) * 128], hT[:, kc, cs], kc == 0, kc == 7,
                           R=Wgu_k[sl] + [("hT", kc, c)], W=[gbk])
                    for kc in range(8):
                        mm(ub[:, :], Wgu[sl][:, kc, 512 + t * 128:512 + (t + 1) * 128], hT[:, kc, cs], kc == 0, kc == 7,
                           R=Wgu_k[sl] + [("hT", kc, c)], W=[ubk])
                    s = slot("t1", 2)
                    act(t1[:, s, :], gb[:, :], AF.Silu, R=[gbk], W=[("t1", s)])
                    P.add("dve", lambda e, s=s, ub=ub, asl=asl, t=t: e.tensor_tensor(
                        out=actT[:, asl, t, :], in0=ub[:, :], in1=t1[:, s, :], op=ALU.mult),
                        R=[ubk, ("t1", s)], W=actT_k[asl])
                for d_ in range(8):
                    db, dbk = bank((4, 5, 6, 7))
                    for t in range(nt):
                        mm(db[:, :], Wd[sl][:, t, d_ * 128:(d_ + 1) * 128], actT[:, asl, t, :], t == 0, t == nt - 1,
                           R=Wd_k[sl] + actT_k[asl], W=[dbk])
                    P.add("dve", lambda e, d_=d_, cs=cs, db=db: e.tensor_tensor(
                        out=xT[:, d_, cs], in0=db[:, :], in1=xT[:, d_, cs], op=ALU.add),
                        R=[dbk, ("xT", d_, c)], W=[("xT", d_, c)])


    def apb(base_ap, mid):
        a = base_ap.ap
        return bass.AP(base_ap.tensor, base_ap.offset, [list(a[0]), [0, mid], list(a[-1])])

    def tap(name, src, R):
        if name in dbg:
            dma(dbg[name], src, R=R, W=[("dbg", name)])

    def derive_phase(labs):
        lam_init = 0.8 - 0.6 * math.exp(-0.3 * labs)
        for col, src, mul in ((0, 0, 0.125), (1, 2, 32.0 ** -0.5), (2, 4, 0.125), (3, 6, 1.0 - lam_init)):
            P.add("dve", lambda e, col=col, src=src, mul=mul: e.tensor_scalar(
                out=der[:, col:col + 1], in0=ppt[:, src:src + 1], scalar1=mul, scalar2=None, op0=ALU.mult),
                R=[("ppt",)], W=[("der", col)])
        for q, (a, b) in enumerate(((29, 61), (93, 125))):
            P.add("dve", lambda e, q=q, a=a, b=b: e.tensor_tensor(
                out=lam4[:, q, :], in0=ppt[:, a:a + 32], in1=ppt[:, b:b + 32], op=ALU.mult), R=[("ppt",)], W=[("lam4", q)])
            P.add("dve", lambda e, q=q: e.tensor_reduce(out=der[:, 5 + q:6 + q], in_=lam4[:, q, :], axis=AX.X, op=ALU.add),
                  R=[("lam4", q)], W=[("der", 5 + q)])
            act(der[:, 5 + q:6 + q], der[:, 5 + q:6 + q], AF.Exp, R=[("der", 5 + q)], W=[("der", 5 + q)])
        P.add("dve", lambda e: e.tensor_tensor(out=der[:, 4:5], in0=der[:, 6:7], in1=der[:, 5:6], op=ALU.subtract),
              R=[("der", 5), ("der", 6)], W=[("der", 4)])
        P.add("dve", lambda e: e.tensor_scalar(out=der[:, 4:5], in0=der[:, 4:5], scalar1=-lam_init, scalar2=None, op0=ALU.add),
              R=[("der", 4)], W=[("der", 4)])

    def proj_phase(li):
        win = w_in_d[li].rearrange("(kc p) n -> p kc n", p=128)
        WT = [rv(0, 2, BF16).rearrange("p (k n) -> p k n", k=8), rv(2, 2, BF16).rearrange("p (k n) -> p k n", k=8)]
        WT_k = [[("R", 0, 0)], [("R", 0, 1)]]
        Wv = rv(4, 6, BF16).rearrange("p (k n) -> p k n", k=8)
        Wv_k = rk(4, 6)
        ost = rv(12, 4, BF16).rearrange("p (s n) -> p s n", s=4)
        ost_k = [[("R", 3, s_)] for s_ in range(4)]
        Ct = rv(36, 8)
        St = rv(44, 8)
        Ct_k, St_k = rk(36, 8), rk(44, 8)
        FM = []
        for p_ in range(3):
            FM.append(dict(sc=p_, segs=[(O_FQ + 128 * p_, 128)], norm=(64, "bones64", der, 0, ("der", 0)), rope=None))
        for p_ in range(3):
            FM.append(dict(sc=3 + p_, segs=[(O_FK + 128 * p_, 128)], norm=(64, "bones64", ppt, 1, ("ppt",)), rope=None))
        for p_ in range(2):
            FM.append(dict(sc=6 + p_, segs=[(O_DQ + 128 * p_, 128)], norm=(32, "bones32", der, 1, ("der", 1)), rope=32))
        for p_ in range(2):
            FM.append(dict(sc=8 + p_, segs=[(O_DK + 128 * p_, 128)], norm=(32, "bones32", ppt, 3, ("ppt",)), rope=32))
        for p_ in range(3):
            FM.append(dict(sc=10 + p_, segs=[(O_SQ + 128 * p_, 128)], norm=(64, "bones64", der, 2, ("der", 2)), rope=64))
        FM.append(dict(sc=13, segs=[(O_SK, 64), (O_SK, 64)], norm=(64, "bones64", ppt, 5, ("ppt",)), rope=64))
        FM.append(dict(sc=14, segs=[(O_IK, 64), (O_IK, 64)], norm=None, rope=64))
        for p_ in range(2):
            FM.append(dict(sc=15 + p_, segs=[(O_IQ + 128 * p_, 128)], norm=None, rope=64))

        def load_tile(ti):
            sl = ti % 2
            o = 0
            for col0, n_ in FM[ti]["segs"]:
                load_w(WT[sl][:, :, o:o + n_], WT_k[sl], win[:, :, col0:col0 + n_])
                o += n_

        cur_rope = None
        SKB = Skew(1)
        SKC = Skew(1)
        load_tile(0)
        for ti, T in enumerate(FM):
            if ti + 1 < len(FM):
                load_tile(ti + 1)
            sl = ti % 2
            if T["rope"] is not None and T["rope"] != cur_rope:
                SKB.flush()
                SKC.flush()
                cur_rope = T["rope"]
                ro = 0 if cur_rope == 32 else 2
                dma(Ct, rope_d[ro], R=[], W=Ct_k)
                dma(St, rope_d[ro + 1], R=[], W=St_k)
            for c in range(NCH):
                cs = slice(c * 512, (c + 1) * 512)
                pj, pjk = bank((0, 1, 2))
                for kc in range(8):
                    mm(pj[:, :], WT[sl][:, kc, :], hT[:, kc, cs], kc == 0, kc == 7, R=WT_k[sl] + [("hT", kc, c)], W=[pjk])
                sq_ = None
                if T["norm"] is not None:
                    sq_ = slot("sqt", 2)
                    act(sqt[:, sq_, :], pj[:, :], AF.Square, R=[pjk], W=[("sqt", sq_)])

                def stageB(T=T, c=c, cs=cs, pj=pj, pjk=pjk, sq_=sq_):
                    os_ = slot("ost", 4)
                    qs_ = None
                    if T["rope"] is not None:
                        qs_ = slot("qn", 2)
                        tgt, tgtk = qn[:, qs_, :], [("qn", qs_)]
                    else:
                        tgt, tgtk = ost[:, os_, :], ost_k[os_]
                    if T["norm"] is not None:
                        bs_, bname, gt, gcol, gkey = T["norm"]
                        sb_, sbk = bank((3, 4))
                        mm(sb_[:, :], CST[bname][:, :], sqt[:, sq_, :], True, True, R=[("sqt", sq_), ("k", bname)], W=[sbk])
                        r_ = slot("rs", 2)
                        act(rs[:, r_, :], sb_[:, :], AF.Ln, R=[sbk] + KEPS, W=[("rs", r_)], bias=eps_ap, scale=1.0 / bs_)
                        act(rs[:, r_, :], rs[:, r_, :], AF.Exp, R=[("rs", r_)], W=[("rs", r_)], scale=-0.5)
                        P.add("dve", lambda e: e.scalar_tensor_tensor(
                            out=tgt, in0=pj[:, :], scalar=gt[:, gcol:gcol + 1], in1=rs[:, r_, :], op0=ALU.mult, op1=ALU.mult),
                            R=[pjk, gkey, ("rs", r_)], W=tgtk)
                    else:
                        act(tgt, pj[:, :], AF.Copy, R=[pjk], W=tgtk)

                    def stageC():
                        if T["rope"] is not None:
                            pname = "prot%d" % T["rope"]
                            rp, rpk = bank((5, 6))
                            mm(rp[:, :], CST[pname][:, :], qn[:, qs_, :], True, True, R=[("qn", qs_), ("k", pname)], W=[rpk])
                            a_ = slot("t1", 2)
                            b_ = slot("t2", 2)
                            P.add("dve", lambda e: e.tensor_tensor(out=t1[:, a_, :], in0=rp[:, :], in1=St[:, cs], op=ALU.mult),
                                  R=[rpk] + St_k, W=[("t1", a_)])
                            P.add("pool", lambda e: e.tensor_tensor(out=t2[:, b_, :], in0=qn[:, qs_, :], in1=Ct[:, cs], op=ALU.mult),
                                  R=[("qn", qs_)] + Ct_k, W=[("t2", b_)])
                            P.add("pool", lambda e: e.tensor_tensor(out=ost[:, os_, :], in0=t1[:, a_, :], in1=t2[:, b_, :], op=ALU.add),
                                  R=[("t1", a_), ("t2", b_)], W=ost_k[os_])
                        dma(sc_fm[T["sc"], :, cs], ost[:, os_, :], R=ost_k[os_], W=[("scfm", T["sc"], c)])
                    SKC.push(stageC)
                SKB.push(stageB)
        SKB.flush()
        SKC.flush()

        def tm_group(col_segs, ncols, handler):
            o = 0
            for col0, n_ in col_segs:
                load_w(Wv[:, :, o:o + n_], Wv_k, win[:, :, col0:col0 + n_])
                o += n_
            handler()

        def v_pairs(npairs, sc0):
            for a in range(npairs):
                for jg in range(4):
                    tv, tvk = bank((6, 7))
                    for jj in range(4):
                        j = jg * 4 + jj
                        for kc in range(8):
                            mm(tv[:, jj * 128:(jj + 1) * 128], hT[:, kc, j * 128:(j + 1) * 128], Wv[:, kc, a * 128:(a + 1) * 128],
                               kc == 0, kc == 7, R=Wv_k + [("hT", kc, jg)], W=[tvk])
                    os_ = slot("ost", 4)
                    act(ost[:, os_, :], tv[:, :], AF.Copy, R=[tvk], W=ost_k[os_])
                    dma(sc_v[sc0 + a, :, jg * 4:(jg + 1) * 4, :], ost[:, os_, :].rearrange("p (j n) -> p j n", j=4),
                        R=ost_k[os_], W=[("scv", sc0 + a, jg)])

        tm_group([(O_FV, 128), (O_FV + 128, 128), (O_FV + 256, 128)], 384, lambda: v_pairs(3, 0))
        tm_group([(O_DV, 128), (O_DV + 128, 128)], 256, lambda: v_pairs(2, 3))

        def small():
            for jg in range(4):
                tv, tvk = bank((6, 7))
                for jj in range(4):
                    j = jg * 4 + jj
                    for kc in range(8):
                        mm(tv[:, jj * 74:(jj + 1) * 74], hT[:, kc, j * 128:(j + 1) * 128], Wv[:, kc, 0:74],
                           kc == 0, kc == 7, R=Wv_k + [("hT", kc, jg)], W=[tvk])
                tv3 = tv[:, 0:296].rearrange("p (j n) -> p j n", j=4)
                os_ = slot("ost", 4)
                act(ost[:, os_, 0:256].rearrange("p (j n) -> p j n", j=4), tv3[:, :, 0:64], AF.Copy, R=[tvk], W=ost_k[os_])
                dma(sc_sv[:, jg * 4:(jg + 1) * 4, :], ost[:, os_, 0:256].rearrange("p (j n) -> p j n", j=4),
                    R=ost_k[os_], W=[("scsv", jg)])
                P.add("dve", lambda e, jg=jg, tv3=tv3: e.tensor_copy(out=FFt[:, jg * 4:(jg + 1) * 4, :], in_=tv3[:, :, 64:70]),
                      R=[tvk], W=[("FFt", jg)])
                P.add("dve", lambda e, jg=jg, tv3=tv3: e.tensor_scalar(
                    out=IWs[:, jg * 4:(jg + 1) * 4, :], in0=tv3[:, :, 70:74], scalar1=0.0625, scalar2=None, op0=ALU.mult),
                    R=[tvk], W=[("IWs", jg)])

        tm_group([(O_SV, 64), (O_FF, 6), (O_IW, 4)], 74, small)

    Qp = [rv(0, 4, BF16), rv(16, 4, BF16)]
    Kp = [rv(4, 4, BF16), rv(20, 4, BF16)]
    Vp = [rv(8, 8, BF16).rearrange("p (j s d) -> p j s d", j=16, s=4), rv(24, 8, BF16).rearrange("p (j s d) -> p j s d", j=16, s=4)]
    Qp_k, Kp_k, Vp_k = [rk(0, 4), rk(16, 4)], [rk(4, 4), rk(20, 4)], [rk(8, 8), rk(24, 8)]
    identb = CST["irep"][:, 0:128]

    def load_pair(sl, qi, ki, vi):
        dma(Qp[sl], sc_fm[qi], R=[("scfm", qi)], W=Qp_k[sl])
        dma(Kp[sl], sc_fm[ki], R=[("scfm", ki)], W=Kp_k[sl])
        dma(Vp[sl][:, :, 0, :], sc_v[vi][:, :, 0:64], R=[("scv", vi)], W=Vp_k[sl])
        dma(Vp[sl][:, :, 3, :], sc_v[vi][:, :, 64:128], R=[("scv", vi)], W=Vp_k[sl])
        P.add("pool", lambda e: e.memset(Vp[sl][:, :, 1:3, :], 1.0), W=Vp_k[sl])

    SKA = Skew(2)

    def attn_map(sl, e_, base, kdim, c, h, fox, acc, acck):
        nj = 4 * c + 4
        for j in range(nj):
            n0 = max(j * 128, c * 512)
            wN = (c + 1) * 512 - n0
            off = n0 - c * 512
            diag = j >= 4 * c
            st, stk = bank((4, 5, 6, 7))
            kw = {}
            if kdim == 32:
                kw = dict(tile_position=(base, 0))
            mm(st[:, off:off + wN], Kp[sl][base:base + kdim, j * 128:(j + 1) * 128], Qp[sl][base:base + kdim, n0:n0 + wN],
               True, (not fox) and (not diag), R=Kp_k[sl] + Qp_k[sl], W=[stk], **kw)
            if fox:
                mm(st[:, off:off + wN], selv[0:96, h, :], CT[0:96, n0:n0 + wN], False, not diag,
                   R=[("k", "sel")] + CT_k, W=[stk])
            if diag:
                mm(st[:, off:off + 128], identb, CST["cb"][:, :], False, True, R=[("k", "irep"), ("k", "cb")], W=[stk])
            ps_ = slot("pt", 4)
            if fox:
                act(pt[:, ps_, off:off + wN], st[:, off:off + wN], AF.Exp, R=[stk, ("negc",)], W=[("pt", ps_)],
                    bias=negc[:, j, h:h + 1])
            else:
                act(pt[:, ps_, off:off + wN], st[:, off:off + wN], AF.Exp, R=[stk], W=[("pt", ps_)])
            SKA.push(lambda j=j, off=off, wN=wN, ps_=ps_: mm(
                acc[:, off:off + wN], Vp[sl][:, j, 2 * e_:2 * e_ + 2, :].rearrange("p s d -> p (s d)"), pt[:, ps_, off:off + wN],
                j == 0, j == nj - 1, R=Vp_k[sl] + [("pt", ps_)], W=[acck]))

    CT = rv(36, 4, BF16)
    CT_k = rk(36, 4)
    CS = rv(40, 6).rearrange("p (j n) -> p j n", j=16)
    CS_k = rk(40, 6)
    selv = CST["sel"][:, :].rearrange("p (h n) -> p h n", h=6)

    def fox_phase():
        P.add("dve", lambda e: e.tensor_tensor(out=Lt[:, :, :], in0=FFt[:, :, :], in1=apb(ppt[:, 23:29], 16), op=ALU.add),
              R=[("FFt",), ("ppt",)], W=[("Lt",)])
        act(Lt[:, :, :], Lt[:, :, :], AF.Exp, R=[("Lt",)], W=[("Lt",)], scale=-1.0)
        act(Lt[:, :, :], Lt[:, :, :], AF.Ln, R=[("Lt",)] + KEPS, W=[("Lt",)], bias=one_ap)
        cps, cpsk = bank((0,))
        for i in range(16):
            for j in range(i + 1):
                mm(cps[:, i * 6:(i + 1) * 6], CST["tri" if j == i else "negones"][:, :], Lt[:, j, :], j == 0, j == i,
                   R=[("Lt",), ("k", "tri"), ("k", "negones")], W=[cpsk])
        cps3 = cps[:, 0:96].rearrange("p (j n) -> p j n", j=16)
        P.add("dve", lambda e: e.tensor_scalar(out=negc[:, :, :], in0=cps3, scalar1=-1.0, scalar2=None, op0=ALU.mult),
              R=[cpsk], W=[("negc",)])
        P.add("pool", lambda e: e.memset(CS[:, :, :], 0.0), W=CS_k)
        P.add("dve", lambda e: e.tensor_copy(out=chb[:, :, :], in_=cps3), R=[cpsk], W=[("chb",)])
        P.add("dve", lambda e: e.tensor_copy(out=CS[:, :, 0:6], in_=chb[:, :, :]), R=[("chb",)], W=CS_k)
        P.add("dve", lambda e: e.tensor_tensor(out=r1t[:, :, :], in0=cps3, in1=chb[:, :, :], op=ALU.subtract),
              R=[cpsk, ("chb",)], W=[("r1t",)])
        P.add("dve", lambda e: e.tensor_copy(out=chb[:, :, :], in_=r1t[:, :, :]), R=[("r1t",)], W=[("chb",)])
        P.add("dve", lambda e: e.tensor_copy(out=CS[:, :, 32:38], in_=chb[:, :, :]), R=[("chb",)], W=CS_k)
        P.add("dve", lambda e: e.tensor_tensor(out=r1t[:, :, :], in0=r1t[:, :, :], in1=chb[:, :, :], op=ALU.subtract),
              R=[("r1t",), ("chb",)], W=[("r1t",)])
        P.add("dve", lambda e: e.tensor_copy(out=chb[:, :, :], in_=r1t[:, :, :]), R=[("r1t",)], W=[("chb",)])
        P.add("dve", lambda e: e.tensor_copy(out=CS[:, :, 64:70], in_=chb[:, :, :]), R=[("chb",)], W=CS_k)
        for q_ in range(4):
            ctp, ctpk = bank((1, 2, 3))
            for jj in range(4):
                j = q_ * 4 + jj
                P.add("pe", lambda e, ctp=ctp, jj=jj, j=j: e.transpose(
                    out=ctp[0:96, jj * 128:(jj + 1) * 128], in_=CS[:, j, :], identity=CST["ident"][:, :]),
                    R=CS_k + [("k", "ident")], W=[ctpk])
            act(CT[0:96, q_ * 512:(q_ + 1) * 512], ctp[0:96, :], AF.Copy, R=[ctpk], W=CT_k)
        tap("negc", negc[:, :, :], [("negc",)])
        load_pair(0, 0, 3, 0)
        for p_ in range(3):
            sl = p_ % 2
            SKA.flush()
            if p_ + 1 < 3:
                load_pair((p_ + 1) % 2, p_ + 1, 3 + p_ + 1, p_ + 1)
            for c in range(NCH):
                cs = slice(c * 512, (c + 1) * 512)
                for e_ in range(2):
                    base = 64 * e_
                    acc, acck = bank((0, 1, 2))
                    attn_map(sl, e_, base, 64, c, 2 * p_ + e_, True, acc, acck)
                    def fin(base=base, acc=acc, acck=acck, p_=p_, cs=cs, c=c):
                        O = slice(base, base + 64)
                        Dn = slice(64 - base, 128 - base)
                        rc = slot("rct", 2)
                        P.add("dve", lambda e: e.reciprocal(out=rct[O, rc, :], in_=acc[Dn, :]), R=[acck], W=[("rct", rc)])
                        P.add("dve", lambda e: e.tensor_tensor(out=hT[O, p_, cs], in0=acc[O, :], in1=rct[O, rc, :], op=ALU.mult),
                              R=[acck, ("rct", rc)], W=[("hT", p_, c)])
                    SKA.push(fin)
        SKA.flush()

    def diff_phase():
        load_pair(0, 6, 8, 3)
        for d_ in range(2):
            sl = d_ % 2
            SKA.flush()
            if d_ == 0:
                load_pair(1, 7, 9, 4)
            for c in range(NCH):
                cs = slice(c * 512, (c + 1) * 512)
                od = slot("t1", 2)
                for e_ in range(2):
                    accs = []
                    for m_ in range(2):
                        base = 64 * e_ + 32 * m_
                        acc, acck = bank((0, 1, 2))
                        attn_map(sl, e_, base, 32, c, 0, False, acc, acck)
                        accs.append((acc, acck))
                    def comb(e_=e_, accs=accs, od=od):
                        O = slice(64 * e_, 64 * e_ + 64)
                        Dn = slice(64 - 64 * e_, 128 - 64 * e_)
                        (a1, a1k), (a2, a2k) = accs
                        ra = slot("rct", 2)
                        rb = slot("rct", 2)
                        ob = slot("t2", 2)
                        P.add("dve", lambda e: e.reciprocal(out=rct[O, ra, :], in_=a1[Dn, :]), R=[a1k], W=[("rct", ra)])
                        P.add("dve", lambda e: e.reciprocal(out=rct[O, rb, :], in_=a2[Dn, :]), R=[a2k], W=[("rct", rb)])
                        P.add("dve", lambda e: e.tensor_scalar(out=rct[O, rb, :], in0=rct[O, rb, :], scalar1=der[O, 4:5],
                                                               scalar2=None, op0=ALU.mult),
                              R=[("rct", rb), ("der", 4)], W=[("rct", rb)])
                        P.add("dve", lambda e: e.tensor_tensor(out=t1[O, od, :], in0=a1[O, :], in1=rct[O, ra, :], op=ALU.mult),
                              R=[a1k, ("rct", ra)], W=[("t1", od)])
                        P.add("dve", lambda e: e.tensor_tensor(out=t2[O, ob, :], in0=a2[O, :], in1=rct[O, rb, :], op=ALU.mult),
                              R=[a2k, ("rct", rb)], W=[("t2", ob)])
                        P.add("pool", lambda e: e.tensor_tensor(out=t1[O, od, :], in0=t1[O, od, :], in1=t2[O, ob, :], op=ALU.add),
                              R=[("t1", od), ("t2", ob)], W=[("t1", od)])
                    SKA.push(comb)

                def subln(od=od, d_=d_, cs=cs, c=c):
                    sq_ = slot("sqt", 2)
                    act(sqt[:, sq_, :], t1[:, od, :], AF.Square, R=[("t1", od)], W=[("sqt", sq_)])
                    sb_, sbk = bank((3,))
                    mm(sb_[:, :], CST["bones64"][:, :], sqt[:, sq_, :], True, True, R=[("sqt", sq_), ("k", "bones64")], W=[sbk])
                    r_ = slot("rs", 2)
                    act(rs[:, r_, :], sb_[:, :], AF.Ln, R=[sbk] + KEPS, W=[("rs", r_)], bias=eps_ap, scale=1.0 / 64)
                    act(rs[:, r_, :], rs[:, r_, :], AF.Exp, R=[("rs", r_)], W=[("rs", r_)], scale=-0.5)
                    P.add("dve", lambda e: e.scalar_tensor_tensor(
                        out=hT[:, 3 + d_, cs], in0=t1[:, od, :], scalar=der[:, 3:4], in1=rs[:, r_, :], op0=ALU.mult, op1=ALU.mult),
                        R=[("t1", od), ("der", 3), ("rs", r_)], W=[("hT", 3 + d_, c)])
                SKA.push(subln)
        SKA.flush()

    def dsa_phase():
        Qs = rv(0, 12, BF16).rearrange("p (t n) -> p t n", t=3)
        Qs_k = rk(0, 12)
        SKK, SKK_k = rv(12, 4, BF16), rk(12, 4)
        IKK, IKK_k = rv(16, 4, BF16), rk(16, 4)
        IQ = rv(20, 8, BF16).rearrange("p (t n) -> p t n", t=2)
        IQ_k = rk(20, 8)
        SVa = rv(28, 6, BF16).rearrange("p (j s d) -> p j s d", j=16, s=3)
        SVa_k = rk(28, 6)
        accb, acc_k = rv(36, 8), rk(36, 8)
        MBs = [(rv(44, 4, BF16), rk(44, 4)), (rv(56, 4, BF16), rk(56, 4))]
        PERT, PERT_k = rv(48, 8), rk(48, 8)
        junk = t2[:, :, :].rearrange("p a n -> p (a n)").bitcast(BF16)
        junk_k = [("t2",)]
        for t_ in range(3):
            dma(Qs[:, t_, :], sc_fm[10 + t_], R=[("scfm", 10 + t_)], W=Qs_k)
        dma(SKK, sc_fm[13], R=[("scfm", 13)], W=SKK_k)
        dma(IKK, sc_fm[14], R=[("scfm", 14)], W=IKK_k)
        for t_ in range(2):
            dma(IQ[:, t_, :], sc_fm[15 + t_], R=[("scfm", 15 + t_)], W=IQ_k)
        dma(SVa[:, :, 0, :], sc_sv, R=[("scsv",)], W=SVa_k)
        dma(SVa[:, :, 2, :], sc_sv, R=[("scsv",)], W=SVa_k)
        P.add("pool", lambda e: e.memset(SVa[:, :, 1, :], 1.0), W=SVa_k)
        dma(PERT, pert_d, R=[], W=PERT_k)

        accs_ = [(accb, acc_k), (accB_t[:, :], [("accB",)])]
        junks = [(junk, junk_k), (t1[:, :, :].rearrange("p a n -> p (a n)").bitcast(BF16), [("t1",)])]

        def index_block(i):
            q = i % 2
            acc_, acck_ = accs_[q]
            nk = (i + 1) * 128
            nch = (nk + 511) // 512
            for hh in range(4):
                tl, base = hh // 2, 64 * (hh % 2)
                for m_ in range(nch):
                    w_ = min(512, nk - 512 * m_)
                    dp, dpk = bank((0, 1))
                    mm(dp[:, 0:w_], IQ[base:base + 64, tl, i * 128:(i + 1) * 128], IKK[base:base + 64, 512 * m_:512 * m_ + w_],
                       True, True, R=IQ_k + IKK_k, W=[dpk])
                    act(dp[:, 0:w_], dp[:, 0:w_], AF.Relu, R=[dpk], W=[dpk])
                    src = PERT if hh == 0 else acc_
                    srck = PERT_k if hh == 0 else acck_
                    P.add("dve", lambda e, dp=dp, w_=w_, m_=m_, hh=hh, src=src: e.scalar_tensor_tensor(
                        out=acc_[:, 512 * m_:512 * m_ + w_], in0=dp[:, 0:w_], scalar=IWs[:, i, hh:hh + 1],
                        in1=src[:, 512 * m_:512 * m_ + w_], op0=ALU.mult, op1=ALU.add),
                        R=[dpk, ("IWs",)] + srck, W=acck_)
            P.add("dve", lambda e: e.tensor_reduce(out=bis2[:, q, 0:1], in_=acc_[:, 0:nk], axis=AX.X, op=ALU.max,
                                                   apply_absolute_value=True), R=acck_, W=[("bis2", q, 0)])
            P.add("dve", lambda e: e.tensor_tensor(out=acc_[:, i * 128:(i + 1) * 128], in0=acc_[:, i * 128:(i + 1) * 128],
                                                   in1=CST["cbt"][:, :], op=ALU.add), R=acck_ + [("k", "cbt")], W=acck_)
            P.add("dve", lambda e: e.tensor_scalar(out=STt2[:, q, :], in0=CST["pow2"][:, :], scalar1=bis2[:, q, 0:1], scalar2=None,
                                                   op0=ALU.mult), R=[("bis2", q, 0), ("k", "pow2")], W=[("STt2", q)])
            P.add("dve", lambda e: e.memset(bis2[:, q, 1:2], 0.0), W=[("bis2", q, 1)])

        def bisect_pair(iA, iB):
            blocks = [b_ for b_ in (iA, iB) if b_ is not None]
            for k in range(KBIS):
                for i in blocks:
                    q = i % 2
                    acc_, acck_ = accs_[q]
                    jk, jkk = junks[q]
                    nk = (i + 1) * 128
                    if q == 0:
                        P.add("dve", lambda e, acc_=acc_, jk=jk, nk=nk, q=q: e.tensor_scalar(
                            out=jk[:, 0:nk], in0=acc_[:, 0:nk], scalar1=bis2[:, q, 1:2], scalar2=None,
                            op0=ALU.is_gt, op1=ALU.add, accum_out=bis2[:, q, 2:3]),
                            R=acck_ + [("bis2", q, 1)], W=jkk + [("bis2", q, 2)])
                    else:
                        act(jk[:, 0:nk], acc_[:, 0:nk], AF.Sign, R=acck_ + [("bis2", q, 1)], W=jkk + [("bis2", q, 2)],
                            bias=bis2[:, q, 1:2], scale=-1.0, accum_out=bis2[:, q, 2:3])
                for i in blocks:
                    q = i % 2
                    nk = (i + 1) * 128
                    if q == 0:
                        P.add("dve", lambda e, k=k, q=q: e.tensor_scalar(
                            out=bis2[:, q, 3:4], in0=bis2[:, q, 2:3], scalar1=TOPK - 0.5, scalar2=STt2[:, q, 32 + k:33 + k],
                            op0=ALU.is_gt, op1=ALU.mult), R=[("bis2", q, 2), ("STt2", q)], W=[("bis2", q, 3)])
                    else:
                        P.add("dve", lambda e, k=k, q=q, nk=nk: e.tensor_scalar(
                            out=bis2[:, q, 3:4], in0=bis2[:, q, 2:3], scalar1=float(nk - 2 * TOPK + 1),
                            scalar2=STt2[:, q, 32 + k:33 + k], op0=ALU.is_lt, op1=ALU.mult),
                            R=[("bis2", q, 2), ("STt2", q)], W=[("bis2", q, 3)])
                    P.add("dve", lambda e, k=k, q=q: e.scalar_tensor_tensor(
                        out=bis2[:, q, 1:2], in0=bis2[:, q, 3:4], scalar=STt2[:, q, k:k + 1], in1=bis2[:, q, 1:2],
                        op0=ALU.subtract, op1=ALU.add),
                        R=[("bis2", q, 3), ("bis2", q, 1), ("STt2", q)], W=[("bis2", q, 1)])
            for i in blocks:
                q = i % 2
                acc_, acck_ = accs_[q]
                MB, MB_k = MBs[q]
                nk = (i + 1) * 128
                P.add("dve", lambda e, acc_=acc_, MB=MB, nk=nk, q=q: e.tensor_scalar(
                    out=MB[:, 0:nk], in0=acc_[:, 0:nk], scalar1=bis2[:, q, 1:2], scalar2=NEG, op0=ALU.is_le, op1=ALU.mult),
                    R=acck_ + [("bis2", q, 1)], W=MB_k)

        SKD = Skew(1)

        def attend_block(i):
            MB, MB_k = MBs[i % 2]
            c = i // 4
            qs_ = slice(i * 128, (i + 1) * 128)
            accE, accEk = bank((6,))
            accO, accOk = bank((7,))
            for j in range(i + 1):
                ks_ = slice(j * 128, (j + 1) * 128)
                sts = []
                for par in range(2):
                    st, stk = bank((2, 3, 4, 5))
                    pr = slice(64 * par, 64 * par + 64)
                    for hi in range(3):
                        mm(st[:, hi * 128:(hi + 1) * 128], SKK[pr, ks_], Qs[pr, hi, qs_], hi == 0, False,
                           R=SKK_k + Qs_k, W=[stk])
                    mm(st[:, 0:384], MB[:, ks_], CST["irep"][:, 0:384], False, True, R=MB_k + [("k", "irep")], W=[stk])
                    sts.append((st, stk))
                pss = []
                for par in range(2):
                    ps_ = slot("pt", 4)
                    act(pt[:, ps_, 0:384], sts[par][0][:, 0:384], AF.Exp, R=[sts[par][1]], W=[("pt", ps_)])
                    pss.append(ps_)
                def pv(j=j, pss=pss):
                    mm(accE[:, 0:384], SVa[:, j, 0:2, :].rearrange("p s d -> p (s d)"), pt[:, pss[0], 0:384], j == 0, j == i,
                       R=SVa_k + [("pt", pss[0])], W=[accEk])
                    mm(accO[:, 0:384], SVa[:, j, 1:3, :].rearrange("p s d -> p (s d)"), pt[:, pss[1], 0:384], j == 0, j == i,
                       R=SVa_k + [("pt", pss[1])], W=[accOk])
                SKD.push(pv)

            def fin():
                for par, (acc, acck) in enumerate(((accE, accEk), (accO, accOk))):
                    O = slice(64 * par, 64 * par + 64)
                    Dn = slice(64 - 64 * par, 128 - 64 * par)
                    rc = slot("rct", 2)
                    P.add("dve", lambda e, O=O, Dn=Dn, rc=rc, acc=acc: e.reciprocal(out=rct[O, rc, 0:384], in_=acc[Dn, 0:384]),
                          R=[acck], W=[("rct", rc)])
                    P.add("dve", lambda e, O=O, rc=rc, acc=acc: e.tensor_tensor(
                        out=hT[O, 5:8, qs_], in0=acc[O, 0:384].rearrange("p (h n) -> p h n", h=3),
                        in1=rct[O, rc, 0:384].rearrange("p (h n) -> p h n", h=3), op=ALU.mult),
                        R=[acck, ("rct", rc)], W=[("hT", 5, c), ("hT", 6, c), ("hT", 7, c)])
            SKD.push(fin)

        index_block(0)
        index_block(1)
        for m_ in range(8):
            bisect_pair(2 * m_, 2 * m_ + 1)
            attend_block(2 * m_)
            if m_ + 1 < 8:
                index_block(2 * m_ + 2)
            attend_block(2 * m_ + 1)
            if m_ + 1 < 8:
                index_block(2 * m_ + 3)
            SKD.flush()
        SKD.flush()
        tap("acc15", accB_t[:, :], [("accB",)])
        tap("MB15", MBs[1][0], MBs[1][1])
        tap("bis15", bis2[:, 1, :], [("bis2",)])
        tap("STt", STt2[:, 1, :], [("STt2",)])

    def wout_phase(li):
        Wo = [rv(56, 2, BF16).rearrange("p (k n) -> p k n", k=8), rv(58, 2, BF16).rearrange("p (k n) -> p k n", k=8)]
        Wo_k = [[("R", 14, 0)], [("R", 14, 1)]]
        wsrc = w_out_d[li].rearrange("(kt p) n -> p kt n", p=128)
        load_w(Wo[0], Wo_k[0], wsrc[:, :, 0:128])
        for d_ in range(8):
            sl = d_ % 2
            if d_ + 1 < 8:
                load_w(Wo[1 - sl], Wo_k[1 - sl], wsrc[:, :, (d_ + 1) * 128:(d_ + 2) * 128])
            for c in range(NCH):
                cs = slice(c * 512, (c + 1) * 512)
                bk, bkk = bank((0, 1, 2, 3, 4, 5, 6, 7))
                for kt in range(8):
                    mm(bk[:, :], Wo[sl][:, kt, :], hT[:, kt, cs], kt == 0, kt == 7, R=Wo_k[sl] + [("hT", kt, c)], W=[bkk])
                P.add("dve", lambda e, d_=d_, cs=cs, bk=bk: e.tensor_tensor(
                    out=xT[:, d_, cs], in0=bk[:, :], in1=xT[:, d_, cs], op=ALU.add),
                    R=[bkk, ("xT", d_, c)], W=[("xT", d_, c)])

    for li, labs in enumerate(layer_ids):
        dma(ppt[:, :], pp_d[li], R=[], W=[("ppt",)])
        derive_phase(labs)
        norm_phase(7)
        proj_phase(li)
        if li == 0:
            tap("sc_fm", sc_fm, [("scfm",)])
            tap("sc_v", sc_v, [("scv",)])
            tap("sc_sv", sc_sv, [("scsv",)])
            tap("FFt", FFt[:, :, :], [("FFt",)])
            tap("IWs", IWs[:, :, :], [("IWs",)])
        fox_phase()
        diff_phase()
        dsa_phase()
        if li == 0:
            tap("cat", hT[:, :, :], [("hT",)])
        wout_phase(li)
        if li == 0:
            tap("x1", xT[:, :, :], [("xT",)])
        norm_phase(15)
        ffn_phase(li)

    for kc in range(8):
        dma(y_d[kc * 128:(kc + 1) * 128, :], xT[:, kc, :], R=[("xT", kc)], W=[("y", kc)])

    P.emit(nc, stack)
    stack.close()
    return nc, P.stats


_NC_CACHE = {}


def _get_nc(layer_ids):
    key = tuple(layer_ids)
    if key not in _NC_CACHE:
        _NC_CACHE[key] = build_nc(list(layer_ids))[0]
    return _NC_CACHE[key]


def kernel(**inputs):
    inp = {k: np.asarray(v) for k, v in inputs.items()}
    x = inp["x"].astype(np.float32, copy=False)
    B = x.shape[0]
    cst = _consts()
    base = {}
    for nm, w in _CONST_SHAPES:
        base["c_" + nm] = np.ascontiguousarray(cst[nm], dtype=np.float32)
    base["c_rope"] = cst["rope"]
    base["c_pert"] = cst["pert"]
    layer_ids = list(range(DEPTH))
    nc = _get_nc(layer_ids)
    base["w_in"] = np.ascontiguousarray(inp["w_in"], dtype=np.float32)
    base["w_out"] = np.ascontiguousarray(inp["w_out"], dtype=np.float32)
    base["w_gu"] = np.ascontiguousarray(inp["w_gate_up"], dtype=np.float32)
    base["w_dn"] = np.ascontiguousarray(inp["w_down"], dtype=np.float32)
    base["pp"] = np.stack([_pack_pp(inp, l) for l in layer_ids]).astype(np.float32)
    in_maps = []
    for b in range(B):
        m = dict(base)
        m["xT"] = np.ascontiguousarray(x[b].T)
        in_maps.append(m)
    res = run_bass_kernel_spmd(nc, in_maps, core_ids=list(range(B)))
    out = np.stack([np.asarray(r["yT"]).T for r in res.results]).astype(np.float32)
    return out
```

```python
import math
from contextlib import ExitStack
import numpy as np
import concourse.bass as bass
import concourse.mybir as mybir
from concourse.bass_utils import run_bass_kernel_spmd

F32 = mybir.dt.float32
BF16 = mybir.dt.bfloat16
ALU = mybir.AluOpType
AF = mybir.ActivationFunctionType
AX = mybir.AxisListType

D = 1024
S = 2048
DEPTH = 4
NCH = 4
FFN_H = 2816
INW = 2762
EPS = 1e-6
NEG = -30000.0
KBIS = 16
TOPK = 256
PERT_EPS = 2.0 ** -20
NPP = 157

O_FQ, O_FK, O_FV, O_FF = 0, 384, 768, 1152
O_DQ, O_DK, O_DV = 1158, 1414, 1670
O_SQ, O_SK, O_SV, O_IQ, O_IK, O_IW = 1926, 2310, 2374, 2438, 2694, 2758


class Prog:
    ENGS = ("pe", "act", "dve", "pool", "sp")
    NDMA = 24

    def __init__(self):
        self.ops = []

    def add(self, eng, fn, R=(), W=(), dma=False):
        R = tuple(R)
        W = tuple(W) + tuple(k for k in R if k[0] == "ps" and k not in W)
        R = tuple(k for k in R if k[0] != "ps")
        self.ops.append((eng, fn, R, W, dma))

    def _deps(self):
        lastw = {}
        readers = {}
        desc = {}
        deps_all = []

        def related(k):
            out = []
            for i in range(1, len(k)):
                p = k[:i]
                if p in lastw or p in readers:
                    out.append(p)
            out.extend(desc.get(k, ()))
            return out

        def register(k):
            if k in lastw or k in readers:
                return
            for i in range(1, len(k) + 1):
                desc.setdefault(k[:i], set()).add(k)

        for i, (eng, fn, R, W, dma) in enumerate(self.ops):
            d = set()
            for k in R:
                register(k)
                lastw.setdefault(k, None)
                for r in related(k):
                    w = lastw.get(r)
                    if w is not None:
                        d.add(w)
            for k in W:
                register(k)
                lastw.setdefault(k, None)
                for r in related(k):
                    w = lastw.get(r)
                    if w is not None:
                        d.add(w)
                    d.update(readers.get(r, ()))
            d.discard(i)
            for k in R:
                readers.setdefault(k, []).append(i)
            for k in W:
                lastw[k] = i
                for r in desc.get(k, ()):
                    if r in readers:
                        readers[r] = []
                    lastw[r] = i
            deps_all.append(d)
        return deps_all

    def emit(self, nc, stack):
        ops = self.ops
        deps_all = self._deps()
        n = len(ops)
        sig = [False] * n
        for i, d in enumerate(deps_all):
            e_i = ops[i][0]
            for j in d:
                if ops[j][0] == "pe" and e_i == "pe":
                    continue
                sig[j] = True
        dma_prev = {}
        dma_slot = {}
        nd = 0
        for i, op in enumerate(ops):
            if op[4]:
                s = nd % self.NDMA
                nd += 1
                dma_slot[i] = s
                if s in dma_prev:
                    deps_all[i].add(dma_prev[s])
                dma_prev[s] = i
                sig[i] = True
        EPOCH = 1000
        esems = {e: [] for e in ("pe", "act", "dve", "pool")}
        dsem = [stack.enter_context(nc.semaphore("dsem%d" % k)) for k in range(min(self.NDMA, max(nd, 1)))]
        count = {e: 0 for e in esems}
        dcount = [0] * self.NDMA
        known = {e: {} for e in self.ENGS}
        event = [None] * n
        vc = [None] * n
        plan = {e: [] for e in self.ENGS}
        nwaits = 0
        Z = (0, 0)

        def sem_of(src, ep):
            if isinstance(src, tuple):
                return dsem[src[1]]
            lst = esems[src]
            while len(lst) <= ep:
                lst.append(stack.enter_context(nc.semaphore("sem_%s_%d" % (src, len(lst)))))
            return lst[ep]

        for i, (eng, fn, R, W, dma) in enumerate(ops):
            kn = known[eng]
            wm = {}
            for j in sorted(deps_all[i]):
                if ops[j][0] == "pe" and eng == "pe":
                    continue
                src, val = event[j]
                if kn.get(src, Z) >= val:
                    continue
                if wm.get(src, Z) < val:
                    wm[src] = val
                for s2, v2 in vc[j].items():
                    if kn.get(s2, Z) < v2:
                        kn[s2] = v2
            nwaits += len(wm)
            inc = None
            if sig[i]:
                if dma:
                    s = dma_slot[i]
                    dcount[s] += 16
                    event[i] = (("d", s), (0, dcount[s]))
                    inc = (dsem[s], 16)
                else:
                    ep, cn = divmod(count[eng], EPOCH)
                    count[eng] += 1
                    event[i] = (eng, (ep, cn + 1))
                    inc = (sem_of(eng, ep), 1)
                v = dict(kn)
                v[event[i][0]] = event[i][1]
                vc[i] = v
            plan[eng].append((fn, [(sem_of(src, val[0]), val[1]) for src, val in wm.items()], inc))
        self.stats = dict(n_ops=n, n_waits=nwaits, counts=dict(count), n_dma=nd,
                          n_sems=len(dsem) + sum(len(v) for v in esems.values()))

        block = stack.enter_context(nc.Block())

        def run(engine, items):
            for fn, waits, inc in items:
                for sem, val in waits:
                    engine.wait_ge(sem, val)
                ins = fn(engine)
                if inc is not None:
                    ins.then_inc(inc[0], inc[1])

        @block.tensor
        def _(e):
            run(e, plan["pe"])

        @block.scalar
        def _(e):
            run(e, plan["act"])

        @block.vector
        def _(e):
            run(e, plan["dve"])

        @block.gpsimd
        def _(e):
            run(e, plan["pool"])

        @block.sync
        def _(e):
            run(e, plan["sp"])
            for s in range(len(dsem)):
                if dcount[s] > 0:
                    e.wait_ge(dsem[s], dcount[s])


class Skew:
    def __init__(self, lag):
        self.q = []
        self.lag = lag

    def push(self, fn):
        self.q.append(fn)
        while len(self.q) > self.lag:
            self.q.pop(0)()

    def flush(self):
        while self.q:
            self.q.pop(0)()


def _rope_tab(head_dim, rows_rep):
    rot = head_dim // 4
    half = rot // 2
    inv = (1.0 / (np.float32(500000.0) ** (np.arange(0, rot, 2, dtype=np.float32) / np.float32(rot)))).astype(np.float32)
    ang = np.arange(S, dtype=np.float32)[:, None] * inv[None, :]
    cos = np.cos(ang).astype(np.float32).T
    sin = np.sin(ang).astype(np.float32).T
    C = np.ones((128, S), np.float32)
    Sn = np.zeros((128, S), np.float32)
    for b in range(128 // head_dim):
        o = b * head_dim
        C[o:o + half] = cos
        C[o + half:o + rot] = cos
        Sn[o:o + half] = sin
        Sn[o + half:o + rot] = sin
    P = np.zeros((128, 128), np.float32)
    for b in range(128 // head_dim):
        o = b * head_dim
        for r in range(half):
            P[o + r + half, o + r] = -1.0
            P[o + r, o + r + half] = 1.0
    return C, Sn, P


def _consts():
    c = {}
    eye = np.eye(128, dtype=np.float32)
    idx = np.arange(128)
    c["ident"] = eye
    c["tri"] = -(idx[:, None] <= idx[None, :]).astype(np.float32)
    c["negones"] = -np.ones((128, 128), np.float32)
    c["ones"] = np.ones((128, 128), np.float32)
    b64 = np.zeros((128, 128), np.float32)
    b64[:64, :64] = 1
    b64[64:, 64:] = 1
    c["bones64"] = b64
    b32 = np.zeros((128, 128), np.float32)
    for b in range(4):
        b32[32 * b:32 * b + 32, 32 * b:32 * b + 32] = 1
    c["bones32"] = b32
    sel = np.zeros((128, 6, 128), np.float32)
    for h in range(6):
        sel[h, h, :] = 1
        sel[32 + h, h, :] = 1
        sel[64 + h, h, :] = 1
    c["sel"] = sel.reshape(128, 768)
    c["cb"] = np.where(idx[:, None] <= idx[None, :], 0.0, NEG).astype(np.float32)
    c["cbt"] = np.where(idx[None, :] <= idx[:, None], 0.0, NEG).astype(np.float32)
    c["irep"] = np.tile(eye, (1, 4))
    C64, S64, P64 = _rope_tab(64, 2)
    C32, S32, P32 = _rope_tab(32, 4)
    c["prot64"] = P64
    c["prot32"] = P32
    c["rope"] = np.stack([C32, S32, C64, S64]).astype(np.float32)
    c["pert"] = np.tile((-PERT_EPS * np.arange(S, dtype=np.float32))[None, :], (128, 1)).astype(np.float32)
    p2 = np.zeros((128, 64), np.float32)
    for k in range(32):
        p2[:, k] = 2.0 ** (-k)
        p2[:, 32 + k] = 2.0 ** (1 - k)
    c["pow2"] = p2
    return c


_CONST_SHAPES = [("ident", 128), ("tri", 128), ("negones", 128), ("ones", 128), ("bones64", 128), ("bones32", 128),
                 ("sel", 768), ("cb", 128), ("cbt", 128), ("irep", 512), ("prot64", 128), ("prot32", 128), ("pow2", 64)]


def _pack_pp(inp, l):
    pp = np.zeros((128, NPP), np.float32)
    p = np.arange(128)
    pp[:, 0] = inp["fox_qn"][l][p % 64]
    pp[:, 1] = inp["fox_kn"][l][p % 64]
    pp[:, 2] = inp["diff_qn"][l][p % 32]
    pp[:, 3] = inp["diff_kn"][l][p % 32]
    pp[:, 4] = inp["dsa_qn"][l][p % 64]
    pp[:, 5] = inp["dsa_kn"][l][p % 64]
    pp[:, 6] = inp["diff_subln"][l][p % 64]
    pp[:, 7:15] = inp["attn_norm"][l].reshape(8, 128).T
    pp[:, 15:23] = inp["ffn_norm"][l].reshape(8, 128).T
    pp[:, 23:29] = inp["fox_fb"][l][None, :]
    pp[:, 29:61] = inp["diff_lq1"][l][None, :]
    pp[:, 61:93] = inp["diff_lk1"][l][None, :]
    pp[:, 93:125] = inp["diff_lq2"][l][None, :]
    pp[:, 125:157] = inp["diff_lk2"][l][None, :]
    return pp


def build_nc(layer_ids, debug=None):
    NL = len(layer_ids)
    nc = bass.Bass("TRN2", target_bir_lowering=False)
    stack = ExitStack()
    P = Prog()

    def dram(name, shape, dt=F32, kind="ExternalInput"):
        return nc.dram_tensor(name, list(shape), dt, kind=kind).ap()

    x_d = dram("xT", [D, S])
    y_d = dram("yT", [D, S], kind="ExternalOutput")
    w_in_d = dram("w_in", [NL, D, INW])
    w_out_d = dram("w_out", [NL, D, D])
    w_gu_d = dram("w_gu", [NL, D, 2 * FFN_H])
    w_dn_d = dram("w_dn", [NL, FFN_H, D])
    pp_d = dram("pp", [NL, 128, NPP])
    cst_d = {nm: dram("c_" + nm, [128, w]) for nm, w in _CONST_SHAPES}
    rope_d = dram("c_rope", [4, 128, S])
    pert_d = dram("c_pert", [128, S])
    sc_fm = dram("sc_fm", [17, 128, S], BF16, kind="Internal")
    sc_v = dram("sc_v", [5, 128, 16, 128], BF16, kind="Internal")
    sc_sv = dram("sc_sv", [128, 16, 64], BF16, kind="Internal")
    dbg = {}
    if debug:
        for nm, shape, dt in debug:
            dbg[nm] = dram("dbg_" + nm, shape, dt, kind="ExternalOutput")

    def sb(name, shape, dt=F32):
        return stack.enter_context(nc.sbuf_tensor(name, list(shape), dt))

    def ps(name, shape, dt=F32):
        return stack.enter_context(nc.psum_tensor(name, list(shape), dt))

    xT = sb("xT_sb", [128, 8, S])
    hT = sb("hT_sb", [128, 8, S], BF16)
    RW = 15 * 1024
    Rg = sb("R_sb", [128, RW])
    stg = sb("stg_sb", [128, 2, 1024])
    banks = [ps("bank%d" % b, [128, 512]) for b in range(8)]

    def rv(off_kb, size_kb, dt=F32):
        a = Rg[:, off_kb * 256:(off_kb + size_kb) * 256]
        if dt == BF16:
            a = a.bitcast(BF16)
        return a

    def rk(off_kb, size_kb):
        return [("R", pg) for pg in range(off_kb // 4, (off_kb + size_kb + 3) // 4)]

    sqt = sb("sqt", [128, 2, 512], BF16)
    rs = sb("rs", [128, 2, 512])
    qn = sb("qn", [128, 3, 512], BF16)
    t1 = sb("t1", [128, 2, 512])
    t2 = sb("t2", [128, 2, 512])
    pt = sb("pt", [128, 4, 512], BF16)
    rct = sb("rct", [128, 2, 512])
    ppt = sb("ppt", [128, NPP])
    der = sb("der", [128, 16])
    FFt = sb("FFt", [128, 16, 6])
    IWs = sb("IWs", [128, 16, 4])
    Lt = sb("Lt", [128, 16, 6])
    negc = sb("negc", [128, 16, 6])
    chb = sb("chb", [128, 16, 6], BF16)
    r1t = sb("r1t", [128, 16, 6])
    bis = sb("bis", [128, 8])
    STt = sb("STt", [128, 64])
    lam4 = sb("lam4", [128, 4, 32])
    CST = {}
    for nm, w in _CONST_SHAPES:
        f32c = nm in ("ident", "tri", "negones", "cbt", "pow2")
        CST[nm] = sb("k_" + nm, [128, w], F32 if f32c else BF16)
    epsc = sb("epsc", [128, 2])
    accB_t = sb("accB_t", [128, 2048])
    bis2 = sb("bis2", [128, 2, 4])
    STt2 = sb("STt2", [128, 2, 64])

    bank_rr = {}

    def bank(pool):
        i = bank_rr.get(pool, 0)
        bank_rr[pool] = i + 1
        b = pool[i % len(pool)]
        return banks[b], ("ps", b)

    slot_rr = {}

    def slot(name, nslots):
        i = slot_rr.get(name, 0)
        slot_rr[name] = i + 1
        return i % nslots

    def dma(out, in_, R, W):
        P.add("sp", lambda e: e.dma_start(out=out, in_=in_), R=R, W=W, dma=True)

    def mm(out, lhsT, rhs, start, stop, R, W, **kw):
        P.add("pe", lambda e: e.matmul(out, lhsT=lhsT, rhs=rhs, start=start, stop=stop, **kw), R=R, W=W)

    def act(out, in_, func, R, W, bias=0.0, scale=1.0, accum_out=None):
        if accum_out is None:
            P.add("act", lambda e: e.activation(out=out, in_=in_, func=func, bias=bias, scale=scale), R=R, W=W)
        else:
            P.add("act", lambda e: e.activation(out=out, in_=in_, func=func, bias=bias, scale=scale, accum_out=accum_out),
                  R=R, W=W)

    def recip(out, in_, R, W):
        act(out, in_, AF.Ln, R=R, W=W)
        act(out, out, AF.Exp, R=W, W=W, scale=-1.0)

    def load_w(dst, dst_keys, src_ap):
        s = slot("stg", 2)
        n = 1
        for d_ in src_ap.shape[1:]:
            n *= d_
        sv = stg[:, s, 0:n]
        if len(src_ap.shape) == 3:
            sv = sv.rearrange("p (a b) -> p a b", a=src_ap.shape[1])
        dma(sv, src_ap, R=[], W=[("stg", s)])
        P.add("pool", lambda e: e.tensor_copy(out=dst, in_=sv), R=[("stg", s)], W=dst_keys)

    for kc in range(8):
        dma(xT[:, kc, :], x_d[kc * 128:(kc + 1) * 128, :], R=[], W=[("xT", kc)])
    for nm, w in _CONST_SHAPES:
        if CST[nm].dtype == F32:
            dma(CST[nm][:, :], cst_d[nm][:, :], R=[], W=[("k", nm)])
        else:
            for o in range(0, w, 512):
                ww = min(512, w - o)
                s = slot("t1", 2)
                dma(t1[:, s, 0:ww], cst_d[nm][:, o:o + ww], R=[], W=[("t1", s)])
                P.add("pool", lambda e, nm=nm, o=o, ww=ww, s=s: e.tensor_copy(out=CST[nm][:, o:o + ww], in_=t1[:, s, 0:ww]),
                      R=[("t1", s)], W=[("k", nm)])
    P.add("pool", lambda e: e.memset(epsc[:, 0:1], EPS), W=[("epsc",)])
    P.add("pool", lambda e: e.memset(epsc[:, 1:2], 1.0), W=[("epsc",)])
    KEPS = [("epsc",)]
    eps_ap = epsc[:, 0:1]
    one_ap = epsc[:, 1:2]

    def norm_phase(gcol0):
        for c in range(NCH):
            cs = slice(c * 512, (c + 1) * 512)
            bk, bkk = bank((0, 1))
            for kc in range(8):
                s = slot("sqt", 2)
                act(sqt[:, s, :], xT[:, kc, cs], AF.Square, R=[("xT", kc, c)], W=[("sqt", s)])
                mm(bk[:, :], CST["ones"][:, :], sqt[:, s, :], kc == 0, kc == 7, R=[("sqt", s), ("k", "ones")], W=[bkk])
            s = slot("rs", 2)
            act(rs[:, s, :], bk[:, :], AF.Ln, R=[bkk] + KEPS, W=[("rs", s)], bias=eps_ap, scale=1.0 / D)
            act(rs[:, s, :], rs[:, s, :], AF.Exp, R=[("rs", s)], W=[("rs", s)], scale=-0.5)
            for kc in range(8):
                P.add("dve", lambda e, kc=kc, cs=cs, s=s: e.scalar_tensor_tensor(
                    out=hT[:, kc, cs], in0=xT[:, kc, cs], scalar=ppt[:, gcol0 + kc:gcol0 + kc + 1], in1=rs[:, s, :],
                    op0=ALU.mult, op1=ALU.mult), R=[("xT", kc, c), ("ppt",), ("rs", s)], W=[("hT", kc, c)])

    def ffn_phase(li):
        groups = [(g * 512, 4) for g in range(5)] + [(2560, 2)]
        Wgu = [rv(0, 16, BF16).rearrange("p (k n) -> p k n", k=8), rv(16, 16, BF16).rearrange("p (k n) -> p k n", k=8)]
        Wgu_k = [rk(0, 16), rk(16, 16)]
        Wd = [rv(32, 8, BF16).rearrange("p (k n) -> p k n", k=4), rv(40, 8, BF16).rearrange("p (k n) -> p k n", k=4)]
        Wd_k = [rk(32, 8), rk(40, 8)]
        actT = rv(48, 8, BF16).rearrange("p (s k n) -> p s k n", s=2, k=4)
        actT_k = [rk(48, 4), rk(52, 4)]
        win = w_gu_d[li].rearrange("(kc p) n -> p kc n", p=128)

        def load_group(gi):
            h0, nt = groups[gi]
            sl = gi % 2
            for t in range(nt):
                load_w(Wgu[sl][:, :, t * 128:(t + 1) * 128], Wgu_k[sl], win[:, :, h0 + t * 128:h0 + (t + 1) * 128])
                load_w(Wgu[sl][:, :, 512 + t * 128:512 + (t + 1) * 128], Wgu_k[sl],
                       win[:, :, FFN_H + h0 + t * 128:FFN_H + h0 + (t + 1) * 128])
                load_w(Wd[sl][:, t, :], Wd_k[sl], w_dn_d[li, h0 + t * 128:h0 + (t + 1) * 128, :])

        load_group(0)
        for gi in range(len(groups)):
            if gi + 1 < len(groups):
                load_group(gi + 1)
            h0, nt = groups[gi]
            sl = gi % 2
            for c in range(NCH):
                cs = slice(c * 512, (c + 1) * 512)
                asl = slot("actT", 2)
                for t in range(nt):
                    gb, gbk = bank((0, 1, 2, 3))
                    ub, ubk = bank((0, 1, 2, 3))
                    for kc in range(8):
                        mm(gb[:, :], Wgu[sl][:, kc, t * 128:(t + 1) * 128], hT[:, kc, cs], kc == 0, kc == 7,
                           R=Wgu_k[sl] + [("hT", kc, c)], W=[gbk])
                    for kc in range(8):
                        mm(ub[:, :], Wgu[sl][:, kc, 512 + t * 128:512 + (t + 1) * 128], hT[:, kc, cs], kc == 0, kc == 7,
                           R=Wgu_k[sl] + [("hT", kc, c)], W=[ubk])
                    s = slot("t1", 2)
                    act(t1[:, s, :], gb[:, :], AF.Silu, R=[gbk], W=[("t1", s)])
                    P.add("dve", lambda e, s=s, ub=ub, asl=asl, t=t: e.tensor_tensor(
                        out=actT[:, asl, t, :], in0=ub[:, :], in1=t1[:, s, :], op=ALU.mult),
                        R=[ubk, ("t1", s)], W=actT_k[asl])
                for d_ in range(8):
                    db, dbk = bank((4, 5, 6, 7))
                    for t in range(nt):
                        mm(db[:, :], Wd[sl][:, t, d_ * 128:(d_ + 1) * 128], actT[:, asl, t, :], t == 0, t == nt - 1,
                           R=Wd_k[sl] + actT_k[asl], W=[dbk])
                    P.add("dve", lambda e, d_=d_, cs=cs, db=db: e.tensor_tensor(
                        out=xT[:, d_, cs], in0=db[:, :], in1=xT[:, d_, cs], op=ALU.add),
                        R=[dbk, ("xT", d_, c)], W=[("xT", d_, c)])


    def apb(base_ap, mid):
        a = base_ap.ap
        return bass.AP(base_ap.tensor, base_ap.offset, [list(a[0]), [0, mid], list(a[-1])])

    def tap(name, src, R):
        if name in dbg:
            dma(dbg[name], src, R=R, W=[("dbg", name)])

    def derive_phase(labs):
        lam_init = 0.8 - 0.6 * math.exp(-0.3 * labs)
        for col, src, mul in ((0, 0, 0.125), (1, 2, 32.0 ** -0.5), (2, 4, 0.125), (3, 6, 1.0 - lam_init)):
            P.add("dve", lambda e, col=col, src=src, mul=mul: e.tensor_scalar(
                out=der[:, col:col + 1], in0=ppt[:, src:src + 1], scalar1=mul, scalar2=None, op0=ALU.mult),
                R=[("ppt",)], W=[("der", col)])
        for q, (a, b) in enumerate(((29, 61), (93, 125))):
            P.add("dve", lambda e, q=q, a=a, b=b: e.tensor_tensor(
                out=lam4[:, q, :], in0=ppt[:, a:a + 32], in1=ppt[:, b:b + 32], op=ALU.mult), R=[("ppt",)], W=[("lam4", q)])
            P.add("dve", lambda e, q=q: e.tensor_reduce(out=der[:, 5 + q:6 + q], in_=lam4[:, q, :], axis=AX.X, op=ALU.add),
                  R=[("lam4", q)], W=[("der", 5 + q)])
            act(der[:, 5 + q:6 + q], der[:, 5 + q:6 + q], AF.Exp, R=[("der", 5 + q)], W=[("der", 5 + q)])
        P.add("dve", lambda e: e.tensor_tensor(out=der[:, 4:5], in0=der[:, 6:7], in1=der[:, 5:6], op=ALU.subtract),
              R=[("der", 5), ("der", 6)], W=[("der", 4)])
        P.add("dve", lambda e: e.tensor_scalar(out=der[:, 4:5], in0=der[:, 4:5], scalar1=-lam_init, scalar2=None, op0=ALU.add),
              R=[("der", 4)], W=[("der", 4)])

    def proj_phase(li):
        win = w_in_d[li].rearrange("(kc p) n -> p kc n", p=128)
        WT = [rv(0, 2, BF16).rearrange("p (k n) -> p k n", k=8), rv(2, 2, BF16).rearrange("p (k n) -> p k n", k=8)]
        WT_k = [[("R", 0, 0)], [("R", 0, 1)]]
        Wv = rv(4, 6, BF16).rearrange("p (k n) -> p k n", k=8)
        Wv_k = rk(4, 6)
        ost = rv(12, 4, BF16).rearrange("p (s n) -> p s n", s=4)
        ost_k = [[("R", 3, s_)] for s_ in range(4)]
        Ct = rv(36, 8)
        St = rv(44, 8)
        Ct_k, St_k = rk(36, 8), rk(44, 8)
        FM = []
        for p_ in range(3):
            FM.append(dict(sc=p_, segs=[(O_FQ + 128 * p_, 128)], norm=(64, "bones64", der, 0, ("der", 0)), rope=None))
        for p_ in range(3):
            FM.append(dict(sc=3 + p_, segs=[(O_FK + 128 * p_, 128)], norm=(64, "bones64", ppt, 1, ("ppt",)), rope=None))
        for p_ in range(2):
            FM.append(dict(sc=6 + p_, segs=[(O_DQ + 128 * p_, 128)], norm=(32, "bones32", der, 1, ("der", 1)), rope=32))
        for p_ in range(2):
            FM.append(dict(sc=8 + p_, segs=[(O_DK + 128 * p_, 128)], norm=(32, "bones32", ppt, 3, ("ppt",)), rope=32))
        for p_ in range(3):
            FM.append(dict(sc=10 + p_, segs=[(O_SQ + 128 * p_, 128)], norm=(64, "bones64", der, 2, ("der", 2)), rope=64))
        FM.append(dict(sc=13, segs=[(O_SK, 64), (O_SK, 64)], norm=(64, "bones64", ppt, 5, ("ppt",)), rope=64))
        FM.append(dict(sc=14, segs=[(O_IK, 64), (O_IK, 64)], norm=None, rope=64))
        for p_ in range(2):
            FM.append(dict(sc=15 + p_, segs=[(O_IQ + 128 * p_, 128)], norm=None, rope=64))

        def load_tile(ti):
            sl = ti % 2
            o = 0
            for col0, n_ in FM[ti]["segs"]:
                load_w(WT[sl][:, :, o:o + n_], WT_k[sl], win[:, :, col0:col0 + n_])
                o += n_

        cur_rope = None
        SKB = Skew(1)
        SKC = Skew(2)
        load_tile(0)
        for ti, T in enumerate(FM):
            if ti + 1 < len(FM):
                load_tile(ti + 1)
            sl = ti % 2
            if T["rope"] is not None and T["rope"] != cur_rope:
                SKB.flush()
                SKC.flush()
                cur_rope = T["rope"]
                ro = 0 if cur_rope == 32 else 2
                dma(Ct, rope_d[ro], R=[], W=Ct_k)
                dma(St, rope_d[ro + 1], R=[], W=St_k)
            for c in range(NCH):
                cs = slice(c * 512, (c + 1) * 512)
                pj, pjk = bank((0, 1, 2))
                for kc in range(8):
                    mm(pj[:, :], WT[sl][:, kc, :], hT[:, kc, cs], kc == 0, kc == 7, R=WT_k[sl] + [("hT", kc, c)], W=[pjk])
                sq_ = None
                if T["norm"] is not None:
                    sq_ = slot("sqt", 2)
                    act(sqt[:, sq_, :], pj[:, :], AF.Square, R=[pjk], W=[("sqt", sq_)])

                def stageB(T=T, c=c, cs=cs, pj=pj, pjk=pjk, sq_=sq_):
                    os_ = slot("ost", 4)
                    qs_ = None
                    if T["rope"] is not None:
                        qs_ = slot("qn", 3)
                        tgt, tgtk = qn[:, qs_, :], [("qn", qs_)]
                    else:
                        tgt, tgtk = ost[:, os_, :], ost_k[os_]
                    if T["norm"] is not None:
                        bs_, bname, gt, gcol, gkey = T["norm"]
                        sb_, sbk = bank((3, 4))
                        mm(sb_[:, :], CST[bname][:, :], sqt[:, sq_, :], True, True, R=[("sqt", sq_), ("k", bname)], W=[sbk])
                        r_ = slot("rs", 2)
                        act(rs[:, r_, :], sb_[:, :], AF.Ln, R=[sbk] + KEPS, W=[("rs", r_)], bias=eps_ap, scale=1.0 / bs_)
                        act(rs[:, r_, :], rs[:, r_, :], AF.Exp, R=[("rs", r_)], W=[("rs", r_)], scale=-0.5)
                        P.add("dve", lambda e: e.scalar_tensor_tensor(
                            out=tgt, in0=pj[:, :], scalar=gt[:, gcol:gcol + 1], in1=rs[:, r_, :], op0=ALU.mult, op1=ALU.mult),
                            R=[pjk, gkey, ("rs", r_)], W=tgtk)
                    else:
                        act(tgt, pj[:, :], AF.Copy, R=[pjk], W=tgtk)

                    def stageC():
                        if T["rope"] is not None:
                            pname = "prot%d" % T["rope"]
                            rp, rpk = bank((5, 6))
                            mm(rp[:, :], CST[pname][:, :], qn[:, qs_, :], True, True, R=[("qn", qs_), ("k", pname)], W=[rpk])
                            a_ = slot("t1", 2)
                            b_ = slot("t2", 2)
                            P.add("dve", lambda e: e.tensor_tensor(out=t1[:, a_, :], in0=rp[:, :], in1=St[:, cs], op=ALU.mult),
                                  R=[rpk] + St_k, W=[("t1", a_)])
                            P.add("pool", lambda e: e.tensor_tensor(out=t2[:, b_, :], in0=qn[:, qs_, :], in1=Ct[:, cs], op=ALU.mult),
                                  R=[("qn", qs_)] + Ct_k, W=[("t2", b_)])
                            P.add("dve", lambda e: e.tensor_tensor(out=ost[:, os_, :], in0=t1[:, a_, :], in1=t2[:, b_, :], op=ALU.add),
                                  R=[("t1", a_), ("t2", b_)], W=ost_k[os_])
                        dma(sc_fm[T["sc"], :, cs], ost[:, os_, :], R=ost_k[os_], W=[("scfm", T["sc"], c)])
                    SKC.push(stageC)
                SKB.push(stageB)
        SKB.flush()
        SKC.flush()

        def tm_group(col_segs, ncols, handler):
            o = 0
            for col0, n_ in col_segs:
                load_w(Wv[:, :, o:o + n_], Wv_k, win[:, :, col0:col0 + n_])
                o += n_
            handler()

        def v_pairs(npairs, sc0):
            for a in range(npairs):
                for jg in range(4):
                    tv, tvk = bank((6, 7))
                    for jj in range(4):
                        j = jg * 4 + jj
                        for kc in range(8):
                            mm(tv[:, jj * 128:(jj + 1) * 128], hT[:, kc, j * 128:(j + 1) * 128], Wv[:, kc, a * 128:(a + 1) * 128],
                               kc == 0, kc == 7, R=Wv_k + [("hT", kc, jg)], W=[tvk])
                    os_ = slot("ost", 4)
                    act(ost[:, os_, :], tv[:, :], AF.Copy, R=[tvk], W=ost_k[os_])
                    dma(sc_v[sc0 + a, :, jg * 4:(jg + 1) * 4, :], ost[:, os_, :].rearrange("p (j n) -> p j n", j=4),
                        R=ost_k[os_], W=[("scv", sc0 + a, jg)])

        tm_group([(O_FV, 128), (O_FV + 128, 128), (O_FV + 256, 128)], 384, lambda: v_pairs(3, 0))
        tm_group([(O_DV, 128), (O_DV + 128, 128)], 256, lambda: v_pairs(2, 3))

        def small():
            for jg in range(4):
                tv, tvk = bank((6, 7))
                for jj in range(4):
                    j = jg * 4 + jj
                    for kc in range(8):
                        mm(tv[:, jj * 74:(jj + 1) * 74], hT[:, kc, j * 128:(j + 1) * 128], Wv[:, kc, 0:74],
                           kc == 0, kc == 7, R=Wv_k + [("hT", kc, jg)], W=[tvk])
                tv3 = tv[:, 0:296].rearrange("p (j n) -> p j n", j=4)
                os_ = slot("ost", 4)
                act(ost[:, os_, 0:256].rearrange("p (j n) -> p j n", j=4), tv3[:, :, 0:64], AF.Copy, R=[tvk], W=ost_k[os_])
                dma(sc_sv[:, jg * 4:(jg + 1) * 4, :], ost[:, os_, 0:256].rearrange("p (j n) -> p j n", j=4),
                    R=ost_k[os_], W=[("scsv", jg)])
                P.add("dve", lambda e, jg=jg, tv3=tv3: e.tensor_copy(out=FFt[:, jg * 4:(jg + 1) * 4, :], in_=tv3[:, :, 64:70]),
                      R=[tvk], W=[("FFt", jg)])
                P.add("dve", lambda e, jg=jg, tv3=tv3: e.tensor_scalar(
                    out=IWs[:, jg * 4:(jg + 1) * 4, :], in0=tv3[:, :, 70:74], scalar1=0.0625, scalar2=None, op0=ALU.mult),
                    R=[tvk], W=[("IWs", jg)])

        tm_group([(O_SV, 64), (O_FF, 6), (O_IW, 4)], 74, small)

    Qp = [rv(0, 4, BF16), rv(16, 4, BF16)]
    Kp = [rv(4, 4, BF16), rv(20, 4, BF16)]
    Vp = [rv(8, 8, BF16).rearrange("p (j s d) -> p j s d", j=16, s=4), rv(24, 8, BF16).rearrange("p (j s d) -> p j s d", j=16, s=4)]
    Qp_k, Kp_k, Vp_k = [rk(0, 4), rk(16, 4)], [rk(4, 4), rk(20, 4)], [rk(8, 8), rk(24, 8)]
    identb = CST["irep"][:, 0:128]

    def load_pair(sl, qi, ki, vi):
        dma(Qp[sl], sc_fm[qi], R=[("scfm", qi)], W=Qp_k[sl])
        dma(Kp[sl], sc_fm[ki], R=[("scfm", ki)], W=Kp_k[sl])
        dma(Vp[sl][:, :, 0, :], sc_v[vi][:, :, 0:64], R=[("scv", vi)], W=Vp_k[sl])
        dma(Vp[sl][:, :, 3, :], sc_v[vi][:, :, 64:128], R=[("scv", vi)], W=Vp_k[sl])
        P.add("pool", lambda e: e.memset(Vp[sl][:, :, 1:3, :], 1.0), W=Vp_k[sl])

    SKA = Skew(2)

    def attn_map(sl, e_, base, kdim, c, h, fox, acc, acck):
        nj = 4 * c + 4
        for j in range(nj):
            n0 = max(j * 128, c * 512)
            wN = (c + 1) * 512 - n0
            off = n0 - c * 512
            diag = j >= 4 * c
            st, stk = bank((4, 5, 6, 7))
            kw = {}
            if kdim == 32:
                kw = dict(tile_position=(base, 0))
            mm(st[:, off:off + wN], Kp[sl][base:base + kdim, j * 128:(j + 1) * 128], Qp[sl][base:base + kdim, n0:n0 + wN],
               True, (not fox) and (not diag), R=Kp_k[sl] + Qp_k[sl], W=[stk], **kw)
            if fox:
                mm(st[:, off:off + wN], selv[0:96, h, :], CT[0:96, n0:n0 + wN], False, not diag,
                   R=[("k", "sel")] + CT_k, W=[stk])
            if diag:
                mm(st[:, off:off + 128], identb, CST["cb"][:, :], False, True, R=[("k", "irep"), ("k", "cb")], W=[stk])
            ps_ = slot("pt", 4)
            if fox:
                act(pt[:, ps_, off:off + wN], st[:, off:off + wN], AF.Exp, R=[stk, ("negc",)], W=[("pt", ps_)],
                    bias=negc[:, j, h:h + 1])
            else:
                act(pt[:, ps_, off:off + wN], st[:, off:off + wN], AF.Exp, R=[stk], W=[("pt", ps_)])
            SKA.push(lambda j=j, off=off, wN=wN, ps_=ps_: mm(
                acc[:, off:off + wN], Vp[sl][:, j, 2 * e_:2 * e_ + 2, :].rearrange("p s d -> p (s d)"), pt[:, ps_, off:off + wN],
                j == 0, j == nj - 1, R=Vp_k[sl] + [("pt", ps_)], W=[acck]))

    CT = rv(36, 4, BF16)
    CT_k = rk(36, 4)
    CS = rv(40, 6).rearrange("p (j n) -> p j n", j=16)
    CS_k = rk(40, 6)
    selv = CST["sel"][:, :].rearrange("p (h n) -> p h n", h=6)

    def fox_phase():
        P.add("dve", lambda e: e.tensor_tensor(out=Lt[:, :, :], in0=FFt[:, :, :], in1=apb(ppt[:, 23:29], 16), op=ALU.add),
              R=[("FFt",), ("ppt",)], W=[("Lt",)])
        act(Lt[:, :, :], Lt[:, :, :], AF.Exp, R=[("Lt",)], W=[("Lt",)], scale=-1.0)
        act(Lt[:, :, :], Lt[:, :, :], AF.Ln, R=[("Lt",)] + KEPS, W=[("Lt",)], bias=one_ap)
        cps, cpsk = bank((0,))
        for i in range(16):
            for j in range(i + 1):
                mm(cps[:, i * 6:(i + 1) * 6], CST["tri" if j == i else "negones"][:, :], Lt[:, j, :], j == 0, j == i,
                   R=[("Lt",), ("k", "tri"), ("k", "negones")], W=[cpsk])
        cps3 = cps[:, 0:96].rearrange("p (j n) -> p j n", j=16)
        P.add("dve", lambda e: e.tensor_scalar(out=negc[:, :, :], in0=cps3, scalar1=-1.0, scalar2=None, op0=ALU.mult),
              R=[cpsk], W=[("negc",)])
        P.add("pool", lambda e: e.memset(CS[:, :, :], 0.0), W=CS_k)
        P.add("dve", lambda e: e.tensor_copy(out=chb[:, :, :], in_=cps3), R=[cpsk], W=[("chb",)])
        P.add("dve", lambda e: e.tensor_copy(out=CS[:, :, 0:6], in_=chb[:, :, :]), R=[("chb",)], W=CS_k)
        P.add("dve", lambda e: e.tensor_tensor(out=r1t[:, :, :], in0=cps3, in1=chb[:, :, :], op=ALU.subtract),
              R=[cpsk, ("chb",)], W=[("r1t",)])
        P.add("dve", lambda e: e.tensor_copy(out=chb[:, :, :], in_=r1t[:, :, :]), R=[("r1t",)], W=[("chb",)])
        P.add("dve", lambda e: e.tensor_copy(out=CS[:, :, 32:38], in_=chb[:, :, :]), R=[("chb",)], W=CS_k)
        P.add("dve", lambda e: e.tensor_tensor(out=r1t[:, :, :], in0=r1t[:, :, :], in1=chb[:, :, :], op=ALU.subtract),
              R=[("r1t",), ("chb",)], W=[("r1t",)])
        P.add("dve", lambda e: e.tensor_copy(out=chb[:, :, :], in_=r1t[:, :, :]), R=[("r1t",)], W=[("chb",)])
        P.add("dve", lambda e: e.tensor_copy(out=CS[:, :, 64:70], in_=chb[:, :, :]), R=[("chb",)], W=CS_k)
        for q_ in range(4):
            ctp, ctpk = bank((1, 2, 3))
            for jj in range(4):
                j = q_ * 4 + jj
                P.add("pe", lambda e, ctp=ctp, jj=jj, j=j: e.transpose(
                    out=ctp[0:96, jj * 128:(jj + 1) * 128], in_=CS[:, j, :], identity=CST["ident"][:, :]),
                    R=CS_k + [("k", "ident")], W=[ctpk])
            act(CT[0:96, q_ * 512:(q_ + 1) * 512], ctp[0:96, :], AF.Copy, R=[ctpk], W=CT_k)
        tap("negc", negc[:, :, :], [("negc",)])
        load_pair(0, 0, 3, 0)
        for p_ in range(3):
            sl = p_ % 2
            SKA.flush()
            if p_ + 1 < 3:
                load_pair((p_ + 1) % 2, p_ + 1, 3 + p_ + 1, p_ + 1)
            for c in range(NCH):
                cs = slice(c * 512, (c + 1) * 512)
                for e_ in range(2):
                    base = 64 * e_
                    acc, acck = bank((0, 1, 2))
                    attn_map(sl, e_, base, 64, c, 2 * p_ + e_, True, acc, acck)
                    def fin(base=base, acc=acc, acck=acck, p_=p_, cs=cs, c=c):
                        O = slice(base, base + 64)
                        Dn = slice(64 - base, 128 - base)
                        rc = slot("rct", 2)
                        recip(rct[O, rc, :], acc[Dn, :], R=[acck], W=[("rct", rc)])
                        P.add("dve", lambda e: e.tensor_tensor(out=hT[O, p_, cs], in0=acc[O, :], in1=rct[O, rc, :], op=ALU.mult),
                              R=[acck, ("rct", rc)], W=[("hT", p_, c)])
                    SKA.push(fin)
        SKA.flush()

    def diff_phase():
        load_pair(0, 6, 8, 3)
        for d_ in range(2):
            sl = d_ % 2
            SKA.flush()
            if d_ == 0:
                load_pair(1, 7, 9, 4)
            for c in range(NCH):
                cs = slice(c * 512, (c + 1) * 512)
                od = slot("t1", 2)
                for e_ in range(2):
                    accs = []
                    for m_ in range(2):
                        base = 64 * e_ + 32 * m_
                        acc, acck = bank((0, 1, 2))
                        attn_map(sl, e_, base, 32, c, 0, False, acc, acck)
                        accs.append((acc, acck))
                    def comb(e_=e_, accs=accs, od=od):
                        O = slice(64 * e_, 64 * e_ + 64)
                        Dn = slice(64 - 64 * e_, 128 - 64 * e_)
                        (a1, a1k), (a2, a2k) = accs
                        ra = slot("rct", 2)
                        rb = slot("rct", 2)
                        ob = slot("t2", 2)
                        recip(rct[O, ra, :], a1[Dn, :], R=[a1k], W=[("rct", ra)])
                        recip(rct[O, rb, :], a2[Dn, :], R=[a2k], W=[("rct", rb)])
                        P.add("dve", lambda e: e.tensor_scalar(out=rct[O, rb, :], in0=rct[O, rb, :], scalar1=der[O, 4:5],
                                                               scalar2=None, op0=ALU.mult),
                              R=[("rct", rb), ("der", 4)], W=[("rct", rb)])
                        P.add("dve", lambda e: e.tensor_tensor(out=t1[O, od, :], in0=a1[O, :], in1=rct[O, ra, :], op=ALU.mult),
                              R=[a1k, ("rct", ra)], W=[("t1", od)])
                        P.add("dve", lambda e: e.tensor_tensor(out=t2[O, ob, :], in0=a2[O, :], in1=rct[O, rb, :], op=ALU.mult),
                              R=[a2k, ("rct", rb)], W=[("t2", ob)])
                        P.add("pool", lambda e: e.tensor_tensor(out=t1[O, od, :], in0=t1[O, od, :], in1=t2[O, ob, :], op=ALU.add),
                              R=[("t1", od), ("t2", ob)], W=[("t1", od)])
                    SKA.push(comb)

                def subln(od=od, d_=d_, cs=cs, c=c):
                    sq_ = slot("sqt", 2)
                    act(sqt[:, sq_, :], t1[:, od, :], AF.Square, R=[("t1", od)], W=[("sqt", sq_)])
                    sb_, sbk = bank((3,))
                    mm(sb_[:, :], CST["bones64"][:, :], sqt[:, sq_, :], True, True, R=[("sqt", sq_), ("k", "bones64")], W=[sbk])
                    r_ = slot("rs", 2)
                    act(rs[:, r_, :], sb_[:, :], AF.Ln, R=[sbk] + KEPS, W=[("rs", r_)], bias=eps_ap, scale=1.0 / 64)
                    act(rs[:, r_, :], rs[:, r_, :], AF.Exp, R=[("rs", r_)], W=[("rs", r_)], scale=-0.5)
                    P.add("dve", lambda e: e.scalar_tensor_tensor(
                        out=hT[:, 3 + d_, cs], in0=t1[:, od, :], scalar=der[:, 3:4], in1=rs[:, r_, :], op0=ALU.mult, op1=ALU.mult),
                        R=[("t1", od), ("der", 3), ("rs", r_)], W=[("hT", 3 + d_, c)])
                SKA.push(subln)
        SKA.flush()

    def dsa_phase():
        Qs = rv(0, 12, BF16).rearrange("p (t n) -> p t n", t=3)
        Qs_k = rk(0, 12)
        SKK, SKK_k = rv(12, 4, BF16), rk(12, 4)
        IKK, IKK_k = rv(16, 4, BF16), rk(16, 4)
        IQ = rv(20, 8, BF16).rearrange("p (t n) -> p t n", t=2)
        IQ_k = rk(20, 8)
        SVa = rv(28, 6, BF16).rearrange("p (j s d) -> p j s d", j=16, s=3)
        SVa_k = rk(28, 6)
        accb, acc_k = rv(36, 8), rk(36, 8)
        MBs = [(rv(44, 4, BF16), rk(44, 4)), (rv(56, 4, BF16), rk(56, 4))]
        PERT, PERT_k = rv(48, 8), rk(48, 8)
        junk = t2[:, :, :].rearrange("p a n -> p (a n)").bitcast(BF16)
        junk_k = [("t2",)]
        for t_ in range(3):
            dma(Qs[:, t_, :], sc_fm[10 + t_], R=[("scfm", 10 + t_)], W=Qs_k)
        dma(SKK, sc_fm[13], R=[("scfm", 13)], W=SKK_k)
        dma(IKK, sc_fm[14], R=[("scfm", 14)], W=IKK_k)
        for t_ in range(2):
            dma(IQ[:, t_, :], sc_fm[15 + t_], R=[("scfm", 15 + t_)], W=IQ_k)
        dma(SVa[:, :, 0, :], sc_sv, R=[("scsv",)], W=SVa_k)
        dma(SVa[:, :, 2, :], sc_sv, R=[("scsv",)], W=SVa_k)
        P.add("pool", lambda e: e.memset(SVa[:, :, 1, :], 1.0), W=SVa_k)
        dma(PERT, pert_d, R=[], W=PERT_k)

        accs_ = [(accb, acc_k), (accB_t[:, :], [("accB",)])]
        junks = [(junk, junk_k), (t1[:, :, :].rearrange("p a n -> p (a n)").bitcast(BF16), [("t1",)])]

        def index_block(i):
            q = i % 2
            acc_, acck_ = accs_[q]
            nk = (i + 1) * 128
            nch = (nk + 511) // 512
            for hh in range(4):
                tl, base = hh // 2, 64 * (hh % 2)
                for m_ in range(nch):
                    w_ = min(512, nk - 512 * m_)
                    dp, dpk = bank((0, 1))
                    mm(dp[:, 0:w_], IQ[base:base + 64, tl, i * 128:(i + 1) * 128], IKK[base:base + 64, 512 * m_:512 * m_ + w_],
                       True, True, R=IQ_k + IKK_k, W=[dpk])
                    act(dp[:, 0:w_], dp[:, 0:w_], AF.Relu, R=[dpk], W=[dpk])
                    src = PERT if hh == 0 else acc_
                    srck = PERT_k if hh == 0 else acck_
                    P.add("dve", lambda e, dp=dp, w_=w_, m_=m_, hh=hh, src=src: e.scalar_tensor_tensor(
                        out=acc_[:, 512 * m_:512 * m_ + w_], in0=dp[:, 0:w_], scalar=IWs[:, i, hh:hh + 1],
                        in1=src[:, 512 * m_:512 * m_ + w_], op0=ALU.mult, op1=ALU.add),
                        R=[dpk, ("IWs",)] + srck, W=acck_)
            P.add("dve", lambda e: e.tensor_reduce(out=bis2[:, q, 0:1], in_=acc_[:, 0:nk], axis=AX.X, op=ALU.max,
                                                   apply_absolute_value=True), R=acck_, W=[("bis2", q, 0)])
            P.add("dve", lambda e: e.tensor_tensor(out=acc_[:, i * 128:(i + 1) * 128], in0=acc_[:, i * 128:(i + 1) * 128],
                                                   in1=CST["cbt"][:, :], op=ALU.add), R=acck_ + [("k", "cbt")], W=acck_)
            P.add("dve", lambda e: e.tensor_scalar(out=STt2[:, q, :], in0=CST["pow2"][:, :], scalar1=bis2[:, q, 0:1], scalar2=None,
                                                   op0=ALU.mult), R=[("bis2", q, 0), ("k", "pow2")], W=[("STt2", q)])
            P.add("dve", lambda e: e.memset(bis2[:, q, 1:2], 0.0), W=[("bis2", q, 1)])

        def bisect_pair(iA, iB, zsteps):
            blocks = [b_ for b_ in (iA, iB) if b_ is not None]
            zsteps = list(zsteps)
            per_it = (len(zsteps) + KBIS - 1) // KBIS
            for k in range(KBIS):
                for _ in range(per_it):
                    if zsteps:
                        zsteps.pop(0)()
                for i in blocks:
                    q = i % 2
                    acc_, acck_ = accs_[q]
                    jk, jkk = junks[q]
                    nk = (i + 1) * 128
                    if q == 0:
                        P.add("dve", lambda e, acc_=acc_, jk=jk, nk=nk, q=q: e.tensor_scalar(
                            out=jk[:, 0:nk], in0=acc_[:, 0:nk], scalar1=bis2[:, q, 1:2], scalar2=None,
                            op0=ALU.is_gt, op1=ALU.add, accum_out=bis2[:, q, 2:3]),
                            R=acck_ + [("bis2", q, 1)], W=jkk + [("bis2", q, 2)])
                    else:
                        act(jk[:, 0:nk], acc_[:, 0:nk], AF.Sign, R=acck_ + [("bis2", q, 1)], W=jkk + [("bis2", q, 2)],
                            bias=bis2[:, q, 1:2], scale=-1.0, accum_out=bis2[:, q, 2:3])
                for i in blocks:
                    q = i % 2
                    nk = (i + 1) * 128
                    if q == 0:
                        P.add("dve", lambda e, k=k, q=q: e.tensor_scalar(
                            out=bis2[:, q, 3:4], in0=bis2[:, q, 2:3], scalar1=TOPK - 0.5, scalar2=STt2[:, q, 32 + k:33 + k],
                            op0=ALU.is_gt, op1=ALU.mult), R=[("bis2", q, 2), ("STt2", q)], W=[("bis2", q, 3)])
                    else:
                        P.add("dve", lambda e, k=k, q=q, nk=nk: e.tensor_scalar(
                            out=bis2[:, q, 3:4], in0=bis2[:, q, 2:3], scalar1=float(nk - 2 * TOPK + 1),
                            scalar2=STt2[:, q, 32 + k:33 + k], op0=ALU.is_lt, op1=ALU.mult),
                            R=[("bis2", q, 2), ("STt2", q)], W=[("bis2", q, 3)])
                    P.add("dve", lambda e, k=k, q=q: e.scalar_tensor_tensor(
                        out=bis2[:, q, 1:2], in0=bis2[:, q, 3:4], scalar=STt2[:, q, k:k + 1], in1=bis2[:, q, 1:2],
                        op0=ALU.subtract, op1=ALU.add),
                        R=[("bis2", q, 3), ("bis2", q, 1), ("STt2", q)], W=[("bis2", q, 1)])
            while zsteps:
                zsteps.pop(0)()
            for i in blocks:
                q = i % 2
                acc_, acck_ = accs_[q]
                MB, MB_k = MBs[q]
                nk = (i + 1) * 128
                P.add("dve", lambda e, acc_=acc_, MB=MB, nk=nk, q=q: e.tensor_scalar(
                    out=MB[:, 0:nk], in0=acc_[:, 0:nk], scalar1=bis2[:, q, 1:2], scalar2=NEG, op0=ALU.is_le, op1=ALU.mult),
                    R=acck_ + [("bis2", q, 1)], W=MB_k)

        SKD = Skew(1)
        accE, accEk = banks[6], ("ps", 6)
        accO, accOk = banks[7], ("ps", 7)

        def attend_steps(i):
            MB, MB_k = MBs[i % 2]
            c = i // 4
            qs_ = slice(i * 128, (i + 1) * 128)
            steps = []

            def step(j):
                ks_ = slice(j * 128, (j + 1) * 128)
                sts = []
                for par in range(2):
                    st, stk = bank((2, 3, 4, 5))
                    pr = slice(64 * par, 64 * par + 64)
                    for hi in range(3):
                        mm(st[:, hi * 128:(hi + 1) * 128], SKK[pr, ks_], Qs[pr, hi, qs_], hi == 0, False,
                           R=SKK_k + Qs_k, W=[stk])
                    mm(st[:, 0:384], MB[:, ks_], CST["irep"][:, 0:384], False, True, R=MB_k + [("k", "irep")], W=[stk])
                    sts.append((st, stk))
                pss = []
                for par in range(2):
                    ps_ = slot("pt", 4)
                    act(pt[:, ps_, 0:384], sts[par][0][:, 0:384], AF.Exp, R=[sts[par][1]], W=[("pt", ps_)])
                    pss.append(ps_)

                def pv():
                    mm(accE[:, 0:384], SVa[:, j, 0:2, :].rearrange("p s d -> p (s d)"), pt[:, pss[0], 0:384], j == 0, j == i,
                       R=SVa_k + [("pt", pss[0])], W=[accEk])
                    mm(accO[:, 0:384], SVa[:, j, 1:3, :].rearrange("p s d -> p (s d)"), pt[:, pss[1], 0:384], j == 0, j == i,
                       R=SVa_k + [("pt", pss[1])], W=[accOk])
                SKD.push(pv)

            def fin():
                for par, (acc, acck) in enumerate(((accE, accEk), (accO, accOk))):
                    O = slice(64 * par, 64 * par + 64)
                    Dn = slice(64 - 64 * par, 128 - 64 * par)
                    rc = slot("rct", 2)
                    recip(rct[O, rc, 0:384], acc[Dn, 0:384], R=[acck], W=[("rct", rc)])
                    P.add("dve", lambda e, O=O, rc=rc, acc=acc: e.tensor_tensor(
                        out=hT[O, 5:8, qs_], in0=acc[O, 0:384].rearrange("p (h n) -> p h n", h=3),
                        in1=rct[O, rc, 0:384].rearrange("p (h n) -> p h n", h=3), op=ALU.mult),
                        R=[acck, ("rct", rc)], W=[("hT", 5, c), ("hT", 6, c), ("hT", 7, c)])

            for j in range(i + 1):
                steps.append(lambda j=j: step(j))
            steps.append(lambda: SKD.push(fin))
            return steps

        index_block(0)
        index_block(1)
        prev_steps = []
        for m_ in range(8):
            bisect_pair(2 * m_, 2 * m_ + 1, prev_steps)
            prev_steps = attend_steps(2 * m_) + attend_steps(2 * m_ + 1)
            if m_ + 1 < 8:
                index_block(2 * m_ + 2)
                index_block(2 * m_ + 3)
        for st_ in prev_steps:
            st_()
        SKD.flush()
        tap("acc15", accB_t[:, :], [("accB",)])
        tap("MB15", MBs[1][0], MBs[1][1])
        tap("bis15", bis2[:, 1, :], [("bis2",)])
        tap("STt", STt2[:, 1, :], [("STt2",)])

    def wout_phase(li):
        Wo = [rv(56, 2, BF16).rearrange("p (k n) -> p k n", k=8), rv(58, 2, BF16).rearrange("p (k n) -> p k n", k=8)]
        Wo_k = [[("R", 14, 0)], [("R", 14, 1)]]
        wsrc = w_out_d[li].rearrange("(kt p) n -> p kt n", p=128)
        load_w(Wo[0], Wo_k[0], wsrc[:, :, 0:128])
        for d_ in range(8):
            sl = d_ % 2
            if d_ + 1 < 8:
                load_w(Wo[1 - sl], Wo_k[1 - sl], wsrc[:, :, (d_ + 1) * 128:(d_ + 2) * 128])
            for c in range(NCH):
                cs = slice(c * 512, (c + 1) * 512)
                bk, bkk = bank((0, 1, 2, 3, 4, 5, 6, 7))
                for kt in range(8):
                    mm(bk[:, :], Wo[sl][:, kt, :], hT[:, kt, cs], kt == 0, kt == 7, R=Wo_k[sl] + [("hT", kt, c)], W=[bkk])
                P.add("dve", lambda e, d_=d_, cs=cs, bk=bk: e.tensor_tensor(
                    out=xT[:, d_, cs], in0=bk[:, :], in1=xT[:, d_, cs], op=ALU.add),
                    R=[bkk, ("xT", d_, c)], W=[("xT", d_, c)])

    for li, labs in enumerate(layer_ids):
        dma(ppt[:, :], pp_d[li], R=[], W=[("ppt",)])
        derive_phase(labs)
        norm_phase(7)
        proj_phase(li)
        if li == 0:
            tap("sc_fm", sc_fm, [("scfm",)])
            tap("sc_v", sc_v, [("scv",)])
            tap("sc_sv", sc_sv, [("scsv",)])
            tap("FFt", FFt[:, :, :], [("FFt",)])
            tap("IWs", IWs[:, :, :], [("IWs",)])
        fox_phase()
        diff_phase()
        dsa_phase()
        if li == 0:
            tap("cat", hT[:, :, :], [("hT",)])
        wout_phase(li)
        if li == 0:
            tap("x1", xT[:, :, :], [("xT",)])
        norm_phase(15)
        ffn_phase(li)

    for kc in range(8):
        dma(y_d[kc * 128:(kc + 1) * 128, :], xT[:, kc, :], R=[("xT", kc)], W=[("y", kc)])

    P.emit(nc, stack)
    stack.close()
    return nc, P.stats


_NC_CACHE = {}


def _get_nc(layer_ids):
    key = tuple(layer_ids)
    if key not in _NC_CACHE:
        _NC_CACHE[key] = build_nc(list(layer_ids))[0]
    return _NC_CACHE[key]


def kernel(**inputs):
    inp = {k: np.asarray(v) for k, v in inputs.items()}
    x = inp["x"].astype(np.float32, copy=False)
    B = x.shape[0]
    cst = _consts()
    base = {}
    for nm, w in _CONST_SHAPES:
        base["c_" + nm] = np.ascontiguousarray(cst[nm], dtype=np.float32)
    base["c_rope"] = cst["rope"]
    base["c_pert"] = cst["pert"]
    layer_ids = list(range(DEPTH))
    nc = _get_nc(layer_ids)
    base["w_in"] = np.ascontiguousarray(inp["w_in"], dtype=np.float32)
    base["w_out"] = np.ascontiguousarray(inp["w_out"], dtype=np.float32)
    base["w_gu"] = np.ascontiguousarray(inp["w_gate_up"], dtype=np.float32)
    base["w_dn"] = np.ascontiguousarray(inp["w_down"], dtype=np.float32)
    base["pp"] = np.stack([_pack_pp(inp, l) for l in layer_ids]).astype(np.float32)
    in_maps = []
    for b in range(B):
        m = dict(base)
        m["xT"] = np.ascontiguousarray(x[b].T)
        in_maps.append(m)
    res = run_bass_kernel_spmd(nc, in_maps, core_ids=list(range(B)))
    out = np.stack([np.asarray(r["yT"]).T for r in res.results]).astype(np.float32)
    return out
```

```python
import math
from contextlib import ExitStack
import numpy as np
import concourse.bass as bass
import concourse.mybir as mybir
from concourse.bass_utils import run_bass_kernel_spmd

F32 = mybir.dt.float32
BF16 = mybir.dt.bfloat16
ALU = mybir.AluOpType
AF = mybir.ActivationFunctionType
AX = mybir.AxisListType

D = 1024
S = 2048
DEPTH = 4
NCH = 4
FFN_H = 2816
INW = 2762
EPS = 1e-6
NEG = -30000.0
KBIS = 16
TOPK = 256
PERT_EPS = 2.0 ** -20
NPP = 157

O_FQ, O_FK, O_FV, O_FF = 0, 384, 768, 1152
O_DQ, O_DK, O_DV = 1158, 1414, 1670
O_SQ, O_SK, O_SV, O_IQ, O_IK, O_IW = 1926, 2310, 2374, 2438, 2694, 2758


class Prog:
    ENGS = ("pe", "act", "dve", "pool", "sp")
    NDMA = 24

    def __init__(self):
        self.ops = []

    def add(self, eng, fn, R=(), W=(), dma=False):
        R = tuple(R)
        W = tuple(W) + tuple(k for k in R if k[0] == "ps" and k not in W)
        R = tuple(k for k in R if k[0] != "ps")
        self.ops.append((eng, fn, R, W, dma))

    def _deps(self):
        lastw = {}
        readers = {}
        desc = {}
        deps_all = []

        def related(k):
            out = []
            for i in range(1, len(k)):
                p = k[:i]
                if p in lastw or p in readers:
                    out.append(p)
            out.extend(desc.get(k, ()))
            return out

        def register(k):
            if k in lastw or k in readers:
                return
            for i in range(1, len(k) + 1):
                desc.setdefault(k[:i], set()).add(k)

        for i, (eng, fn, R, W, dma) in enumerate(self.ops):
            d = set()
            for k in R:
                register(k)
                lastw.setdefault(k, None)
                for r in related(k):
                    w = lastw.get(r)
                    if w is not None:
                        d.add(w)
            for k in W:
                register(k)
                lastw.setdefault(k, None)
                for r in related(k):
                    w = lastw.get(r)
                    if w is not None:
                        d.add(w)
                    d.update(readers.get(r, ()))
            d.discard(i)
            for k in R:
                readers.setdefault(k, []).append(i)
            for k in W:
                lastw[k] = i
                for r in desc.get(k, ()):
                    if r in readers:
                        readers[r] = []
                    lastw[r] = i
            deps_all.append(d)
        return deps_all

    def emit(self, nc, stack):
        ops = self.ops
        deps_all = self._deps()
        n = len(ops)
        sig = [False] * n
        for i, d in enumerate(deps_all):
            e_i = ops[i][0]
            for j in d:
                if ops[j][0] == "pe" and e_i == "pe":
                    continue
                sig[j] = True
        dma_prev = {}
        dma_slot = {}
        nd = 0
        for i, op in enumerate(ops):
            if op[4]:
                s = nd % self.NDMA
                nd += 1
                dma_slot[i] = s
                if s in dma_prev:
                    deps_all[i].add(dma_prev[s])
                dma_prev[s] = i
                sig[i] = True
        EPOCH = 1000
        esems = {e: [] for e in ("pe", "act", "dve", "pool")}
        dsem = [stack.enter_context(nc.semaphore("dsem%d" % k)) for k in range(min(self.NDMA, max(nd, 1)))]
        count = {e: 0 for e in esems}
        dcount = [0] * self.NDMA
        known = {e: {} for e in self.ENGS}
        event = [None] * n
        vc = [None] * n
        plan = {e: [] for e in self.ENGS}
        nwaits = 0
        Z = (0, 0)

        def sem_of(src, ep):
            if isinstance(src, tuple):
                return dsem[src[1]]
            lst = esems[src]
            while len(lst) <= ep:
                lst.append(stack.enter_context(nc.semaphore("sem_%s_%d" % (src, len(lst)))))
            return lst[ep]

        for i, (eng, fn, R, W, dma) in enumerate(ops):
            kn = known[eng]
            wm = {}
            for j in sorted(deps_all[i]):
                if ops[j][0] == "pe" and eng == "pe":
                    continue
                src, val = event[j]
                if kn.get(src, Z) >= val:
                    continue
                if wm.get(src, Z) < val:
                    wm[src] = val
                for s2, v2 in vc[j].items():
                    if kn.get(s2, Z) < v2:
                        kn[s2] = v2
            nwaits += len(wm)
            inc = None
            if sig[i]:
                if dma:
                    s = dma_slot[i]
                    dcount[s] += 16
                    event[i] = (("d", s), (0, dcount[s]))
                    inc = (dsem[s], 16)
                else:
                    ep, cn = divmod(count[eng], EPOCH)
                    count[eng] += 1
                    event[i] = (eng, (ep, cn + 1))
                    inc = (sem_of(eng, ep), 1)
                v = dict(kn)
                v[event[i][0]] = event[i][1]
                vc[i] = v
            plan[eng].append((fn, [(sem_of(src, val[0]), val[1]) for src, val in wm.items()], inc))
        self.stats = dict(n_ops=n, n_waits=nwaits, counts=dict(count), n_dma=nd,
                          n_sems=len(dsem) + sum(len(v) for v in esems.values()))

        block = stack.enter_context(nc.Block())

        def run(engine, items):
            for fn, waits, inc in items:
                for sem, val in waits:
                    engine.wait_ge(sem, val)
                ins = fn(engine)
                if inc is not None:
                    ins.then_inc(inc[0], inc[1])

        @block.tensor
        def _(e):
            run(e, plan["pe"])

        @block.scalar
        def _(e):
            run(e, plan["act"])

        @block.vector
        def _(e):
            run(e, plan["dve"])

        @block.gpsimd
        def _(e):
            run(e, plan["pool"])

        @block.sync
        def _(e):
            run(e, plan["sp"])
            for s in range(len(dsem)):
                if dcount[s] > 0:
                    e.wait_ge(dsem[s], dcount[s])


class Skew:
    def __init__(self, lag):
        self.q = []
        self.lag = lag

    def push(self, fn):
        self.q.append(fn)
        while len(self.q) > self.lag:
            self.q.pop(0)()

    def flush(self):
        while self.q:
            self.q.pop(0)()


def _rope_tab(head_dim, rows_rep):
    rot = head_dim // 4
    half = rot // 2
    inv = (1.0 / (np.float32(500000.0) ** (np.arange(0, rot, 2, dtype=np.float32) / np.float32(rot)))).astype(np.float32)
    ang = np.arange(S, dtype=np.float32)[:, None] * inv[None, :]
    cos = np.cos(ang).astype(np.float32).T
    sin = np.sin(ang).astype(np.float32).T
    C = np.ones((128, S), np.float32)
    Sn = np.zeros((128, S), np.float32)
    for b in range(128 // head_dim):
        o = b * head_dim
        C[o:o + half] = cos
        C[o + half:o + rot] = cos
        Sn[o:o + half] = sin
        Sn[o + half:o + rot] = sin
    P = np.zeros((128, 128), np.float32)
    for b in range(128 // head_dim):
        o = b * head_dim
        for r in range(half):
            P[o + r + half, o + r] = -1.0
            P[o + r, o + r + half] = 1.0
    return C, Sn, P


def _consts():
    c = {}
    eye = np.eye(128, dtype=np.float32)
    idx = np.arange(128)
    c["ident"] = eye
    c["tri"] = -(idx[:, None] <= idx[None, :]).astype(np.float32)
    c["negones"] = -np.ones((128, 128), np.float32)
    c["ones"] = np.ones((128, 128), np.float32)
    b64 = np.zeros((128, 128), np.float32)
    b64[:64, :64] = 1
    b64[64:, 64:] = 1
    c["bones64"] = b64
    b32 = np.zeros((128, 128), np.float32)
    for b in range(4):
        b32[32 * b:32 * b + 32, 32 * b:32 * b + 32] = 1
    c["bones32"] = b32
    sel = np.zeros((128, 6, 128), np.float32)
    for h in range(6):
        sel[h, h, :] = 1
        sel[32 + h, h, :] = 1
        sel[64 + h, h, :] = 1
    c["sel"] = sel.reshape(128, 768)
    c["cb"] = np.where(idx[:, None] <= idx[None, :], 0.0, NEG).astype(np.float32)
    c["cbt"] = np.where(idx[None, :] <= idx[:, None], 0.0, NEG).astype(np.float32)
    c["irep"] = np.tile(eye, (1, 4))
    C64, S64, P64 = _rope_tab(64, 2)
    C32, S32, P32 = _rope_tab(32, 4)
    c["prot64"] = P64
    c["prot32"] = P32
    c["rope"] = np.stack([C32, S32, C64, S64]).astype(np.float32)
    c["pert"] = np.tile((-PERT_EPS * np.arange(S, dtype=np.float32))[None, :], (128, 1)).astype(np.float32)
    p2 = np.zeros((128, 64), np.float32)
    for k in range(32):
        p2[:, k] = 2.0 ** (-k)
        p2[:, 32 + k] = 2.0 ** (1 - k)
    c["pow2"] = p2
    return c


_CONST_SHAPES = [("ident", 128), ("tri", 128), ("negones", 128), ("ones", 128), ("bones64", 128), ("bones32", 128),
                 ("sel", 768), ("cb", 128), ("cbt", 128), ("irep", 512), ("prot64", 128), ("prot32", 128), ("pow2", 64)]


def _pack_pp(inp, l):
    pp = np.zeros((128, NPP), np.float32)
    p = np.arange(128)
    pp[:, 0] = inp["fox_qn"][l][p % 64]
    pp[:, 1] = inp["fox_kn"][l][p % 64]
    pp[:, 2] = inp["diff_qn"][l][p % 32]
    pp[:, 3] = inp["diff_kn"][l][p % 32]
    pp[:, 4] = inp["dsa_qn"][l][p % 64]
    pp[:, 5] = inp["dsa_kn"][l][p % 64]
    pp[:, 6] = inp["diff_subln"][l][p % 64]
    pp[:, 7:15] = inp["attn_norm"][l].reshape(8, 128).T
    pp[:, 15:23] = inp["ffn_norm"][l].reshape(8, 128).T
    pp[:, 23:29] = inp["fox_fb"][l][None, :]
    pp[:, 29:61] = inp["diff_lq1"][l][None, :]
    pp[:, 61:93] = inp["diff_lk1"][l][None, :]
    pp[:, 93:125] = inp["diff_lq2"][l][None, :]
    pp[:, 125:157] = inp["diff_lk2"][l][None, :]
    return pp


def build_nc(layer_ids, debug=None):
    NL = len(layer_ids)
    nc = bass.Bass("TRN2", target_bir_lowering=False)
    stack = ExitStack()
    P = Prog()

    def dram(name, shape, dt=F32, kind="ExternalInput"):
        return nc.dram_tensor(name, list(shape), dt, kind=kind).ap()

    x_d = dram("xT", [D, S])
    y_d = dram("yT", [D, S], kind="ExternalOutput")
    w_in_d = dram("w_in", [NL, D, INW])
    w_out_d = dram("w_out", [NL, D, D])
    w_gu_d = dram("w_gu", [NL, D, 2 * FFN_H])
    w_dn_d = dram("w_dn", [NL, FFN_H, D])
    pp_d = dram("pp", [NL, 128, NPP])
    cst_d = {nm: dram("c_" + nm, [128, w]) for nm, w in _CONST_SHAPES}
    rope_d = dram("c_rope", [4, 128, S])
    pert_d = dram("c_pert", [128, S])
    sc_fm = dram("sc_fm", [17, 128, S], BF16, kind="Internal")
    sc_v = dram("sc_v", [5, 128, 16, 128], BF16, kind="Internal")
    sc_sv = dram("sc_sv", [128, 16, 64], BF16, kind="Internal")
    sc_pad = dram("sc_pad", [20, 128, S], BF16, kind="Internal")
    dbg = {}
    if debug:
        for nm, shape, dt in debug:
            dbg[nm] = dram("dbg_" + nm, shape, dt, kind="ExternalOutput")

    def sb(name, shape, dt=F32):
        return stack.enter_context(nc.sbuf_tensor(name, list(shape), dt))

    def ps(name, shape, dt=F32):
        return stack.enter_context(nc.psum_tensor(name, list(shape), dt))

    xT = sb("xT_sb", [128, 8, S])
    hT = sb("hT_sb", [128, 8, S], BF16)
    RW = 15 * 1024
    Rg = sb("R_sb", [128, RW])
    stg = sb("stg_sb", [128, 2, 1024])
    banks = [ps("bank%d" % b, [128, 512]) for b in range(8)]

    def rv(off_kb, size_kb, dt=F32):
        a = Rg[:, off_kb * 256:(off_kb + size_kb) * 256]
        if dt == BF16:
            a = a.bitcast(BF16)
        return a

    def rk(off_kb, size_kb):
        return [("R", pg) for pg in range(off_kb // 4, (off_kb + size_kb + 3) // 4)]

    sqt = sb("sqt", [128, 2, 512], BF16)
    rs = sb("rs", [128, 2, 512])
    qn = sb("qn", [128, 3, 512], BF16)
    t1 = sb("t1", [128, 2, 512])
    t2 = sb("t2", [128, 2, 512])
    pt = sb("pt", [128, 4, 512], BF16)
    rct = sb("rct", [128, 2, 512])
    ppt = sb("ppt", [128, NPP])
    der = sb("der", [128, 16])
    FFt = sb("FFt", [128, 16, 6])
    IWs = sb("IWs", [128, 16, 4])
    Lt = sb("Lt", [128, 16, 6])
    negc = sb("negc", [128, 16, 6])
    chb = sb("chb", [128, 16, 6], BF16)
    r1t = sb("r1t", [128, 16, 6])
    bis = sb("bis", [128, 8])
    STt = sb("STt", [128, 64])
    lam4 = sb("lam4", [128, 4, 32])
    CST = {}
    for nm, w in _CONST_SHAPES:
        f32c = nm in ("ident", "tri", "negones", "cbt", "pow2")
        CST[nm] = sb("k_" + nm, [128, w], F32 if f32c else BF16)
    epsc = sb("epsc", [128, 2])
    accB_t = sb("accB_t", [128, 2048])
    bis2 = sb("bis2", [128, 2, 4])
    STt2 = sb("STt2", [128, 2, 64])

    bank_rr = {}

    def bank(pool):
        i = bank_rr.get(pool, 0)
        bank_rr[pool] = i + 1
        b = pool[i % len(pool)]
        return banks[b], ("ps", b)

    slot_rr = {}

    def slot(name, nslots):
        i = slot_rr.get(name, 0)
        slot_rr[name] = i + 1
        return i % nslots

    def dma(out, in_, R, W):
        P.add("sp", lambda e: e.dma_start(out=out, in_=in_), R=R, W=W, dma=True)

    def mm(out, lhsT, rhs, start, stop, R, W, **kw):
        P.add("pe", lambda e: e.matmul(out, lhsT=lhsT, rhs=rhs, start=start, stop=stop, **kw), R=R, W=W)

    def act(out, in_, func, R, W, bias=0.0, scale=1.0, accum_out=None):
        if accum_out is None:
            P.add("act", lambda e: e.activation(out=out, in_=in_, func=func, bias=bias, scale=scale), R=R, W=W)
        else:
            P.add("act", lambda e: e.activation(out=out, in_=in_, func=func, bias=bias, scale=scale, accum_out=accum_out),
                  R=R, W=W)

    def recip(out, in_, R, W):
        act(out, in_, AF.Ln, R=R, W=W)
        act(out, out, AF.Exp, R=W, W=W, scale=-1.0)

    def load_w(dst, dst_keys, src_ap):
        s = slot("stg", 2)
        n = 1
        for d_ in src_ap.shape[1:]:
            n *= d_
        sv = stg[:, s, 0:n]
        if len(src_ap.shape) == 3:
            sv = sv.rearrange("p (a b) -> p a b", a=src_ap.shape[1])
        dma(sv, src_ap, R=[], W=[("stg", s)])
        P.add("pool", lambda e: e.tensor_copy(out=dst, in_=sv), R=[("stg", s)], W=dst_keys)

    for kc in range(8):
        dma(xT[:, kc, :], x_d[kc * 128:(kc + 1) * 128, :], R=[], W=[("xT", kc)])
    for nm, w in _CONST_SHAPES:
        if CST[nm].dtype == F32:
            dma(CST[nm][:, :], cst_d[nm][:, :], R=[], W=[("k", nm)])
        else:
            for o in range(0, w, 512):
                ww = min(512, w - o)
                s = slot("t1", 2)
                dma(t1[:, s, 0:ww], cst_d[nm][:, o:o + ww], R=[], W=[("t1", s)])
                P.add("pool", lambda e, nm=nm, o=o, ww=ww, s=s: e.tensor_copy(out=CST[nm][:, o:o + ww], in_=t1[:, s, 0:ww]),
                      R=[("t1", s)], W=[("k", nm)])
    P.add("pool", lambda e: e.memset(epsc[:, 0:1], EPS), W=[("epsc",)])
    P.add("pool", lambda e: e.memset(epsc[:, 1:2], 1.0), W=[("epsc",)])
    KEPS = [("epsc",)]
    P.add("pool", lambda e: e.memset(hT[:, 0, :], 0.0), W=[("hT", 0)])
    P.add("pool", lambda e: e.memset(hT[:, 1, :], 1.0), W=[("hT", 1)])
    for t_ in range(20):
        dma(sc_pad[t_], hT[:, 0, :], R=[("hT", 0)], W=[("scp", t_)])
    for p_ in range(3):
        dma(sc_pad[6 + 2 * p_, 64:67, :], hT[64:67, 1, :], R=[("hT", 1)], W=[("scp", 6 + 2 * p_)])
        dma(sc_pad[7 + 2 * p_, 0:3, :], hT[0:3, 1, :], R=[("hT", 1)], W=[("scp", 7 + 2 * p_)])
    eps_ap = epsc[:, 0:1]
    one_ap = epsc[:, 1:2]

    def norm_phase(gcol0):
        for c in range(NCH):
            cs = slice(c * 512, (c + 1) * 512)
            bk, bkk = bank((0, 1))
            for kc in range(8):
                s = slot("sqt", 2)
                act(sqt[:, s, :], xT[:, kc, cs], AF.Square, R=[("xT", kc, c)], W=[("sqt", s)])
                mm(bk[:, :], CST["ones"][:, :], sqt[:, s, :], kc == 0, kc == 7, R=[("sqt", s), ("k", "ones")], W=[bkk])
            s = slot("rs", 2)
            act(rs[:, s, :], bk[:, :], AF.Ln, R=[bkk] + KEPS, W=[("rs", s)], bias=eps_ap, scale=1.0 / D)
            act(rs[:, s, :], rs[:, s, :], AF.Exp, R=[("rs", s)], W=[("rs", s)], scale=-0.5)
            for kc in range(8):
                P.add("dve", lambda e, kc=kc, cs=cs, s=s: e.scalar_tensor_tensor(
                    out=hT[:, kc, cs], in0=xT[:, kc, cs], scalar=ppt[:, gcol0 + kc:gcol0 + kc + 1], in1=rs[:, s, :],
                    op0=ALU.mult, op1=ALU.mult), R=[("xT", kc, c), ("ppt",), ("rs", s)], W=[("hT", kc, c)])

    def ffn_phase(li):
        groups = [(g * 512, 4) for g in range(5)] + [(2560, 2)]
        Wgu = [rv(0, 16, BF16).rearrange("p (k n) -> p k n", k=8), rv(16, 16, BF16).rearrange("p (k n) -> p k n", k=8)]
        Wgu_k = [rk(0, 16), rk(16, 16)]
        Wd = [rv(32, 8, BF16).rearrange("p (k n) -> p k n", k=4), rv(40, 8, BF16).rearrange("p (k n) -> p k n", k=4)]
        Wd_k = [rk(32, 8), rk(40, 8)]
        actT = rv(48, 8, BF16).rearrange("p (s k n) -> p s k n", s=2, k=4)
        actT_k = [rk(48, 4), rk(52, 4)]
        win = w_gu_d[li].rearrange("(kc p) n -> p kc n", p=128)

        def load_group(gi):
            h0, nt = groups[gi]
            sl = gi % 2
            for t in range(nt):
                load_w(Wgu[sl][:, :, t * 128:(t + 1) * 128], Wgu_k[sl], win[:, :, h0 + t * 128:h0 + (t + 1) * 128])
                load_w(Wgu[sl][:, :, 512 + t * 128:512 + (t + 1) * 128], Wgu_k[sl],
                       win[:, :, FFN_H + h0 + t * 128:FFN_H + h0 + (t + 1) * 128])
                load_w(Wd[sl][:, t, :], Wd_k[sl], w_dn_d[li, h0 + t * 128:h0 + (t + 1) * 128, :])

        load_group(0)
        for gi in range(len(groups)):
            if gi + 1 < len(groups):
                load_group(gi + 1)
            h0, nt = groups[gi]
            sl = gi % 2
            for c in range(NCH):
                cs = slice(c * 512, (c + 1) * 512)
                asl = slot("actT", 2)
                for t in range(nt):
                    gb, gbk = bank((0, 1, 2, 3))
                    ub, ubk = bank((0, 1, 2, 3))
                    for kc in range(8):
                        mm(gb[:, :], Wgu[sl][:, kc, t * 128:(t + 1) * 128], hT[:, kc, cs], kc == 0, kc == 7,
                           R=Wgu_k[sl] + [("hT", kc, c)], W=[gbk])
                    for kc in range(8):
                        mm(ub[:, :], Wgu[sl][:, kc, 512 + t * 128:512 + (t + 1) * 128], hT[:, kc, cs], kc == 0, kc == 7,
                           R=Wgu_k[sl] + [("hT", kc, c)], W=[ubk])
                    s = slot("t1", 2)
                    act(t1[:, s, :], gb[:, :], AF.Silu, R=[gbk], W=[("t1", s)])
                    P.add("dve", lambda e, s=s, ub=ub, asl=asl, t=t: e.tensor_tensor(
                        out=actT[:, asl, t, :], in0=ub[:, :], in1=t1[:, s, :], op=ALU.mult),
                        R=[ubk, ("t1", s)], W=actT_k[asl])
                for d_ in range(8):
                    db, dbk = bank((4, 5, 6, 7))
                    for t in range(nt):
                        mm(db[:, :], Wd[sl][:, t, d_ * 128:(d_ + 1) * 128], actT[:, asl, t, :], t == 0, t == nt - 1,
                           R=Wd_k[sl] + actT_k[asl], W=[dbk])
                    P.add("dve", lambda e, d_=d_, cs=cs, db=db: e.tensor_tensor(
                        out=xT[:, d_, cs], in0=db[:, :], in1=xT[:, d_, cs], op=ALU.add),
                        R=[dbk, ("xT", d_, c)], W=[("xT", d_, c)])


    def apb(base_ap, mid):
        a = base_ap.ap
        return bass.AP(base_ap.tensor, base_ap.offset, [list(a[0]), [0, mid], list(a[-1])])

    def tap(name, src, R):
        if name in dbg:
            dma(dbg[name], src, R=R, W=[("dbg", name)])

    def derive_phase(labs):
        lam_init = 0.8 - 0.6 * math.exp(-0.3 * labs)
        for col, src, mul in ((0, 0, 0.125), (1, 2, 32.0 ** -0.5), (2, 4, 0.125), (3, 6, 1.0 - lam_init)):
            P.add("dve", lambda e, col=col, src=src, mul=mul: e.tensor_scalar(
                out=der[:, col:col + 1], in0=ppt[:, src:src + 1], scalar1=mul, scalar2=None, op0=ALU.mult),
                R=[("ppt",)], W=[("der", col)])
        for q, (a, b) in enumerate(((29, 61), (93, 125))):
            P.add("dve", lambda e, q=q, a=a, b=b: e.tensor_tensor(
                out=lam4[:, q, :], in0=ppt[:, a:a + 32], in1=ppt[:, b:b + 32], op=ALU.mult), R=[("ppt",)], W=[("lam4", q)])
            P.add("dve", lambda e, q=q: e.tensor_reduce(out=der[:, 5 + q:6 + q], in_=lam4[:, q, :], axis=AX.X, op=ALU.add),
                  R=[("lam4", q)], W=[("der", 5 + q)])
            act(der[:, 5 + q:6 + q], der[:, 5 + q:6 + q], AF.Exp, R=[("der", 5 + q)], W=[("der", 5 + q)])
        P.add("dve", lambda e: e.tensor_tensor(out=der[:, 4:5], in0=der[:, 6:7], in1=der[:, 5:6], op=ALU.subtract),
              R=[("der", 5), ("der", 6)], W=[("der", 4)])
        P.add("dve", lambda e: e.tensor_scalar(out=der[:, 4:5], in0=der[:, 4:5], scalar1=-lam_init, scalar2=None, op0=ALU.add),
              R=[("der", 4)], W=[("der", 4)])

    def proj_phase(li):
        win = w_in_d[li].rearrange("(kc p) n -> p kc n", p=128)
        WT = [rv(0, 2, BF16).rearrange("p (k n) -> p k n", k=8), rv(2, 2, BF16).rearrange("p (k n) -> p k n", k=8)]
        WT_k = [[("R", 0, 0)], [("R", 0, 1)]]
        Wv = rv(4, 6, BF16).rearrange("p (k n) -> p k n", k=8)
        Wv_k = rk(4, 6)
        ost = rv(12, 4, BF16).rearrange("p (s n) -> p s n", s=4)
        ost_k = [[("R", 3, s_)] for s_ in range(4)]
        Ct = rv(36, 8)
        St = rv(44, 8)
        Ct_k, St_k = rk(36, 8), rk(44, 8)
        FM = []
        for p_ in range(3):
            FM.append(dict(sc=p_, segs=[(O_FQ + 128 * p_, 128)], norm=(64, "bones64", der, 0, ("der", 0)), rope=None,
                           outs=[(2 * p_, 0, 64), (2 * p_ + 1, 64, 64)]))
        for p_ in range(3):
            FM.append(dict(sc=3 + p_, segs=[(O_FK + 128 * p_, 128)], norm=(64, "bones64", ppt, 1, ("ppt",)), rope=None,
                           outs=[(6 + 2 * p_, 0, 64), (7 + 2 * p_, 64, 64)]))
        for p_ in range(2):
            FM.append(dict(sc=6 + p_, segs=[(O_DQ + 128 * p_, 128)], norm=(32, "bones32", der, 1, ("der", 1)), rope=32,
                           outs=[(12 + 4 * p_ + q_, 32 * q_, 32) for q_ in range(4)]))
        for p_ in range(2):
            FM.append(dict(sc=8 + p_, segs=[(O_DK + 128 * p_, 128)], norm=(32, "bones32", ppt, 3, ("ppt",)), rope=32))
        for p_ in range(3):
            FM.append(dict(sc=10 + p_, segs=[(O_SQ + 128 * p_, 128)], norm=(64, "bones64", der, 2, ("der", 2)), rope=64))
        FM.append(dict(sc=13, segs=[(O_SK, 64), (O_SK, 64)], norm=(64, "bones64", ppt, 5, ("ppt",)), rope=64))
        FM.append(dict(sc=14, segs=[(O_IK, 64), (O_IK, 64)], norm=None, rope=64))
        for p_ in range(2):
            FM.append(dict(sc=15 + p_, segs=[(O_IQ + 128 * p_, 128)], norm=None, rope=64))

        def load_tile(ti):
            sl = ti % 2
            o = 0
            for col0, n_ in FM[ti]["segs"]:
                load_w(WT[sl][:, :, o:o + n_], WT_k[sl], win[:, :, col0:col0 + n_])
                o += n_

        cur_rope = None
        SKB = Skew(1)
        SKC = Skew(2)
        load_tile(0)
        for ti, T in enumerate(FM):
            if ti + 1 < len(FM):
                load_tile(ti + 1)
            sl = ti % 2
            if T["rope"] is not None and T["rope"] != cur_rope:
                SKB.flush()
                SKC.flush()
                cur_rope = T["rope"]
                ro = 0 if cur_rope == 32 else 2
                dma(Ct, rope_d[ro], R=[], W=Ct_k)
                dma(St, rope_d[ro + 1], R=[], W=St_k)
            for c in range(NCH):
                cs = slice(c * 512, (c + 1) * 512)
                pj, pjk = bank((0, 1, 2))
                for kc in range(8):
                    mm(pj[:, :], WT[sl][:, kc, :], hT[:, kc, cs], kc == 0, kc == 7, R=WT_k[sl] + [("hT", kc, c)], W=[pjk])
                sq_ = None
                if T["norm"] is not None:
                    sq_ = slot("sqt", 2)
                    act(sqt[:, sq_, :], pj[:, :], AF.Square, R=[pjk], W=[("sqt", sq_)])

                def stageB(T=T, c=c, cs=cs, pj=pj, pjk=pjk, sq_=sq_):
                    os_ = slot("ost", 4)
                    qs_ = None
                    if T["rope"] is not None:
                        qs_ = slot("qn", 3)
                        tgt, tgtk = qn[:, qs_, :], [("qn", qs_)]
                    else:
                        tgt, tgtk = ost[:, os_, :], ost_k[os_]
                    if T["norm"] is not None:
                        bs_, bname, gt, gcol, gkey = T["norm"]
                        sb_, sbk = bank((3, 4))
                        mm(sb_[:, :], CST[bname][:, :], sqt[:, sq_, :], True, True, R=[("sqt", sq_), ("k", bname)], W=[sbk])
                        r_ = slot("rs", 2)
                        act(rs[:, r_, :], sb_[:, :], AF.Ln, R=[sbk] + KEPS, W=[("rs", r_)], bias=eps_ap, scale=1.0 / bs_)
                        act(rs[:, r_, :], rs[:, r_, :], AF.Exp, R=[("rs", r_)], W=[("rs", r_)], scale=-0.5)
                        P.add("dve", lambda e: e.scalar_tensor_tensor(
                            out=tgt, in0=pj[:, :], scalar=gt[:, gcol:gcol + 1], in1=rs[:, r_, :], op0=ALU.mult, op1=ALU.mult),
                            R=[pjk, gkey, ("rs", r_)], W=tgtk)
                    else:
                        act(tgt, pj[:, :], AF.Copy, R=[pjk], W=tgtk)

                    def stageC():
                        if T["rope"] is not None:
                            pname = "prot%d" % T["rope"]
                            rp, rpk = bank((5, 6))
                            mm(rp[:, :], CST[pname][:, :], qn[:, qs_, :], True, True, R=[("qn", qs_), ("k", pname)], W=[rpk])
                            a_ = slot("t1", 2)
                            b_ = slot("t2", 2)
                            P.add("dve", lambda e: e.tensor_tensor(out=t1[:, a_, :], in0=rp[:, :], in1=St[:, cs], op=ALU.mult),
                                  R=[rpk] + St_k, W=[("t1", a_)])
                            P.add("pool", lambda e: e.tensor_tensor(out=t2[:, b_, :], in0=qn[:, qs_, :], in1=Ct[:, cs], op=ALU.mult),
                                  R=[("qn", qs_)] + Ct_k, W=[("t2", b_)])
                            P.add("dve", lambda e: e.tensor_tensor(out=ost[:, os_, :], in0=t1[:, a_, :], in1=t2[:, b_, :], op=ALU.add),
                                  R=[("t1", a_), ("t2", b_)], W=ost_k[os_])
                        if "outs" in T:
                            for tl_, r0_, nr_ in T["outs"]:
                                dma(sc_pad[tl_, r0_:r0_ + nr_, cs], ost[r0_:r0_ + nr_, os_, :], R=ost_k[os_], W=[("scp", tl_, c)])
                        else:
                            dma(sc_fm[T["sc"], :, cs], ost[:, os_, :], R=ost_k[os_], W=[("scfm", T["sc"], c)])
                    SKC.push(stageC)
                SKB.push(stageB)
        SKB.flush()
        SKC.flush()

        def tm_group(col_segs, ncols, handler):
            o = 0
            for col0, n_ in col_segs:
                load_w(Wv[:, :, o:o + n_], Wv_k, win[:, :, col0:col0 + n_])
                o += n_
            handler()

        def v_pairs(npairs, sc0):
            for a in range(npairs):
                for jg in range(4):
                    tv, tvk = bank((6, 7))
                    for jj in range(4):
                        j = jg * 4 + jj
                        for kc in range(8):
                            mm(tv[:, jj * 128:(jj + 1) * 128], hT[:, kc, j * 128:(j + 1) * 128], Wv[:, kc, a * 128:(a + 1) * 128],
                               kc == 0, kc == 7, R=Wv_k + [("hT", kc, jg)], W=[tvk])
                    os_ = slot("ost", 4)
                    act(ost[:, os_, :], tv[:, :], AF.Copy, R=[tvk], W=ost_k[os_])
                    dma(sc_v[sc0 + a, :, jg * 4:(jg + 1) * 4, :], ost[:, os_, :].rearrange("p (j n) -> p j n", j=4),
                        R=ost_k[os_], W=[("scv", sc0 + a, jg)])

        tm_group([(O_FV, 128), (O_FV + 128, 128), (O_FV + 256, 128)], 384, lambda: v_pairs(3, 0))
        tm_group([(O_DV, 128), (O_DV + 128, 128)], 256, lambda: v_pairs(2, 3))

        def small():
            for jg in range(4):
                tv, tvk = bank((6, 7))
                for jj in range(4):
                    j = jg * 4 + jj
                    for kc in range(8):
                        mm(tv[:, jj * 74:(jj + 1) * 74], hT[:, kc, j * 128:(j + 1) * 128], Wv[:, kc, 0:74],
                           kc == 0, kc == 7, R=Wv_k + [("hT", kc, jg)], W=[tvk])
                tv3 = tv[:, 0:296].rearrange("p (j n) -> p j n", j=4)
                os_ = slot("ost", 4)
                act(ost[:, os_, 0:256].rearrange("p (j n) -> p j n", j=4), tv3[:, :, 0:64], AF.Copy, R=[tvk], W=ost_k[os_])
                dma(sc_sv[:, jg * 4:(jg + 1) * 4, :], ost[:, os_, 0:256].rearrange("p (j n) -> p j n", j=4),
                    R=ost_k[os_], W=[("scsv", jg)])
                P.add("dve", lambda e, jg=jg, tv3=tv3: e.tensor_copy(out=FFt[:, jg * 4:(jg + 1) * 4, :], in_=tv3[:, :, 64:70]),
                      R=[tvk], W=[("FFt", jg)])
                P.add("dve", lambda e, jg=jg, tv3=tv3: e.tensor_scalar(
                    out=IWs[:, jg * 4:(jg + 1) * 4, :], in0=tv3[:, :, 70:74], scalar1=0.0625, scalar2=None, op0=ALU.mult),
                    R=[tvk], W=[("IWs", jg)])

        tm_group([(O_SV, 64), (O_FF, 6), (O_IW, 4)], 74, small)

    TL = [[rv(28 * s_ + 4 * i_, 4, BF16) for i_ in range(4)] for s_ in range(2)]
    TL_k = [[rk(28 * s_ + 4 * i_, 4) for i_ in range(4)] for s_ in range(2)]
    Kd = [rv(28 * s_ + 16, 4, BF16) for s_ in range(2)]
    Kd_k = [rk(28 * s_ + 16, 4) for s_ in range(2)]
    Vp = [rv(28 * s_ + 20, 8, BF16).rearrange("p (j s d) -> p j s d", j=16, s=4) for s_ in range(2)]
    Vp_k = [rk(28 * s_ + 20, 8) for s_ in range(2)]
    identb = CST["irep"][:, 0:128]

    def load_v(sl, vi):
        dma(Vp[sl][:, :, 0, :], sc_v[vi][:, :, 0:64], R=[("scv", vi)], W=Vp_k[sl])
        dma(Vp[sl][:, :, 3, :], sc_v[vi][:, :, 64:128], R=[("scv", vi)], W=Vp_k[sl])
        P.add("pool", lambda e: e.memset(Vp[sl][:, :, 1:3, :], 1.0), W=Vp_k[sl])

    def load_fox_pair(sl, p_):
        for i_, t_ in enumerate((2 * p_, 2 * p_ + 1, 6 + 2 * p_, 7 + 2 * p_)):
            dma(TL[sl][i_], sc_pad[t_], R=[("scp", t_)], W=TL_k[sl][i_])
        load_v(sl, p_)

    def load_diff_pair(sl, d_):
        for i_ in range(4):
            dma(TL[sl][i_], sc_pad[12 + 4 * d_ + i_], R=[("scp", 12 + 4 * d_ + i_)], W=TL_k[sl][i_])
        dma(Kd[sl], sc_fm[8 + d_], R=[("scfm", 8 + d_)], W=Kd_k[sl])
        load_v(sl, 3 + d_)

    SKA = Skew(2)

    def attn_map(sl, e_, lhs, lhsk, rhs, rhsk, c, h, fox, acc, acck):
        nj = 4 * c + 4
        for j in range(nj):
            n0 = max(j * 128, c * 512)
            wN = (c + 1) * 512 - n0
            off = n0 - c * 512
            diag = j >= 4 * c
            st, stk = bank((4, 5, 6, 7))
            mm(st[:, off:off + wN], lhs[:, j * 128:(j + 1) * 128], rhs[:, n0:n0 + wN],
               True, not diag, R=lhsk + rhsk, W=[stk])
            if diag:
                mm(st[:, off:off + 128], identb, CST["cb"][:, :], False, True, R=[("k", "irep"), ("k", "cb")], W=[stk])
            ps_ = slot("pt", 4)
            if fox:
                act(pt[:, ps_, off:off + wN], st[:, off:off + wN], AF.Exp, R=[stk, ("negc",)], W=[("pt", ps_)],
                    bias=negc[:, j, h:h + 1])
            else:
                act(pt[:, ps_, off:off + wN], st[:, off:off + wN], AF.Exp, R=[stk], W=[("pt", ps_)])
            SKA.push(lambda j=j, off=off, wN=wN, ps_=ps_: mm(
                acc[:, off:off + wN], Vp[sl][:, j, 2 * e_:2 * e_ + 2, :].rearrange("p s d -> p (s d)"), pt[:, ps_, off:off + wN],
                j == 0, j == nj - 1, R=Vp_k[sl] + [("pt", ps_)], W=[acck]))

    CT = rv(56, 4, BF16)
    CT_k = rk(56, 4)
    CS = rv(28, 6).rearrange("p (j n) -> p j n", j=16)
    CS_k = rk(28, 6)
    selv = CST["sel"][:, :].rearrange("p (h n) -> p h n", h=6)

    def fox_phase():
        P.add("dve", lambda e: e.tensor_tensor(out=Lt[:, :, :], in0=FFt[:, :, :], in1=apb(ppt[:, 23:29], 16), op=ALU.add),
              R=[("FFt",), ("ppt",)], W=[("Lt",)])
        act(Lt[:, :, :], Lt[:, :, :], AF.Exp, R=[("Lt",)], W=[("Lt",)], scale=-1.0)
        act(Lt[:, :, :], Lt[:, :, :], AF.Ln, R=[("Lt",)] + KEPS, W=[("Lt",)], bias=one_ap)
        cps, cpsk = bank((0,))
        for i in range(16):
            for j in range(i + 1):
                mm(cps[:, i * 6:(i + 1) * 6], CST["tri" if j == i else "negones"][:, :], Lt[:, j, :], j == 0, j == i,
                   R=[("Lt",), ("k", "tri"), ("k", "negones")], W=[cpsk])
        cps3 = cps[:, 0:96].rearrange("p (j n) -> p j n", j=16)
        P.add("dve", lambda e: e.tensor_scalar(out=negc[:, :, :], in0=cps3, scalar1=-1.0, scalar2=None, op0=ALU.mult),
              R=[cpsk], W=[("negc",)])
        P.add("pool", lambda e: e.memset(CS[:, :, :], 0.0), W=CS_k)
        P.add("dve", lambda e: e.tensor_copy(out=chb[:, :, :], in_=cps3), R=[cpsk], W=[("chb",)])
        P.add("dve", lambda e: e.tensor_copy(out=CS[:, :, 0:6], in_=chb[:, :, :]), R=[("chb",)], W=CS_k)
        P.add("dve", lambda e: e.tensor_tensor(out=r1t[:, :, :], in0=cps3, in1=chb[:, :, :], op=ALU.subtract),
              R=[cpsk, ("chb",)], W=[("r1t",)])
        P.add("dve", lambda e: e.tensor_copy(out=chb[:, :, :], in_=r1t[:, :, :]), R=[("r1t",)], W=[("chb",)])
        P.add("dve", lambda e: e.tensor_copy(out=CS[:, :, 32:38], in_=chb[:, :, :]), R=[("chb",)], W=CS_k)
        P.add("dve", lambda e: e.tensor_tensor(out=r1t[:, :, :], in0=r1t[:, :, :], in1=chb[:, :, :], op=ALU.subtract),
              R=[("r1t",), ("chb",)], W=[("r1t",)])
        P.add("dve", lambda e: e.tensor_copy(out=chb[:, :, :], in_=r1t[:, :, :]), R=[("r1t",)], W=[("chb",)])
        P.add("dve", lambda e: e.tensor_copy(out=CS[:, :, 64:70], in_=chb[:, :, :]), R=[("chb",)], W=CS_k)
        for q_ in range(4):
            ctp, ctpk = bank((1, 2, 3))
            for jj in range(4):
                j = q_ * 4 + jj
                P.add("pe", lambda e, ctp=ctp, jj=jj, j=j: e.transpose(
                    out=ctp[0:96, jj * 128:(jj + 1) * 128], in_=CS[:, j, :], identity=CST["ident"][:, :]),
                    R=CS_k + [("k", "ident")], W=[ctpk])
            act(CT[0:96, q_ * 512:(q_ + 1) * 512], ctp[0:96, :], AF.Copy, R=[ctpk], W=CT_k)
        tap("negc", negc[:, :, :], [("negc",)])
        for h_ in range(6):
            tl_ = 2 * (h_ // 2) + (h_ % 2)
            r0_ = 64 if h_ % 2 == 0 else 0
            for q_ in range(3):
                dma(sc_pad[tl_, r0_ + q_:r0_ + q_ + 1, :], CT[32 * q_ + h_:32 * q_ + h_ + 1, :], R=CT_k, W=[("scp", tl_, "c")])
        load_fox_pair(0, 0)
        for p_ in range(3):
            sl = p_ % 2
            SKA.flush()
            if p_ + 1 < 3:
                load_fox_pair((p_ + 1) % 2, p_ + 1)
            for c in range(NCH):
                cs = slice(c * 512, (c + 1) * 512)
                for e_ in range(2):
                    base = 64 * e_
                    acc, acck = bank((0, 1, 2))
                    attn_map(sl, e_, TL[sl][2 + e_], TL_k[sl][2 + e_], TL[sl][e_], TL_k[sl][e_], c, 2 * p_ + e_, True, acc, acck)
                    def fin(base=base, acc=acc, acck=acck, p_=p_, cs=cs, c=c):
                        O = slice(base, base + 64)
                        Dn = slice(64 - base, 128 - base)
                        rc = slot("rct", 2)
                        recip(rct[O, rc, :], acc[Dn, :], R=[acck], W=[("rct", rc)])
                        P.add("dve", lambda e: e.tensor_tensor(out=hT[O, p_, cs], in0=acc[O, :], in1=rct[O, rc, :], op=ALU.mult),
                              R=[acck, ("rct", rc)], W=[("hT", p_, c)])
                    SKA.push(fin)
        SKA.flush()

    def diff_phase():
        load_diff_pair(0, 0)
        for d_ in range(2):
            sl = d_ % 2
            SKA.flush()
            if d_ == 0:
                load_diff_pair(1, 1)
            for c in range(NCH):
                cs = slice(c * 512, (c + 1) * 512)
                od = slot("t1", 2)
                for e_ in range(2):
                    accs = []
                    for m_ in range(2):
                        base = 64 * e_ + 32 * m_
                        acc, acck = bank((0, 1, 2))
                        attn_map(sl, e_, Kd[sl], Kd_k[sl], TL[sl][2 * e_ + m_], TL_k[sl][2 * e_ + m_], c, 0, False, acc, acck)
                        accs.append((acc, acck))
                    def comb(e_=e_, accs=accs, od=od):
                        O = slice(64 * e_, 64 * e_ + 64)
                        Dn = slice(64 - 64 * e_, 128 - 64 * e_)
                        (a1, a1k), (a2, a2k) = accs
                        ra = slot("rct", 2)
                        rb = slot("rct", 2)
                        ob = slot("t2", 2)
                        recip(rct[O, ra, :], a1[Dn, :], R=[a1k], W=[("rct", ra)])
                        recip(rct[O, rb, :], a2[Dn, :], R=[a2k], W=[("rct", rb)])
                        P.add("dve", lambda e: e.tensor_scalar(out=rct[O, rb, :], in0=rct[O, rb, :], scalar1=der[O, 4:5],
                                                               scalar2=None, op0=ALU.mult),
                              R=[("rct", rb), ("der", 4)], W=[("rct", rb)])
                        P.add("dve", lambda e: e.tensor_tensor(out=t1[O, od, :], in0=a1[O, :], in1=rct[O, ra, :], op=ALU.mult),
                              R=[a1k, ("rct", ra)], W=[("t1", od)])
                        P.add("dve", lambda e: e.tensor_tensor(out=t2[O, ob, :], in0=a2[O, :], in1=rct[O, rb, :], op=ALU.mult),
                              R=[a2k, ("rct", rb)], W=[("t2", ob)])
                        P.add("pool", lambda e: e.tensor_tensor(out=t1[O, od, :], in0=t1[O, od, :], in1=t2[O, ob, :], op=ALU.add),
                              R=[("t1", od), ("t2", ob)], W=[("t1", od)])
                    SKA.push(comb)

                def subln(od=od, d_=d_, cs=cs, c=c):
                    sq_ = slot("sqt", 2)
                    act(sqt[:, sq_, :], t1[:, od, :], AF.Square, R=[("t1", od)], W=[("sqt", sq_)])
                    sb_, sbk = bank((3,))
                    mm(sb_[:, :], CST["bones64"][:, :], sqt[:, sq_, :], True, True, R=[("sqt", sq_), ("k", "bones64")], W=[sbk])
                    r_ = slot("rs", 2)
                    act(rs[:, r_, :], sb_[:, :], AF.Ln, R=[sbk] + KEPS, W=[("rs", r_)], bias=eps_ap, scale=1.0 / 64)
                    act(rs[:, r_, :], rs[:, r_, :], AF.Exp, R=[("rs", r_)], W=[("rs", r_)], scale=-0.5)
                    P.add("dve", lambda e: e.scalar_tensor_tensor(
                        out=hT[:, 3 + d_, cs], in0=t1[:, od, :], scalar=der[:, 3:4], in1=rs[:, r_, :], op0=ALU.mult, op1=ALU.mult),
                        R=[("t1", od), ("der", 3), ("rs", r_)], W=[("hT", 3 + d_, c)])
                SKA.push(subln)
        SKA.flush()

    def dsa_phase():
        Qs = rv(0, 12, BF16).rearrange("p (t n) -> p t n", t=3)
        Qs_k = rk(0, 12)
        SKK, SKK_k = rv(12, 4, BF16), rk(12, 4)
        IKK, IKK_k = rv(16, 4, BF16), rk(16, 4)
        IQ = rv(20, 8, BF16).rearrange("p (t n) -> p t n", t=2)
        IQ_k = rk(20, 8)
        SVa = rv(28, 6, BF16).rearrange("p (j s d) -> p j s d", j=16, s=3)
        SVa_k = rk(28, 6)
        accb, acc_k = rv(36, 8), rk(36, 8)
        MBs = [(rv(44, 4, BF16), rk(44, 4)), (rv(56, 4, BF16), rk(56, 4))]
        PERT, PERT_k = rv(48, 8), rk(48, 8)
        junk = t2[:, :, :].rearrange("p a n -> p (a n)").bitcast(BF16)
        junk_k = [("t2",)]
        for t_ in range(3):
            dma(Qs[:, t_, :], sc_fm[10 + t_], R=[("scfm", 10 + t_)], W=Qs_k)
        dma(SKK, sc_fm[13], R=[("scfm", 13)], W=SKK_k)
        dma(IKK, sc_fm[14], R=[("scfm", 14)], W=IKK_k)
        for t_ in range(2):
            dma(IQ[:, t_, :], sc_fm[15 + t_], R=[("scfm", 15 + t_)], W=IQ_k)
        dma(SVa[:, :, 0, :], sc_sv, R=[("scsv",)], W=SVa_k)
        dma(SVa[:, :, 2, :], sc_sv, R=[("scsv",)], W=SVa_k)
        P.add("pool", lambda e: e.memset(SVa[:, :, 1, :], 1.0), W=SVa_k)
        dma(PERT, pert_d, R=[], W=PERT_k)

        accs_ = [(accb, acc_k), (accB_t[:, :], [("accB",)])]
        junks = [(junk, junk_k), (t1[:, :, :].rearrange("p a n -> p (a n)").bitcast(BF16), [("t1",)])]

        def index_block(i):
            q = i % 2
            acc_, acck_ = accs_[q]
            nk = (i + 1) * 128
            nch = (nk + 511) // 512
            for hh in range(4):
                tl, base = hh // 2, 64 * (hh % 2)
                for m_ in range(nch):
                    w_ = min(512, nk - 512 * m_)
                    dp, dpk = bank((0, 1))
                    mm(dp[:, 0:w_], IQ[base:base + 64, tl, i * 128:(i + 1) * 128], IKK[base:base + 64, 512 * m_:512 * m_ + w_],
                       True, True, R=IQ_k + IKK_k, W=[dpk])
                    act(dp[:, 0:w_], dp[:, 0:w_], AF.Relu, R=[dpk], W=[dpk])
                    src = PERT if hh == 0 else acc_
                    srck = PERT_k if hh == 0 else acck_
                    P.add("dve", lambda e, dp=dp, w_=w_, m_=m_, hh=hh, src=src: e.scalar_tensor_tensor(
                        out=acc_[:, 512 * m_:512 * m_ + w_], in0=dp[:, 0:w_], scalar=IWs[:, i, hh:hh + 1],
                        in1=src[:, 512 * m_:512 * m_ + w_], op0=ALU.mult, op1=ALU.add),
                        R=[dpk, ("IWs",)] + srck, W=acck_)
            P.add("dve", lambda e: e.tensor_reduce(out=bis2[:, q, 0:1], in_=acc_[:, 0:nk], axis=AX.X, op=ALU.max,
                                                   apply_absolute_value=True), R=acck_, W=[("bis2", q, 0)])
            P.add("dve", lambda e: e.tensor_tensor(out=acc_[:, i * 128:(i + 1) * 128], in0=acc_[:, i * 128:(i + 1) * 128],
                                                   in1=CST["cbt"][:, :], op=ALU.add), R=acck_ + [("k", "cbt")], W=acck_)
            P.add("dve", lambda e: e.tensor_scalar(out=STt2[:, q, :], in0=CST["pow2"][:, :], scalar1=bis2[:, q, 0:1], scalar2=None,
                                                   op0=ALU.mult), R=[("bis2", q, 0), ("k", "pow2")], W=[("STt2", q)])
            P.add("dve", lambda e: e.memset(bis2[:, q, 1:2], 0.0), W=[("bis2", q, 1)])

        def bisect_pair(iA, iB, zsteps):
            blocks = [b_ for b_ in (iA, iB) if b_ is not None]
            zsteps = list(zsteps)
            per_it = (len(zsteps) + KBIS - 1) // KBIS
            for k in range(KBIS):
                for _ in range(per_it):
                    if zsteps:
                        zsteps.pop(0)()
                for i in blocks:
                    q = i % 2
                    acc_, acck_ = accs_[q]
                    jk, jkk = junks[q]
                    nk = (i + 1) * 128
                    if q == 0:
                        P.add("dve", lambda e, acc_=acc_, jk=jk, nk=nk, q=q: e.tensor_scalar(
                            out=jk[:, 0:nk], in0=acc_[:, 0:nk], scalar1=bis2[:, q, 1:2], scalar2=None,
                            op0=ALU.is_gt, op1=ALU.add, accum_out=bis2[:, q, 2:3]),
                            R=acck_ + [("bis2", q, 1)], W=jkk + [("bis2", q, 2)])
                    else:
                        act(jk[:, 0:nk], acc_[:, 0:nk], AF.Sign, R=acck_ + [("bis2", q, 1)], W=jkk + [("bis2", q, 2)],
                            bias=bis2[:, q, 1:2], scale=-1.0, accum_out=bis2[:, q, 2:3])
                for i in blocks:
                    q = i % 2
                    nk = (i + 1) * 128
                    if q == 0:
                        P.add("dve", lambda e, k=k, q=q: e.tensor_scalar(
                            out=bis2[:, q, 3:4], in0=bis2[:, q, 2:3], scalar1=TOPK - 0.5, scalar2=STt2[:, q, 32 + k:33 + k],
                            op0=ALU.is_gt, op1=ALU.mult), R=[("bis2", q, 2), ("STt2", q)], W=[("bis2", q, 3)])
                    else:
                        P.add("dve", lambda e, k=k, q=q, nk=nk: e.tensor_scalar(
                            out=bis2[:, q, 3:4], in0=bis2[:, q, 2:3], scalar1=float(nk - 2 * TOPK + 1),
                            scalar2=STt2[:, q, 32 + k:33 + k], op0=ALU.is_lt, op1=ALU.mult),
                            R=[("bis2", q, 2), ("STt2", q)], W=[("bis2", q, 3)])
                    P.add("dve", lambda e, k=k, q=q: e.scalar_tensor_tensor(
                        out=bis2[:, q, 1:2], in0=bis2[:, q, 3:4], scalar=STt2[:, q, k:k + 1], in1=bis2[:, q, 1:2],
                        op0=ALU.subtract, op1=ALU.add),
                        R=[("bis2", q, 3), ("bis2", q, 1), ("STt2", q)], W=[("bis2", q, 1)])
            while zsteps:
                zsteps.pop(0)()
            for i in blocks:
                q = i % 2
                acc_, acck_ = accs_[q]
                MB, MB_k = MBs[q]
                nk = (i + 1) * 128
                P.add("dve", lambda e, acc_=acc_, MB=MB, nk=nk, q=q: e.tensor_scalar(
                    out=MB[:, 0:nk], in0=acc_[:, 0:nk], scalar1=bis2[:, q, 1:2], scalar2=NEG, op0=ALU.is_le, op1=ALU.mult),
                    R=acck_ + [("bis2", q, 1)], W=MB_k)

        SKD = Skew(1)
        accE, accEk = banks[6], ("ps", 6)
        accO, accOk = banks[7], ("ps", 7)

        def attend_steps(i):
            MB, MB_k = MBs[i % 2]
            c = i // 4
            qs_ = slice(i * 128, (i + 1) * 128)
            steps = []

            def step(j):
                ks_ = slice(j * 128, (j + 1) * 128)
                sts = []
                for par in range(2):
                    st, stk = bank((2, 3, 4, 5))
                    pr = slice(64 * par, 64 * par + 64)
                    for hi in range(3):
                        mm(st[:, hi * 128:(hi + 1) * 128], SKK[pr, ks_], Qs[pr, hi, qs_], hi == 0, False,
                           R=SKK_k + Qs_k, W=[stk])
                    mm(st[:, 0:384], MB[:, ks_], CST["irep"][:, 0:384], False, True, R=MB_k + [("k", "irep")], W=[stk])
                    sts.append((st, stk))
                pss = []
                for par in range(2):
                    ps_ = slot("pt", 4)
                    act(pt[:, ps_, 0:384], sts[par][0][:, 0:384], AF.Exp, R=[sts[par][1]], W=[("pt", ps_)])
                    pss.append(ps_)

                def pv():
                    mm(accE[:, 0:384], SVa[:, j, 0:2, :].rearrange("p s d -> p (s d)"), pt[:, pss[0], 0:384], j == 0, j == i,
                       R=SVa_k + [("pt", pss[0])], W=[accEk])
                    mm(accO[:, 0:384], SVa[:, j, 1:3, :].rearrange("p s d -> p (s d)"), pt[:, pss[1], 0:384], j == 0, j == i,
                       R=SVa_k + [("pt", pss[1])], W=[accOk])
                SKD.push(pv)

            def fin():
                for par, (acc, acck) in enumerate(((accE, accEk), (accO, accOk))):
                    O = slice(64 * par, 64 * par + 64)
                    Dn = slice(64 - 64 * par, 128 - 64 * par)
                    rc = slot("rct", 2)
                    recip(rct[O, rc, 0:384], acc[Dn, 0:384], R=[acck], W=[("rct", rc)])
                    P.add("dve", lambda e, O=O, rc=rc, acc=acc: e.tensor_tensor(
                        out=hT[O, 5:8, qs_], in0=acc[O, 0:384].rearrange("p (h n) -> p h n", h=3),
                        in1=rct[O, rc, 0:384].rearrange("p (h n) -> p h n", h=3), op=ALU.mult),
                        R=[acck, ("rct", rc)], W=[("hT", 5, c), ("hT", 6, c), ("hT", 7, c)])

            for j in range(i + 1):
                steps.append(lambda j=j: step(j))
            steps.append(lambda: SKD.push(fin))
            return steps

        index_block(0)
        index_block(1)
        prev_steps = []
        for m_ in range(8):
            bisect_pair(2 * m_, 2 * m_ + 1, prev_steps)
            prev_steps = attend_steps(2 * m_) + attend_steps(2 * m_ + 1)
            if m_ + 1 < 8:
                index_block(2 * m_ + 2)
                index_block(2 * m_ + 3)
        for st_ in prev_steps:
            st_()
        SKD.flush()
        tap("acc15", accB_t[:, :], [("accB",)])
        tap("MB15", MBs[1][0], MBs[1][1])
        tap("bis15", bis2[:, 1, :], [("bis2",)])
        tap("STt", STt2[:, 1, :], [("STt2",)])

    def wout_phase(li):
        Wo = [rv(56, 2, BF16).rearrange("p (k n) -> p k n", k=8), rv(58, 2, BF16).rearrange("p (k n) -> p k n", k=8)]
        Wo_k = [[("R", 14, 0)], [("R", 14, 1)]]
        wsrc = w_out_d[li].rearrange("(kt p) n -> p kt n", p=128)
        load_w(Wo[0], Wo_k[0], wsrc[:, :, 0:128])
        for d_ in range(8):
            sl = d_ % 2
            if d_ + 1 < 8:
                load_w(Wo[1 - sl], Wo_k[1 - sl], wsrc[:, :, (d_ + 1) * 128:(d_ + 2) * 128])
            for c in range(NCH):
                cs = slice(c * 512, (c + 1) * 512)
                bk, bkk = bank((0, 1, 2, 3, 4, 5, 6, 7))
                for kt in range(8):
                    mm(bk[:, :], Wo[sl][:, kt, :], hT[:, kt, cs], kt == 0, kt == 7, R=Wo_k[sl] + [("hT", kt, c)], W=[bkk])
                P.add("dve", lambda e, d_=d_, cs=cs, bk=bk: e.tensor_tensor(
                    out=xT[:, d_, cs], in0=bk[:, :], in1=xT[:, d_, cs], op=ALU.add),
                    R=[bkk, ("xT", d_, c)], W=[("xT", d_, c)])

    for li, labs in enumerate(layer_ids):
        dma(ppt[:, :], pp_d[li], R=[], W=[("ppt",)])
        derive_phase(labs)
        norm_phase(7)
        proj_phase(li)
        if li == 0:
            tap("sc_fm", sc_fm, [("scfm",)])
            tap("sc_v", sc_v, [("scv",)])
            tap("sc_sv", sc_sv, [("scsv",)])
            tap("FFt", FFt[:, :, :], [("FFt",)])
            tap("IWs", IWs[:, :, :], [("IWs",)])
        fox_phase()
        diff_phase()
        dsa_phase()
        if li == 0:
            tap("cat", hT[:, :, :], [("hT",)])
        wout_phase(li)
        if li == 0:
            tap("x1", xT[:, :, :], [("xT",)])
        norm_phase(15)
        ffn_phase(li)

    for kc in range(8):
        dma(y_d[kc * 128:(kc + 1) * 128, :], xT[:, kc, :], R=[("xT", kc)], W=[("y", kc)])

    P.emit(nc, stack)
    stack.close()
    return nc, P.stats


_NC_CACHE = {}


def _get_nc(layer_ids):
    key = tuple(layer_ids)
    if key not in _NC_CACHE:
        _NC_CACHE[key] = build_nc(list(layer_ids))[0]
    return _NC_CACHE[key]


def kernel(**inputs):
    inp = {k: np.asarray(v) for k, v in inputs.items()}
    x = inp["x"].astype(np.float32, copy=False)
    B = x.shape[0]
    cst = _consts()
    base = {}
    for nm, w in _CONST_SHAPES:
        base["c_" + nm] = np.ascontiguousarray(cst[nm], dtype=np.float32)
    base["c_rope"] = cst["rope"]
    base["c_pert"] = cst["pert"]
    layer_ids = list(range(DEPTH))
    nc = _get_nc(layer_ids)
    base["w_in"] = np.ascontiguousarray(inp["w_in"], dtype=np.float32)
    base["w_out"] = np.ascontiguousarray(inp["w_out"], dtype=np.float32)
    base["w_gu"] = np.ascontiguousarray(inp["w_gate_up"], dtype=np.float32)
    base["w_dn"] = np.ascontiguousarray(inp["w_down"], dtype=np.float32)
    base["pp"] = np.stack([_pack_pp(inp, l) for l in layer_ids]).astype(np.float32)
    in_maps = []
    for b in range(B):
        m = dict(base)
        m["xT"] = np.ascontiguousarray(x[b].T)
        in_maps.append(m)
    res = run_bass_kernel_spmd(nc, in_maps, core_ids=list(range(B)))
    out = np.stack([np.asarray(r["yT"]).T for r in res.results]).astype(np.float32)
    return out
```

```python
import math
from contextlib import ExitStack
import numpy as np
import concourse.bass as bass
import concourse.mybir as mybir
from concourse.bass_utils import run_bass_kernel_spmd

F32 = mybir.dt.float32
BF16 = mybir.dt.bfloat16
ALU = mybir.AluOpType
AF = mybir.ActivationFunctionType
AX = mybir.AxisListType

D = 1024
S = 2048
DEPTH = 4
NCH = 4
FFN_H = 2816
INW = 2762
EPS = 1e-6
NEG = -30000.0
KBIS = 14
TOPK = 256
PERT_EPS = 2.0 ** -20
NPP = 157

O_FQ, O_FK, O_FV, O_FF = 0, 384, 768, 1152
O_DQ, O_DK, O_DV = 1158, 1414, 1670
O_SQ, O_SK, O_SV, O_IQ, O_IK, O_IW = 1926, 2310, 2374, 2438, 2694, 2758


class Prog:
    ENGS = ("pe", "act", "dve", "pool", "sp")
    NDMA = 24

    def __init__(self):
        self.ops = []

    def add(self, eng, fn, R=(), W=(), dma=False):
        R = tuple(R)
        W = tuple(W) + tuple(k for k in R if k[0] == "ps" and k not in W)
        R = tuple(k for k in R if k[0] != "ps")
        self.ops.append((eng, fn, R, W, dma))

    def _deps(self):
        lastw = {}
        readers = {}
        desc = {}
        deps_all = []

        def related(k):
            out = []
            for i in range(1, len(k)):
                p = k[:i]
                if p in lastw or p in readers:
                    out.append(p)
            out.extend(desc.get(k, ()))
            return out

        def register(k):
            if k in lastw or k in readers:
                return
            for i in range(1, len(k) + 1):
                desc.setdefault(k[:i], set()).add(k)

        for i, (eng, fn, R, W, dma) in enumerate(self.ops):
            d = set()
            for k in R:
                register(k)
                lastw.setdefault(k, None)
                for r in related(k):
                    w = lastw.get(r)
                    if w is not None:
                        d.add(w)
            for k in W:
                register(k)
                lastw.setdefault(k, None)
                for r in related(k):
                    w = lastw.get(r)
                    if w is not None:
                        d.add(w)
                    d.update(readers.get(r, ()))
            d.discard(i)
            for k in R:
                readers.setdefault(k, []).append(i)
            for k in W:
                lastw[k] = i
                for r in desc.get(k, ()):
                    if r in readers:
                        readers[r] = []
                    lastw[r] = i
            deps_all.append(d)
        return deps_all

    def emit(self, nc, stack):
        ops = self.ops
        deps_all = self._deps()
        n = len(ops)
        sig = [False] * n
        for i, d in enumerate(deps_all):
            e_i = ops[i][0]
            for j in d:
                if ops[j][0] == "pe" and e_i == "pe":
                    continue
                sig[j] = True
        dma_prev = {}
        dma_slot = {}
        nd = 0
        for i, op in enumerate(ops):
            if op[4]:
                s = nd % self.NDMA
                nd += 1
                dma_slot[i] = s
                if s in dma_prev:
                    deps_all[i].add(dma_prev[s])
                dma_prev[s] = i
                sig[i] = True
        EPOCH = 1000
        esems = {e: [] for e in ("pe", "act", "dve", "pool")}
        dsem = [stack.enter_context(nc.semaphore("dsem%d" % k)) for k in range(min(self.NDMA, max(nd, 1)))]
        count = {e: 0 for e in esems}
        dcount = [0] * self.NDMA
        known = {e: {} for e in self.ENGS}
        event = [None] * n
        vc = [None] * n
        plan = {e: [] for e in self.ENGS}
        nwaits = 0
        Z = (0, 0)

        def sem_of(src, ep):
            if isinstance(src, tuple):
                return dsem[src[1]]
            lst = esems[src]
            while len(lst) <= ep:
                lst.append(stack.enter_context(nc.semaphore("sem_%s_%d" % (src, len(lst)))))
            return lst[ep]

        for i, (eng, fn, R, W, dma) in enumerate(ops):
            kn = known[eng]
            wm = {}
            for j in sorted(deps_all[i]):
                if ops[j][0] == "pe" and eng == "pe":
                    continue
                src, val = event[j]
                if kn.get(src, Z) >= val:
                    continue
                if wm.get(src, Z) < val:
                    wm[src] = val
                for s2, v2 in vc[j].items():
                    if kn.get(s2, Z) < v2:
                        kn[s2] = v2
            nwaits += len(wm)
            inc = None
            if sig[i]:
                if dma:
                    s = dma_slot[i]
                    dcount[s] += 16
                    event[i] = (("d", s), (0, dcount[s]))
                    inc = (dsem[s], 16)
                else:
                    ep, cn = divmod(count[eng], EPOCH)
                    count[eng] += 1
                    event[i] = (eng, (ep, cn + 1))
                    inc = (sem_of(eng, ep), 1)
                v = dict(kn)
                v[event[i][0]] = event[i][1]
                vc[i] = v
            plan[eng].append((fn, [(sem_of(src, val[0]), val[1]) for src, val in wm.items()], inc))
        self.stats = dict(n_ops=n, n_waits=nwaits, counts=dict(count), n_dma=nd,
                          n_sems=len(dsem) + sum(len(v) for v in esems.values()))

        block = stack.enter_context(nc.Block())

        def run(engine, items):
            for fn, waits, inc in items:
                for sem, val in waits:
                    engine.wait_ge(sem, val)
                ins = fn(engine)
                if inc is not None:
                    ins.then_inc(inc[0], inc[1])

        @block.tensor
        def _(e):
            run(e, plan["pe"])

        @block.scalar
        def _(e):
            run(e, plan["act"])

        @block.vector
        def _(e):
            run(e, plan["dve"])

        @block.gpsimd
        def _(e):
            run(e, plan["pool"])

        @block.sync
        def _(e):
            run(e, plan["sp"])
            for s in range(len(dsem)):
                if dcount[s] > 0:
                    e.wait_ge(dsem[s], dcount[s])


class Skew:
    def __init__(self, lag):
        self.q = []
        self.lag = lag

    def push(self, fn):
        self.q.append(fn)
        while len(self.q) > self.lag:
            self.q.pop(0)()

    def flush(self):
        while self.q:
            self.q.pop(0)()


def _rope_tab(head_dim, rows_rep):
    rot = head_dim // 4
    half = rot // 2
    inv = (1.0 / (np.float32(500000.0) ** (np.arange(0, rot, 2, dtype=np.float32) / np.float32(rot)))).astype(np.float32)
    ang = np.arange(S, dtype=np.float32)[:, None] * inv[None, :]
    cos = np.cos(ang).astype(np.float32).T
    sin = np.sin(ang).astype(np.float32).T
    C = np.ones((128, S), np.float32)
    Sn = np.zeros((128, S), np.float32)
    for b in range(128 // head_dim):
        o = b * head_dim
        C[o:o + half] = cos
        C[o + half:o + rot] = cos
        Sn[o:o + half] = sin
        Sn[o + half:o + rot] = sin
    P = np.zeros((128, 128), np.float32)
    for b in range(128 // head_dim):
        o = b * head_dim
        for r in range(half):
            P[o + r + half, o + r] = -1.0
            P[o + r, o + r + half] = 1.0
    return C, Sn, P


def _consts():
    c = {}
    eye = np.eye(128, dtype=np.float32)
    idx = np.arange(128)
    c["ident"] = eye
    c["tri"] = -(idx[:, None] <= idx[None, :]).astype(np.float32)
    c["negones"] = -np.ones((128, 128), np.float32)
    c["ones"] = np.ones((128, 128), np.float32)
    b64 = np.zeros((128, 128), np.float32)
    b64[:64, :64] = 1
    b64[64:, 64:] = 1
    c["bones64"] = b64
    b32 = np.zeros((128, 128), np.float32)
    for b in range(4):
        b32[32 * b:32 * b + 32, 32 * b:32 * b + 32] = 1
    c["bones32"] = b32
    sel = np.zeros((128, 6, 128), np.float32)
    for h in range(6):
        sel[h, h, :] = 1
        sel[32 + h, h, :] = 1
        sel[64 + h, h, :] = 1
    c["sel"] = sel.reshape(128, 768)
    c["cb"] = np.where(idx[:, None] <= idx[None, :], 0.0, NEG).astype(np.float32)
    c["cbt"] = np.where(idx[None, :] <= idx[:, None], 0.0, NEG).astype(np.float32)
    c["irep"] = np.tile(eye, (1, 4))
    C64, S64, P64 = _rope_tab(64, 2)
    C32, S32, P32 = _rope_tab(32, 4)
    c["prot64"] = P64
    c["prot32"] = P32
    c["rope"] = np.stack([C32, S32, C64, S64]).astype(np.float32)
    c["pert"] = np.tile((-PERT_EPS * np.arange(S, dtype=np.float32))[None, :], (128, 1)).astype(np.float32)
    p2 = np.zeros((128, 64), np.float32)
    for k in range(32):
        p2[:, k] = 2.0 ** (-k)
        p2[:, 32 + k] = 2.0 ** (1 - k)
    c["pow2"] = p2
    return c


_CONST_SHAPES = [("ident", 128), ("tri", 128), ("negones", 128), ("ones", 128), ("bones64", 128), ("bones32", 128),
                 ("cb", 128), ("cbt", 128), ("irep", 512), ("prot64", 128), ("prot32", 128), ("pow2", 64)]


def _pack_pp(inp, l):
    pp = np.zeros((128, NPP), np.float32)
    p = np.arange(128)
    pp[:, 0] = inp["fox_qn"][l][p % 64]
    pp[:, 1] = inp["fox_kn"][l][p % 64]
    pp[:, 2] = inp["diff_qn"][l][p % 32]
    pp[:, 3] = inp["diff_kn"][l][p % 32]
    pp[:, 4] = inp["dsa_qn"][l][p % 64]
    pp[:, 5] = inp["dsa_kn"][l][p % 64]
    pp[:, 6] = inp["diff_subln"][l][p % 64]
    pp[:, 7:15] = inp["attn_norm"][l].reshape(8, 128).T
    pp[:, 15:23] = inp["ffn_norm"][l].reshape(8, 128).T
    pp[:, 23:29] = inp["fox_fb"][l][None, :]
    pp[:, 29:61] = inp["diff_lq1"][l][None, :]
    pp[:, 61:93] = inp["diff_lk1"][l][None, :]
    pp[:, 93:125] = inp["diff_lq2"][l][None, :]
    pp[:, 125:157] = inp["diff_lk2"][l][None, :]
    return pp


def build_nc(layer_ids, debug=None):
    NL = len(layer_ids)
    nc = bass.Bass("TRN2", target_bir_lowering=False)
    stack = ExitStack()
    P = Prog()

    def dram(name, shape, dt=F32, kind="ExternalInput"):
        return nc.dram_tensor(name, list(shape), dt, kind=kind).ap()

    x_d = dram("xT", [D, S])
    y_d = dram("yT", [D, S], kind="ExternalOutput")
    w_in_d = dram("w_in", [NL, D, INW])
    w_out_d = dram("w_out", [NL, D, D])
    w_gu_d = dram("w_gu", [NL, D, 2 * FFN_H])
    w_dn_d = dram("w_dn", [NL, FFN_H, D])
    pp_d = dram("pp", [NL, 128, NPP])
    cst_d = {nm: dram("c_" + nm, [128, w]) for nm, w in _CONST_SHAPES}
    rope_d = dram("c_rope", [4, 128, S])
    pert_d = dram("c_pert", [128, S])
    sc_fm = dram("sc_fm", [17, 128, S], BF16, kind="Internal")
    sc_v = dram("sc_v", [5, 128, 16, 128], BF16, kind="Internal")
    sc_sv = dram("sc_sv", [128, 16, 64], BF16, kind="Internal")
    sc_pad = dram("sc_pad", [20, 128, S], BF16, kind="Internal")
    dbg = {}
    if debug:
        for nm, shape, dt in debug:
            dbg[nm] = dram("dbg_" + nm, shape, dt, kind="ExternalOutput")

    def sb(name, shape, dt=F32):
        return stack.enter_context(nc.sbuf_tensor(name, list(shape), dt))

    def ps(name, shape, dt=F32):
        return stack.enter_context(nc.psum_tensor(name, list(shape), dt))

    xT = sb("xT_sb", [128, 8, S])
    hT = sb("hT_sb", [128, 8, S], BF16)
    RW = 15 * 1024
    Rg = sb("R_sb", [128, RW])
    stg = sb("stg_sb", [128, 2, 1024])
    banks = [ps("bank%d" % b, [128, 512]) for b in range(8)]

    def rv(off_kb, size_kb, dt=F32):
        a = Rg[:, off_kb * 256:(off_kb + size_kb) * 256]
        if dt == BF16:
            a = a.bitcast(BF16)
        return a

    def rk(off_kb, size_kb):
        return [("R", pg) for pg in range(off_kb // 4, (off_kb + size_kb + 3) // 4)]

    sqt = sb("sqt", [128, 2, 512], BF16)
    rs = sb("rs", [128, 2, 512])
    qn = sb("qn", [128, 3, 512], BF16)
    t1 = sb("t1", [128, 2, 512])
    t2 = sb("t2", [128, 2, 512])
    pt = sb("pt", [128, 4, 512], BF16)
    rct = sb("rct", [128, 2, 512])
    ppt = sb("ppt", [128, NPP])
    der = sb("der", [128, 16])
    FFt = sb("FFt", [128, 16, 6])
    IWs = sb("IWs", [128, 16, 4])
    Lt = sb("Lt", [128, 16, 6])
    negc = sb("negc", [128, 16, 6])
    chb = sb("chb", [128, 16, 6], BF16)
    r1t = sb("r1t", [128, 16, 6])
    scA = sb("scA", [128, 16, 6])
    scB = sb("scB", [128, 16, 6])
    csb = sb("csb", [128, 16, 6])
    lam4 = sb("lam4", [128, 4, 32])
    CST = {}
    for nm, w in _CONST_SHAPES:
        f32c = nm in ("ident", "tri", "negones", "cbt", "pow2")
        CST[nm] = sb("k_" + nm, [128, w], F32 if f32c else BF16)
    epsc = sb("epsc", [128, 2])
    accB_t = sb("accB_t", [128, 2048])
    bis2 = sb("bis2", [128, 2, 4])
    STt2 = sb("STt2", [128, 2, 64])

    bank_rr = {}

    def bank(pool):
        i = bank_rr.get(pool, 0)
        bank_rr[pool] = i + 1
        b = pool[i % len(pool)]
        return banks[b], ("ps", b)

    slot_rr = {}

    def slot(name, nslots):
        i = slot_rr.get(name, 0)
        slot_rr[name] = i + 1
        return i % nslots

    def dma(out, in_, R, W):
        P.add("sp", lambda e: e.dma_start(out=out, in_=in_), R=R, W=W, dma=True)

    def mm(out, lhsT, rhs, start, stop, R, W, **kw):
        P.add("pe", lambda e: e.matmul(out, lhsT=lhsT, rhs=rhs, start=start, stop=stop, **kw), R=R, W=W)

    def act(out, in_, func, R, W, bias=0.0, scale=1.0, accum_out=None):
        if accum_out is None:
            P.add("act", lambda e: e.activation(out=out, in_=in_, func=func, bias=bias, scale=scale), R=R, W=W)
        else:
            P.add("act", lambda e: e.activation(out=out, in_=in_, func=func, bias=bias, scale=scale, accum_out=accum_out),
                  R=R, W=W)

    def recip(out, in_, R, W):
        act(out, in_, AF.Ln, R=R, W=W)
        act(out, out, AF.Exp, R=W, W=W, scale=-1.0)

    def load_w(dst, dst_keys, src_ap):
        s = slot("stg", 2)
        n = 1
        for d_ in src_ap.shape[1:]:
            n *= d_
        sv = stg[:, s, 0:n]
        if len(src_ap.shape) == 3:
            sv = sv.rearrange("p (a b) -> p a b", a=src_ap.shape[1])
        dma(sv, src_ap, R=[], W=[("stg", s)])
        P.add("pool", lambda e: e.tensor_copy(out=dst, in_=sv), R=[("stg", s)], W=dst_keys)

    for kc in range(8):
        dma(xT[:, kc, :], x_d[kc * 128:(kc + 1) * 128, :], R=[], W=[("xT", kc)])
    for nm, w in _CONST_SHAPES:
        if CST[nm].dtype == F32:
            dma(CST[nm][:, :], cst_d[nm][:, :], R=[], W=[("k", nm)])
        else:
            for o in range(0, w, 512):
                ww = min(512, w - o)
                s = slot("t1", 2)
                dma(t1[:, s, 0:ww], cst_d[nm][:, o:o + ww], R=[], W=[("t1", s)])
                P.add("pool", lambda e, nm=nm, o=o, ww=ww, s=s: e.tensor_copy(out=CST[nm][:, o:o + ww], in_=t1[:, s, 0:ww]),
                      R=[("t1", s)], W=[("k", nm)])
    P.add("pool", lambda e: e.memset(epsc[:, 0:1], EPS), W=[("epsc",)])
    P.add("pool", lambda e: e.memset(epsc[:, 1:2], 1.0), W=[("epsc",)])
    KEPS = [("epsc",)]
    P.add("pool", lambda e: e.memset(hT[:, 0, :], 0.0), W=[("hT", 0)])
    P.add("pool", lambda e: e.memset(hT[:, 1, :], 1.0), W=[("hT", 1)])
    for t_ in range(20):
        dma(sc_pad[t_], hT[:, 0, :], R=[("hT", 0)], W=[("scp", t_)])
    for p_ in range(3):
        dma(sc_pad[6 + 2 * p_, 64:67, :], hT[64:67, 1, :], R=[("hT", 1)], W=[("scp", 6 + 2 * p_)])
        dma(sc_pad[7 + 2 * p_, 0:3, :], hT[0:3, 1, :], R=[("hT", 1)], W=[("scp", 7 + 2 * p_)])
    eps_ap = epsc[:, 0:1]
    one_ap = epsc[:, 1:2]

    def norm_phase(gcol0):
        for c in range(NCH):
            cs = slice(c * 512, (c + 1) * 512)
            bk, bkk = bank((0, 1))
            for kc in range(8):
                s = slot("sqt", 2)
                act(sqt[:, s, :], xT[:, kc, cs], AF.Square, R=[("xT", kc, c)], W=[("sqt", s)])
                mm(bk[:, :], CST["ones"][:, :], sqt[:, s, :], kc == 0, kc == 7, R=[("sqt", s), ("k", "ones")], W=[bkk])
            s = slot("rs", 2)
            act(rs[:, s, :], bk[:, :], AF.Ln, R=[bkk] + KEPS, W=[("rs", s)], bias=eps_ap, scale=1.0 / D)
            act(rs[:, s, :], rs[:, s, :], AF.Exp, R=[("rs", s)], W=[("rs", s)], scale=-0.5)
            for kc in range(8):
                P.add("dve", lambda e, kc=kc, cs=cs, s=s: e.scalar_tensor_tensor(
                    out=hT[:, kc, cs], in0=xT[:, kc, cs], scalar=ppt[:, gcol0 + kc:gcol0 + kc + 1], in1=rs[:, s, :],
                    op0=ALU.mult, op1=ALU.mult), R=[("xT", kc, c), ("ppt",), ("rs", s)], W=[("hT", kc, c)])

    def ffn_phase(li):
        groups = [(g * 512, 4) for g in range(5)] + [(2560, 2)]
        Wgu = [rv(0, 16, BF16).rearrange("p (k n) -> p k n", k=8), rv(16, 16, BF16).rearrange("p (k n) -> p k n", k=8)]
        Wgu_k = [rk(0, 16), rk(16, 16)]
        Wd = [rv(32, 8, BF16).rearrange("p (k n) -> p k n", k=4), rv(40, 8, BF16).rearrange("p (k n) -> p k n", k=4)]
        Wd_k = [rk(32, 8), rk(40, 8)]
        actT = rv(48, 8, BF16).rearrange("p (s k n) -> p s k n", s=2, k=4)
        actT_k = [rk(48, 4), rk(52, 4)]
        win = w_gu_d[li].rearrange("(kc p) n -> p kc n", p=128)

        def load_group(gi):
            h0, nt = groups[gi]
            sl = gi % 2
            for t in range(nt):
                load_w(Wgu[sl][:, :, t * 128:(t + 1) * 128], Wgu_k[sl], win[:, :, h0 + t * 128:h0 + (t + 1) * 128])
                load_w(Wgu[sl][:, :, 512 + t * 128:512 + (t + 1) * 128], Wgu_k[sl],
                       win[:, :, FFN_H + h0 + t * 128:FFN_H + h0 + (t + 1) * 128])
                load_w(Wd[sl][:, t, :], Wd_k[sl], w_dn_d[li, h0 + t * 128:h0 + (t + 1) * 128, :])

        load_group(0)
        for gi in range(len(groups)):
            if gi + 1 < len(groups):
                load_group(gi + 1)
            h0, nt = groups[gi]
            sl = gi % 2
            for c in range(NCH):
                cs = slice(c * 512, (c + 1) * 512)
                asl = slot("actT", 2)
                for t in range(nt):
                    gb, gbk = bank((0, 1, 2, 3))
                    ub, ubk = bank((0, 1, 2, 3))
                    for kc in range(8):
                        mm(gb[:, :], Wgu[sl][:, kc, t * 128:(t + 1) * 128], hT[:, kc, cs], kc == 0, kc == 7,
                           R=Wgu_k[sl] + [("hT", kc, c)], W=[gbk])
                    for kc in range(8):
                        mm(ub[:, :], Wgu[sl][:, kc, 512 + t * 128:512 + (t + 1) * 128], hT[:, kc, cs], kc == 0, kc == 7,
                           R=Wgu_k[sl] + [("hT", kc, c)], W=[ubk])
                    s = slot("t1", 2)
                    act(t1[:, s, :], gb[:, :], AF.Silu, R=[gbk], W=[("t1", s)])
                    P.add("dve", lambda e, s=s, ub=ub, asl=asl, t=t: e.tensor_tensor(
                        out=actT[:, asl, t, :], in0=ub[:, :], in1=t1[:, s, :], op=ALU.mult),
                        R=[ubk, ("t1", s)], W=actT_k[asl])
                for d_ in range(8):
                    db, dbk = bank((4, 5, 6, 7))
                    for t in range(nt):
                        mm(db[:, :], Wd[sl][:, t, d_ * 128:(d_ + 1) * 128], actT[:, asl, t, :], t == 0, t == nt - 1,
                           R=Wd_k[sl] + actT_k[asl], W=[dbk])
                    P.add("dve", lambda e, d_=d_, cs=cs, db=db: e.tensor_tensor(
                        out=xT[:, d_, cs], in0=db[:, :], in1=xT[:, d_, cs], op=ALU.add),
                        R=[dbk, ("xT", d_, c)], W=[("xT", d_, c)])


    def apb(base_ap, mid):
        a = base_ap.ap
        return bass.AP(base_ap.tensor, base_ap.offset, [list(a[0]), [0, mid], list(a[-1])])

    def tap(name, src, R):
        if name in dbg:
            dma(dbg[name], src, R=R, W=[("dbg", name)])

    def derive_phase(labs):
        lam_init = 0.8 - 0.6 * math.exp(-0.3 * labs)
        for col, src, mul in ((0, 0, 0.125), (1, 2, 32.0 ** -0.5), (2, 4, 0.125), (3, 6, 1.0 - lam_init)):
            P.add("dve", lambda e, col=col, src=src, mul=mul: e.tensor_scalar(
                out=der[:, col:col + 1], in0=ppt[:, src:src + 1], scalar1=mul, scalar2=None, op0=ALU.mult),
                R=[("ppt",)], W=[("der", col)])
        for q, (a, b) in enumerate(((29, 61), (93, 125))):
            P.add("dve", lambda e, q=q, a=a, b=b: e.tensor_tensor(
                out=lam4[:, q, :], in0=ppt[:, a:a + 32], in1=ppt[:, b:b + 32], op=ALU.mult), R=[("ppt",)], W=[("lam4", q)])
            P.add("dve", lambda e, q=q: e.tensor_reduce(out=der[:, 5 + q:6 + q], in_=lam4[:, q, :], axis=AX.X, op=ALU.add),
                  R=[("lam4", q)], W=[("der", 5 + q)])
            act(der[:, 5 + q:6 + q], der[:, 5 + q:6 + q], AF.Exp, R=[("der", 5 + q)], W=[("der", 5 + q)])
        P.add("dve", lambda e: e.tensor_tensor(out=der[:, 4:5], in0=der[:, 6:7], in1=der[:, 5:6], op=ALU.subtract),
              R=[("der", 5), ("der", 6)], W=[("der", 4)])
        P.add("dve", lambda e: e.tensor_scalar(out=der[:, 4:5], in0=der[:, 4:5], scalar1=-lam_init, scalar2=None, op0=ALU.add),
              R=[("der", 4)], W=[("der", 4)])

    def proj_phase(li):
        win = w_in_d[li].rearrange("(kc p) n -> p kc n", p=128)
        WT = [rv(0, 2, BF16).rearrange("p (k n) -> p k n", k=8), rv(2, 2, BF16).rearrange("p (k n) -> p k n", k=8)]
        WT_k = [[("R", 0, 0)], [("R", 0, 1)]]
        Wv = rv(4, 6, BF16).rearrange("p (k n) -> p k n", k=8)
        Wv_k = rk(4, 6)
        ost = rv(12, 4, BF16).rearrange("p (s n) -> p s n", s=4)
        ost_k = [[("R", 3, s_)] for s_ in range(4)]
        Ct = rv(36, 8)
        St = rv(44, 8)
        Ct_k, St_k = rk(36, 8), rk(44, 8)
        FM = []
        for p_ in range(3):
            FM.append(dict(sc=p_, segs=[(O_FQ + 128 * p_, 128)], norm=(64, "bones64", der, 0, ("der", 0)), rope=None,
                           outs=[(2 * p_, 0, 64), (2 * p_ + 1, 64, 64)]))
        for p_ in range(3):
            FM.append(dict(sc=3 + p_, segs=[(O_FK + 128 * p_, 128)], norm=(64, "bones64", ppt, 1, ("ppt",)), rope=None,
                           outs=[(6 + 2 * p_, 0, 64), (7 + 2 * p_, 64, 64)]))
        for p_ in range(2):
            FM.append(dict(sc=6 + p_, segs=[(O_DQ + 128 * p_, 128)], norm=(32, "bones32", der, 1, ("der", 1)), rope=32,
                           outs=[(12 + 4 * p_ + q_, 32 * q_, 32) for q_ in range(4)]))
        for p_ in range(2):
            FM.append(dict(sc=8 + p_, segs=[(O_DK + 128 * p_, 128)], norm=(32, "bones32", ppt, 3, ("ppt",)), rope=32))
        for p_ in range(3):
            FM.append(dict(sc=10 + p_, segs=[(O_SQ + 128 * p_, 128)], norm=(64, "bones64", der, 2, ("der", 2)), rope=64))
        FM.append(dict(sc=13, segs=[(O_SK, 64), (O_SK, 64)], norm=(64, "bones64", ppt, 5, ("ppt",)), rope=64))
        FM.append(dict(sc=14, segs=[(O_IK, 64), (O_IK, 64)], norm=None, rope=64))
        for p_ in range(2):
            FM.append(dict(sc=15 + p_, segs=[(O_IQ + 128 * p_, 128)], norm=None, rope=64))

        def load_tile(ti):
            sl = ti % 2
            o = 0
            for col0, n_ in FM[ti]["segs"]:
                load_w(WT[sl][:, :, o:o + n_], WT_k[sl], win[:, :, col0:col0 + n_])
                o += n_

        cur_rope = None
        SKB = Skew(1)
        SKC = Skew(2)
        load_tile(0)
        for ti, T in enumerate(FM):
            if ti + 1 < len(FM):
                load_tile(ti + 1)
            sl = ti % 2
            if T["rope"] is not None and T["rope"] != cur_rope:
                SKB.flush()
                SKC.flush()
                cur_rope = T["rope"]
                ro = 0 if cur_rope == 32 else 2
                dma(Ct, rope_d[ro], R=[], W=Ct_k)
                dma(St, rope_d[ro + 1], R=[], W=St_k)
            for c in range(NCH):
                cs = slice(c * 512, (c + 1) * 512)
                pj, pjk = bank((0, 1, 2))
                for kc in range(8):
                    mm(pj[:, :], WT[sl][:, kc, :], hT[:, kc, cs], kc == 0, kc == 7, R=WT_k[sl] + [("hT", kc, c)], W=[pjk])
                sq_ = None
                if T["norm"] is not None:
                    sq_ = slot("sqt", 2)
                    act(sqt[:, sq_, :], pj[:, :], AF.Square, R=[pjk], W=[("sqt", sq_)])

                def stageB(T=T, c=c, cs=cs, pj=pj, pjk=pjk, sq_=sq_):
                    os_ = slot("ost", 4)
                    qs_ = None
                    if T["rope"] is not None:
                        qs_ = slot("qn", 3)
                        tgt, tgtk = qn[:, qs_, :], [("qn", qs_)]
                    else:
                        tgt, tgtk = ost[:, os_, :], ost_k[os_]
                    if T["norm"] is not None:
                        bs_, bname, gt, gcol, gkey = T["norm"]
                        sb_, sbk = bank((3, 4))
                        mm(sb_[:, :], CST[bname][:, :], sqt[:, sq_, :], True, True, R=[("sqt", sq_), ("k", bname)], W=[sbk])
                        r_ = slot("rs", 2)
                        act(rs[:, r_, :], sb_[:, :], AF.Ln, R=[sbk] + KEPS, W=[("rs", r_)], bias=eps_ap, scale=1.0 / bs_)
                        act(rs[:, r_, :], rs[:, r_, :], AF.Exp, R=[("rs", r_)], W=[("rs", r_)], scale=-0.5)
                        P.add("dve", lambda e: e.scalar_tensor_tensor(
                            out=tgt, in0=pj[:, :], scalar=gt[:, gcol:gcol + 1], in1=rs[:, r_, :], op0=ALU.mult, op1=ALU.mult),
                            R=[pjk, gkey, ("rs", r_)], W=tgtk)
                    else:
                        act(tgt, pj[:, :], AF.Copy, R=[pjk], W=tgtk)

                    def stageC():
                        if T["rope"] is not None:
                            pname = "prot%d" % T["rope"]
                            rp, rpk = bank((5, 6))
                            mm(rp[:, :], CST[pname][:, :], qn[:, qs_, :], True, True, R=[("qn", qs_), ("k", pname)], W=[rpk])
                            a_ = slot("t1", 2)
                            b_ = slot("t2", 2)
                            P.add("dve", lambda e: e.tensor_tensor(out=t1[:, a_, :], in0=rp[:, :], in1=St[:, cs], op=ALU.mult),
                                  R=[rpk] + St_k, W=[("t1", a_)])
                            P.add("pool", lambda e: e.tensor_tensor(out=t2[:, b_, :], in0=qn[:, qs_, :], in1=Ct[:, cs], op=ALU.mult),
                                  R=[("qn", qs_)] + Ct_k, W=[("t2", b_)])
                            P.add("dve", lambda e: e.tensor_tensor(out=ost[:, os_, :], in0=t1[:, a_, :], in1=t2[:, b_, :], op=ALU.add),
                                  R=[("t1", a_), ("t2", b_)], W=ost_k[os_])
                        if "outs" in T:
                            for tl_, r0_, nr_ in T["outs"]:
                                dma(sc_pad[tl_, r0_:r0_ + nr_, cs], ost[r0_:r0_ + nr_, os_, :], R=ost_k[os_], W=[("scp", tl_, c)])
                        else:
                            dma(sc_fm[T["sc"], :, cs], ost[:, os_, :], R=ost_k[os_], W=[("scfm", T["sc"], c)])
                    SKC.push(stageC)
                SKB.push(stageB)
        SKB.flush()
        SKC.flush()

        def tm_group(col_segs, ncols, handler):
            o = 0
            for col0, n_ in col_segs:
                load_w(Wv[:, :, o:o + n_], Wv_k, win[:, :, col0:col0 + n_])
                o += n_
            handler()

        def v_pairs(npairs, sc0):
            for a in range(npairs):
                for jg in range(4):
                    tv, tvk = bank((6, 7))
                    for jj in range(4):
                        j = jg * 4 + jj
                        for kc in range(8):
                            mm(tv[:, jj * 128:(jj + 1) * 128], hT[:, kc, j * 128:(j + 1) * 128], Wv[:, kc, a * 128:(a + 1) * 128],
                               kc == 0, kc == 7, R=Wv_k + [("hT", kc, jg)], W=[tvk])
                    os_ = slot("ost", 4)
                    act(ost[:, os_, :], tv[:, :], AF.Copy, R=[tvk], W=ost_k[os_])
                    dma(sc_v[sc0 + a, :, jg * 4:(jg + 1) * 4, :], ost[:, os_, :].rearrange("p (j n) -> p j n", j=4),
                        R=ost_k[os_], W=[("scv", sc0 + a, jg)])

        tm_group([(O_FV, 128), (O_FV + 128, 128), (O_FV + 256, 128)], 384, lambda: v_pairs(3, 0))
        tm_group([(O_DV, 128), (O_DV + 128, 128)], 256, lambda: v_pairs(2, 3))

        def small():
            for jg in range(4):
                tv, tvk = bank((6, 7))
                for jj in range(4):
                    j = jg * 4 + jj
                    for kc in range(8):
                        mm(tv[:, jj * 74:(jj + 1) * 74], hT[:, kc, j * 128:(j + 1) * 128], Wv[:, kc, 0:74],
                           kc == 0, kc == 7, R=Wv_k + [("hT", kc, jg)], W=[tvk])
                tv3 = tv[:, 0:296].rearrange("p (j n) -> p j n", j=4)
                os_ = slot("ost", 4)
                act(ost[:, os_, 0:256].rearrange("p (j n) -> p j n", j=4), tv3[:, :, 0:64], AF.Copy, R=[tvk], W=ost_k[os_])
                dma(sc_sv[:, jg * 4:(jg + 1) * 4, :], ost[:, os_, 0:256].rearrange("p (j n) -> p j n", j=4),
                    R=ost_k[os_], W=[("scsv", jg)])
                P.add("dve", lambda e, jg=jg, tv3=tv3: e.tensor_copy(out=FFt[:, jg * 4:(jg + 1) * 4, :], in_=tv3[:, :, 64:70]),
                      R=[tvk], W=[("FFt", jg)])
                P.add("dve", lambda e, jg=jg, tv3=tv3: e.tensor_scalar(
                    out=IWs[:, jg * 4:(jg + 1) * 4, :], in0=tv3[:, :, 70:74], scalar1=0.0625, scalar2=None, op0=ALU.mult),
                    R=[tvk], W=[("IWs", jg)])

        tm_group([(O_SV, 64), (O_FF, 6), (O_IW, 4)], 74, small)

    TL = [[rv(28 * s_ + 4 * i_, 4, BF16) for i_ in range(4)] for s_ in range(2)]
    TL_k = [[rk(28 * s_ + 4 * i_, 4) for i_ in range(4)] for s_ in range(2)]
    Kd = [rv(28 * s_ + 16, 4, BF16) for s_ in range(2)]
    Kd_k = [rk(28 * s_ + 16, 4) for s_ in range(2)]
    Vp = [rv(28 * s_ + 20, 8, BF16).rearrange("p (j s d) -> p j s d", j=16, s=4) for s_ in range(2)]
    Vp_k = [rk(28 * s_ + 20, 8) for s_ in range(2)]
    identb = CST["irep"][:, 0:128]

    def load_v(sl, vi):
        dma(Vp[sl][:, :, 0, :], sc_v[vi][:, :, 0:64], R=[("scv", vi)], W=Vp_k[sl])
        dma(Vp[sl][:, :, 3, :], sc_v[vi][:, :, 64:128], R=[("scv", vi)], W=Vp_k[sl])
        P.add("pool", lambda e: e.memset(Vp[sl][:, :, 1:3, :], 1.0), W=Vp_k[sl])

    def load_fox_pair(sl, p_):
        for i_, t_ in enumerate((2 * p_, 2 * p_ + 1, 6 + 2 * p_, 7 + 2 * p_)):
            dma(TL[sl][i_], sc_pad[t_], R=[("scp", t_)], W=TL_k[sl][i_])
        load_v(sl, p_)

    def load_diff_pair(sl, d_):
        for i_ in range(4):
            dma(TL[sl][i_], sc_pad[12 + 4 * d_ + i_], R=[("scp", 12 + 4 * d_ + i_)], W=TL_k[sl][i_])
        dma(Kd[sl], sc_fm[8 + d_], R=[("scfm", 8 + d_)], W=Kd_k[sl])
        load_v(sl, 3 + d_)

    SKA = Skew(3)

    def attn_map(sl, e_, lhs, lhsk, rhs, rhsk, c, h, fox, acc, acck):
        nj = 4 * c + 4
        for j in range(nj):
            n0 = max(j * 128, c * 512)
            wN = (c + 1) * 512 - n0
            off = n0 - c * 512
            diag = j >= 4 * c
            st, stk = bank((4, 5, 6, 7))
            mm(st[:, off:off + wN], lhs[:, j * 128:(j + 1) * 128], rhs[:, n0:n0 + wN],
               True, not diag, R=lhsk + rhsk, W=[stk])
            if diag:
                mm(st[:, off:off + 128], identb, CST["cb"][:, :], False, True, R=[("k", "irep"), ("k", "cb")], W=[stk])
            ps_ = slot("pt", 4)
            if fox:
                act(pt[:, ps_, off:off + wN], st[:, off:off + wN], AF.Exp, R=[stk, ("negc",)], W=[("pt", ps_)],
                    bias=negc[:, j, h:h + 1])
            else:
                act(pt[:, ps_, off:off + wN], st[:, off:off + wN], AF.Exp, R=[stk], W=[("pt", ps_)])
            SKA.push(lambda j=j, off=off, wN=wN, ps_=ps_: mm(
                acc[:, off:off + wN], Vp[sl][:, j, 2 * e_:2 * e_ + 2, :].rearrange("p s d -> p (s d)"), pt[:, ps_, off:off + wN],
                j == 0, j == nj - 1, R=Vp_k[sl] + [("pt", ps_)], W=[acck]))

    CT = rv(56, 4, BF16)
    CT_k = rk(56, 4)
    CS = rv(28, 6).rearrange("p (j n) -> p j n", j=16)
    CS_k = rk(28, 6)

    def fox_phase():
        P.add("dve", lambda e: e.tensor_tensor(out=Lt[:, :, :], in0=FFt[:, :, :], in1=apb(ppt[:, 23:29], 16), op=ALU.add),
              R=[("FFt",), ("ppt",)], W=[("Lt",)])
        act(Lt[:, :, :], Lt[:, :, :], AF.Exp, R=[("Lt",)], W=[("Lt",)], scale=-1.0)
        act(Lt[:, :, :], Lt[:, :, :], AF.Ln, R=[("Lt",)] + KEPS, W=[("Lt",)], bias=one_ap)
        Ltf = Lt[:, :, :].rearrange("p j n -> p (j n)")
        cps, cpsk = bank((0,))
        mm(cps[:, 0:96], CST["negones"][:, :], Ltf, True, True, R=[("Lt",), ("k", "negones")], W=[cpsk])
        cp2, cp2k = bank((1,))
        mm(cp2[:, 0:96], CST["tri"][:, :], Ltf, True, True, R=[("Lt",), ("k", "tri")], W=[cp2k])
        cpsv = cps[:, 0:96].rearrange("p (j n) -> p j n", j=16)
        cp23 = cp2[:, 0:96].rearrange("p (j n) -> p j n", j=16)
        P.add("dve", lambda e: e.tensor_copy(out=scA[:, :, :], in_=cpsv), R=[cpsk], W=[("scA",)])
        bufs = [(scA, ("scA",)), (scB, ("scB",))]
        cur = 0
        for sh in (1, 2, 4, 8):
            (a_, ak), (b_, bk) = bufs[cur], bufs[1 - cur]
            P.add("dve", lambda e, a_=a_, b_=b_, sh=sh: e.tensor_copy(out=b_[:, 0:sh, :], in_=a_[:, 0:sh, :]), R=[ak], W=[bk])
            P.add("dve", lambda e, a_=a_, b_=b_, sh=sh: e.tensor_tensor(out=b_[:, sh:16, :], in0=a_[:, sh:16, :],
                                                                   in1=a_[:, 0:16 - sh, :], op=ALU.add), R=[ak], W=[bk])
            cur = 1 - cur
        inc_, inck = bufs[cur]
        P.add("dve", lambda e: e.tensor_copy(out=csb[:, 0:1, :], in_=cp23[:, 0:1, :]), R=[cp2k], W=[("csb",)])
        P.add("dve", lambda e: e.tensor_tensor(out=csb[:, 1:16, :], in0=cp23[:, 1:16, :], in1=inc_[:, 0:15, :], op=ALU.add),
              R=[cp2k, inck], W=[("csb",)])
        cpsk = ("csb",)
        cps3 = csb[:, :, :]
        P.add("dve", lambda e: e.tensor_scalar(out=negc[:, :, :], in0=cps3, scalar1=-1.0, scalar2=None, op0=ALU.mult),
              R=[cpsk], W=[("negc",)])
        P.add("pool", lambda e: e.memset(CS[:, :, :], 0.0), W=CS_k)
        P.add("dve", lambda e: e.tensor_copy(out=chb[:, :, :], in_=cps3), R=[cpsk], W=[("chb",)])
        P.add("dve", lambda e: e.tensor_copy(out=CS[:, :, 0:6], in_=chb[:, :, :]), R=[("chb",)], W=CS_k)
        P.add("dve", lambda e: e.tensor_tensor(out=r1t[:, :, :], in0=cps3, in1=chb[:, :, :], op=ALU.subtract),
              R=[cpsk, ("chb",)], W=[("r1t",)])
        P.add("dve", lambda e: e.tensor_copy(out=chb[:, :, :], in_=r1t[:, :, :]), R=[("r1t",)], W=[("chb",)])
        P.add("dve", lambda e: e.tensor_copy(out=CS[:, :, 32:38], in_=chb[:, :, :]), R=[("chb",)], W=CS_k)
        P.add("dve", lambda e: e.tensor_tensor(out=r1t[:, :, :], in0=r1t[:, :, :], in1=chb[:, :, :], op=ALU.subtract),
              R=[("r1t",), ("chb",)], W=[("r1t",)])
        P.add("dve", lambda e: e.tensor_copy(out=chb[:, :, :], in_=r1t[:, :, :]), R=[("r1t",)], W=[("chb",)])
        P.add("dve", lambda e: e.tensor_copy(out=CS[:, :, 64:70], in_=chb[:, :, :]), R=[("chb",)], W=CS_k)
        for q_ in range(4):
            ctp, ctpk = bank((1, 2, 3))
            for jj in range(4):
                j = q_ * 4 + jj
                P.add("pe", lambda e, ctp=ctp, jj=jj, j=j: e.transpose(
                    out=ctp[0:96, jj * 128:(jj + 1) * 128], in_=CS[:, j, :], identity=CST["ident"][:, :]),
                    R=CS_k + [("k", "ident")], W=[ctpk])
            act(CT[0:96, q_ * 512:(q_ + 1) * 512], ctp[0:96, :], AF.Copy, R=[ctpk], W=CT_k)
        tap("negc", negc[:, :, :], [("negc",)])
        for h_ in range(6):
            tl_ = 2 * (h_ // 2) + (h_ % 2)
            r0_ = 64 if h_ % 2 == 0 else 0
            for q_ in range(3):
                dma(sc_pad[tl_, r0_ + q_:r0_ + q_ + 1, :], CT[32 * q_ + h_:32 * q_ + h_ + 1, :], R=CT_k, W=[("scp", tl_, "c")])
        load_fox_pair(0, 0)
        for p_ in range(3):
            sl = p_ % 2
            SKA.flush()
            if p_ + 1 < 3:
                load_fox_pair((p_ + 1) % 2, p_ + 1)
            for c in range(NCH):
                cs = slice(c * 512, (c + 1) * 512)
                for e_ in range(2):
                    base = 64 * e_
                    acc, acck = bank((0, 1, 2))
                    attn_map(sl, e_, TL[sl][2 + e_], TL_k[sl][2 + e_], TL[sl][e_], TL_k[sl][e_], c, 2 * p_ + e_, True, acc, acck)
                    def fin(base=base, acc=acc, acck=acck, p_=p_, cs=cs, c=c):
                        O = slice(base, base + 64)
                        Dn = slice(64 - base, 128 - base)
                        rc = slot("rct", 2)
                        recip(rct[O, rc, :], acc[Dn, :], R=[acck], W=[("rct", rc)])
                        P.add("dve", lambda e: e.tensor_tensor(out=hT[O, p_, cs], in0=acc[O, :], in1=rct[O, rc, :], op=ALU.mult),
                              R=[acck, ("rct", rc)], W=[("hT", p_, c)])
                    SKA.push(fin)
        SKA.flush()

    def diff_phase():
        load_diff_pair(0, 0)
        for d_ in range(2):
            sl = d_ % 2
            SKA.flush()
            if d_ == 0:
                load_diff_pair(1, 1)
            for c in range(NCH):
                cs = slice(c * 512, (c + 1) * 512)
                od = slot("t1", 2)
                for e_ in range(2):
                    accs = []
                    for m_ in range(2):
                        base = 64 * e_ + 32 * m_
                        acc, acck = bank((0, 1, 2))
                        attn_map(sl, e_, Kd[sl], Kd_k[sl], TL[sl][2 * e_ + m_], TL_k[sl][2 * e_ + m_], c, 0, False, acc, acck)
                        accs.append((acc, acck))
                    def comb(e_=e_, accs=accs, od=od):
                        O = slice(64 * e_, 64 * e_ + 64)
                        Dn = slice(64 - 64 * e_, 128 - 64 * e_)
                        (a1, a1k), (a2, a2k) = accs
                        ra = slot("rct", 2)
                        rb = slot("rct", 2)
                        ob = slot("t2", 2)
                        recip(rct[O, ra, :], a1[Dn, :], R=[a1k], W=[("rct", ra)])
                        recip(rct[O, rb, :], a2[Dn, :], R=[a2k], W=[("rct", rb)])
                        P.add("dve", lambda e: e.tensor_scalar(out=rct[O, rb, :], in0=rct[O, rb, :], scalar1=der[O, 4:5],
                                                               scalar2=None, op0=ALU.mult),
                              R=[("rct", rb), ("der", 4)], W=[("rct", rb)])
                        P.add("dve", lambda e: e.tensor_tensor(out=t1[O, od, :], in0=a1[O, :], in1=rct[O, ra, :], op=ALU.mult),
                              R=[a1k, ("rct", ra)], W=[("t1", od)])
                        P.add("dve", lambda e: e.tensor_tensor(out=t2[O, ob, :], in0=a2[O, :], in1=rct[O, rb, :], op=ALU.mult),
                              R=[a2k, ("rct", rb)], W=[("t2", ob)])
                        P.add("pool", lambda e: e.tensor_tensor(out=t1[O, od, :], in0=t1[O, od, :], in1=t2[O, ob, :], op=ALU.add),
                              R=[("t1", od), ("t2", ob)], W=[("t1", od)])
                    SKA.push(comb)

                def subln(od=od, d_=d_, cs=cs, c=c):
                    sq_ = slot("sqt", 2)
                    act(sqt[:, sq_, :], t1[:, od, :], AF.Square, R=[("t1", od)], W=[("sqt", sq_)])
                    sb_, sbk = bank((3,))
                    mm(sb_[:, :], CST["bones64"][:, :], sqt[:, sq_, :], True, True, R=[("sqt", sq_), ("k", "bones64")], W=[sbk])
                    r_ = slot("rs", 2)
                    act(rs[:, r_, :], sb_[:, :], AF.Ln, R=[sbk] + KEPS, W=[("rs", r_)], bias=eps_ap, scale=1.0 / 64)
                    act(rs[:, r_, :], rs[:, r_, :], AF.Exp, R=[("rs", r_)], W=[("rs", r_)], scale=-0.5)
                    P.add("dve", lambda e: e.scalar_tensor_tensor(
                        out=hT[:, 3 + d_, cs], in0=t1[:, od, :], scalar=der[:, 3:4], in1=rs[:, r_, :], op0=ALU.mult, op1=ALU.mult),
                        R=[("t1", od), ("der", 3), ("rs", r_)], W=[("hT", 3 + d_, c)])
                SKA.push(subln)
        SKA.flush()

    def dsa_phase():
        Qs = rv(0, 12, BF16).rearrange("p (t n) -> p t n", t=3)
        Qs_k = rk(0, 12)
        SKK, SKK_k = rv(12, 4, BF16), rk(12, 4)
        IKK, IKK_k = rv(16, 4, BF16), rk(16, 4)
        IQ = rv(20, 8, BF16).rearrange("p (t n) -> p t n", t=2)
        IQ_k = rk(20, 8)
        SVa = rv(28, 6, BF16).rearrange("p (j s d) -> p j s d", j=16, s=3)
        SVa_k = rk(28, 6)
        accb, acc_k = rv(36, 8), rk(36, 8)
        MBs = [(rv(44, 4, BF16), rk(44, 4)), (rv(56, 4, BF16), rk(56, 4))]
        PERT, PERT_k = rv(48, 8), rk(48, 8)
        junk = t2[:, :, :].rearrange("p a n -> p (a n)").bitcast(BF16)
        junk_k = [("t2",)]
        for t_ in range(3):
            dma(Qs[:, t_, :], sc_fm[10 + t_], R=[("scfm", 10 + t_)], W=Qs_k)
        dma(SKK, sc_fm[13], R=[("scfm", 13)], W=SKK_k)
        dma(IKK, sc_fm[14], R=[("scfm", 14)], W=IKK_k)
        for t_ in range(2):
            dma(IQ[:, t_, :], sc_fm[15 + t_], R=[("scfm", 15 + t_)], W=IQ_k)
        dma(SVa[:, :, 0, :], sc_sv, R=[("scsv",)], W=SVa_k)
        dma(SVa[:, :, 2, :], sc_sv, R=[("scsv",)], W=SVa_k)
        P.add("pool", lambda e: e.memset(SVa[:, :, 1, :], 1.0), W=SVa_k)
        dma(PERT, pert_d, R=[], W=PERT_k)

        accs_ = [(accb, acc_k), (accB_t[:, :], [("accB",)])]
        junks = [(junk, junk_k), (t1[:, :, :].rearrange("p a n -> p (a n)").bitcast(BF16), [("t1",)])]

        def index_block(i):
            q = i % 2
            acc_, acck_ = accs_[q]
            nk = (i + 1) * 128
            nch = (nk + 511) // 512
            for hh in range(4):
                tl, base = hh // 2, 64 * (hh % 2)
                for m_ in range(nch):
                    w_ = min(512, nk - 512 * m_)
                    dp, dpk = bank((0, 1))
                    mm(dp[:, 0:w_], IQ[base:base + 64, tl, i * 128:(i + 1) * 128], IKK[base:base + 64, 512 * m_:512 * m_ + w_],
                       True, True, R=IQ_k + IKK_k, W=[dpk])
                    act(dp[:, 0:w_], dp[:, 0:w_], AF.Relu, R=[dpk], W=[dpk])
                    src = PERT if hh == 0 else acc_
                    srck = PERT_k if hh == 0 else acck_
                    P.add("dve", lambda e, dp=dp, w_=w_, m_=m_, hh=hh, src=src: e.scalar_tensor_tensor(
                        out=acc_[:, 512 * m_:512 * m_ + w_], in0=dp[:, 0:w_], scalar=IWs[:, i, hh:hh + 1],
                        in1=src[:, 512 * m_:512 * m_ + w_], op0=ALU.mult, op1=ALU.add),
                        R=[dpk, ("IWs",)] + srck, W=acck_)
            P.add("dve", lambda e: e.tensor_reduce(out=bis2[:, q, 0:1], in_=acc_[:, 0:nk], axis=AX.X, op=ALU.max,
                                                   apply_absolute_value=True), R=acck_, W=[("bis2", q, 0)])
            P.add("dve", lambda e: e.tensor_tensor(out=acc_[:, i * 128:(i + 1) * 128], in0=acc_[:, i * 128:(i + 1) * 128],
                                                   in1=CST["cbt"][:, :], op=ALU.add), R=acck_ + [("k", "cbt")], W=acck_)
            P.add("dve", lambda e: e.tensor_scalar(out=STt2[:, q, :], in0=CST["pow2"][:, :], scalar1=bis2[:, q, 0:1], scalar2=None,
                                                   op0=ALU.mult), R=[("bis2", q, 0), ("k", "pow2")], W=[("STt2", q)])
            P.add("dve", lambda e: e.memset(bis2[:, q, 1:2], 0.0), W=[("bis2", q, 1)])

        def bisect_pair(iA, iB, zsteps):
            blocks = [b_ for b_ in (iA, iB) if b_ is not None]
            zsteps = list(zsteps)
            per_it = (len(zsteps) + KBIS - 1) // KBIS
            for k in range(KBIS):
                for _ in range(per_it):
                    if zsteps:
                        zsteps.pop(0)()
                for i in blocks:
                    q = i % 2
                    acc_, acck_ = accs_[q]
                    jk, jkk = junks[q]
                    nk = (i + 1) * 128
                    if q == 0:
                        P.add("dve", lambda e, acc_=acc_, jk=jk, nk=nk, q=q: e.tensor_scalar(
                            out=jk[:, 0:nk], in0=acc_[:, 0:nk], scalar1=bis2[:, q, 1:2], scalar2=None,
                            op0=ALU.is_gt, op1=ALU.add, accum_out=bis2[:, q, 2:3]),
                            R=acck_ + [("bis2", q, 1)], W=jkk + [("bis2", q, 2)])
                    else:
                        act(jk[:, 0:nk], acc_[:, 0:nk], AF.Sign, R=acck_ + [("bis2", q, 1)], W=jkk + [("bis2", q, 2)],
                            bias=bis2[:, q, 1:2], scale=-1.0, accum_out=bis2[:, q, 2:3])
                for i in blocks:
                    q = i % 2
                    nk = (i + 1) * 128
                    if q == 0:
                        P.add("dve", lambda e, k=k, q=q: e.tensor_scalar(
                            out=bis2[:, q, 3:4], in0=bis2[:, q, 2:3], scalar1=TOPK - 0.5, scalar2=STt2[:, q, 32 + k:33 + k],
                            op0=ALU.is_gt, op1=ALU.mult), R=[("bis2", q, 2), ("STt2", q)], W=[("bis2", q, 3)])
                    else:
                        P.add("dve", lambda e, k=k, q=q, nk=nk: e.tensor_scalar(
                            out=bis2[:, q, 3:4], in0=bis2[:, q, 2:3], scalar1=float(nk - 2 * TOPK + 1),
                            scalar2=STt2[:, q, 32 + k:33 + k], op0=ALU.is_lt, op1=ALU.mult),
                            R=[("bis2", q, 2), ("STt2", q)], W=[("bis2", q, 3)])
                    P.add("dve", lambda e, k=k, q=q: e.scalar_tensor_tensor(
                        out=bis2[:, q, 1:2], in0=bis2[:, q, 3:4], scalar=STt2[:, q, k:k + 1], in1=bis2[:, q, 1:2],
                        op0=ALU.subtract, op1=ALU.add),
                        R=[("bis2", q, 3), ("bis2", q, 1), ("STt2", q)], W=[("bis2", q, 1)])
            while zsteps:
                zsteps.pop(0)()
            for i in blocks:
                q = i % 2
                acc_, acck_ = accs_[q]
                MB, MB_k = MBs[q]
                nk = (i + 1) * 128
                P.add("dve", lambda e, acc_=acc_, MB=MB, nk=nk, q=q: e.tensor_scalar(
                    out=MB[:, 0:nk], in0=acc_[:, 0:nk], scalar1=bis2[:, q, 1:2], scalar2=NEG, op0=ALU.is_le, op1=ALU.mult),
                    R=acck_ + [("bis2", q, 1)], W=MB_k)

        SKD = Skew(1)
        accE, accEk = banks[6], ("ps", 6)
        accO, accOk = banks[7], ("ps", 7)

        def attend_steps(i):
            MB, MB_k = MBs[i % 2]
            c = i // 4
            qs_ = slice(i * 128, (i + 1) * 128)
            steps = []

            def step(j):
                ks_ = slice(j * 128, (j + 1) * 128)
                sts = []
                for par in range(2):
                    st, stk = bank((2, 3, 4, 5))
                    pr = slice(64 * par, 64 * par + 64)
                    for hi in range(3):
                        mm(st[:, hi * 128:(hi + 1) * 128], SKK[pr, ks_], Qs[pr, hi, qs_], hi == 0, False,
                           R=SKK_k + Qs_k, W=[stk])
                    mm(st[:, 0:384], MB[:, ks_], CST["irep"][:, 0:384], False, True, R=MB_k + [("k", "irep")], W=[stk])
                    sts.append((st, stk))
                pss = []
                for par in range(2):
                    ps_ = slot("pt", 4)
                    act(pt[:, ps_, 0:384], sts[par][0][:, 0:384], AF.Exp, R=[sts[par][1]], W=[("pt", ps_)])
                    pss.append(ps_)

                def pv():
                    mm(accE[:, 0:384], SVa[:, j, 0:2, :].rearrange("p s d -> p (s d)"), pt[:, pss[0], 0:384], j == 0, j == i,
                       R=SVa_k + [("pt", pss[0])], W=[accEk])
                    mm(accO[:, 0:384], SVa[:, j, 1:3, :].rearrange("p s d -> p (s d)"), pt[:, pss[1], 0:384], j == 0, j == i,
                       R=SVa_k + [("pt", pss[1])], W=[accOk])
                SKD.push(pv)

            def fin():
                for par, (acc, acck) in enumerate(((accE, accEk), (accO, accOk))):
                    O = slice(64 * par, 64 * par + 64)
                    Dn = slice(64 - 64 * par, 128 - 64 * par)
                    rc = slot("rct", 2)
                    recip(rct[O, rc, 0:384], acc[Dn, 0:384], R=[acck], W=[("rct", rc)])
                    P.add("dve", lambda e, O=O, rc=rc, acc=acc: e.tensor_tensor(
                        out=hT[O, 5:8, qs_], in0=acc[O, 0:384].rearrange("p (h n) -> p h n", h=3),
                        in1=rct[O, rc, 0:384].rearrange("p (h n) -> p h n", h=3), op=ALU.mult),
                        R=[acck, ("rct", rc)], W=[("hT", 5, c), ("hT", 6, c), ("hT", 7, c)])

            for j in range(i + 1):
                steps.append(lambda j=j: step(j))
            steps.append(lambda: SKD.push(fin))
            return steps

        index_block(0)
        index_block(1)
        prev_steps = []
        for m_ in range(8):
            bisect_pair(2 * m_, 2 * m_ + 1, prev_steps)
            prev_steps = attend_steps(2 * m_) + attend_steps(2 * m_ + 1)
            if m_ + 1 < 8:
                index_block(2 * m_ + 2)
                index_block(2 * m_ + 3)
        for st_ in prev_steps:
            st_()
        SKD.flush()
        tap("acc15", accB_t[:, :], [("accB",)])
        tap("MB15", MBs[1][0], MBs[1][1])
        tap("bis15", bis2[:, 1, :], [("bis2",)])
        tap("STt", STt2[:, 1, :], [("STt2",)])

    def wout_phase(li):
        Wo = [rv(56, 2, BF16).rearrange("p (k n) -> p k n", k=8), rv(58, 2, BF16).rearrange("p (k n) -> p k n", k=8)]
        Wo_k = [[("R", 14, 0)], [("R", 14, 1)]]
        wsrc = w_out_d[li].rearrange("(kt p) n -> p kt n", p=128)
        load_w(Wo[0], Wo_k[0], wsrc[:, :, 0:128])
        for d_ in range(8):
            sl = d_ % 2
            if d_ + 1 < 8:
                load_w(Wo[1 - sl], Wo_k[1 - sl], wsrc[:, :, (d_ + 1) * 128:(d_ + 2) * 128])
            for c in range(NCH):
                cs = slice(c * 512, (c + 1) * 512)
                bk, bkk = bank((0, 1, 2, 3, 4, 5, 6, 7))
                for kt in range(8):
                    mm(bk[:, :], Wo[sl][:, kt, :], hT[:, kt, cs], kt == 0, kt == 7, R=Wo_k[sl] + [("hT", kt, c)], W=[bkk])
                P.add("dve", lambda e, d_=d_, cs=cs, bk=bk: e.tensor_tensor(
                    out=xT[:, d_, cs], in0=bk[:, :], in1=xT[:, d_, cs], op=ALU.add),
                    R=[bkk, ("xT", d_, c)], W=[("xT", d_, c)])

    for li, labs in enumerate(layer_ids):
        dma(ppt[:, :], pp_d[li], R=[], W=[("ppt",)])
        derive_phase(labs)
        norm_phase(7)
        proj_phase(li)
        if li == 0:
            tap("sc_fm", sc_fm, [("scfm",)])
            tap("sc_v", sc_v, [("scv",)])
            tap("sc_sv", sc_sv, [("scsv",)])
            tap("FFt", FFt[:, :, :], [("FFt",)])
            tap("IWs", IWs[:, :, :], [("IWs",)])
        fox_phase()
        diff_phase()
        dsa_phase()
        if li == 0:
            tap("cat", hT[:, :, :], [("hT",)])
        wout_phase(li)
        if li == 0:
            tap("x1", xT[:, :, :], [("xT",)])
        norm_phase(15)
        ffn_phase(li)

    for kc in range(8):
        dma(y_d[kc * 128:(kc + 1) * 128, :], xT[:, kc, :], R=[("xT", kc)], W=[("y", kc)])

    P.emit(nc, stack)
    stack.close()
    return nc, P.stats


_NC_CACHE = {}


def _get_nc(layer_ids):
    key = tuple(layer_ids)
    if key not in _NC_CACHE:
        _NC_CACHE[key] = build_nc(list(layer_ids))[0]
    return _NC_CACHE[key]


def kernel(**inputs):
    inp = {k: np.asarray(v) for k, v in inputs.items()}
    x = inp["x"].astype(np.float32, copy=False)
    B = x.shape[0]
    cst = _consts()
    base = {}
    for nm, w in _CONST_SHAPES:
        base["c_" + nm] = np.ascontiguousarray(cst[nm], dtype=np.float32)
    base["c_rope"] = cst["rope"]
    base["c_pert"] = cst["pert"]
    layer_ids = list(range(DEPTH))
    nc = _get_nc(layer_ids)
    base["w_in"] = np.ascontiguousarray(inp["w_in"], dtype=np.float32)
    base["w_out"] = np.ascontiguousarray(inp["w_out"], dtype=np.float32)
    base["w_gu"] = np.ascontiguousarray(inp["w_gate_up"], dtype=np.float32)
    base["w_dn"] = np.ascontiguousarray(inp["w_down"], dtype=np.float32)
    base["pp"] = np.stack([_pack_pp(inp, l) for l in layer_ids]).astype(np.float32)
    in_maps = []
    for b in range(B):
        m = dict(base)
        m["xT"] = np.ascontiguousarray(x[b].T)
        in_maps.append(m)
    res = run_bass_kernel_spmd(nc, in_maps, core_ids=list(range(B)))
    out = np.stack([np.asarray(r["yT"]).T for r in res.results]).astype(np.float32)
    return out
```

```python
import math
from contextlib import ExitStack
import numpy as np
import concourse.bass as bass
import concourse.mybir as mybir
from concourse.bass_utils import run_bass_kernel_spmd

F32 = mybir.dt.float32
BF16 = mybir.dt.bfloat16
ALU = mybir.AluOpType
AF = mybir.ActivationFunctionType
AX = mybir.AxisListType

D = 1024
S = 2048
DEPTH = 4
NCH = 4
FFN_H = 2816
INW = 2762
EPS = 1e-6
NEG = -30000.0
KBIS = 14
TOPK = 256
PERT_EPS = 2.0 ** -20
NPP = 157

O_FQ, O_FK, O_FV, O_FF = 0, 384, 768, 1152
O_DQ, O_DK, O_DV = 1158, 1414, 1670
O_SQ, O_SK, O_SV, O_IQ, O_IK, O_IW = 1926, 2310, 2374, 2438, 2694, 2758


class Prog:
    ENGS = ("pe", "act", "dve", "pool", "sp")
    NDMA = 24

    def __init__(self):
        self.ops = []

    def add(self, eng, fn, R=(), W=(), dma=False):
        R = tuple(R)
        W = tuple(W) + tuple(k for k in R if k[0] == "ps" and k not in W)
        R = tuple(k for k in R if k[0] != "ps")
        self.ops.append((eng, fn, R, W, dma))

    def _deps(self):
        lastw = {}
        readers = {}
        desc = {}
        deps_all = []

        def related(k):
            out = []
            for i in range(1, len(k)):
                p = k[:i]
                if p in lastw or p in readers:
                    out.append(p)
            out.extend(desc.get(k, ()))
            return out

        def register(k):
            if k in lastw or k in readers:
                return
            for i in range(1, len(k) + 1):
                desc.setdefault(k[:i], set()).add(k)

        for i, (eng, fn, R, W, dma) in enumerate(self.ops):
            d = set()
            for k in R:
                register(k)
                lastw.setdefault(k, None)
                for r in related(k):
                    w = lastw.get(r)
                    if w is not None:
                        d.add(w)
            for k in W:
                register(k)
                lastw.setdefault(k, None)
                for r in related(k):
                    w = lastw.get(r)
                    if w is not None:
                        d.add(w)
                    d.update(readers.get(r, ()))
            d.discard(i)
            for k in R:
                readers.setdefault(k, []).append(i)
            for k in W:
                lastw[k] = i
                for r in desc.get(k, ()):
                    if r in readers:
                        readers[r] = []
                    lastw[r] = i
            deps_all.append(d)
        return deps_all

    def emit(self, nc, stack):
        ops = self.ops
        deps_all = self._deps()
        n = len(ops)
        sig = [False] * n
        for i, d in enumerate(deps_all):
            e_i = ops[i][0]
            for j in d:
                if ops[j][0] == "pe" and e_i == "pe":
                    continue
                sig[j] = True
        dma_prev = {}
        dma_slot = {}
        nd = 0
        for i, op in enumerate(ops):
            if op[4]:
                s = nd % self.NDMA
                nd += 1
                dma_slot[i] = s
                if s in dma_prev:
                    deps_all[i].add(dma_prev[s])
                dma_prev[s] = i
                sig[i] = True
        EPOCH = 1000
        esems = {e: [] for e in ("pe", "act", "dve", "pool")}
        dsem = [stack.enter_context(nc.semaphore("dsem%d" % k)) for k in range(min(self.NDMA, max(nd, 1)))]
        count = {e: 0 for e in esems}
        dcount = [0] * self.NDMA
        known = {e: {} for e in self.ENGS}
        event = [None] * n
        vc = [None] * n
        plan = {e: [] for e in self.ENGS}
        nwaits = 0
        Z = (0, 0)

        def sem_of(src, ep):
            if isinstance(src, tuple):
                return dsem[src[1]]
            lst = esems[src]
            while len(lst) <= ep:
                lst.append(stack.enter_context(nc.semaphore("sem_%s_%d" % (src, len(lst)))))
            return lst[ep]

        for i, (eng, fn, R, W, dma) in enumerate(ops):
            kn = known[eng]
            wm = {}
            for j in sorted(deps_all[i]):
                if ops[j][0] == "pe" and eng == "pe":
                    continue
                src, val = event[j]
                if kn.get(src, Z) >= val:
                    continue
                if wm.get(src, Z) < val:
                    wm[src] = val
                for s2, v2 in vc[j].items():
                    if kn.get(s2, Z) < v2:
                        kn[s2] = v2
            nwaits += len(wm)
            inc = None
            if sig[i]:
                if dma:
                    s = dma_slot[i]
                    dcount[s] += 16
                    event[i] = (("d", s), (0, dcount[s]))
                    inc = (dsem[s], 16)
                else:
                    ep, cn = divmod(count[eng], EPOCH)
                    count[eng] += 1
                    event[i] = (eng, (ep, cn + 1))
                    inc = (sem_of(eng, ep), 1)
                v = dict(kn)
                v[event[i][0]] = event[i][1]
                vc[i] = v
            plan[eng].append((fn, [(sem_of(src, val[0]), val[1]) for src, val in wm.items()], inc))
        self.stats = dict(n_ops=n, n_waits=nwaits, counts=dict(count), n_dma=nd,
                          n_sems=len(dsem) + sum(len(v) for v in esems.values()))

        block = stack.enter_context(nc.Block())

        def run(engine, items):
            for fn, waits, inc in items:
                for sem, val in waits:
                    engine.wait_ge(sem, val)
                ins = fn(engine)
                if inc is not None:
                    ins.then_inc(inc[0], inc[1])

        @block.tensor
        def _(e):
            run(e, plan["pe"])

        @block.scalar
        def _(e):
            run(e, plan["act"])

        @block.vector
        def _(e):
            run(e, plan["dve"])

        @block.gpsimd
        def _(e):
            run(e, plan["pool"])

        @block.sync
        def _(e):
            run(e, plan["sp"])
            for s in range(len(dsem)):
                if dcount[s] > 0:
                    e.wait_ge(dsem[s], dcount[s])


class Skew:
    def __init__(self, lag):
        self.q = []
        self.lag = lag

    def push(self, fn):
        self.q.append(fn)
        while len(self.q) > self.lag:
            self.q.pop(0)()

    def flush(self):
        while self.q:
            self.q.pop(0)()


def _rope_tab(head_dim, rows_rep):
    rot = head_dim // 4
    half = rot // 2
    inv = (1.0 / (np.float32(500000.0) ** (np.arange(0, rot, 2, dtype=np.float32) / np.float32(rot)))).astype(np.float32)
    ang = np.arange(S, dtype=np.float32)[:, None] * inv[None, :]
    cos = np.cos(ang).astype(np.float32).T
    sin = np.sin(ang).astype(np.float32).T
    C = np.ones((128, S), np.float32)
    Sn = np.zeros((128, S), np.float32)
    for b in range(128 // head_dim):
        o = b * head_dim
        C[o:o + half] = cos
        C[o + half:o + rot] = cos
        Sn[o:o + half] = sin
        Sn[o + half:o + rot] = sin
    P = np.zeros((128, 128), np.float32)
    for b in range(128 // head_dim):
        o = b * head_dim
        for r in range(half):
            P[o + r + half, o + r] = -1.0
            P[o + r, o + r + half] = 1.0
    return C, Sn, P


def _consts():
    c = {}
    eye = np.eye(128, dtype=np.float32)
    idx = np.arange(128)
    c["ident"] = eye
    c["tri"] = -(idx[:, None] <= idx[None, :]).astype(np.float32)
    c["negones"] = -np.ones((128, 128), np.float32)
    c["ones"] = np.ones((128, 128), np.float32)
    b64 = np.zeros((128, 128), np.float32)
    b64[:64, :64] = 1
    b64[64:, 64:] = 1
    c["bones64"] = b64
    b32 = np.zeros((128, 128), np.float32)
    for b in range(4):
        b32[32 * b:32 * b + 32, 32 * b:32 * b + 32] = 1
    c["bones32"] = b32
    sel = np.zeros((128, 6, 128), np.float32)
    for h in range(6):
        sel[h, h, :] = 1
        sel[32 + h, h, :] = 1
        sel[64 + h, h, :] = 1
    c["sel"] = sel.reshape(128, 768)
    c["cb"] = np.where(idx[:, None] <= idx[None, :], 0.0, NEG).astype(np.float32)
    c["cbt"] = np.where(idx[None, :] <= idx[:, None], 0.0, NEG).astype(np.float32)
    c["irep"] = np.tile(eye, (1, 4))
    C64, S64, P64 = _rope_tab(64, 2)
    C32, S32, P32 = _rope_tab(32, 4)
    c["prot64"] = P64
    c["prot32"] = P32
    c["rope"] = np.stack([C32, S32, C64, S64]).astype(np.float32)
    c["pert"] = np.tile((-PERT_EPS * np.arange(S, dtype=np.float32))[None, :], (128, 1)).astype(np.float32)
    p2 = np.zeros((128, 64), np.float32)
    for k in range(32):
        p2[:, k] = 2.0 ** (-k)
        p2[:, 32 + k] = 2.0 ** (1 - k)
    c["pow2"] = p2
    return c


_CONST_SHAPES = [("ident", 128), ("tri", 128), ("negones", 128), ("ones", 128), ("bones64", 128), ("bones32", 128),
                 ("cb", 128), ("cbt", 128), ("irep", 512), ("prot64", 128), ("prot32", 128), ("pow2", 64)]


def _pack_pp(inp, l):
    pp = np.zeros((128, NPP), np.float32)
    p = np.arange(128)
    pp[:, 0] = inp["fox_qn"][l][p % 64]
    pp[:, 1] = inp["fox_kn"][l][p % 64]
    pp[:, 2] = inp["diff_qn"][l][p % 32]
    pp[:, 3] = inp["diff_kn"][l][p % 32]
    pp[:, 4] = inp["dsa_qn"][l][p % 64]
    pp[:, 5] = inp["dsa_kn"][l][p % 64]
    pp[:, 6] = inp["diff_subln"][l][p % 64]
    pp[:, 7:15] = inp["attn_norm"][l].reshape(8, 128).T
    pp[:, 15:23] = inp["ffn_norm"][l].reshape(8, 128).T
    pp[:, 23:29] = inp["fox_fb"][l][None, :]
    pp[:, 29:61] = inp["diff_lq1"][l][None, :]
    pp[:, 61:93] = inp["diff_lk1"][l][None, :]
    pp[:, 93:125] = inp["diff_lq2"][l][None, :]
    pp[:, 125:157] = inp["diff_lk2"][l][None, :]
    return pp


def build_nc(layer_ids, debug=None):
    NL = len(layer_ids)
    nc = bass.Bass("TRN2", target_bir_lowering=False)
    stack = ExitStack()
    P = Prog()

    def dram(name, shape, dt=F32, kind="ExternalInput"):
        return nc.dram_tensor(name, list(shape), dt, kind=kind).ap()

    x_d = dram("xT", [D, S])
    y_d = dram("yT", [D, S], kind="ExternalOutput")
    w_in_d = dram("w_in", [NL, D, INW])
    w_out_d = dram("w_out", [NL, D, D])
    w_gu_d = dram("w_gu", [NL, D, 2 * FFN_H])
    w_dn_d = dram("w_dn", [NL, FFN_H, D])
    pp_d = dram("pp", [NL, 128, NPP])
    cst_d = {nm: dram("c_" + nm, [128, w]) for nm, w in _CONST_SHAPES}
    rope_d = dram("c_rope", [4, 128, S])
    pert_d = dram("c_pert", [128, S])
    sc_fm = dram("sc_fm", [17, 128, S], BF16, kind="Internal")
    sc_v = dram("sc_v", [5, 128, 16, 128], BF16, kind="Internal")
    sc_sv = dram("sc_sv", [128, 16, 64], BF16, kind="Internal")
    sc_pad = dram("sc_pad", [20, 128, S], BF16, kind="Internal")
    dbg = {}
    if debug:
        for nm, shape, dt in debug:
            dbg[nm] = dram("dbg_" + nm, shape, dt, kind="ExternalOutput")

    def sb(name, shape, dt=F32):
        return stack.enter_context(nc.sbuf_tensor(name, list(shape), dt))

    def ps(name, shape, dt=F32):
        return stack.enter_context(nc.psum_tensor(name, list(shape), dt))

    xT = sb("xT_sb", [128, 8, S])
    hT = sb("hT_sb", [128, 8, S], BF16)
    RW = 15 * 1024
    Rg = sb("R_sb", [128, RW])
    stg = sb("stg_sb", [128, 2, 1024])
    banks = [ps("bank%d" % b, [128, 512]) for b in range(8)]

    def rv(off_kb, size_kb, dt=F32):
        a = Rg[:, off_kb * 256:(off_kb + size_kb) * 256]
        if dt == BF16:
            a = a.bitcast(BF16)
        return a

    def rk(off_kb, size_kb):
        return [("R", pg) for pg in range(off_kb // 4, (off_kb + size_kb + 3) // 4)]

    sqt = sb("sqt", [128, 2, 512], BF16)
    rs = sb("rs", [128, 2, 512])
    qn = sb("qn", [128, 3, 512], BF16)
    t1 = sb("t1", [128, 2, 512])
    t2 = sb("t2", [128, 2, 512])
    pt = sb("pt", [128, 4, 512], BF16)
    rct = sb("rct", [128, 2, 512])
    ppt = sb("ppt", [128, NPP])
    der = sb("der", [128, 16])
    FFt = sb("FFt", [128, 16, 6])
    IWs = sb("IWs", [128, 16, 4])
    Lt = sb("Lt", [128, 16, 6])
    negc = sb("negc", [128, 16, 6])
    chb = sb("chb", [128, 16, 6], BF16)
    r1t = sb("r1t", [128, 16, 6])
    scA = sb("scA", [128, 16, 6])
    scB = sb("scB", [128, 16, 6])
    csb = sb("csb", [128, 16, 6])
    lam4 = sb("lam4", [128, 4, 32])
    CST = {}
    for nm, w in _CONST_SHAPES:
        f32c = nm in ("ident", "tri", "negones", "cbt", "pow2")
        CST[nm] = sb("k_" + nm, [128, w], F32 if f32c else BF16)
    epsc = sb("epsc", [128, 2])
    accB_t = sb("accB_t", [128, 2048])
    bis2 = sb("bis2", [128, 2, 4])
    STt2 = sb("STt2", [128, 2, 64])

    bank_rr = {}

    def bank(pool):
        i = bank_rr.get(pool, 0)
        bank_rr[pool] = i + 1
        b = pool[i % len(pool)]
        return banks[b], ("ps", b)

    slot_rr = {}

    def slot(name, nslots):
        i = slot_rr.get(name, 0)
        slot_rr[name] = i + 1
        return i % nslots

    def dma(out, in_, R, W):
        P.add("sp", lambda e: e.dma_start(out=out, in_=in_), R=R, W=W, dma=True)

    def mm(out, lhsT, rhs, start, stop, R, W, **kw):
        P.add("pe", lambda e: e.matmul(out, lhsT=lhsT, rhs=rhs, start=start, stop=stop, **kw), R=R, W=W)

    def act(out, in_, func, R, W, bias=0.0, scale=1.0, accum_out=None):
        if accum_out is None:
            P.add("act", lambda e: e.activation(out=out, in_=in_, func=func, bias=bias, scale=scale), R=R, W=W)
        else:
            P.add("act", lambda e: e.activation(out=out, in_=in_, func=func, bias=bias, scale=scale, accum_out=accum_out),
                  R=R, W=W)

    def recip(out, in_, R, W, on_act=True):
        if on_act:
            act(out, in_, AF.Ln, R=R, W=W)
            act(out, out, AF.Exp, R=W, W=W, scale=-1.0)
        else:
            P.add("dve", lambda e: e.reciprocal(out=out, in_=in_), R=R, W=W)

    def load_w(dst, dst_keys, src_ap):
        s = slot("stg", 2)
        n = 1
        for d_ in src_ap.shape[1:]:
            n *= d_
        sv = stg[:, s, 0:n]
        if len(src_ap.shape) == 3:
            sv = sv.rearrange("p (a b) -> p a b", a=src_ap.shape[1])
        dma(sv, src_ap, R=[], W=[("stg", s)])
        P.add("pool", lambda e: e.tensor_copy(out=dst, in_=sv), R=[("stg", s)], W=dst_keys)

    for kc in range(8):
        dma(xT[:, kc, :], x_d[kc * 128:(kc + 1) * 128, :], R=[], W=[("xT", kc)])
    for nm, w in _CONST_SHAPES:
        if CST[nm].dtype == F32:
            dma(CST[nm][:, :], cst_d[nm][:, :], R=[], W=[("k", nm)])
        else:
            for o in range(0, w, 512):
                ww = min(512, w - o)
                s = slot("t1", 2)
                dma(t1[:, s, 0:ww], cst_d[nm][:, o:o + ww], R=[], W=[("t1", s)])
                P.add("pool", lambda e, nm=nm, o=o, ww=ww, s=s: e.tensor_copy(out=CST[nm][:, o:o + ww], in_=t1[:, s, 0:ww]),
                      R=[("t1", s)], W=[("k", nm)])
    P.add("pool", lambda e: e.memset(epsc[:, 0:1], EPS), W=[("epsc",)])
    P.add("pool", lambda e: e.memset(epsc[:, 1:2], 1.0), W=[("epsc",)])
    KEPS = [("epsc",)]
    P.add("pool", lambda e: e.memset(hT[:, 0, :], 0.0), W=[("hT", 0)])
    P.add("pool", lambda e: e.memset(hT[:, 1, :], 1.0), W=[("hT", 1)])
    for t_ in range(20):
        dma(sc_pad[t_], hT[:, 0, :], R=[("hT", 0)], W=[("scp", t_)])
    for p_ in range(3):
        dma(sc_pad[6 + 2 * p_, 64:67, :], hT[64:67, 1, :], R=[("hT", 1)], W=[("scp", 6 + 2 * p_)])
        dma(sc_pad[7 + 2 * p_, 0:3, :], hT[0:3, 1, :], R=[("hT", 1)], W=[("scp", 7 + 2 * p_)])
    eps_ap = epsc[:, 0:1]
    one_ap = epsc[:, 1:2]

    def norm_phase(gcol0):
        for c in range(NCH):
            cs = slice(c * 512, (c + 1) * 512)
            bk, bkk = bank((0, 1))
            for kc in range(8):
                s = slot("sqt", 2)
                act(sqt[:, s, :], xT[:, kc, cs], AF.Square, R=[("xT", kc, c)], W=[("sqt", s)])
                mm(bk[:, :], CST["ones"][:, :], sqt[:, s, :], kc == 0, kc == 7, R=[("sqt", s), ("k", "ones")], W=[bkk])
            s = slot("rs", 2)
            act(rs[:, s, :], bk[:, :], AF.Ln, R=[bkk] + KEPS, W=[("rs", s)], bias=eps_ap, scale=1.0 / D)
            act(rs[:, s, :], rs[:, s, :], AF.Exp, R=[("rs", s)], W=[("rs", s)], scale=-0.5)
            for kc in range(8):
                P.add("dve", lambda e, kc=kc, cs=cs, s=s: e.scalar_tensor_tensor(
                    out=hT[:, kc, cs], in0=xT[:, kc, cs], scalar=ppt[:, gcol0 + kc:gcol0 + kc + 1], in1=rs[:, s, :],
                    op0=ALU.mult, op1=ALU.mult), R=[("xT", kc, c), ("ppt",), ("rs", s)], W=[("hT", kc, c)])

    def ffn_phase(li):
        groups = [(g * 512, 4) for g in range(5)] + [(2560, 2)]
        Wgu = [rv(0, 16, BF16).rearrange("p (k n) -> p k n", k=8), rv(16, 16, BF16).rearrange("p (k n) -> p k n", k=8)]
        Wgu_k = [rk(0, 16), rk(16, 16)]
        Wd = [rv(32, 8, BF16).rearrange("p (k n) -> p k n", k=4), rv(40, 8, BF16).rearrange("p (k n) -> p k n", k=4)]
        Wd_k = [rk(32, 8), rk(40, 8)]
        actT = rv(48, 8, BF16).rearrange("p (s k n) -> p s k n", s=2, k=4)
        actT_k = [rk(48, 4), rk(52, 4)]
        win = w_gu_d[li].rearrange("(kc p) n -> p kc n", p=128)

        def load_group(gi):
            h0, nt = groups[gi]
            sl = gi % 2
            for t in range(nt):
                load_w(Wgu[sl][:, :, t * 128:(t + 1) * 128], Wgu_k[sl], win[:, :, h0 + t * 128:h0 + (t + 1) * 128])
                load_w(Wgu[sl][:, :, 512 + t * 128:512 + (t + 1) * 128], Wgu_k[sl],
                       win[:, :, FFN_H + h0 + t * 128:FFN_H + h0 + (t + 1) * 128])
                load_w(Wd[sl][:, t, :], Wd_k[sl], w_dn_d[li, h0 + t * 128:h0 + (t + 1) * 128, :])

        load_group(0)
        for gi in range(len(groups)):
            if gi + 1 < len(groups):
                load_group(gi + 1)
            h0, nt = groups[gi]
            sl = gi % 2
            for c in range(NCH):
                cs = slice(c * 512, (c + 1) * 512)
                asl = slot("actT", 2)
                for t in range(nt):
                    gb, gbk = bank((0, 1, 2, 3))
                    ub, ubk = bank((0, 1, 2, 3))
                    for kc in range(8):
                        mm(gb[:, :], Wgu[sl][:, kc, t * 128:(t + 1) * 128], hT[:, kc, cs], kc == 0, kc == 7,
                           R=Wgu_k[sl] + [("hT", kc, c)], W=[gbk])
                    for kc in range(8):
                        mm(ub[:, :], Wgu[sl][:, kc, 512 + t * 128:512 + (t + 1) * 128], hT[:, kc, cs], kc == 0, kc == 7,
                           R=Wgu_k[sl] + [("hT", kc, c)], W=[ubk])
                    s = slot("t1", 2)
                    act(t1[:, s, :], gb[:, :], AF.Silu, R=[gbk], W=[("t1", s)])
                    P.add("dve", lambda e, s=s, ub=ub, asl=asl, t=t: e.tensor_tensor(
                        out=actT[:, asl, t, :], in0=ub[:, :], in1=t1[:, s, :], op=ALU.mult),
                        R=[ubk, ("t1", s)], W=actT_k[asl])
                for d_ in range(8):
                    db, dbk = bank((4, 5, 6, 7))
                    for t in range(nt):
                        mm(db[:, :], Wd[sl][:, t, d_ * 128:(d_ + 1) * 128], actT[:, asl, t, :], t == 0, t == nt - 1,
                           R=Wd_k[sl] + actT_k[asl], W=[dbk])
                    P.add("dve", lambda e, d_=d_, cs=cs, db=db: e.tensor_tensor(
                        out=xT[:, d_, cs], in0=db[:, :], in1=xT[:, d_, cs], op=ALU.add),
                        R=[dbk, ("xT", d_, c)], W=[("xT", d_, c)])


    def apb(base_ap, mid):
        a = base_ap.ap
        return bass.AP(base_ap.tensor, base_ap.offset, [list(a[0]), [0, mid], list(a[-1])])

    def tap(name, src, R):
        if name in dbg:
            dma(dbg[name], src, R=R, W=[("dbg", name)])

    def derive_phase(labs):
        lam_init = 0.8 - 0.6 * math.exp(-0.3 * labs)
        for col, src, mul in ((0, 0, 0.125), (1, 2, 32.0 ** -0.5), (2, 4, 0.125), (3, 6, 1.0 - lam_init)):
            P.add("dve", lambda e, col=col, src=src, mul=mul: e.tensor_scalar(
                out=der[:, col:col + 1], in0=ppt[:, src:src + 1], scalar1=mul, scalar2=None, op0=ALU.mult),
                R=[("ppt",)], W=[("der", col)])
        for q, (a, b) in enumerate(((29, 61), (93, 125))):
            P.add("dve", lambda e, q=q, a=a, b=b: e.tensor_tensor(
                out=lam4[:, q, :], in0=ppt[:, a:a + 32], in1=ppt[:, b:b + 32], op=ALU.mult), R=[("ppt",)], W=[("lam4", q)])
            P.add("dve", lambda e, q=q: e.tensor_reduce(out=der[:, 5 + q:6 + q], in_=lam4[:, q, :], axis=AX.X, op=ALU.add),
                  R=[("lam4", q)], W=[("der", 5 + q)])
            act(der[:, 5 + q:6 + q], der[:, 5 + q:6 + q], AF.Exp, R=[("der", 5 + q)], W=[("der", 5 + q)])
        P.add("dve", lambda e: e.tensor_tensor(out=der[:, 4:5], in0=der[:, 6:7], in1=der[:, 5:6], op=ALU.subtract),
              R=[("der", 5), ("der", 6)], W=[("der", 4)])
        P.add("dve", lambda e: e.tensor_scalar(out=der[:, 4:5], in0=der[:, 4:5], scalar1=-lam_init, scalar2=None, op0=ALU.add),
              R=[("der", 4)], W=[("der", 4)])

    def proj_phase(li):
        win = w_in_d[li].rearrange("(kc p) n -> p kc n", p=128)
        WT = [rv(0, 2, BF16).rearrange("p (k n) -> p k n", k=8), rv(2, 2, BF16).rearrange("p (k n) -> p k n", k=8)]
        WT_k = [[("R", 0, 0)], [("R", 0, 1)]]
        Wv = rv(4, 6, BF16).rearrange("p (k n) -> p k n", k=8)
        Wv_k = rk(4, 6)
        ost = rv(12, 4, BF16).rearrange("p (s n) -> p s n", s=4)
        ost_k = [[("R", 3, s_)] for s_ in range(4)]
        Ct = rv(36, 8)
        St = rv(44, 8)
        Ct_k, St_k = rk(36, 8), rk(44, 8)
        FM = []
        for p_ in range(3):
            FM.append(dict(sc=p_, segs=[(O_FQ + 128 * p_, 128)], norm=(64, "bones64", der, 0, ("der", 0)), rope=None,
                           outs=[(2 * p_, 0, 64), (2 * p_ + 1, 64, 64)]))
        for p_ in range(3):
            FM.append(dict(sc=3 + p_, segs=[(O_FK + 128 * p_, 128)], norm=(64, "bones64", ppt, 1, ("ppt",)), rope=None,
                           outs=[(6 + 2 * p_, 0, 64), (7 + 2 * p_, 64, 64)]))
        for p_ in range(2):
            FM.append(dict(sc=6 + p_, segs=[(O_DQ + 128 * p_, 128)], norm=(32, "bones32", der, 1, ("der", 1)), rope=32,
                           outs=[(12 + 4 * p_ + q_, 32 * q_, 32) for q_ in range(4)]))
        for p_ in range(2):
            FM.append(dict(sc=8 + p_, segs=[(O_DK + 128 * p_, 128)], norm=(32, "bones32", ppt, 3, ("ppt",)), rope=32))
        for p_ in range(3):
            FM.append(dict(sc=10 + p_, segs=[(O_SQ + 128 * p_, 128)], norm=(64, "bones64", der, 2, ("der", 2)), rope=64))
        FM.append(dict(sc=13, segs=[(O_SK, 64), (O_SK, 64)], norm=(64, "bones64", ppt, 5, ("ppt",)), rope=64))
        FM.append(dict(sc=14, segs=[(O_IK, 64), (O_IK, 64)], norm=None, rope=64))
        for p_ in range(2):
            FM.append(dict(sc=15 + p_, segs=[(O_IQ + 128 * p_, 128)], norm=None, rope=64))

        def load_tile(ti):
            sl = ti % 2
            o = 0
            for col0, n_ in FM[ti]["segs"]:
                load_w(WT[sl][:, :, o:o + n_], WT_k[sl], win[:, :, col0:col0 + n_])
                o += n_

        cur_rope = None
        SKB = Skew(1)
        SKC = Skew(2)
        load_tile(0)
        for ti, T in enumerate(FM):
            if ti + 1 < len(FM):
                load_tile(ti + 1)
            sl = ti % 2
            if T["rope"] is not None and T["rope"] != cur_rope:
                SKB.flush()
                SKC.flush()
                cur_rope = T["rope"]
                ro = 0 if cur_rope == 32 else 2
                dma(Ct, rope_d[ro], R=[], W=Ct_k)
                dma(St, rope_d[ro + 1], R=[], W=St_k)
            for c in range(NCH):
                cs = slice(c * 512, (c + 1) * 512)
                pj, pjk = bank((0, 1, 2))
                for kc in range(8):
                    mm(pj[:, :], WT[sl][:, kc, :], hT[:, kc, cs], kc == 0, kc == 7, R=WT_k[sl] + [("hT", kc, c)], W=[pjk])
                sq_ = None
                if T["norm"] is not None:
                    sq_ = slot("sqt", 2)
                    act(sqt[:, sq_, :], pj[:, :], AF.Square, R=[pjk], W=[("sqt", sq_)])

                def stageB(T=T, c=c, cs=cs, pj=pj, pjk=pjk, sq_=sq_):
                    os_ = slot("ost", 4)
                    qs_ = None
                    if T["rope"] is not None:
                        qs_ = slot("qn", 3)
                        tgt, tgtk = qn[:, qs_, :], [("qn", qs_)]
                    else:
                        tgt, tgtk = ost[:, os_, :], ost_k[os_]
                    if T["norm"] is not None:
                        bs_, bname, gt, gcol, gkey = T["norm"]
                        sb_, sbk = bank((3, 4))
                        mm(sb_[:, :], CST[bname][:, :], sqt[:, sq_, :], True, True, R=[("sqt", sq_), ("k", bname)], W=[sbk])
                        r_ = slot("rs", 2)
                        act(rs[:, r_, :], sb_[:, :], AF.Ln, R=[sbk] + KEPS, W=[("rs", r_)], bias=eps_ap, scale=1.0 / bs_)
                        act(rs[:, r_, :], rs[:, r_, :], AF.Exp, R=[("rs", r_)], W=[("rs", r_)], scale=-0.5)
                        P.add("dve", lambda e: e.scalar_tensor_tensor(
                            out=tgt, in0=pj[:, :], scalar=gt[:, gcol:gcol + 1], in1=rs[:, r_, :], op0=ALU.mult, op1=ALU.mult),
                            R=[pjk, gkey, ("rs", r_)], W=tgtk)
                    else:
                        act(tgt, pj[:, :], AF.Copy, R=[pjk], W=tgtk)

                    def stageC():
                        if T["rope"] is not None:
                            pname = "prot%d" % T["rope"]
                            rp, rpk = bank((5, 6))
                            mm(rp[:, :], CST[pname][:, :], qn[:, qs_, :], True, True, R=[("qn", qs_), ("k", pname)], W=[rpk])
                            a_ = slot("t1", 2)
                            b_ = slot("t2", 2)
                            P.add("dve", lambda e: e.tensor_tensor(out=t1[:, a_, :], in0=rp[:, :], in1=St[:, cs], op=ALU.mult),
                                  R=[rpk] + St_k, W=[("t1", a_)])
                            P.add("pool", lambda e: e.tensor_tensor(out=t2[:, b_, :], in0=qn[:, qs_, :], in1=Ct[:, cs], op=ALU.mult),
                                  R=[("qn", qs_)] + Ct_k, W=[("t2", b_)])
                            P.add("dve", lambda e: e.tensor_tensor(out=ost[:, os_, :], in0=t1[:, a_, :], in1=t2[:, b_, :], op=ALU.add),
                                  R=[("t1", a_), ("t2", b_)], W=ost_k[os_])
                        if "outs" in T:
                            for tl_, r0_, nr_ in T["outs"]:
                                dma(sc_pad[tl_, r0_:r0_ + nr_, cs], ost[r0_:r0_ + nr_, os_, :], R=ost_k[os_], W=[("scp", tl_, c)])
                        else:
                            dma(sc_fm[T["sc"], :, cs], ost[:, os_, :], R=ost_k[os_], W=[("scfm", T["sc"], c)])
                    SKC.push(stageC)
                SKB.push(stageB)
        SKB.flush()
        SKC.flush()

        def tm_group(col_segs, ncols, handler):
            o = 0
            for col0, n_ in col_segs:
                load_w(Wv[:, :, o:o + n_], Wv_k, win[:, :, col0:col0 + n_])
                o += n_
            handler()

        def v_pairs(npairs, sc0):
            for a in range(npairs):
                for jg in range(4):
                    tv, tvk = bank((6, 7))
                    for jj in range(4):
                        j = jg * 4 + jj
                        for kc in range(8):
                            mm(tv[:, jj * 128:(jj + 1) * 128], hT[:, kc, j * 128:(j + 1) * 128], Wv[:, kc, a * 128:(a + 1) * 128],
                               kc == 0, kc == 7, R=Wv_k + [("hT", kc, jg)], W=[tvk])
                    os_ = slot("ost", 4)
                    act(ost[:, os_, :], tv[:, :], AF.Copy, R=[tvk], W=ost_k[os_])
                    dma(sc_v[sc0 + a, :, jg * 4:(jg + 1) * 4, :], ost[:, os_, :].rearrange("p (j n) -> p j n", j=4),
                        R=ost_k[os_], W=[("scv", sc0 + a, jg)])

        tm_group([(O_FV, 128), (O_FV + 128, 128), (O_FV + 256, 128)], 384, lambda: v_pairs(3, 0))
        tm_group([(O_DV, 128), (O_DV + 128, 128)], 256, lambda: v_pairs(2, 3))

        def small():
            for jg in range(4):
                tv, tvk = bank((6, 7))
                for jj in range(4):
                    j = jg * 4 + jj
                    for kc in range(8):
                        mm(tv[:, jj * 74:(jj + 1) * 74], hT[:, kc, j * 128:(j + 1) * 128], Wv[:, kc, 0:74],
                           kc == 0, kc == 7, R=Wv_k + [("hT", kc, jg)], W=[tvk])
                tv3 = tv[:, 0:296].rearrange("p (j n) -> p j n", j=4)
                os_ = slot("ost", 4)
                act(ost[:, os_, 0:256].rearrange("p (j n) -> p j n", j=4), tv3[:, :, 0:64], AF.Copy, R=[tvk], W=ost_k[os_])
                dma(sc_sv[:, jg * 4:(jg + 1) * 4, :], ost[:, os_, 0:256].rearrange("p (j n) -> p j n", j=4),
                    R=ost_k[os_], W=[("scsv", jg)])
                P.add("dve", lambda e, jg=jg, tv3=tv3: e.tensor_copy(out=FFt[:, jg * 4:(jg + 1) * 4, :], in_=tv3[:, :, 64:70]),
                      R=[tvk], W=[("FFt", jg)])
                P.add("dve", lambda e, jg=jg, tv3=tv3: e.tensor_scalar(
                    out=IWs[:, jg * 4:(jg + 1) * 4, :], in0=tv3[:, :, 70:74], scalar1=0.0625, scalar2=None, op0=ALU.mult),
                    R=[tvk], W=[("IWs", jg)])

        tm_group([(O_SV, 64), (O_FF, 6), (O_IW, 4)], 74, small)

    TL = [[rv(28 * s_ + 4 * i_, 4, BF16) for i_ in range(4)] for s_ in range(2)]
    TL_k = [[rk(28 * s_ + 4 * i_, 4) for i_ in range(4)] for s_ in range(2)]
    Kd = [rv(28 * s_ + 16, 4, BF16) for s_ in range(2)]
    Kd_k = [rk(28 * s_ + 16, 4) for s_ in range(2)]
    Vp = [rv(28 * s_ + 20, 8, BF16).rearrange("p (j s d) -> p j s d", j=16, s=4) for s_ in range(2)]
    Vp_k = [rk(28 * s_ + 20, 8) for s_ in range(2)]
    identb = CST["irep"][:, 0:128]

    def load_v(sl, vi):
        dma(Vp[sl][:, :, 0, :], sc_v[vi][:, :, 0:64], R=[("scv", vi)], W=Vp_k[sl])
        dma(Vp[sl][:, :, 3, :], sc_v[vi][:, :, 64:128], R=[("scv", vi)], W=Vp_k[sl])
        P.add("pool", lambda e: e.memset(Vp[sl][:, :, 1:3, :], 1.0), W=Vp_k[sl])

    def load_fox_pair(sl, p_):
        for i_, t_ in enumerate((2 * p_, 2 * p_ + 1, 6 + 2 * p_, 7 + 2 * p_)):
            dma(TL[sl][i_], sc_pad[t_], R=[("scp", t_)], W=TL_k[sl][i_])
        load_v(sl, p_)

    def load_diff_pair(sl, d_):
        for i_ in range(4):
            dma(TL[sl][i_], sc_pad[12 + 4 * d_ + i_], R=[("scp", 12 + 4 * d_ + i_)], W=TL_k[sl][i_])
        dma(Kd[sl], sc_fm[8 + d_], R=[("scfm", 8 + d_)], W=Kd_k[sl])
        load_v(sl, 3 + d_)

    SKA = Skew(3)

    def attn_map(sl, e_, lhs, lhsk, rhs, rhsk, c, h, fox, acc, acck):
        nj = 4 * c + 4
        for j in range(nj):
            n0 = max(j * 128, c * 512)
            wN = (c + 1) * 512 - n0
            off = n0 - c * 512
            diag = j >= 4 * c
            st, stk = bank((4, 5, 6, 7))
            mm(st[:, off:off + wN], lhs[:, j * 128:(j + 1) * 128], rhs[:, n0:n0 + wN],
               True, not diag, R=lhsk + rhsk, W=[stk])
            if diag:
                mm(st[:, off:off + 128], identb, CST["cb"][:, :], False, True, R=[("k", "irep"), ("k", "cb")], W=[stk])
            ps_ = slot("pt", 4)
            if fox:
                act(pt[:, ps_, off:off + wN], st[:, off:off + wN], AF.Exp, R=[stk, ("negc",)], W=[("pt", ps_)],
                    bias=negc[:, j, h:h + 1])
            else:
                act(pt[:, ps_, off:off + wN], st[:, off:off + wN], AF.Exp, R=[stk], W=[("pt", ps_)])
            SKA.push(lambda j=j, off=off, wN=wN, ps_=ps_: mm(
                acc[:, off:off + wN], Vp[sl][:, j, 2 * e_:2 * e_ + 2, :].rearrange("p s d -> p (s d)"), pt[:, ps_, off:off + wN],
                j == 0, j == nj - 1, R=Vp_k[sl] + [("pt", ps_)], W=[acck]))

    CT = rv(56, 4, BF16)
    CT_k = rk(56, 4)
    CS = rv(28, 6).rearrange("p (j n) -> p j n", j=16)
    CS_k = rk(28, 6)

    def fox_phase():
        P.add("dve", lambda e: e.tensor_tensor(out=Lt[:, :, :], in0=FFt[:, :, :], in1=apb(ppt[:, 23:29], 16), op=ALU.add),
              R=[("FFt",), ("ppt",)], W=[("Lt",)])
        act(Lt[:, :, :], Lt[:, :, :], AF.Exp, R=[("Lt",)], W=[("Lt",)], scale=-1.0)
        act(Lt[:, :, :], Lt[:, :, :], AF.Ln, R=[("Lt",)] + KEPS, W=[("Lt",)], bias=one_ap)
        Ltf = Lt[:, :, :].rearrange("p j n -> p (j n)")
        cps, cpsk = bank((0,))
        mm(cps[:, 0:96], CST["negones"][:, :], Ltf, True, True, R=[("Lt",), ("k", "negones")], W=[cpsk])
        cp2, cp2k = bank((1,))
        mm(cp2[:, 0:96], CST["tri"][:, :], Ltf, True, True, R=[("Lt",), ("k", "tri")], W=[cp2k])
        cpsv = cps[:, 0:96].rearrange("p (j n) -> p j n", j=16)
        cp23 = cp2[:, 0:96].rearrange("p (j n) -> p j n", j=16)
        P.add("dve", lambda e: e.tensor_copy(out=scA[:, :, :], in_=cpsv), R=[cpsk], W=[("scA",)])
        bufs = [(scA, ("scA",)), (scB, ("scB",))]
        cur = 0
        for sh in (1, 2, 4, 8):
            (a_, ak), (b_, bk) = bufs[cur], bufs[1 - cur]
            P.add("dve", lambda e, a_=a_, b_=b_, sh=sh: e.tensor_copy(out=b_[:, 0:sh, :], in_=a_[:, 0:sh, :]), R=[ak], W=[bk])
            P.add("dve", lambda e, a_=a_, b_=b_, sh=sh: e.tensor_tensor(out=b_[:, sh:16, :], in0=a_[:, sh:16, :],
                                                                   in1=a_[:, 0:16 - sh, :], op=ALU.add), R=[ak], W=[bk])
            cur = 1 - cur
        inc_, inck = bufs[cur]
        P.add("dve", lambda e: e.tensor_copy(out=csb[:, 0:1, :], in_=cp23[:, 0:1, :]), R=[cp2k], W=[("csb",)])
        P.add("dve", lambda e: e.tensor_tensor(out=csb[:, 1:16, :], in0=cp23[:, 1:16, :], in1=inc_[:, 0:15, :], op=ALU.add),
              R=[cp2k, inck], W=[("csb",)])
        cpsk = ("csb",)
        cps3 = csb[:, :, :]
        P.add("dve", lambda e: e.tensor_scalar(out=negc[:, :, :], in0=cps3, scalar1=-1.0, scalar2=None, op0=ALU.mult),
              R=[cpsk], W=[("negc",)])
        P.add("pool", lambda e: e.memset(CS[:, :, :], 0.0), W=CS_k)
        P.add("dve", lambda e: e.tensor_copy(out=chb[:, :, :], in_=cps3), R=[cpsk], W=[("chb",)])
        P.add("dve", lambda e: e.tensor_copy(out=CS[:, :, 0:6], in_=chb[:, :, :]), R=[("chb",)], W=CS_k)
        P.add("dve", lambda e: e.tensor_tensor(out=r1t[:, :, :], in0=cps3, in1=chb[:, :, :], op=ALU.subtract),
              R=[cpsk, ("chb",)], W=[("r1t",)])
        P.add("dve", lambda e: e.tensor_copy(out=chb[:, :, :], in_=r1t[:, :, :]), R=[("r1t",)], W=[("chb",)])
        P.add("dve", lambda e: e.tensor_copy(out=CS[:, :, 32:38], in_=chb[:, :, :]), R=[("chb",)], W=CS_k)
        P.add("dve", lambda e: e.tensor_tensor(out=r1t[:, :, :], in0=r1t[:, :, :], in1=chb[:, :, :], op=ALU.subtract),
              R=[("r1t",), ("chb",)], W=[("r1t",)])
        P.add("dve", lambda e: e.tensor_copy(out=chb[:, :, :], in_=r1t[:, :, :]), R=[("r1t",)], W=[("chb",)])
        P.add("dve", lambda e: e.tensor_copy(out=CS[:, :, 64:70], in_=chb[:, :, :]), R=[("chb",)], W=CS_k)
        for q_ in range(4):
            ctp, ctpk = bank((1, 2, 3))
            for jj in range(4):
                j = q_ * 4 + jj
                P.add("pe", lambda e, ctp=ctp, jj=jj, j=j: e.transpose(
                    out=ctp[0:96, jj * 128:(jj + 1) * 128], in_=CS[:, j, :], identity=CST["ident"][:, :]),
                    R=CS_k + [("k", "ident")], W=[ctpk])
            act(CT[0:96, q_ * 512:(q_ + 1) * 512], ctp[0:96, :], AF.Copy, R=[ctpk], W=CT_k)
        tap("negc", negc[:, :, :], [("negc",)])
        for h_ in range(6):
            tl_ = 2 * (h_ // 2) + (h_ % 2)
            r0_ = 64 if h_ % 2 == 0 else 0
            for q_ in range(3):
                dma(sc_pad[tl_, r0_ + q_:r0_ + q_ + 1, :], CT[32 * q_ + h_:32 * q_ + h_ + 1, :], R=CT_k, W=[("scp", tl_, "c")])
        load_fox_pair(0, 0)
        for p_ in range(3):
            sl = p_ % 2
            SKA.flush()
            if p_ + 1 < 3:
                load_fox_pair((p_ + 1) % 2, p_ + 1)
            for c in range(NCH):
                cs = slice(c * 512, (c + 1) * 512)
                for e_ in range(2):
                    base = 64 * e_
                    acc, acck = bank((0, 1, 2))
                    attn_map(sl, e_, TL[sl][2 + e_], TL_k[sl][2 + e_], TL[sl][e_], TL_k[sl][e_], c, 2 * p_ + e_, True, acc, acck)
                    def fin(base=base, acc=acc, acck=acck, p_=p_, cs=cs, c=c):
                        O = slice(base, base + 64)
                        Dn = slice(64 - base, 128 - base)
                        rc = slot("rct", 2)
                        recip(rct[O, rc, :], acc[Dn, :], R=[acck], W=[("rct", rc)], on_act=False)
                        P.add("dve", lambda e: e.tensor_tensor(out=hT[O, p_, cs], in0=acc[O, :], in1=rct[O, rc, :], op=ALU.mult),
                              R=[acck, ("rct", rc)], W=[("hT", p_, c)])
                    SKA.push(fin)
        SKA.flush()

    def diff_phase():
        load_diff_pair(0, 0)
        for d_ in range(2):
            sl = d_ % 2
            SKA.flush()
            if d_ == 0:
                load_diff_pair(1, 1)
            for c in range(NCH):
                cs = slice(c * 512, (c + 1) * 512)
                od = slot("t1", 2)
                for e_ in range(2):
                    accs = []
                    for m_ in range(2):
                        base = 64 * e_ + 32 * m_
                        acc, acck = bank((0, 1, 2))
                        attn_map(sl, e_, Kd[sl], Kd_k[sl], TL[sl][2 * e_ + m_], TL_k[sl][2 * e_ + m_], c, 0, False, acc, acck)
                        accs.append((acc, acck))
                    def comb(e_=e_, accs=accs, od=od):
                        O = slice(64 * e_, 64 * e_ + 64)
                        Dn = slice(64 - 64 * e_, 128 - 64 * e_)
                        (a1, a1k), (a2, a2k) = accs
                        ra = slot("rct", 2)
                        rb = slot("rct", 2)
                        ob = slot("t2", 2)
                        recip(rct[O, ra, :], a1[Dn, :], R=[a1k], W=[("rct", ra)], on_act=False)
                        recip(rct[O, rb, :], a2[Dn, :], R=[a2k], W=[("rct", rb)], on_act=False)
                        P.add("dve", lambda e: e.tensor_scalar(out=rct[O, rb, :], in0=rct[O, rb, :], scalar1=der[O, 4:5],
                                                               scalar2=None, op0=ALU.mult),
                              R=[("rct", rb), ("der", 4)], W=[("rct", rb)])
                        P.add("dve", lambda e: e.tensor_tensor(out=t1[O, od, :], in0=a1[O, :], in1=rct[O, ra, :], op=ALU.mult),
                              R=[a1k, ("rct", ra)], W=[("t1", od)])
                        P.add("dve", lambda e: e.tensor_tensor(out=t2[O, ob, :], in0=a2[O, :], in1=rct[O, rb, :], op=ALU.mult),
                              R=[a2k, ("rct", rb)], W=[("t2", ob)])
                        P.add("pool", lambda e: e.tensor_tensor(out=t1[O, od, :], in0=t1[O, od, :], in1=t2[O, ob, :], op=ALU.add),
                              R=[("t1", od), ("t2", ob)], W=[("t1", od)])
                    SKA.push(comb)

                def subln(od=od, d_=d_, cs=cs, c=c):
                    sq_ = slot("sqt", 2)
                    act(sqt[:, sq_, :], t1[:, od, :], AF.Square, R=[("t1", od)], W=[("sqt", sq_)])
                    sb_, sbk = bank((3,))
                    mm(sb_[:, :], CST["bones64"][:, :], sqt[:, sq_, :], True, True, R=[("sqt", sq_), ("k", "bones64")], W=[sbk])
                    r_ = slot("rs", 2)
                    act(rs[:, r_, :], sb_[:, :], AF.Ln, R=[sbk] + KEPS, W=[("rs", r_)], bias=eps_ap, scale=1.0 / 64)
                    act(rs[:, r_, :], rs[:, r_, :], AF.Exp, R=[("rs", r_)], W=[("rs", r_)], scale=-0.5)
                    P.add("dve", lambda e: e.scalar_tensor_tensor(
                        out=hT[:, 3 + d_, cs], in0=t1[:, od, :], scalar=der[:, 3:4], in1=rs[:, r_, :], op0=ALU.mult, op1=ALU.mult),
                        R=[("t1", od), ("der", 3), ("rs", r_)], W=[("hT", 3 + d_, c)])
                SKA.push(subln)
        SKA.flush()

    def dsa_phase():
        Qs = rv(0, 12, BF16).rearrange("p (t n) -> p t n", t=3)
        Qs_k = rk(0, 12)
        SKK, SKK_k = rv(12, 4, BF16), rk(12, 4)
        IKK, IKK_k = rv(16, 4, BF16), rk(16, 4)
        IQ = rv(20, 8, BF16).rearrange("p (t n) -> p t n", t=2)
        IQ_k = rk(20, 8)
        SVa = rv(28, 6, BF16).rearrange("p (j s d) -> p j s d", j=16, s=3)
        SVa_k = rk(28, 6)
        accb, acc_k = rv(36, 8), rk(36, 8)
        MBs = [(rv(44, 4, BF16), rk(44, 4)), (rv(56, 4, BF16), rk(56, 4))]
        PERT, PERT_k = rv(48, 8), rk(48, 8)
        junk = t2[:, :, :].rearrange("p a n -> p (a n)").bitcast(BF16)
        junk_k = [("t2",)]
        for t_ in range(3):
            dma(Qs[:, t_, :], sc_fm[10 + t_], R=[("scfm", 10 + t_)], W=Qs_k)
        dma(SKK, sc_fm[13], R=[("scfm", 13)], W=SKK_k)
        dma(IKK, sc_fm[14], R=[("scfm", 14)], W=IKK_k)
        for t_ in range(2):
            dma(IQ[:, t_, :], sc_fm[15 + t_], R=[("scfm", 15 + t_)], W=IQ_k)
        dma(SVa[:, :, 0, :], sc_sv, R=[("scsv",)], W=SVa_k)
        dma(SVa[:, :, 2, :], sc_sv, R=[("scsv",)], W=SVa_k)
        P.add("pool", lambda e: e.memset(SVa[:, :, 1, :], 1.0), W=SVa_k)
        dma(PERT, pert_d, R=[], W=PERT_k)

        accs_ = [(accb, acc_k), (accB_t[:, :], [("accB",)])]
        junks = [(junk, junk_k), (t1[:, :, :].rearrange("p a n -> p (a n)").bitcast(BF16), [("t1",)])]

        def index_block(i):
            q = i % 2
            acc_, acck_ = accs_[q]
            nk = (i + 1) * 128
            nch = (nk + 511) // 512
            for hh in range(4):
                tl, base = hh // 2, 64 * (hh % 2)
                for m_ in range(nch):
                    w_ = min(512, nk - 512 * m_)
                    dp, dpk = bank((0, 1))
                    mm(dp[:, 0:w_], IQ[base:base + 64, tl, i * 128:(i + 1) * 128], IKK[base:base + 64, 512 * m_:512 * m_ + w_],
                       True, True, R=IQ_k + IKK_k, W=[dpk])
                    act(dp[:, 0:w_], dp[:, 0:w_], AF.Relu, R=[dpk], W=[dpk])
                    src = PERT if hh == 0 else acc_
                    srck = PERT_k if hh == 0 else acck_
                    P.add("dve", lambda e, dp=dp, w_=w_, m_=m_, hh=hh, src=src: e.scalar_tensor_tensor(
                        out=acc_[:, 512 * m_:512 * m_ + w_], in0=dp[:, 0:w_], scalar=IWs[:, i, hh:hh + 1],
                        in1=src[:, 512 * m_:512 * m_ + w_], op0=ALU.mult, op1=ALU.add),
                        R=[dpk, ("IWs",)] + srck, W=acck_)
            P.add("dve", lambda e: e.tensor_reduce(out=bis2[:, q, 0:1], in_=acc_[:, 0:nk], axis=AX.X, op=ALU.max,
                                                   apply_absolute_value=True), R=acck_, W=[("bis2", q, 0)])
            P.add("dve", lambda e: e.tensor_tensor(out=acc_[:, i * 128:(i + 1) * 128], in0=acc_[:, i * 128:(i + 1) * 128],
                                                   in1=CST["cbt"][:, :], op=ALU.add), R=acck_ + [("k", "cbt")], W=acck_)
            P.add("dve", lambda e: e.tensor_scalar(out=STt2[:, q, :], in0=CST["pow2"][:, :], scalar1=bis2[:, q, 0:1], scalar2=None,
                                                   op0=ALU.mult), R=[("bis2", q, 0), ("k", "pow2")], W=[("STt2", q)])
            P.add("dve", lambda e: e.memset(bis2[:, q, 1:2], 0.0), W=[("bis2", q, 1)])

        def bisect_pair(iA, iB, zsteps):
            blocks = [b_ for b_ in (iA, iB) if b_ is not None]
            zsteps = list(zsteps)
            per_it = (len(zsteps) + KBIS - 1) // KBIS
            for k in range(KBIS):
                for _ in range(per_it):
                    if zsteps:
                        zsteps.pop(0)()
                for i in blocks:
                    q = i % 2
                    acc_, acck_ = accs_[q]
                    jk, jkk = junks[q]
                    nk = (i + 1) * 128
                    if q == 0:
                        P.add("dve", lambda e, acc_=acc_, jk=jk, nk=nk, q=q: e.tensor_scalar(
                            out=jk[:, 0:nk], in0=acc_[:, 0:nk], scalar1=bis2[:, q, 1:2], scalar2=None,
                            op0=ALU.is_gt, op1=ALU.add, accum_out=bis2[:, q, 2:3]),
                            R=acck_ + [("bis2", q, 1)], W=jkk + [("bis2", q, 2)])
                    else:
                        act(jk[:, 0:nk], acc_[:, 0:nk], AF.Sign, R=acck_ + [("bis2", q, 1)], W=jkk + [("bis2", q, 2)],
                            bias=bis2[:, q, 1:2], scale=-1.0, accum_out=bis2[:, q, 2:3])
                for i in blocks:
                    q = i % 2
                    nk = (i + 1) * 128
                    if q == 0:
                        P.add("dve", lambda e, k=k, q=q: e.tensor_scalar(
                            out=bis2[:, q, 3:4], in0=bis2[:, q, 2:3], scalar1=TOPK - 0.5, scalar2=STt2[:, q, 32 + k:33 + k],
                            op0=ALU.is_gt, op1=ALU.mult), R=[("bis2", q, 2), ("STt2", q)], W=[("bis2", q, 3)])
                    else:
                        P.add("dve", lambda e, k=k, q=q, nk=nk: e.tensor_scalar(
                            out=bis2[:, q, 3:4], in0=bis2[:, q, 2:3], scalar1=float(nk - 2 * TOPK + 1),
                            scalar2=STt2[:, q, 32 + k:33 + k], op0=ALU.is_lt, op1=ALU.mult),
                            R=[("bis2", q, 2), ("STt2", q)], W=[("bis2", q, 3)])
                    P.add("dve", lambda e, k=k, q=q: e.scalar_tensor_tensor(
                        out=bis2[:, q, 1:2], in0=bis2[:, q, 3:4], scalar=STt2[:, q, k:k + 1], in1=bis2[:, q, 1:2],
                        op0=ALU.subtract, op1=ALU.add),
                        R=[("bis2", q, 3), ("bis2", q, 1), ("STt2", q)], W=[("bis2", q, 1)])
            while zsteps:
                zsteps.pop(0)()
            for i in blocks:
                q = i % 2
                acc_, acck_ = accs_[q]
                MB, MB_k = MBs[q]
                nk = (i + 1) * 128
                P.add("dve", lambda e, acc_=acc_, MB=MB, nk=nk, q=q: e.tensor_scalar(
                    out=MB[:, 0:nk], in0=acc_[:, 0:nk], scalar1=bis2[:, q, 1:2], scalar2=NEG, op0=ALU.is_le, op1=ALU.mult),
                    R=acck_ + [("bis2", q, 1)], W=MB_k)

        SKD = Skew(1)
        accE, accEk = banks[6], ("ps", 6)
        accO, accOk = banks[7], ("ps", 7)

        def attend_steps(i):
            MB, MB_k = MBs[i % 2]
            c = i // 4
            qs_ = slice(i * 128, (i + 1) * 128)
            steps = []

            def step(j):
                ks_ = slice(j * 128, (j + 1) * 128)
                sts = []
                for par in range(2):
                    st, stk = bank((2, 3, 4, 5))
                    pr = slice(64 * par, 64 * par + 64)
                    for hi in range(3):
                        mm(st[:, hi * 128:(hi + 1) * 128], SKK[pr, ks_], Qs[pr, hi, qs_], hi == 0, False,
                           R=SKK_k + Qs_k, W=[stk])
                    mm(st[:, 0:384], MB[:, ks_], CST["irep"][:, 0:384], False, True, R=MB_k + [("k", "irep")], W=[stk])
                    sts.append((st, stk))
                pss = []
                for par in range(2):
                    ps_ = slot("pt", 4)
                    act(pt[:, ps_, 0:384], sts[par][0][:, 0:384], AF.Exp, R=[sts[par][1]], W=[("pt", ps_)])
                    pss.append(ps_)

                def pv():
                    mm(accE[:, 0:384], SVa[:, j, 0:2, :].rearrange("p s d -> p (s d)"), pt[:, pss[0], 0:384], j == 0, j == i,
                       R=SVa_k + [("pt", pss[0])], W=[accEk])
                    mm(accO[:, 0:384], SVa[:, j, 1:3, :].rearrange("p s d -> p (s d)"), pt[:, pss[1], 0:384], j == 0, j == i,
                       R=SVa_k + [("pt", pss[1])], W=[accOk])
                SKD.push(pv)

            def fin():
                for par, (acc, acck) in enumerate(((accE, accEk), (accO, accOk))):
                    O = slice(64 * par, 64 * par + 64)
                    Dn = slice(64 - 64 * par, 128 - 64 * par)
                    rc = slot("rct", 2)
                    recip(rct[O, rc, 0:384], acc[Dn, 0:384], R=[acck], W=[("rct", rc)])
                    P.add("dve", lambda e, O=O, rc=rc, acc=acc: e.tensor_tensor(
                        out=hT[O, 5:8, qs_], in0=acc[O, 0:384].rearrange("p (h n) -> p h n", h=3),
                        in1=rct[O, rc, 0:384].rearrange("p (h n) -> p h n", h=3), op=ALU.mult),
                        R=[acck, ("rct", rc)], W=[("hT", 5, c), ("hT", 6, c), ("hT", 7, c)])

            for j in range(i + 1):
                steps.append(lambda j=j: step(j))
            steps.append(lambda: SKD.push(fin))
            return steps

        index_block(0)
        index_block(1)
        prev_steps = []
        for m_ in range(8):
            bisect_pair(2 * m_, 2 * m_ + 1, prev_steps)
            prev_steps = attend_steps(2 * m_) + attend_steps(2 * m_ + 1)
            if m_ + 1 < 8:
                index_block(2 * m_ + 2)
                index_block(2 * m_ + 3)
        for st_ in prev_steps:
            st_()
        SKD.flush()
        tap("acc15", accB_t[:, :], [("accB",)])
        tap("MB15", MBs[1][0], MBs[1][1])
        tap("bis15", bis2[:, 1, :], [("bis2",)])
        tap("STt", STt2[:, 1, :], [("STt2",)])

    def wout_phase(li):
        Wo = [rv(56, 2, BF16).rearrange("p (k n) -> p k n", k=8), rv(58, 2, BF16).rearrange("p (k n) -> p k n", k=8)]
        Wo_k = [[("R", 14, 0)], [("R", 14, 1)]]
        wsrc = w_out_d[li].rearrange("(kt p) n -> p kt n", p=128)
        load_w(Wo[0], Wo_k[0], wsrc[:, :, 0:128])
        for d_ in range(8):
            sl = d_ % 2
            if d_ + 1 < 8:
                load_w(Wo[1 - sl], Wo_k[1 - sl], wsrc[:, :, (d_ + 1) * 128:(d_ + 2) * 128])
            for c in range(NCH):
                cs = slice(c * 512, (c + 1) * 512)
                bk, bkk = bank((0, 1, 2, 3, 4, 5, 6, 7))
                for kt in range(8):
                    mm(bk[:, :], Wo[sl][:, kt, :], hT[:, kt, cs], kt == 0, kt == 7, R=Wo_k[sl] + [("hT", kt, c)], W=[bkk])
                P.add("dve", lambda e, d_=d_, cs=cs, bk=bk: e.tensor_tensor(
                    out=xT[:, d_, cs], in0=bk[:, :], in1=xT[:, d_, cs], op=ALU.add),
                    R=[bkk, ("xT", d_, c)], W=[("xT", d_, c)])

    for li, labs in enumerate(layer_ids):
        dma(ppt[:, :], pp_d[li], R=[], W=[("ppt",)])
        derive_phase(labs)
        norm_phase(7)
        proj_phase(li)
        if li == 0:
            tap("sc_fm", sc_fm, [("scfm",)])
            tap("sc_v", sc_v, [("scv",)])
            tap("sc_sv", sc_sv, [("scsv",)])
            tap("FFt", FFt[:, :, :], [("FFt",)])
            tap("IWs", IWs[:, :, :], [("IWs",)])
        fox_phase()
        diff_phase()
        dsa_phase()
        if li == 0:
            tap("cat", hT[:, :, :], [("hT",)])
        wout_phase(li)
        if li == 0:
            tap("x1", xT[:, :, :], [("xT",)])
        norm_phase(15)
        ffn_phase(li)

    for kc in range(8):
        dma(y_d[kc * 128:(kc + 1) * 128, :], xT[:, kc, :], R=[("xT", kc)], W=[("y", kc)])

    P.emit(nc, stack)
    stack.close()
    return nc, P.stats


_NC_CACHE = {}


def _get_nc(layer_ids):
    key = tuple(layer_ids)
    if key not in _NC_CACHE:
        _NC_CACHE[key] = build_nc(list(layer_ids))[0]
    return _NC_CACHE[key]


def kernel(**inputs):
    inp = {k: np.asarray(v) for k, v in inputs.items()}
    x = inp["x"].astype(np.float32, copy=False)
    B = x.shape[0]
    cst = _consts()
    base = {}
    for nm, w in _CONST_SHAPES:
        base["c_" + nm] = np.ascontiguousarray(cst[nm], dtype=np.float32)
    base["c_rope"] = cst["rope"]
    base["c_pert"] = cst["pert"]
    layer_ids = list(range(DEPTH))
    nc = _get_nc(layer_ids)
    base["w_in"] = np.ascontiguousarray(inp["w_in"], dtype=np.float32)
    base["w_out"] = np.ascontiguousarray(inp["w_out"], dtype=np.float32)
    base["w_gu"] = np.ascontiguousarray(inp["w_gate_up"], dtype=np.float32)
    base["w_dn"] = np.ascontiguousarray(inp["w_down"], dtype=np.float32)
    base["pp"] = np.stack([_pack_pp(inp, l) for l in layer_ids]).astype(np.float32)
    in_maps = []
    for b in range(B):
        m = dict(base)
        m["xT"] = np.ascontiguousarray(x[b].T)
        in_maps.append(m)
    res = run_bass_kernel_spmd(nc, in_maps, core_ids=list(range(B)))
    out = np.stack([np.asarray(r["yT"]).T for r in res.results]).astype(np.float32)
    return out
```

```python
import math
from contextlib import ExitStack
import numpy as np
import concourse.bass as bass
import concourse.mybir as mybir
from concourse.bass_utils import run_bass_kernel_spmd

F32 = mybir.dt.float32
BF16 = mybir.dt.bfloat16
ALU = mybir.AluOpType
AF = mybir.ActivationFunctionType
AX = mybir.AxisListType

D = 1024
S = 2048
DEPTH = 4
NCH = 4
FFN_H = 2816
INW = 2762
EPS = 1e-6
NEG = -30000.0
KBIS = 14
TOPK = 256
PERT_EPS = 2.0 ** -20
NPP = 157

O_FQ, O_FK, O_FV, O_FF = 0, 384, 768, 1152
O_DQ, O_DK, O_DV = 1158, 1414, 1670
O_SQ, O_SK, O_SV, O_IQ, O_IK, O_IW = 1926, 2310, 2374, 2438, 2694, 2758


class Prog:
    ENGS = ("pe", "act", "dve", "pool", "sp")
    NDMA = 24

    def __init__(self):
        self.ops = []

    def add(self, eng, fn, R=(), W=(), dma=False):
        R = tuple(R)
        W = tuple(W) + tuple(k for k in R if k[0] == "ps" and k not in W)
        R = tuple(k for k in R if k[0] != "ps")
        self.ops.append((eng, fn, R, W, dma))

    def _deps(self):
        lastw = {}
        readers = {}
        desc = {}
        deps_all = []

        def related(k):
            out = []
            for i in range(1, len(k)):
                p = k[:i]
                if p in lastw or p in readers:
                    out.append(p)
            out.extend(desc.get(k, ()))
            return out

        def register(k):
            if k in lastw or k in readers:
                return
            for i in range(1, len(k) + 1):
                desc.setdefault(k[:i], set()).add(k)

        for i, (eng, fn, R, W, dma) in enumerate(self.ops):
            d = set()
            for k in R:
                register(k)
                lastw.setdefault(k, None)
                for r in related(k):
                    w = lastw.get(r)
                    if w is not None:
                        d.add(w)
            for k in W:
                register(k)
                lastw.setdefault(k, None)
                for r in related(k):
                    w = lastw.get(r)
                    if w is not None:
                        d.add(w)
                    d.update(readers.get(r, ()))
            d.discard(i)
            for k in R:
                readers.setdefault(k, []).append(i)
            for k in W:
                lastw[k] = i
                for r in desc.get(k, ()):
                    if r in readers:
                        readers[r] = []
                    lastw[r] = i
            deps_all.append(d)
        return deps_all

    def emit(self, nc, stack):
        ops = self.ops
        deps_all = self._deps()
        n = len(ops)
        sig = [False] * n
        for i, d in enumerate(deps_all):
            e_i = ops[i][0]
            for j in d:
                if ops[j][0] == "pe" and e_i == "pe":
                    continue
                sig[j] = True
        dma_prev = {}
        dma_slot = {}
        nd = 0
        for i, op in enumerate(ops):
            if op[4]:
                s = nd % self.NDMA
                nd += 1
                dma_slot[i] = s
                if s in dma_prev:
                    deps_all[i].add(dma_prev[s])
                dma_prev[s] = i
                sig[i] = True
        EPOCH = 1000
        esems = {e: [] for e in ("pe", "act", "dve", "pool")}
        dsem = [stack.enter_context(nc.semaphore("dsem%d" % k)) for k in range(min(self.NDMA, max(nd, 1)))]
        count = {e: 0 for e in esems}
        dcount = [0] * self.NDMA
        known = {e: {} for e in self.ENGS}
        event = [None] * n
        vc = [None] * n
        plan = {e: [] for e in self.ENGS}
        nwaits = 0
        Z = (0, 0)

        def sem_of(src, ep):
            if isinstance(src, tuple):
                return dsem[src[1]]
            lst = esems[src]
            while len(lst) <= ep:
                lst.append(stack.enter_context(nc.semaphore("sem_%s_%d" % (src, len(lst)))))
            return lst[ep]

        for i, (eng, fn, R, W, dma) in enumerate(ops):
            kn = known[eng]
            wm = {}
            for j in sorted(deps_all[i]):
                if ops[j][0] == "pe" and eng == "pe":
                    continue
                src, val = event[j]
                if kn.get(src, Z) >= val:
                    continue
                if wm.get(src, Z) < val:
                    wm[src] = val
                for s2, v2 in vc[j].items():
                    if kn.get(s2, Z) < v2:
                        kn[s2] = v2
            nwaits += len(wm)
            inc = None
            if sig[i]:
                if dma:
                    s = dma_slot[i]
                    dcount[s] += 16
                    event[i] = (("d", s), (0, dcount[s]))
                    inc = (dsem[s], 16)
                else:
                    ep, cn = divmod(count[eng], EPOCH)
                    count[eng] += 1
                    event[i] = (eng, (ep, cn + 1))
                    inc = (sem_of(eng, ep), 1)
                v = dict(kn)
                v[event[i][0]] = event[i][1]
                vc[i] = v
            plan[eng].append((fn, [(sem_of(src, val[0]), val[1]) for src, val in wm.items()], inc))
        self.stats = dict(n_ops=n, n_waits=nwaits, counts=dict(count), n_dma=nd,
                          n_sems=len(dsem) + sum(len(v) for v in esems.values()))

        block = stack.enter_context(nc.Block())

        def run(engine, items):
            for fn, waits, inc in items:
                for sem, val in waits:
                    engine.wait_ge(sem, val)
                ins = fn(engine)
                if inc is not None:
                    ins.then_inc(inc[0], inc[1])

        @block.tensor
        def _(e):
            run(e, plan["pe"])

        @block.scalar
        def _(e):
            run(e, plan["act"])

        @block.vector
        def _(e):
            run(e, plan["dve"])

        @block.gpsimd
        def _(e):
            run(e, plan["pool"])

        @block.sync
        def _(e):
            run(e, plan["sp"])
            for s in range(len(dsem)):
                if dcount[s] > 0:
                    e.wait_ge(dsem[s], dcount[s])


class Skew:
    def __init__(self, lag):
        self.q = []
        self.lag = lag

    def push(self, fn):
        self.q.append(fn)
        while len(self.q) > self.lag:
            self.q.pop(0)()

    def flush(self):
        while self.q:
            self.q.pop(0)()


def _rope_tab(head_dim, rows_rep):
    rot = head_dim // 4
    half = rot // 2
    inv = (1.0 / (np.float32(500000.0) ** (np.arange(0, rot, 2, dtype=np.float32) / np.float32(rot)))).astype(np.float32)
    ang = np.arange(S, dtype=np.float32)[:, None] * inv[None, :]
    cos = np.cos(ang).astype(np.float32).T
    sin = np.sin(ang).astype(np.float32).T
    C = np.ones((128, S), np.float32)
    Sn = np.zeros((128, S), np.float32)
    for b in range(128 // head_dim):
        o = b * head_dim
        C[o:o + half] = cos
        C[o + half:o + rot] = cos
        Sn[o:o + half] = sin
        Sn[o + half:o + rot] = sin
    P = np.zeros((128, 128), np.float32)
    for b in range(128 // head_dim):
        o = b * head_dim
        for r in range(half):
            P[o + r + half, o + r] = -1.0
            P[o + r, o + r + half] = 1.0
    return C, Sn, P


def _consts():
    c = {}
    eye = np.eye(128, dtype=np.float32)
    idx = np.arange(128)
    c["ident"] = eye
    c["tri"] = -(idx[:, None] <= idx[None, :]).astype(np.float32)
    c["negones"] = -np.ones((128, 128), np.float32)
    c["ones"] = np.ones((128, 128), np.float32)
    b64 = np.zeros((128, 128), np.float32)
    b64[:64, :64] = 1
    b64[64:, 64:] = 1
    c["bones64"] = b64
    b32 = np.zeros((128, 128), np.float32)
    for b in range(4):
        b32[32 * b:32 * b + 32, 32 * b:32 * b + 32] = 1
    c["bones32"] = b32
    sel = np.zeros((128, 6, 128), np.float32)
    for h in range(6):
        sel[h, h, :] = 1
        sel[32 + h, h, :] = 1
        sel[64 + h, h, :] = 1
    c["sel"] = sel.reshape(128, 768)
    c["cb"] = np.where(idx[:, None] <= idx[None, :], 0.0, NEG).astype(np.float32)
    c["cbt"] = np.where(idx[None, :] <= idx[:, None], 0.0, NEG).astype(np.float32)
    c["irep"] = np.tile(eye, (1, 4))
    C64, S64, P64 = _rope_tab(64, 2)
    C32, S32, P32 = _rope_tab(32, 4)
    c["prot64"] = P64
    c["prot32"] = P32
    c["rope"] = np.stack([C32, S32, C64, S64]).astype(np.float32)
    c["pert"] = np.tile((-PERT_EPS * np.arange(S, dtype=np.float32))[None, :], (128, 1)).astype(np.float32)
    p2 = np.zeros((128, 64), np.float32)
    for k in range(32):
        p2[:, k] = 2.0 ** (-k)
        p2[:, 32 + k] = 2.0 ** (1 - k)
    c["pow2"] = p2
    return c


_CONST_SHAPES = [("ident", 128), ("tri", 128), ("negones", 128), ("ones", 128), ("bones64", 128), ("bones32", 128),
                 ("cb", 128), ("cbt", 128), ("irep", 512), ("prot64", 128), ("prot32", 128), ("pow2", 64)]


def _pack_pp(inp, l):
    pp = np.zeros((128, NPP), np.float32)
    p = np.arange(128)
    pp[:, 0] = inp["fox_qn"][l][p % 64]
    pp[:, 1] = inp["fox_kn"][l][p % 64]
    pp[:, 2] = inp["diff_qn"][l][p % 32]
    pp[:, 3] = inp["diff_kn"][l][p % 32]
    pp[:, 4] = inp["dsa_qn"][l][p % 64]
    pp[:, 5] = inp["dsa_kn"][l][p % 64]
    pp[:, 6] = inp["diff_subln"][l][p % 64]
    pp[:, 7:15] = inp["attn_norm"][l].reshape(8, 128).T
    pp[:, 15:23] = inp["ffn_norm"][l].reshape(8, 128).T
    pp[:, 23:29] = inp["fox_fb"][l][None, :]
    pp[:, 29:61] = inp["diff_lq1"][l][None, :]
    pp[:, 61:93] = inp["diff_lk1"][l][None, :]
    pp[:, 93:125] = inp["diff_lq2"][l][None, :]
    pp[:, 125:157] = inp["diff_lk2"][l][None, :]
    return pp


def build_nc(layer_ids, debug=None):
    NL = len(layer_ids)
    nc = bass.Bass("TRN2", target_bir_lowering=False)
    stack = ExitStack()
    P = Prog()

    def dram(name, shape, dt=F32, kind="ExternalInput"):
        return nc.dram_tensor(name, list(shape), dt, kind=kind).ap()

    x_d = dram("xT", [D, S])
    y_d = dram("yT", [D, S], kind="ExternalOutput")
    w_in_d = dram("w_in", [NL, D, INW])
    w_out_d = dram("w_out", [NL, D, D])
    w_gu_d = dram("w_gu", [NL, D, 2 * FFN_H])
    w_dn_d = dram("w_dn", [NL, FFN_H, D])
    pp_d = dram("pp", [NL, 128, NPP])
    cst_d = {nm: dram("c_" + nm, [128, w]) for nm, w in _CONST_SHAPES}
    rope_d = dram("c_rope", [4, 128, S])
    pert_d = dram("c_pert", [128, S])
    sc_fm = dram("sc_fm", [17, 128, S], BF16, kind="Internal")
    sc_v = dram("sc_v", [5, 128, 16, 128], BF16, kind="Internal")
    sc_sv = dram("sc_sv", [128, 16, 64], BF16, kind="Internal")
    sc_pad = dram("sc_pad", [20, 128, S], BF16, kind="Internal")
    dbg = {}
    if debug:
        for nm, shape, dt in debug:
            dbg[nm] = dram("dbg_" + nm, shape, dt, kind="ExternalOutput")

    def sb(name, shape, dt=F32):
        return stack.enter_context(nc.sbuf_tensor(name, list(shape), dt))

    def ps(name, shape, dt=F32):
        return stack.enter_context(nc.psum_tensor(name, list(shape), dt))

    xT = sb("xT_sb", [128, 8, S])
    hT = sb("hT_sb", [128, 8, S], BF16)
    RW = 15 * 1024
    Rg = sb("R_sb", [128, RW])
    stg = sb("stg_sb", [128, 2, 1024])
    banks = [ps("bank%d" % b, [128, 512]) for b in range(8)]

    def rv(off_kb, size_kb, dt=F32):
        a = Rg[:, off_kb * 256:(off_kb + size_kb) * 256]
        if dt == BF16:
            a = a.bitcast(BF16)
        return a

    def rk(off_kb, size_kb):
        return [("R", pg) for pg in range(off_kb // 4, (off_kb + size_kb + 3) // 4)]

    sqt = sb("sqt", [128, 2, 512], BF16)
    rs = sb("rs", [128, 2, 512])
    qn = sb("qn", [128, 3, 512], BF16)
    t1 = sb("t1", [128, 2, 512])
    t2 = sb("t2", [128, 2, 512])
    pt = sb("pt", [128, 4, 512], BF16)
    rct = sb("rct", [128, 2, 512])
    ppt = sb("ppt", [128, NPP])
    der = sb("der", [128, 16])
    FFt = sb("FFt", [128, 16, 6])
    IWs = sb("IWs", [128, 16, 4])
    Lt = sb("Lt", [128, 16, 6])
    negc = sb("negc", [128, 16, 6])
    chb = sb("chb", [128, 16, 6], BF16)
    r1t = sb("r1t", [128, 16, 6])
    scA = sb("scA", [128, 16, 6])
    scB = sb("scB", [128, 16, 6])
    csb = sb("csb", [128, 16, 6])
    lam4 = sb("lam4", [128, 4, 32])
    CST = {}
    for nm, w in _CONST_SHAPES:
        f32c = nm in ("ident", "tri", "negones", "cbt", "pow2")
        CST[nm] = sb("k_" + nm, [128, w], F32 if f32c else BF16)
    epsc = sb("epsc", [128, 2])
    accB_t = sb("accB_t", [128, 2048])
    bis2 = sb("bis2", [128, 2, 4])
    STt2 = sb("STt2", [128, 2, 64])

    bank_rr = {}

    def bank(pool):
        i = bank_rr.get(pool, 0)
        bank_rr[pool] = i + 1
        b = pool[i % len(pool)]
        return banks[b], ("ps", b)

    slot_rr = {}

    def slot(name, nslots):
        i = slot_rr.get(name, 0)
        slot_rr[name] = i + 1
        return i % nslots

    def dma(out, in_, R, W):
        P.add("sp", lambda e: e.dma_start(out=out, in_=in_), R=R, W=W, dma=True)

    def mm(out, lhsT, rhs, start, stop, R, W, **kw):
        P.add("pe", lambda e: e.matmul(out, lhsT=lhsT, rhs=rhs, start=start, stop=stop, **kw), R=R, W=W)

    def act(out, in_, func, R, W, bias=0.0, scale=1.0, accum_out=None):
        if accum_out is None:
            P.add("act", lambda e: e.activation(out=out, in_=in_, func=func, bias=bias, scale=scale), R=R, W=W)
        else:
            P.add("act", lambda e: e.activation(out=out, in_=in_, func=func, bias=bias, scale=scale, accum_out=accum_out),
                  R=R, W=W)

    def recip(out, in_, R, W, on_act=True):
        if on_act:
            act(out, in_, AF.Ln, R=R, W=W)
            act(out, out, AF.Exp, R=W, W=W, scale=-1.0)
        else:
            P.add("dve", lambda e: e.reciprocal(out=out, in_=in_), R=R, W=W)

    def load_w(dst, dst_keys, src_ap):
        s = slot("stg", 2)
        n = 1
        for d_ in src_ap.shape[1:]:
            n *= d_
        sv = stg[:, s, 0:n]
        if len(src_ap.shape) == 3:
            sv = sv.rearrange("p (a b) -> p a b", a=src_ap.shape[1])
        dma(sv, src_ap, R=[], W=[("stg", s)])
        P.add("pool", lambda e: e.tensor_copy(out=dst, in_=sv), R=[("stg", s)], W=dst_keys)

    for kc in range(8):
        dma(xT[:, kc, :], x_d[kc * 128:(kc + 1) * 128, :], R=[], W=[("xT", kc)])
    for nm, w in _CONST_SHAPES:
        if CST[nm].dtype == F32:
            dma(CST[nm][:, :], cst_d[nm][:, :], R=[], W=[("k", nm)])
        else:
            for o in range(0, w, 512):
                ww = min(512, w - o)
                s = slot("t1", 2)
                dma(t1[:, s, 0:ww], cst_d[nm][:, o:o + ww], R=[], W=[("t1", s)])
                P.add("pool", lambda e, nm=nm, o=o, ww=ww, s=s: e.tensor_copy(out=CST[nm][:, o:o + ww], in_=t1[:, s, 0:ww]),
                      R=[("t1", s)], W=[("k", nm)])
    P.add("pool", lambda e: e.memset(epsc[:, 0:1], EPS), W=[("epsc",)])
    P.add("pool", lambda e: e.memset(epsc[:, 1:2], 1.0), W=[("epsc",)])
    KEPS = [("epsc",)]
    P.add("pool", lambda e: e.memset(hT[:, 0, :], 0.0), W=[("hT", 0)])
    P.add("pool", lambda e: e.memset(hT[:, 1, :], 1.0), W=[("hT", 1)])
    for t_ in range(20):
        dma(sc_pad[t_], hT[:, 0, :], R=[("hT", 0)], W=[("scp", t_)])
    P.add("pool", lambda e: e.memset(hT[:, 2, :], -1.0), W=[("hT", 2)])
    for p_ in range(3):
        dma(sc_pad[6 + 2 * p_, 64:67, :], hT[64:67, 1, :], R=[("hT", 1)], W=[("scp", 6 + 2 * p_)])
        dma(sc_pad[7 + 2 * p_, 0:3, :], hT[0:3, 1, :], R=[("hT", 1)], W=[("scp", 7 + 2 * p_)])
        dma(sc_pad[2 * p_, 67:70, :], hT[64:67, 2, :], R=[("hT", 2)], W=[("scp", 2 * p_)])
        dma(sc_pad[2 * p_ + 1, 3:6, :], hT[0:3, 2, :], R=[("hT", 2)], W=[("scp", 2 * p_ + 1)])
    eps_ap = epsc[:, 0:1]
    one_ap = epsc[:, 1:2]

    def norm_phase(gcol0):
        for c in range(NCH):
            cs = slice(c * 512, (c + 1) * 512)
            bk, bkk = bank((0, 1))
            for kc in range(8):
                s = slot("sqt", 2)
                act(sqt[:, s, :], xT[:, kc, cs], AF.Square, R=[("xT", kc, c)], W=[("sqt", s)])
                mm(bk[:, :], CST["ones"][:, :], sqt[:, s, :], kc == 0, kc == 7, R=[("sqt", s), ("k", "ones")], W=[bkk])
            s = slot("rs", 2)
            act(rs[:, s, :], bk[:, :], AF.Ln, R=[bkk] + KEPS, W=[("rs", s)], bias=eps_ap, scale=1.0 / D)
            act(rs[:, s, :], rs[:, s, :], AF.Exp, R=[("rs", s)], W=[("rs", s)], scale=-0.5)
            for kc in range(8):
                P.add("dve", lambda e, kc=kc, cs=cs, s=s: e.scalar_tensor_tensor(
                    out=hT[:, kc, cs], in0=xT[:, kc, cs], scalar=ppt[:, gcol0 + kc:gcol0 + kc + 1], in1=rs[:, s, :],
                    op0=ALU.mult, op1=ALU.mult), R=[("xT", kc, c), ("ppt",), ("rs", s)], W=[("hT", kc, c)])

    def ffn_phase(li):
        groups = [(g * 512, 4) for g in range(5)] + [(2560, 2)]
        Wgu = [rv(0, 16, BF16).rearrange("p (k n) -> p k n", k=8), rv(16, 16, BF16).rearrange("p (k n) -> p k n", k=8)]
        Wgu_k = [rk(0, 16), rk(16, 16)]
        Wd = [rv(32, 8, BF16).rearrange("p (k n) -> p k n", k=4), rv(40, 8, BF16).rearrange("p (k n) -> p k n", k=4)]
        Wd_k = [rk(32, 8), rk(40, 8)]
        actT = rv(48, 8, BF16).rearrange("p (s k n) -> p s k n", s=2, k=4)
        actT_k = [rk(48, 4), rk(52, 4)]
        win = w_gu_d[li].rearrange("(kc p) n -> p kc n", p=128)

        def load_group(gi):
            h0, nt = groups[gi]
            sl = gi % 2
            for t in range(nt):
                load_w(Wgu[sl][:, :, t * 128:(t + 1) * 128], Wgu_k[sl], win[:, :, h0 + t * 128:h0 + (t + 1) * 128])
                load_w(Wgu[sl][:, :, 512 + t * 128:512 + (t + 1) * 128], Wgu_k[sl],
                       win[:, :, FFN_H + h0 + t * 128:FFN_H + h0 + (t + 1) * 128])
                load_w(Wd[sl][:, t, :], Wd_k[sl], w_dn_d[li, h0 + t * 128:h0 + (t + 1) * 128, :])

        load_group(0)
        for gi in range(len(groups)):
            if gi + 1 < len(groups):
                load_group(gi + 1)
            h0, nt = groups[gi]
            sl = gi % 2
            for c in range(NCH):
                cs = slice(c * 512, (c + 1) * 512)
                asl = slot("actT", 2)
                for t in range(nt):
                    gb, gbk = bank((0, 1, 2, 3))
                    ub, ubk = bank((0, 1, 2, 3))
                    for kc in range(8):
                        mm(gb[:, :], Wgu[sl][:, kc, t * 128:(t + 1) * 128], hT[:, kc, cs], kc == 0, kc == 7,
                           R=Wgu_k[sl] + [("hT", kc, c)], W=[gbk])
                    for kc in range(8):
                        mm(ub[:, :], Wgu[sl][:, kc, 512 + t * 128:512 + (t + 1) * 128], hT[:, kc, cs], kc == 0, kc == 7,
                           R=Wgu_k[sl] + [("hT", kc, c)], W=[ubk])
                    s = slot("t1", 2)
                    act(t1[:, s, :], gb[:, :], AF.Silu, R=[gbk], W=[("t1", s)])
                    P.add("dve", lambda e, s=s, ub=ub, asl=asl, t=t: e.tensor_tensor(
                        out=actT[:, asl, t, :], in0=ub[:, :], in1=t1[:, s, :], op=ALU.mult),
                        R=[ubk, ("t1", s)], W=actT_k[asl])
                for d_ in range(8):
                    db, dbk = bank((4, 5, 6, 7))
                    for t in range(nt):
                        mm(db[:, :], Wd[sl][:, t, d_ * 128:(d_ + 1) * 128], actT[:, asl, t, :], t == 0, t == nt - 1,
                           R=Wd_k[sl] + actT_k[asl], W=[dbk])
                    P.add("dve", lambda e, d_=d_, cs=cs, db=db: e.tensor_tensor(
                        out=xT[:, d_, cs], in0=db[:, :], in1=xT[:, d_, cs], op=ALU.add),
                        R=[dbk, ("xT", d_, c)], W=[("xT", d_, c)])


    def apb(base_ap, mid):
        a = base_ap.ap
        return bass.AP(base_ap.tensor, base_ap.offset, [list(a[0]), [0, mid], list(a[-1])])

    def tap(name, src, R):
        if name in dbg:
            dma(dbg[name], src, R=R, W=[("dbg", name)])

    def derive_phase(labs):
        lam_init = 0.8 - 0.6 * math.exp(-0.3 * labs)
        for col, src, mul in ((0, 0, 0.125), (1, 2, 32.0 ** -0.5), (2, 4, 0.125), (3, 6, 1.0 - lam_init)):
            P.add("dve", lambda e, col=col, src=src, mul=mul: e.tensor_scalar(
                out=der[:, col:col + 1], in0=ppt[:, src:src + 1], scalar1=mul, scalar2=None, op0=ALU.mult),
                R=[("ppt",)], W=[("der", col)])
        for q, (a, b) in enumerate(((29, 61), (93, 125))):
            P.add("dve", lambda e, q=q, a=a, b=b: e.tensor_tensor(
                out=lam4[:, q, :], in0=ppt[:, a:a + 32], in1=ppt[:, b:b + 32], op=ALU.mult), R=[("ppt",)], W=[("lam4", q)])
            P.add("dve", lambda e, q=q: e.tensor_reduce(out=der[:, 5 + q:6 + q], in_=lam4[:, q, :], axis=AX.X, op=ALU.add),
                  R=[("lam4", q)], W=[("der", 5 + q)])
            act(der[:, 5 + q:6 + q], der[:, 5 + q:6 + q], AF.Exp, R=[("der", 5 + q)], W=[("der", 5 + q)])
        P.add("dve", lambda e: e.tensor_tensor(out=der[:, 4:5], in0=der[:, 6:7], in1=der[:, 5:6], op=ALU.subtract),
              R=[("der", 5), ("der", 6)], W=[("der", 4)])
        P.add("dve", lambda e: e.tensor_scalar(out=der[:, 4:5], in0=der[:, 4:5], scalar1=-lam_init, scalar2=None, op0=ALU.add),
              R=[("der", 4)], W=[("der", 4)])

    def proj_phase(li):
        win = w_in_d[li].rearrange("(kc p) n -> p kc n", p=128)
        WT = [rv(0, 2, BF16).rearrange("p (k n) -> p k n", k=8), rv(2, 2, BF16).rearrange("p (k n) -> p k n", k=8)]
        WT_k = [[("R", 0, 0)], [("R", 0, 1)]]
        Wv = rv(4, 6, BF16).rearrange("p (k n) -> p k n", k=8)
        Wv_k = rk(4, 6)
        ost = rv(12, 4, BF16).rearrange("p (s n) -> p s n", s=4)
        ost_k = [[("R", 3, s_)] for s_ in range(4)]
        Ct = rv(36, 8)
        St = rv(44, 8)
        Ct_k, St_k = rk(36, 8), rk(44, 8)
        FM = []
        for p_ in range(3):
            FM.append(dict(sc=p_, segs=[(O_FQ + 128 * p_, 128)], norm=(64, "bones64", der, 0, ("der", 0)), rope=None,
                           outs=[(2 * p_, 0, 64), (2 * p_ + 1, 64, 64)]))
        for p_ in range(3):
            FM.append(dict(sc=3 + p_, segs=[(O_FK + 128 * p_, 128)], norm=(64, "bones64", ppt, 1, ("ppt",)), rope=None,
                           outs=[(6 + 2 * p_, 0, 64), (7 + 2 * p_, 64, 64)]))
        for p_ in range(2):
            FM.append(dict(sc=6 + p_, segs=[(O_DQ + 128 * p_, 128)], norm=(32, "bones32", der, 1, ("der", 1)), rope=32,
                           outs=[(12 + 4 * p_ + q_, 32 * q_, 32) for q_ in range(4)]))
        for p_ in range(2):
            FM.append(dict(sc=8 + p_, segs=[(O_DK + 128 * p_, 128)], norm=(32, "bones32", ppt, 3, ("ppt",)), rope=32))
        for p_ in range(3):
            FM.append(dict(sc=10 + p_, segs=[(O_SQ + 128 * p_, 128)], norm=(64, "bones64", der, 2, ("der", 2)), rope=64))
        FM.append(dict(sc=13, segs=[(O_SK, 64), (O_SK, 64)], norm=(64, "bones64", ppt, 5, ("ppt",)), rope=64))
        FM.append(dict(sc=14, segs=[(O_IK, 64), (O_IK, 64)], norm=None, rope=64))
        for p_ in range(2):
            FM.append(dict(sc=15 + p_, segs=[(O_IQ + 128 * p_, 128)], norm=None, rope=64))

        def load_tile(ti):
            sl = ti % 2
            o = 0
            for col0, n_ in FM[ti]["segs"]:
                load_w(WT[sl][:, :, o:o + n_], WT_k[sl], win[:, :, col0:col0 + n_])
                o += n_

        cur_rope = None
        SKB = Skew(1)
        SKC = Skew(2)
        load_tile(0)
        for ti, T in enumerate(FM):
            if ti + 1 < len(FM):
                load_tile(ti + 1)
            sl = ti % 2
            if T["rope"] is not None and T["rope"] != cur_rope:
                SKB.flush()
                SKC.flush()
                cur_rope = T["rope"]
                ro = 0 if cur_rope == 32 else 2
                dma(Ct, rope_d[ro], R=[], W=Ct_k)
                dma(St, rope_d[ro + 1], R=[], W=St_k)
            for c in range(NCH):
                cs = slice(c * 512, (c + 1) * 512)
                pj, pjk = bank((0, 1, 2))
                for kc in range(8):
                    mm(pj[:, :], WT[sl][:, kc, :], hT[:, kc, cs], kc == 0, kc == 7, R=WT_k[sl] + [("hT", kc, c)], W=[pjk])
                sq_ = None
                if T["norm"] is not None:
                    sq_ = slot("sqt", 2)
                    act(sqt[:, sq_, :], pj[:, :], AF.Square, R=[pjk], W=[("sqt", sq_)])

                def stageB(T=T, c=c, cs=cs, pj=pj, pjk=pjk, sq_=sq_):
                    os_ = slot("ost", 4)
                    qs_ = None
                    if T["rope"] is not None:
                        qs_ = slot("qn", 3)
                        tgt, tgtk = qn[:, qs_, :], [("qn", qs_)]
                    else:
                        tgt, tgtk = ost[:, os_, :], ost_k[os_]
                    if T["norm"] is not None:
                        bs_, bname, gt, gcol, gkey = T["norm"]
                        sb_, sbk = bank((3, 4))
                        mm(sb_[:, :], CST[bname][:, :], sqt[:, sq_, :], True, True, R=[("sqt", sq_), ("k", bname)], W=[sbk])
                        r_ = slot("rs", 2)
                        act(rs[:, r_, :], sb_[:, :], AF.Ln, R=[sbk] + KEPS, W=[("rs", r_)], bias=eps_ap, scale=1.0 / bs_)
                        act(rs[:, r_, :], rs[:, r_, :], AF.Exp, R=[("rs", r_)], W=[("rs", r_)], scale=-0.5)
                        P.add("dve", lambda e: e.scalar_tensor_tensor(
                            out=tgt, in0=pj[:, :], scalar=gt[:, gcol:gcol + 1], in1=rs[:, r_, :], op0=ALU.mult, op1=ALU.mult),
                            R=[pjk, gkey, ("rs", r_)], W=tgtk)
                    else:
                        act(tgt, pj[:, :], AF.Copy, R=[pjk], W=tgtk)

                    def stageC():
                        if T["rope"] is not None:
                            pname = "prot%d" % T["rope"]
                            rp, rpk = bank((5, 6))
                            mm(rp[:, :], CST[pname][:, :], qn[:, qs_, :], True, True, R=[("qn", qs_), ("k", pname)], W=[rpk])
                            a_ = slot("t1", 2)
                            b_ = slot("t2", 2)
                            P.add("dve", lambda e: e.tensor_tensor(out=t1[:, a_, :], in0=rp[:, :], in1=St[:, cs], op=ALU.mult),
                                  R=[rpk] + St_k, W=[("t1", a_)])
                            P.add("pool", lambda e: e.tensor_tensor(out=t2[:, b_, :], in0=qn[:, qs_, :], in1=Ct[:, cs], op=ALU.mult),
                                  R=[("qn", qs_)] + Ct_k, W=[("t2", b_)])
                            P.add("dve", lambda e: e.tensor_tensor(out=ost[:, os_, :], in0=t1[:, a_, :], in1=t2[:, b_, :], op=ALU.add),
                                  R=[("t1", a_), ("t2", b_)], W=ost_k[os_])
                        if "outs" in T:
                            for tl_, r0_, nr_ in T["outs"]:
                                dma(sc_pad[tl_, r0_:r0_ + nr_, cs], ost[r0_:r0_ + nr_, os_, :], R=ost_k[os_], W=[("scp", tl_, c)])
                        else:
                            dma(sc_fm[T["sc"], :, cs], ost[:, os_, :], R=ost_k[os_], W=[("scfm", T["sc"], c)])
                    SKC.push(stageC)
                SKB.push(stageB)
        SKB.flush()
        SKC.flush()

        def tm_group(col_segs, ncols, handler):
            o = 0
            for col0, n_ in col_segs:
                load_w(Wv[:, :, o:o + n_], Wv_k, win[:, :, col0:col0 + n_])
                o += n_
            handler()

        def v_pairs(npairs, sc0):
            for a in range(npairs):
                for jg in range(4):
                    tv, tvk = bank((6, 7))
                    for jj in range(4):
                        j = jg * 4 + jj
                        for kc in range(8):
                            mm(tv[:, jj * 128:(jj + 1) * 128], hT[:, kc, j * 128:(j + 1) * 128], Wv[:, kc, a * 128:(a + 1) * 128],
                               kc == 0, kc == 7, R=Wv_k + [("hT", kc, jg)], W=[tvk])
                    os_ = slot("ost", 4)
                    act(ost[:, os_, :], tv[:, :], AF.Copy, R=[tvk], W=ost_k[os_])
                    dma(sc_v[sc0 + a, :, jg * 4:(jg + 1) * 4, :], ost[:, os_, :].rearrange("p (j n) -> p j n", j=4),
                        R=ost_k[os_], W=[("scv", sc0 + a, jg)])

        tm_group([(O_FV, 128), (O_FV + 128, 128), (O_FV + 256, 128)], 384, lambda: v_pairs(3, 0))
        tm_group([(O_DV, 128), (O_DV + 128, 128)], 256, lambda: v_pairs(2, 3))

        def small():
            for jg in range(4):
                tv, tvk = bank((6, 7))
                for jj in range(4):
                    j = jg * 4 + jj
                    for kc in range(8):
                        mm(tv[:, jj * 74:(jj + 1) * 74], hT[:, kc, j * 128:(j + 1) * 128], Wv[:, kc, 0:74],
                           kc == 0, kc == 7, R=Wv_k + [("hT", kc, jg)], W=[tvk])
                tv3 = tv[:, 0:296].rearrange("p (j n) -> p j n", j=4)
                os_ = slot("ost", 4)
                act(ost[:, os_, 0:256].rearrange("p (j n) -> p j n", j=4), tv3[:, :, 0:64], AF.Copy, R=[tvk], W=ost_k[os_])
                dma(sc_sv[:, jg * 4:(jg + 1) * 4, :], ost[:, os_, 0:256].rearrange("p (j n) -> p j n", j=4),
                    R=ost_k[os_], W=[("scsv", jg)])
                P.add("dve", lambda e, jg=jg, tv3=tv3: e.tensor_copy(out=FFt[:, jg * 4:(jg + 1) * 4, :], in_=tv3[:, :, 64:70]),
                      R=[tvk], W=[("FFt", jg)])
                P.add("dve", lambda e, jg=jg, tv3=tv3: e.tensor_scalar(
                    out=IWs[:, jg * 4:(jg + 1) * 4, :], in0=tv3[:, :, 70:74], scalar1=0.0625, scalar2=None, op0=ALU.mult),
                    R=[tvk], W=[("IWs", jg)])

        tm_group([(O_SV, 64), (O_FF, 6), (O_IW, 4)], 74, small)

    TL = [[rv(28 * s_ + 4 * i_, 4, BF16) for i_ in range(4)] for s_ in range(2)]
    TL_k = [[rk(28 * s_ + 4 * i_, 4) for i_ in range(4)] for s_ in range(2)]
    Kd = [rv(28 * s_ + 16, 4, BF16) for s_ in range(2)]
    Kd_k = [rk(28 * s_ + 16, 4) for s_ in range(2)]
    Vp = [rv(28 * s_ + 20, 8, BF16).rearrange("p (j s d) -> p j s d", j=16, s=4) for s_ in range(2)]
    Vp_k = [rk(28 * s_ + 20, 8) for s_ in range(2)]
    identb = CST["irep"][:, 0:128]

    def load_v(sl, vi):
        dma(Vp[sl][:, :, 0, :], sc_v[vi][:, :, 0:64], R=[("scv", vi)], W=Vp_k[sl])
        dma(Vp[sl][:, :, 3, :], sc_v[vi][:, :, 64:128], R=[("scv", vi)], W=Vp_k[sl])
        P.add("pool", lambda e: e.memset(Vp[sl][:, :, 1:3, :], 1.0), W=Vp_k[sl])

    def load_fox_pair(sl, p_):
        for i_, t_ in enumerate((2 * p_, 2 * p_ + 1, 6 + 2 * p_, 7 + 2 * p_)):
            dma(TL[sl][i_], sc_pad[t_], R=[("scp", t_)], W=TL_k[sl][i_])
        load_v(sl, p_)

    def load_diff_pair(sl, d_):
        for i_ in range(4):
            dma(TL[sl][i_], sc_pad[12 + 4 * d_ + i_], R=[("scp", 12 + 4 * d_ + i_)], W=TL_k[sl][i_])
        dma(Kd[sl], sc_fm[8 + d_], R=[("scfm", 8 + d_)], W=Kd_k[sl])
        load_v(sl, 3 + d_)

    SKA = Skew(3)

    def attn_map(sl, e_, lhs, lhsk, rhs, rhsk, c, h, fox, acc, acck):
        nj = 4 * c + 4
        for j in range(nj):
            n0 = max(j * 128, c * 512)
            wN = (c + 1) * 512 - n0
            off = n0 - c * 512
            diag = j >= 4 * c
            st, stk = bank((4, 5, 6, 7))
            mm(st[:, off:off + wN], lhs[:, j * 128:(j + 1) * 128], rhs[:, n0:n0 + wN],
               True, not diag, R=lhsk + rhsk, W=[stk])
            if diag:
                mm(st[:, off:off + 128], identb, CST["cb"][:, :], False, True, R=[("k", "irep"), ("k", "cb")], W=[stk])
            ps_ = slot("pt", 4)
            act(pt[:, ps_, off:off + wN], st[:, off:off + wN], AF.Exp, R=[stk], W=[("pt", ps_)])
            SKA.push(lambda j=j, off=off, wN=wN, ps_=ps_: mm(
                acc[:, off:off + wN], Vp[sl][:, j, 2 * e_:2 * e_ + 2, :].rearrange("p s d -> p (s d)"), pt[:, ps_, off:off + wN],
                j == 0, j == nj - 1, R=Vp_k[sl] + [("pt", ps_)], W=[acck]))

    CT = rv(56, 4, BF16)
    CT_k = rk(56, 4)
    CS = rv(28, 6).rearrange("p (j n) -> p j n", j=16)
    CS_k = rk(28, 6)

    def fox_phase():
        P.add("dve", lambda e: e.tensor_tensor(out=Lt[:, :, :], in0=FFt[:, :, :], in1=apb(ppt[:, 23:29], 16), op=ALU.add),
              R=[("FFt",), ("ppt",)], W=[("Lt",)])
        act(Lt[:, :, :], Lt[:, :, :], AF.Exp, R=[("Lt",)], W=[("Lt",)], scale=-1.0)
        act(Lt[:, :, :], Lt[:, :, :], AF.Ln, R=[("Lt",)] + KEPS, W=[("Lt",)], bias=one_ap)
        Ltf = Lt[:, :, :].rearrange("p j n -> p (j n)")
        cps, cpsk = bank((0,))
        mm(cps[:, 0:96], CST["negones"][:, :], Ltf, True, True, R=[("Lt",), ("k", "negones")], W=[cpsk])
        cp2, cp2k = bank((1,))
        mm(cp2[:, 0:96], CST["tri"][:, :], Ltf, True, True, R=[("Lt",), ("k", "tri")], W=[cp2k])
        cpsv = cps[:, 0:96].rearrange("p (j n) -> p j n", j=16)
        cp23 = cp2[:, 0:96].rearrange("p (j n) -> p j n", j=16)
        P.add("dve", lambda e: e.tensor_copy(out=scA[:, :, :], in_=cpsv), R=[cpsk], W=[("scA",)])
        bufs = [(scA, ("scA",)), (scB, ("scB",))]
        cur = 0
        for sh in (1, 2, 4, 8):
            (a_, ak), (b_, bk) = bufs[cur], bufs[1 - cur]
            P.add("dve", lambda e, a_=a_, b_=b_, sh=sh: e.tensor_copy(out=b_[:, 0:sh, :], in_=a_[:, 0:sh, :]), R=[ak], W=[bk])
            P.add("dve", lambda e, a_=a_, b_=b_, sh=sh: e.tensor_tensor(out=b_[:, sh:16, :], in0=a_[:, sh:16, :],
                                                                   in1=a_[:, 0:16 - sh, :], op=ALU.add), R=[ak], W=[bk])
            cur = 1 - cur
        inc_, inck = bufs[cur]
        P.add("dve", lambda e: e.tensor_copy(out=csb[:, 0:1, :], in_=cp23[:, 0:1, :]), R=[cp2k], W=[("csb",)])
        P.add("dve", lambda e: e.tensor_tensor(out=csb[:, 1:16, :], in0=cp23[:, 1:16, :], in1=inc_[:, 0:15, :], op=ALU.add),
              R=[cp2k, inck], W=[("csb",)])
        cpsk = ("csb",)
        cps3 = csb[:, :, :]
        P.add("dve", lambda e: e.tensor_scalar(out=negc[:, :, :], in0=cps3, scalar1=-1.0, scalar2=None, op0=ALU.mult),
              R=[cpsk], W=[("negc",)])
        P.add("pool", lambda e: e.memset(CS[:, :, :], 0.0), W=CS_k)
        P.add("dve", lambda e: e.tensor_copy(out=chb[:, :, :], in_=cps3), R=[cpsk], W=[("chb",)])
        P.add("dve", lambda e: e.tensor_copy(out=CS[:, :, 0:6], in_=chb[:, :, :]), R=[("chb",)], W=CS_k)
        P.add("dve", lambda e: e.tensor_tensor(out=r1t[:, :, :], in0=cps3, in1=chb[:, :, :], op=ALU.subtract),
              R=[cpsk, ("chb",)], W=[("r1t",)])
        P.add("dve", lambda e: e.tensor_copy(out=chb[:, :, :], in_=r1t[:, :, :]), R=[("r1t",)], W=[("chb",)])
        P.add("dve", lambda e: e.tensor_copy(out=CS[:, :, 32:38], in_=chb[:, :, :]), R=[("chb",)], W=CS_k)
        P.add("dve", lambda e: e.tensor_tensor(out=r1t[:, :, :], in0=r1t[:, :, :], in1=chb[:, :, :], op=ALU.subtract),
              R=[("r1t",), ("chb",)], W=[("r1t",)])
        P.add("dve", lambda e: e.tensor_copy(out=chb[:, :, :], in_=r1t[:, :, :]), R=[("r1t",)], W=[("chb",)])
        P.add("dve", lambda e: e.tensor_copy(out=CS[:, :, 64:70], in_=chb[:, :, :]), R=[("chb",)], W=CS_k)
        for q_ in range(4):
            ctp, ctpk = bank((1, 2, 3))
            for jj in range(4):
                j = q_ * 4 + jj
                P.add("pe", lambda e, ctp=ctp, jj=jj, j=j: e.transpose(
                    out=ctp[0:96, jj * 128:(jj + 1) * 128], in_=CS[:, j, :], identity=CST["ident"][:, :]),
                    R=CS_k + [("k", "ident")], W=[ctpk])
            act(CT[0:96, q_ * 512:(q_ + 1) * 512], ctp[0:96, :], AF.Copy, R=[ctpk], W=CT_k)
        tap("negc", negc[:, :, :], [("negc",)])
        for h_ in range(6):
            tl_ = 2 * (h_ // 2) + (h_ % 2)
            r0_ = 64 if h_ % 2 == 0 else 0
            for q_ in range(3):
                dma(sc_pad[tl_, r0_ + q_:r0_ + q_ + 1, :], CT[32 * q_ + h_:32 * q_ + h_ + 1, :], R=CT_k, W=[("scp", tl_, "c")])
                dma(sc_pad[6 + tl_, r0_ + 3 + q_:r0_ + 4 + q_, :], CT[32 * q_ + h_:32 * q_ + h_ + 1, :], R=CT_k,
                    W=[("scp", 6 + tl_, "c")])
        load_fox_pair(0, 0)
        for p_ in range(3):
            sl = p_ % 2
            SKA.flush()
            if p_ + 1 < 3:
                load_fox_pair((p_ + 1) % 2, p_ + 1)
            for c in range(NCH):
                cs = slice(c * 512, (c + 1) * 512)
                for e_ in range(2):
                    base = 64 * e_
                    acc, acck = bank((0, 1, 2))
                    attn_map(sl, e_, TL[sl][2 + e_], TL_k[sl][2 + e_], TL[sl][e_], TL_k[sl][e_], c, 2 * p_ + e_, True, acc, acck)
                    def fin(base=base, acc=acc, acck=acck, p_=p_, cs=cs, c=c):
                        O = slice(base, base + 64)
                        Dn = slice(64 - base, 128 - base)
                        rc = slot("rct", 2)
                        recip(rct[O, rc, :], acc[Dn, :], R=[acck], W=[("rct", rc)], on_act=False)
                        P.add("dve", lambda e: e.tensor_tensor(out=hT[O, p_, cs], in0=acc[O, :], in1=rct[O, rc, :], op=ALU.mult),
                              R=[acck, ("rct", rc)], W=[("hT", p_, c)])
                    SKA.push(fin)
        SKA.flush()

    def diff_phase():
        load_diff_pair(0, 0)
        for d_ in range(2):
            sl = d_ % 2
            SKA.flush()
            if d_ == 0:
                load_diff_pair(1, 1)
            for c in range(NCH):
                cs = slice(c * 512, (c + 1) * 512)
                od = slot("t1", 2)
                for e_ in range(2):
                    accs = []
                    for m_ in range(2):
                        base = 64 * e_ + 32 * m_
                        acc, acck = bank((0, 1, 2))
                        attn_map(sl, e_, Kd[sl], Kd_k[sl], TL[sl][2 * e_ + m_], TL_k[sl][2 * e_ + m_], c, 0, False, acc, acck)
                        accs.append((acc, acck))
                    def comb(e_=e_, accs=accs, od=od):
                        O = slice(64 * e_, 64 * e_ + 64)
                        Dn = slice(64 - 64 * e_, 128 - 64 * e_)
                        (a1, a1k), (a2, a2k) = accs
                        ra = slot("rct", 2)
                        rb = slot("rct", 2)
                        ob = slot("t2", 2)
                        recip(rct[O, ra, :], a1[Dn, :], R=[a1k], W=[("rct", ra)], on_act=False)
                        recip(rct[O, rb, :], a2[Dn, :], R=[a2k], W=[("rct", rb)], on_act=False)
                        P.add("dve", lambda e: e.tensor_scalar(out=rct[O, rb, :], in0=rct[O, rb, :], scalar1=der[O, 4:5],
                                                               scalar2=None, op0=ALU.mult),
                              R=[("rct", rb), ("der", 4)], W=[("rct", rb)])
                        P.add("dve", lambda e: e.tensor_tensor(out=t1[O, od, :], in0=a1[O, :], in1=rct[O, ra, :], op=ALU.mult),
                              R=[a1k, ("rct", ra)], W=[("t1", od)])
                        P.add("dve", lambda e: e.tensor_tensor(out=t2[O, ob, :], in0=a2[O, :], in1=rct[O, rb, :], op=ALU.mult),
                              R=[a2k, ("rct", rb)], W=[("t2", ob)])
                        P.add("pool", lambda e: e.tensor_tensor(out=t1[O, od, :], in0=t1[O, od, :], in1=t2[O, ob, :], op=ALU.add),
                              R=[("t1", od), ("t2", ob)], W=[("t1", od)])
                    SKA.push(comb)

                def subln(od=od, d_=d_, cs=cs, c=c):
                    sq_ = slot("sqt", 2)
                    act(sqt[:, sq_, :], t1[:, od, :], AF.Square, R=[("t1", od)], W=[("sqt", sq_)])
                    sb_, sbk = bank((3,))
                    mm(sb_[:, :], CST["bones64"][:, :], sqt[:, sq_, :], True, True, R=[("sqt", sq_), ("k", "bones64")], W=[sbk])
                    r_ = slot("rs", 2)
                    act(rs[:, r_, :], sb_[:, :], AF.Ln, R=[sbk] + KEPS, W=[("rs", r_)], bias=eps_ap, scale=1.0 / 64)
                    act(rs[:, r_, :], rs[:, r_, :], AF.Exp, R=[("rs", r_)], W=[("rs", r_)], scale=-0.5)
                    P.add("dve", lambda e: e.scalar_tensor_tensor(
                        out=hT[:, 3 + d_, cs], in0=t1[:, od, :], scalar=der[:, 3:4], in1=rs[:, r_, :], op0=ALU.mult, op1=ALU.mult),
                        R=[("t1", od), ("der", 3), ("rs", r_)], W=[("hT", 3 + d_, c)])
                SKA.push(subln)
        SKA.flush()

    def dsa_phase():
        Qs = rv(0, 12, BF16).rearrange("p (t n) -> p t n", t=3)
        Qs_k = rk(0, 12)
        SKK, SKK_k = rv(12, 4, BF16), rk(12, 4)
        IKK, IKK_k = rv(16, 4, BF16), rk(16, 4)
        IQ = rv(20, 8, BF16).rearrange("p (t n) -> p t n", t=2)
        IQ_k = rk(20, 8)
        SVa = rv(28, 6, BF16).rearrange("p (j s d) -> p j s d", j=16, s=3)
        SVa_k = rk(28, 6)
        accb, acc_k = rv(36, 8), rk(36, 8)
        MBs = [(rv(44, 4, BF16), rk(44, 4)), (rv(56, 4, BF16), rk(56, 4))]
        PERT, PERT_k = rv(48, 8), rk(48, 8)
        junk = t2[:, :, :].rearrange("p a n -> p (a n)").bitcast(BF16)
        junk_k = [("t2",)]
        for t_ in range(3):
            dma(Qs[:, t_, :], sc_fm[10 + t_], R=[("scfm", 10 + t_)], W=Qs_k)
        dma(SKK, sc_fm[13], R=[("scfm", 13)], W=SKK_k)
        dma(IKK, sc_fm[14], R=[("scfm", 14)], W=IKK_k)
        for t_ in range(2):
            dma(IQ[:, t_, :], sc_fm[15 + t_], R=[("scfm", 15 + t_)], W=IQ_k)
        dma(SVa[:, :, 0, :], sc_sv, R=[("scsv",)], W=SVa_k)
        dma(SVa[:, :, 2, :], sc_sv, R=[("scsv",)], W=SVa_k)
        P.add("pool", lambda e: e.memset(SVa[:, :, 1, :], 1.0), W=SVa_k)
        dma(PERT, pert_d, R=[], W=PERT_k)

        accs_ = [(accb, acc_k), (accB_t[:, :], [("accB",)])]
        junks = [(junk, junk_k), (t1[:, :, :].rearrange("p a n -> p (a n)").bitcast(BF16), [("t1",)])]

        def index_block(i):
            q = i % 2
            acc_, acck_ = accs_[q]
            nk = (i + 1) * 128
            nch = (nk + 511) // 512
            for hh in range(4):
                tl, base = hh // 2, 64 * (hh % 2)
                for m_ in range(nch):
                    w_ = min(512, nk - 512 * m_)
                    dp, dpk = bank((0, 1))
                    mm(dp[:, 0:w_], IQ[base:base + 64, tl, i * 128:(i + 1) * 128], IKK[base:base + 64, 512 * m_:512 * m_ + w_],
                       True, True, R=IQ_k + IKK_k, W=[dpk])
                    act(dp[:, 0:w_], dp[:, 0:w_], AF.Relu, R=[dpk], W=[dpk])
                    src = PERT if hh == 0 else acc_
                    srck = PERT_k if hh == 0 else acck_
                    P.add("dve", lambda e, dp=dp, w_=w_, m_=m_, hh=hh, src=src: e.scalar_tensor_tensor(
                        out=acc_[:, 512 * m_:512 * m_ + w_], in0=dp[:, 0:w_], scalar=IWs[:, i, hh:hh + 1],
                        in1=src[:, 512 * m_:512 * m_ + w_], op0=ALU.mult, op1=ALU.add),
                        R=[dpk, ("IWs",)] + srck, W=acck_)
            P.add("dve", lambda e: e.tensor_reduce(out=bis2[:, q, 0:1], in_=acc_[:, 0:nk], axis=AX.X, op=ALU.max,
                                                   apply_absolute_value=True), R=acck_, W=[("bis2", q, 0)])
            P.add("dve", lambda e: e.tensor_tensor(out=acc_[:, i * 128:(i + 1) * 128], in0=acc_[:, i * 128:(i + 1) * 128],
                                                   in1=CST["cbt"][:, :], op=ALU.add), R=acck_ + [("k", "cbt")], W=acck_)
            P.add("dve", lambda e: e.tensor_scalar(out=STt2[:, q, :], in0=CST["pow2"][:, :], scalar1=bis2[:, q, 0:1], scalar2=None,
                                                   op0=ALU.mult), R=[("bis2", q, 0), ("k", "pow2")], W=[("STt2", q)])
            P.add("dve", lambda e: e.memset(bis2[:, q, 1:2], 0.0), W=[("bis2", q, 1)])

        def bisect_pair(iA, iB, zsteps):
            blocks = [b_ for b_ in (iA, iB) if b_ is not None]
            zsteps = list(zsteps)
            per_it = (len(zsteps) + KBIS - 1) // KBIS
            for k in range(KBIS):
                for _ in range(per_it):
                    if zsteps:
                        zsteps.pop(0)()
                for i in blocks:
                    q = i % 2
                    acc_, acck_ = accs_[q]
                    jk, jkk = junks[q]
                    nk = (i + 1) * 128
                    if q == 0:
                        P.add("dve", lambda e, acc_=acc_, jk=jk, nk=nk, q=q: e.tensor_scalar(
                            out=jk[:, 0:nk], in0=acc_[:, 0:nk], scalar1=bis2[:, q, 1:2], scalar2=None,
                            op0=ALU.is_gt, op1=ALU.add, accum_out=bis2[:, q, 2:3]),
                            R=acck_ + [("bis2", q, 1)], W=jkk + [("bis2", q, 2)])
                    else:
                        act(jk[:, 0:nk], acc_[:, 0:nk], AF.Sign, R=acck_ + [("bis2", q, 1)], W=jkk + [("bis2", q, 2)],
                            bias=bis2[:, q, 1:2], scale=-1.0, accum_out=bis2[:, q, 2:3])
                for i in blocks:
                    q = i % 2
                    nk = (i + 1) * 128
                    if q == 0:
                        P.add("dve", lambda e, k=k, q=q: e.tensor_scalar(
                            out=bis2[:, q, 3:4], in0=bis2[:, q, 2:3], scalar1=TOPK - 0.5, scalar2=STt2[:, q, 32 + k:33 + k],
                            op0=ALU.is_gt, op1=ALU.mult), R=[("bis2", q, 2), ("STt2", q)], W=[("bis2", q, 3)])
                    else:
                        P.add("dve", lambda e, k=k, q=q, nk=nk: e.tensor_scalar(
                            out=bis2[:, q, 3:4], in0=bis2[:, q, 2:3], scalar1=float(nk - 2 * TOPK + 1),
                            scalar2=STt2[:, q, 32 + k:33 + k], op0=ALU.is_lt, op1=ALU.mult),
                            R=[("bis2", q, 2), ("STt2", q)], W=[("bis2", q, 3)])
                    P.add("dve", lambda e, k=k, q=q: e.scalar_tensor_tensor(
                        out=bis2[:, q, 1:2], in0=bis2[:, q, 3:4], scalar=STt2[:, q, k:k + 1], in1=bis2[:, q, 1:2],
                        op0=ALU.subtract, op1=ALU.add),
                        R=[("bis2", q, 3), ("bis2", q, 1), ("STt2", q)], W=[("bis2", q, 1)])
            while zsteps:
                zsteps.pop(0)()
            for i in blocks:
                q = i % 2
                acc_, acck_ = accs_[q]
                MB, MB_k = MBs[q]
                nk = (i + 1) * 128
                P.add("dve", lambda e, acc_=acc_, MB=MB, nk=nk, q=q: e.tensor_scalar(
                    out=MB[:, 0:nk], in0=acc_[:, 0:nk], scalar1=bis2[:, q, 1:2], scalar2=NEG, op0=ALU.is_le, op1=ALU.mult),
                    R=acck_ + [("bis2", q, 1)], W=MB_k)

        SKD = Skew(1)
        accE, accEk = banks[6], ("ps", 6)
        accO, accOk = banks[7], ("ps", 7)

        def attend_steps(i):
            MB, MB_k = MBs[i % 2]
            c = i // 4
            qs_ = slice(i * 128, (i + 1) * 128)
            steps = []

            def step(j):
                ks_ = slice(j * 128, (j + 1) * 128)
                sts = []
                for par in range(2):
                    st, stk = bank((2, 3, 4, 5))
                    pr = slice(64 * par, 64 * par + 64)
                    for hi in range(3):
                        mm(st[:, hi * 128:(hi + 1) * 128], SKK[pr, ks_], Qs[pr, hi, qs_], hi == 0, False,
                           R=SKK_k + Qs_k, W=[stk])
                    mm(st[:, 0:384], MB[:, ks_], CST["irep"][:, 0:384], False, True, R=MB_k + [("k", "irep")], W=[stk])
                    sts.append((st, stk))
                pss = []
                for par in range(2):
                    ps_ = slot("pt", 4)
                    act(pt[:, ps_, 0:384], sts[par][0][:, 0:384], AF.Exp, R=[sts[par][1]], W=[("pt", ps_)])
                    pss.append(ps_)

                def pv():
                    mm(accE[:, 0:384], SVa[:, j, 0:2, :].rearrange("p s d -> p (s d)"), pt[:, pss[0], 0:384], j == 0, j == i,
                       R=SVa_k + [("pt", pss[0])], W=[accEk])
                    mm(accO[:, 0:384], SVa[:, j, 1:3, :].rearrange("p s d -> p (s d)"), pt[:, pss[1], 0:384], j == 0, j == i,
                       R=SVa_k + [("pt", pss[1])], W=[accOk])
                SKD.push(pv)

            def fin():
                for par, (acc, acck) in enumerate(((accE, accEk), (accO, accOk))):
                    O = slice(64 * par, 64 * par + 64)
                    Dn = slice(64 - 64 * par, 128 - 64 * par)
                    rc = slot("rct", 2)
                    recip(rct[O, rc, 0:384], acc[Dn, 0:384], R=[acck], W=[("rct", rc)])
                    P.add("dve", lambda e, O=O, rc=rc, acc=acc: e.tensor_tensor(
                        out=hT[O, 5:8, qs_], in0=acc[O, 0:384].rearrange("p (h n) -> p h n", h=3),
                        in1=rct[O, rc, 0:384].rearrange("p (h n) -> p h n", h=3), op=ALU.mult),
                        R=[acck, ("rct", rc)], W=[("hT", 5, c), ("hT", 6, c), ("hT", 7, c)])

            for j in range(i + 1):
                steps.append(lambda j=j: step(j))
            steps.append(lambda: SKD.push(fin))
            return steps

        index_block(0)
        index_block(1)
        prev_steps = []
        for m_ in range(8):
            bisect_pair(2 * m_, 2 * m_ + 1, prev_steps)
            prev_steps = attend_steps(2 * m_) + attend_steps(2 * m_ + 1)
            if m_ + 1 < 8:
                index_block(2 * m_ + 2)
                index_block(2 * m_ + 3)
        for st_ in prev_steps:
            st_()
        SKD.flush()
        tap("acc15", accB_t[:, :], [("accB",)])
        tap("MB15", MBs[1][0], MBs[1][1])
        tap("bis15", bis2[:, 1, :], [("bis2",)])
        tap("STt", STt2[:, 1, :], [("STt2",)])

    def wout_phase(li):
        Wo = [rv(56, 2, BF16).rearrange("p (k n) -> p k n", k=8), rv(58, 2, BF16).rearrange("p (k n) -> p k n", k=8)]
        Wo_k = [[("R", 14, 0)], [("R", 14, 1)]]
        wsrc = w_out_d[li].rearrange("(kt p) n -> p kt n", p=128)
        load_w(Wo[0], Wo_k[0], wsrc[:, :, 0:128])
        for d_ in range(8):
            sl = d_ % 2
            if d_ + 1 < 8:
                load_w(Wo[1 - sl], Wo_k[1 - sl], wsrc[:, :, (d_ + 1) * 128:(d_ + 2) * 128])
            for c in range(NCH):
                cs = slice(c * 512, (c + 1) * 512)
                bk, bkk = bank((0, 1, 2, 3, 4, 5, 6, 7))
                for kt in range(8):
                    mm(bk[:, :], Wo[sl][:, kt, :], hT[:, kt, cs], kt == 0, kt == 7, R=Wo_k[sl] + [("hT", kt, c)], W=[bkk])
                P.add("dve", lambda e, d_=d_, cs=cs, bk=bk: e.tensor_tensor(
                    out=xT[:, d_, cs], in0=bk[:, :], in1=xT[:, d_, cs], op=ALU.add),
                    R=[bkk, ("xT", d_, c)], W=[("xT", d_, c)])

    for li, labs in enumerate(layer_ids):
        dma(ppt[:, :], pp_d[li], R=[], W=[("ppt",)])
        derive_phase(labs)
        norm_phase(7)
        proj_phase(li)
        if li == 0:
            tap("sc_fm", sc_fm, [("scfm",)])
            tap("sc_v", sc_v, [("scv",)])
            tap("sc_sv", sc_sv, [("scsv",)])
            tap("FFt", FFt[:, :, :], [("FFt",)])
            tap("IWs", IWs[:, :, :], [("IWs",)])
        fox_phase()
        diff_phase()
        dsa_phase()
        if li == 0:
            tap("cat", hT[:, :, :], [("hT",)])
        wout_phase(li)
        if li == 0:
            tap("x1", xT[:, :, :], [("xT",)])
        norm_phase(15)
        ffn_phase(li)

    for kc in range(8):
        dma(y_d[kc * 128:(kc + 1) * 128, :], xT[:, kc, :], R=[("xT", kc)], W=[("y", kc)])

    P.emit(nc, stack)
    stack.close()
    return nc, P.stats


_NC_CACHE = {}


def _get_nc(layer_ids):
    key = tuple(layer_ids)
    if key not in _NC_CACHE:
        _NC_CACHE[key] = build_nc(list(layer_ids))[0]
    return _NC_CACHE[key]


def kernel(**inputs):
    inp = {k: np.asarray(v) for k, v in inputs.items()}
    x = inp["x"].astype(np.float32, copy=False)
    B = x.shape[0]
    cst = _consts()
    base = {}
    for nm, w in _CONST_SHAPES:
        base["c_" + nm] = np.ascontiguousarray(cst[nm], dtype=np.float32)
    base["c_rope"] = cst["rope"]
    base["c_pert"] = cst["pert"]
    layer_ids = list(range(DEPTH))
    nc = _get_nc(layer_ids)
    base["w_in"] = np.ascontiguousarray(inp["w_in"], dtype=np.float32)
    base["w_out"] = np.ascontiguousarray(inp["w_out"], dtype=np.float32)
    base["w_gu"] = np.ascontiguousarray(inp["w_gate_up"], dtype=np.float32)
    base["w_dn"] = np.ascontiguousarray(inp["w_down"], dtype=np.float32)
    base["pp"] = np.stack([_pack_pp(inp, l) for l in layer_ids]).astype(np.float32)
    in_maps = []
    for b in range(B):
        m = dict(base)
        m["xT"] = np.ascontiguousarray(x[b].T)
        in_maps.append(m)
    res = run_bass_kernel_spmd(nc, in_maps, core_ids=list(range(B)))
    out = np.stack([np.asarray(r["yT"]).T for r in res.results]).astype(np.float32)
    return out
```

```python
import math
from contextlib import ExitStack
import numpy as np
import concourse.bass as bass
import concourse.mybir as mybir
from concourse.bass_utils import run_bass_kernel_spmd

F32 = mybir.dt.float32
BF16 = mybir.dt.bfloat16
ALU = mybir.AluOpType
AF = mybir.ActivationFunctionType
AX = mybir.AxisListType

D = 1024
S = 2048
DEPTH = 4
NCH = 4
FFN_H = 2816
INW = 2762
EPS = 1e-6
NEG = -30000.0
KBIS = 14
TOPK = 256
PERT_EPS = 2.0 ** -20
NPP = 157

O_FQ, O_FK, O_FV, O_FF = 0, 384, 768, 1152
O_DQ, O_DK, O_DV = 1158, 1414, 1670
O_SQ, O_SK, O_SV, O_IQ, O_IK, O_IW = 1926, 2310, 2374, 2438, 2694, 2758


class Prog:
    ENGS = ("pe", "act", "dve", "pool", "sp")
    NDMA = 24

    def __init__(self):
        self.ops = []

    def add(self, eng, fn, R=(), W=(), dma=False):
        R = tuple(R)
        W = tuple(W) + tuple(k for k in R if k[0] == "ps" and k not in W)
        R = tuple(k for k in R if k[0] != "ps")
        self.ops.append((eng, fn, R, W, dma))

    def _deps(self):
        lastw = {}
        readers = {}
        desc = {}
        deps_all = []

        def related(k):
            out = []
            for i in range(1, len(k)):
                p = k[:i]
                if p in lastw or p in readers:
                    out.append(p)
            out.extend(desc.get(k, ()))
            return out

        def register(k):
            if k in lastw or k in readers:
                return
            for i in range(1, len(k) + 1):
                desc.setdefault(k[:i], set()).add(k)

        for i, (eng, fn, R, W, dma) in enumerate(self.ops):
            d = set()
            for k in R:
                register(k)
                lastw.setdefault(k, None)
                for r in related(k):
                    w = lastw.get(r)
                    if w is not None:
                        d.add(w)
            for k in W:
                register(k)
                lastw.setdefault(k, None)
                for r in related(k):
                    w = lastw.get(r)
                    if w is not None:
                        d.add(w)
                    d.update(readers.get(r, ()))
            d.discard(i)
            for k in R:
                readers.setdefault(k, []).append(i)
            for k in W:
                lastw[k] = i
                for r in desc.get(k, ()):
                    if r in readers:
                        readers[r] = []
                    lastw[r] = i
            deps_all.append(d)
        return deps_all

    def emit(self, nc, stack):
        ops = self.ops
        deps_all = self._deps()
        n = len(ops)
        sig = [False] * n
        for i, d in enumerate(deps_all):
            e_i = ops[i][0]
            for j in d:
                if ops[j][0] == "pe" and e_i == "pe":
                    continue
                sig[j] = True
        dma_prev = {}
        dma_slot = {}
        nd = 0
        for i, op in enumerate(ops):
            if op[4]:
                s = nd % self.NDMA
                nd += 1
                dma_slot[i] = s
                if s in dma_prev:
                    deps_all[i].add(dma_prev[s])
                dma_prev[s] = i
                sig[i] = True
        EPOCH = 1000
        esems = {e: [] for e in ("pe", "act", "dve", "pool")}
        dsem = [stack.enter_context(nc.semaphore("dsem%d" % k)) for k in range(min(self.NDMA, max(nd, 1)))]
        count = {e: 0 for e in esems}
        dcount = [0] * self.NDMA
        known = {e: {} for e in self.ENGS}
        event = [None] * n
        vc = [None] * n
        plan = {e: [] for e in self.ENGS}
        nwaits = 0
        Z = (0, 0)

        def sem_of(src, ep):
            if isinstance(src, tuple):
                return dsem[src[1]]
            lst = esems[src]
            while len(lst) <= ep:
                lst.append(stack.enter_context(nc.semaphore("sem_%s_%d" % (src, len(lst)))))
            return lst[ep]

        for i, (eng, fn, R, W, dma) in enumerate(ops):
            kn = known[eng]
            wm = {}
            for j in sorted(deps_all[i]):
                if ops[j][0] == "pe" and eng == "pe":
                    continue
                src, val = event[j]
                if kn.get(src, Z) >= val:
                    continue
                if wm.get(src, Z) < val:
                    wm[src] = val
                for s2, v2 in vc[j].items():
                    if kn.get(s2, Z) < v2:
                        kn[s2] = v2
            nwaits += len(wm)
            inc = None
            if sig[i]:
                if dma:
                    s = dma_slot[i]
                    dcount[s] += 16
                    event[i] = (("d", s), (0, dcount[s]))
                    inc = (dsem[s], 16)
                else:
                    ep, cn = divmod(count[eng], EPOCH)
                    count[eng] += 1
                    event[i] = (eng, (ep, cn + 1))
                    inc = (sem_of(eng, ep), 1)
                v = dict(kn)
                v[event[i][0]] = event[i][1]
                vc[i] = v
            plan[eng].append((fn, [(sem_of(src, val[0]), val[1]) for src, val in wm.items()], inc))
        self.stats = dict(n_ops=n, n_waits=nwaits, counts=dict(count), n_dma=nd,
                          n_sems=len(dsem) + sum(len(v) for v in esems.values()))

        block = stack.enter_context(nc.Block())

        def run(engine, items):
            for fn, waits, inc in items:
                for sem, val in waits:
                    engine.wait_ge(sem, val)
                ins = fn(engine)
                if inc is not None:
                    ins.then_inc(inc[0], inc[1])

        @block.tensor
        def _(e):
            run(e, plan["pe"])

        @block.scalar
        def _(e):
            run(e, plan["act"])

        @block.vector
        def _(e):
            run(e, plan["dve"])

        @block.gpsimd
        def _(e):
            run(e, plan["pool"])

        @block.sync
        def _(e):
            run(e, plan["sp"])
            for s in range(len(dsem)):
                if dcount[s] > 0:
                    e.wait_ge(dsem[s], dcount[s])


class Skew:
    def __init__(self, lag):
        self.q = []
        self.lag = lag

    def push(self, fn):
        self.q.append(fn)
        while len(self.q) > self.lag:
            self.q.pop(0)()

    def flush(self):
        while self.q:
            self.q.pop(0)()


def _rope_tab(head_dim, rows_rep):
    rot = head_dim // 4
    half = rot // 2
    inv = (1.0 / (np.float32(500000.0) ** (np.arange(0, rot, 2, dtype=np.float32) / np.float32(rot)))).astype(np.float32)
    ang = np.arange(S, dtype=np.float32)[:, None] * inv[None, :]
    cos = np.cos(ang).astype(np.float32).T
    sin = np.sin(ang).astype(np.float32).T
    C = np.ones((128, S), np.float32)
    Sn = np.zeros((128, S), np.float32)
    for b in range(128 // head_dim):
        o = b * head_dim
        C[o:o + half] = cos
        C[o + half:o + rot] = cos
        Sn[o:o + half] = sin
        Sn[o + half:o + rot] = sin
    P = np.zeros((128, 128), np.float32)
    for b in range(128 // head_dim):
        o = b * head_dim
        for r in range(half):
            P[o + r + half, o + r] = -1.0
            P[o + r, o + r + half] = 1.0
    return C, Sn, P


def _consts():
    c = {}
    eye = np.eye(128, dtype=np.float32)
    idx = np.arange(128)
    c["ident"] = eye
    c["tri"] = -(idx[:, None] <= idx[None, :]).astype(np.float32)
    c["negones"] = -np.ones((128, 128), np.float32)
    c["ones"] = np.ones((128, 128), np.float32)
    b64 = np.zeros((128, 128), np.float32)
    b64[:64, :64] = 1
    b64[64:, 64:] = 1
    c["bones64"] = b64
    b32 = np.zeros((128, 128), np.float32)
    for b in range(4):
        b32[32 * b:32 * b + 32, 32 * b:32 * b + 32] = 1
    c["bones32"] = b32
    sel = np.zeros((128, 6, 128), np.float32)
    for h in range(6):
        sel[h, h, :] = 1
        sel[32 + h, h, :] = 1
        sel[64 + h, h, :] = 1
    c["sel"] = sel.reshape(128, 768)
    c["cb"] = np.where(idx[:, None] <= idx[None, :], 0.0, NEG).astype(np.float32)
    c["cbt"] = np.where(idx[None, :] <= idx[:, None], 0.0, NEG).astype(np.float32)
    c["irep"] = np.tile(eye, (1, 4))
    C64, S64, P64 = _rope_tab(64, 2)
    C32, S32, P32 = _rope_tab(32, 4)
    c["prot64"] = P64
    c["prot32"] = P32
    c["rope"] = np.stack([C32, S32, C64, S64]).astype(np.float32)
    c["pert"] = np.tile((-PERT_EPS * np.arange(S, dtype=np.float32))[None, :], (128, 1)).astype(np.float32)
    p2 = np.zeros((128, 64), np.float32)
    for k in range(32):
        p2[:, k] = 2.0 ** (-k)
        p2[:, 32 + k] = 2.0 ** (1 - k)
    c["pow2"] = p2
    return c


_CONST_SHAPES = [("ident", 128), ("tri", 128), ("negones", 128), ("ones", 128), ("bones64", 128), ("bones32", 128),
                 ("cb", 128), ("cbt", 128), ("irep", 512), ("prot64", 128), ("prot32", 128), ("pow2", 64)]


def _pack_pp(inp, l):
    pp = np.zeros((128, NPP), np.float32)
    p = np.arange(128)
    pp[:, 0] = inp["fox_qn"][l][p % 64]
    pp[:, 1] = inp["fox_kn"][l][p % 64]
    pp[:, 2] = inp["diff_qn"][l][p % 32]
    pp[:, 3] = inp["diff_kn"][l][p % 32]
    pp[:, 4] = inp["dsa_qn"][l][p % 64]
    pp[:, 5] = inp["dsa_kn"][l][p % 64]
    pp[:, 6] = inp["diff_subln"][l][p % 64]
    pp[:, 7:15] = inp["attn_norm"][l].reshape(8, 128).T
    pp[:, 15:23] = inp["ffn_norm"][l].reshape(8, 128).T
    pp[:, 23:29] = inp["fox_fb"][l][None, :]
    pp[:, 29:61] = inp["diff_lq1"][l][None, :]
    pp[:, 61:93] = inp["diff_lk1"][l][None, :]
    pp[:, 93:125] = inp["diff_lq2"][l][None, :]
    pp[:, 125:157] = inp["diff_lk2"][l][None, :]
    return pp


def build_nc(layer_ids, debug=None):
    NL = len(layer_ids)
    nc = bass.Bass("TRN2", target_bir_lowering=False)
    stack = ExitStack()
    P = Prog()

    def dram(name, shape, dt=F32, kind="ExternalInput"):
        return nc.dram_tensor(name, list(shape), dt, kind=kind).ap()

    x_d = dram("xT", [D, S])
    y_d = dram("yT", [D, S], kind="ExternalOutput")
    w_in_d = dram("w_in", [NL, D, INW])
    w_out_d = dram("w_out", [NL, D, D])
    w_gu_d = dram("w_gu", [NL, D, 2 * FFN_H])
    w_dn_d = dram("w_dn", [NL, FFN_H, D])
    pp_d = dram("pp", [NL, 128, NPP])
    cst_d = {nm: dram("c_" + nm, [128, w]) for nm, w in _CONST_SHAPES}
    rope_d = dram("c_rope", [4, 128, S])
    pert_d = dram("c_pert", [128, S])
    sc_fm = dram("sc_fm", [17, 128, S], BF16, kind="Internal")
    sc_v = dram("sc_v", [5, 128, 16, 128], BF16, kind="Internal")
    sc_sv = dram("sc_sv", [128, 16, 64], BF16, kind="Internal")
    sc_pad = dram("sc_pad", [20, 128, S], BF16, kind="Internal")
    dbg = {}
    if debug:
        for nm, shape, dt in debug:
            dbg[nm] = dram("dbg_" + nm, shape, dt, kind="ExternalOutput")

    def sb(name, shape, dt=F32):
        return stack.enter_context(nc.sbuf_tensor(name, list(shape), dt))

    def ps(name, shape, dt=F32):
        return stack.enter_context(nc.psum_tensor(name, list(shape), dt))

    xT = sb("xT_sb", [128, 8, S])
    hT = sb("hT_sb", [128, 8, S], BF16)
    RW = 15 * 1024
    Rg = sb("R_sb", [128, RW])
    stg = sb("stg_sb", [128, 2, 1024])
    banks = [ps("bank%d" % b, [128, 512]) for b in range(8)]

    def rv(off_kb, size_kb, dt=F32):
        a = Rg[:, off_kb * 256:(off_kb + size_kb) * 256]
        if dt == BF16:
            a = a.bitcast(BF16)
        return a

    def rk(off_kb, size_kb):
        return [("R", pg) for pg in range(off_kb // 4, (off_kb + size_kb + 3) // 4)]

    sqt = sb("sqt", [128, 2, 512], BF16)
    rs = sb("rs", [128, 2, 512])
    qn = sb("qn", [128, 3, 512], BF16)
    t1 = sb("t1", [128, 2, 512])
    t2 = sb("t2", [128, 2, 512])
    pt = sb("pt", [128, 4, 512], BF16)
    rct = sb("rct", [128, 2, 512])
    ppt = sb("ppt", [128, NPP])
    der = sb("der", [128, 16])
    FFt = sb("FFt", [128, 16, 6])
    IWs = sb("IWs", [128, 16, 4])
    Lt = sb("Lt", [128, 16, 6])
    negc = sb("negc", [128, 16, 6])
    chb = sb("chb", [128, 16, 6], BF16)
    r1t = sb("r1t", [128, 16, 6])
    scA = sb("scA", [128, 16, 6])
    scB = sb("scB", [128, 16, 6])
    csb = sb("csb", [128, 16, 6])
    lam4 = sb("lam4", [128, 4, 32])
    CST = {}
    for nm, w in _CONST_SHAPES:
        f32c = nm in ("ident", "tri", "negones", "cbt", "pow2")
        CST[nm] = sb("k_" + nm, [128, w], F32 if f32c else BF16)
    epsc = sb("epsc", [128, 2])
    accB_t = sb("accB_t", [128, 2048])
    bis2 = sb("bis2", [128, 2, 4])
    STt2 = sb("STt2", [128, 2, 64])

    bank_rr = {}

    def bank(pool):
        i = bank_rr.get(pool, 0)
        bank_rr[pool] = i + 1
        b = pool[i % len(pool)]
        return banks[b], ("ps", b)

    slot_rr = {}

    def slot(name, nslots):
        i = slot_rr.get(name, 0)
        slot_rr[name] = i + 1
        return i % nslots

    def dma(out, in_, R, W):
        P.add("sp", lambda e: e.dma_start(out=out, in_=in_), R=R, W=W, dma=True)

    def mm(out, lhsT, rhs, start, stop, R, W, **kw):
        P.add("pe", lambda e: e.matmul(out, lhsT=lhsT, rhs=rhs, start=start, stop=stop, **kw), R=R, W=W)

    def act(out, in_, func, R, W, bias=0.0, scale=1.0, accum_out=None):
        if accum_out is None:
            P.add("act", lambda e: e.activation(out=out, in_=in_, func=func, bias=bias, scale=scale), R=R, W=W)
        else:
            P.add("act", lambda e: e.activation(out=out, in_=in_, func=func, bias=bias, scale=scale, accum_out=accum_out),
                  R=R, W=W)

    def recip(out, in_, R, W, on_act=True):
        if on_act:
            act(out, in_, AF.Ln, R=R, W=W)
            act(out, out, AF.Exp, R=W, W=W, scale=-1.0)
        else:
            P.add("dve", lambda e: e.reciprocal(out=out, in_=in_), R=R, W=W)

    def load_w(dst, dst_keys, src_ap):
        s = slot("stg", 2)
        n = 1
        for d_ in src_ap.shape[1:]:
            n *= d_
        sv = stg[:, s, 0:n]
        if len(src_ap.shape) == 3:
            sv = sv.rearrange("p (a b) -> p a b", a=src_ap.shape[1])
        dma(sv, src_ap, R=[], W=[("stg", s)])
        P.add("pool", lambda e: e.tensor_copy(out=dst, in_=sv), R=[("stg", s)], W=dst_keys)

    for nm, w in _CONST_SHAPES:
        if CST[nm].dtype == F32:
            dma(CST[nm][:, :], cst_d[nm][:, :], R=[], W=[("k", nm)])
        else:
            for o in range(0, w, 512):
                ww = min(512, w - o)
                s = slot("t1", 2)
                dma(t1[:, s, 0:ww], cst_d[nm][:, o:o + ww], R=[], W=[("t1", s)])
                P.add("pool", lambda e, nm=nm, o=o, ww=ww, s=s: e.tensor_copy(out=CST[nm][:, o:o + ww], in_=t1[:, s, 0:ww]),
                      R=[("t1", s)], W=[("k", nm)])
    P.add("pool", lambda e: e.memset(epsc[:, 0:1], EPS), W=[("epsc",)])
    P.add("pool", lambda e: e.memset(epsc[:, 1:2], 1.0), W=[("epsc",)])
    KEPS = [("epsc",)]
    for c in range(NCH):
        for kc in range(8):
            dma(xT[:, kc, c * 512:(c + 1) * 512], x_d[kc * 128:(kc + 1) * 128, c * 512:(c + 1) * 512], R=[], W=[("xT", kc, c)])
    P.add("pool", lambda e: e.memset(hT[:, 0, :], 0.0), W=[("hT", 0)])
    P.add("pool", lambda e: e.memset(hT[:, 1, :], 1.0), W=[("hT", 1)])
    for t_ in range(20):
        dma(sc_pad[t_], hT[:, 0, :], R=[("hT", 0)], W=[("scp", t_)])
    P.add("pool", lambda e: e.memset(hT[:, 2, :], -1.0), W=[("hT", 2)])
    for p_ in range(3):
        dma(sc_pad[6 + 2 * p_, 64:67, :], hT[64:67, 1, :], R=[("hT", 1)], W=[("scp", 6 + 2 * p_)])
        dma(sc_pad[7 + 2 * p_, 0:3, :], hT[0:3, 1, :], R=[("hT", 1)], W=[("scp", 7 + 2 * p_)])
        dma(sc_pad[2 * p_, 67:70, :], hT[64:67, 2, :], R=[("hT", 2)], W=[("scp", 2 * p_)])
        dma(sc_pad[2 * p_ + 1, 3:6, :], hT[0:3, 2, :], R=[("hT", 2)], W=[("scp", 2 * p_ + 1)])
    eps_ap = epsc[:, 0:1]
    one_ap = epsc[:, 1:2]

    def norm_phase(gcol0):
        for c in range(NCH):
            cs = slice(c * 512, (c + 1) * 512)
            bk, bkk = bank((0, 1))
            for kc in range(8):
                s = slot("sqt", 2)
                act(sqt[:, s, :], xT[:, kc, cs], AF.Square, R=[("xT", kc, c)], W=[("sqt", s)])
                mm(bk[:, :], CST["ones"][:, :], sqt[:, s, :], kc == 0, kc == 7, R=[("sqt", s), ("k", "ones")], W=[bkk])
            s = slot("rs", 2)
            act(rs[:, s, :], bk[:, :], AF.Ln, R=[bkk] + KEPS, W=[("rs", s)], bias=eps_ap, scale=1.0 / D)
            act(rs[:, s, :], rs[:, s, :], AF.Exp, R=[("rs", s)], W=[("rs", s)], scale=-0.5)
            for kc in range(8):
                P.add("dve", lambda e, kc=kc, cs=cs, s=s: e.scalar_tensor_tensor(
                    out=hT[:, kc, cs], in0=xT[:, kc, cs], scalar=ppt[:, gcol0 + kc:gcol0 + kc + 1], in1=rs[:, s, :],
                    op0=ALU.mult, op1=ALU.mult), R=[("xT", kc, c), ("ppt",), ("rs", s)], W=[("hT", kc, c)])

    def ffn_phase(li):
        groups = [(g * 512, 4) for g in range(5)] + [(2560, 2)]
        Wgu = [rv(0, 16, BF16).rearrange("p (k n) -> p k n", k=8), rv(16, 16, BF16).rearrange("p (k n) -> p k n", k=8)]
        Wgu_k = [rk(0, 16), rk(16, 16)]
        Wd = [rv(32, 8, BF16).rearrange("p (k n) -> p k n", k=4), rv(40, 8, BF16).rearrange("p (k n) -> p k n", k=4)]
        Wd_k = [rk(32, 8), rk(40, 8)]
        actT = rv(48, 8, BF16).rearrange("p (s k n) -> p s k n", s=2, k=4)
        actT_k = [rk(48, 4), rk(52, 4)]
        win = w_gu_d[li].rearrange("(kc p) n -> p kc n", p=128)

        def load_group(gi):
            h0, nt = groups[gi]
            sl = gi % 2
            for t in range(nt):
                load_w(Wgu[sl][:, :, t * 128:(t + 1) * 128], Wgu_k[sl], win[:, :, h0 + t * 128:h0 + (t + 1) * 128])
                load_w(Wgu[sl][:, :, 512 + t * 128:512 + (t + 1) * 128], Wgu_k[sl],
                       win[:, :, FFN_H + h0 + t * 128:FFN_H + h0 + (t + 1) * 128])
                load_w(Wd[sl][:, t, :], Wd_k[sl], w_dn_d[li, h0 + t * 128:h0 + (t + 1) * 128, :])

        load_group(0)
        for gi in range(len(groups)):
            if gi + 1 < len(groups):
                load_group(gi + 1)
            h0, nt = groups[gi]
            sl = gi % 2
            for c in range(NCH):
                cs = slice(c * 512, (c + 1) * 512)
                asl = slot("actT", 2)
                for t in range(nt):
                    gb, gbk = bank((0, 1, 2, 3))
                    ub, ubk = bank((0, 1, 2, 3))
                    for kc in range(8):
                        mm(gb[:, :], Wgu[sl][:, kc, t * 128:(t + 1) * 128], hT[:, kc, cs], kc == 0, kc == 7,
                           R=Wgu_k[sl] + [("hT", kc, c)], W=[gbk])
                    for kc in range(8):
                        mm(ub[:, :], Wgu[sl][:, kc, 512 + t * 128:512 + (t + 1) * 128], hT[:, kc, cs], kc == 0, kc == 7,
                           R=Wgu_k[sl] + [("hT", kc, c)], W=[ubk])
                    s = slot("t1", 2)
                    act(t1[:, s, :], gb[:, :], AF.Silu, R=[gbk], W=[("t1", s)])
                    P.add("dve", lambda e, s=s, ub=ub, asl=asl, t=t: e.tensor_tensor(
                        out=actT[:, asl, t, :], in0=ub[:, :], in1=t1[:, s, :], op=ALU.mult),
                        R=[ubk, ("t1", s)], W=actT_k[asl])
                for d_ in range(8):
                    db, dbk = bank((4, 5, 6, 7))
                    for t in range(nt):
                        mm(db[:, :], Wd[sl][:, t, d_ * 128:(d_ + 1) * 128], actT[:, asl, t, :], t == 0, t == nt - 1,
                           R=Wd_k[sl] + actT_k[asl], W=[dbk])
                    P.add("dve", lambda e, d_=d_, cs=cs, db=db: e.tensor_tensor(
                        out=xT[:, d_, cs], in0=db[:, :], in1=xT[:, d_, cs], op=ALU.add),
                        R=[dbk, ("xT", d_, c)], W=[("xT", d_, c)])


    def apb(base_ap, mid):
        a = base_ap.ap
        return bass.AP(base_ap.tensor, base_ap.offset, [list(a[0]), [0, mid], list(a[-1])])

    def tap(name, src, R):
        if name in dbg:
            dma(dbg[name], src, R=R, W=[("dbg", name)])

    def derive_phase(labs):
        lam_init = 0.8 - 0.6 * math.exp(-0.3 * labs)
        for col, src, mul in ((0, 0, 0.125), (1, 2, 32.0 ** -0.5), (2, 4, 0.125), (3, 6, 1.0 - lam_init)):
            P.add("dve", lambda e, col=col, src=src, mul=mul: e.tensor_scalar(
                out=der[:, col:col + 1], in0=ppt[:, src:src + 1], scalar1=mul, scalar2=None, op0=ALU.mult),
                R=[("ppt",)], W=[("der", col)])
        for q, (a, b) in enumerate(((29, 61), (93, 125))):
            P.add("dve", lambda e, q=q, a=a, b=b: e.tensor_tensor(
                out=lam4[:, q, :], in0=ppt[:, a:a + 32], in1=ppt[:, b:b + 32], op=ALU.mult), R=[("ppt",)], W=[("lam4", q)])
            P.add("dve", lambda e, q=q: e.tensor_reduce(out=der[:, 5 + q:6 + q], in_=lam4[:, q, :], axis=AX.X, op=ALU.add),
                  R=[("lam4", q)], W=[("der", 5 + q)])
            act(der[:, 5 + q:6 + q], der[:, 5 + q:6 + q], AF.Exp, R=[("der", 5 + q)], W=[("der", 5 + q)])
        P.add("dve", lambda e: e.tensor_tensor(out=der[:, 4:5], in0=der[:, 6:7], in1=der[:, 5:6], op=ALU.subtract),
              R=[("der", 5), ("der", 6)], W=[("der", 4)])
        P.add("dve", lambda e: e.tensor_scalar(out=der[:, 4:5], in0=der[:, 4:5], scalar1=-lam_init, scalar2=None, op0=ALU.add),
              R=[("der", 4)], W=[("der", 4)])

    def proj_phase(li):
        win = w_in_d[li].rearrange("(kc p) n -> p kc n", p=128)
        WT = [rv(0, 2, BF16).rearrange("p (k n) -> p k n", k=8), rv(2, 2, BF16).rearrange("p (k n) -> p k n", k=8)]
        WT_k = [[("R", 0, 0)], [("R", 0, 1)]]
        Wv = rv(4, 6, BF16).rearrange("p (k n) -> p k n", k=8)
        Wv_k = rk(4, 6)
        ost = rv(12, 4, BF16).rearrange("p (s n) -> p s n", s=4)
        ost_k = [[("R", 3, s_)] for s_ in range(4)]
        Ct = rv(36, 8)
        St = rv(44, 8)
        Ct_k, St_k = rk(36, 8), rk(44, 8)
        FM = []
        for p_ in range(3):
            FM.append(dict(sc=p_, segs=[(O_FQ + 128 * p_, 128)], norm=(64, "bones64", der, 0, ("der", 0)), rope=None,
                           outs=[(2 * p_, 0, 64), (2 * p_ + 1, 64, 64)]))
        for p_ in range(3):
            FM.append(dict(sc=3 + p_, segs=[(O_FK + 128 * p_, 128)], norm=(64, "bones64", ppt, 1, ("ppt",)), rope=None,
                           outs=[(6 + 2 * p_, 0, 64), (7 + 2 * p_, 64, 64)]))
        for p_ in range(2):
            FM.append(dict(sc=6 + p_, segs=[(O_DQ + 128 * p_, 128)], norm=(32, "bones32", der, 1, ("der", 1)), rope=32,
                           outs=[(12 + 4 * p_ + q_, 32 * q_, 32) for q_ in range(4)]))
        for p_ in range(2):
            FM.append(dict(sc=8 + p_, segs=[(O_DK + 128 * p_, 128)], norm=(32, "bones32", ppt, 3, ("ppt",)), rope=32))
        for p_ in range(3):
            FM.append(dict(sc=10 + p_, segs=[(O_SQ + 128 * p_, 128)], norm=(64, "bones64", der, 2, ("der", 2)), rope=64))
        FM.append(dict(sc=13, segs=[(O_SK, 64), (O_SK, 64)], norm=(64, "bones64", ppt, 5, ("ppt",)), rope=64))
        FM.append(dict(sc=14, segs=[(O_IK, 64), (O_IK, 64)], norm=None, rope=64))
        for p_ in range(2):
            FM.append(dict(sc=15 + p_, segs=[(O_IQ + 128 * p_, 128)], norm=None, rope=64))

        def load_tile(ti):
            sl = ti % 2
            o = 0
            for col0, n_ in FM[ti]["segs"]:
                load_w(WT[sl][:, :, o:o + n_], WT_k[sl], win[:, :, col0:col0 + n_])
                o += n_

        cur_rope = None
        SKB = Skew(1)
        SKC = Skew(2)
        load_tile(0)
        for ti, T in enumerate(FM):
            if ti + 1 < len(FM):
                load_tile(ti + 1)
            sl = ti % 2
            if T["rope"] is not None and T["rope"] != cur_rope:
                SKB.flush()
                SKC.flush()
                cur_rope = T["rope"]
                ro = 0 if cur_rope == 32 else 2
                dma(Ct, rope_d[ro], R=[], W=Ct_k)
                dma(St, rope_d[ro + 1], R=[], W=St_k)
            for c in range(NCH):
                cs = slice(c * 512, (c + 1) * 512)
                pj, pjk = bank((0, 1, 2))
                for kc in range(8):
                    mm(pj[:, :], WT[sl][:, kc, :], hT[:, kc, cs], kc == 0, kc == 7, R=WT_k[sl] + [("hT", kc, c)], W=[pjk])
                sq_ = None
                if T["norm"] is not None:
                    sq_ = slot("sqt", 2)
                    act(sqt[:, sq_, :], pj[:, :], AF.Square, R=[pjk], W=[("sqt", sq_)])

                def stageB(T=T, c=c, cs=cs, pj=pj, pjk=pjk, sq_=sq_):
                    os_ = slot("ost", 4)
                    qs_ = None
                    if T["rope"] is not None:
                        qs_ = slot("qn", 3)
                        tgt, tgtk = qn[:, qs_, :], [("qn", qs_)]
                    else:
                        tgt, tgtk = ost[:, os_, :], ost_k[os_]
                    if T["norm"] is not None:
                        bs_, bname, gt, gcol, gkey = T["norm"]
                        sb_, sbk = bank((3, 4))
                        mm(sb_[:, :], CST[bname][:, :], sqt[:, sq_, :], True, True, R=[("sqt", sq_), ("k", bname)], W=[sbk])
                        r_ = slot("rs", 2)
                        act(rs[:, r_, :], sb_[:, :], AF.Ln, R=[sbk] + KEPS, W=[("rs", r_)], bias=eps_ap, scale=1.0 / bs_)
                        act(rs[:, r_, :], rs[:, r_, :], AF.Exp, R=[("rs", r_)], W=[("rs", r_)], scale=-0.5)
                        P.add("dve", lambda e: e.scalar_tensor_tensor(
                            out=tgt, in0=pj[:, :], scalar=gt[:, gcol:gcol + 1], in1=rs[:, r_, :], op0=ALU.mult, op1=ALU.mult),
                            R=[pjk, gkey, ("rs", r_)], W=tgtk)
                    else:
                        act(tgt, pj[:, :], AF.Copy, R=[pjk], W=tgtk)

                    def stageC():
                        if T["rope"] is not None:
                            pname = "prot%d" % T["rope"]
                            rp, rpk = bank((5, 6))
                            mm(rp[:, :], CST[pname][:, :], qn[:, qs_, :], True, True, R=[("qn", qs_), ("k", pname)], W=[rpk])
                            a_ = slot("t1", 2)
                            b_ = slot("t2", 2)
                            P.add("dve", lambda e: e.tensor_tensor(out=t1[:, a_, :], in0=rp[:, :], in1=St[:, cs], op=ALU.mult),
                                  R=[rpk] + St_k, W=[("t1", a_)])
                            P.add("pool", lambda e: e.tensor_tensor(out=t2[:, b_, :], in0=qn[:, qs_, :], in1=Ct[:, cs], op=ALU.mult),
                                  R=[("qn", qs_)] + Ct_k, W=[("t2", b_)])
                            P.add("dve", lambda e: e.tensor_tensor(out=ost[:, os_, :], in0=t1[:, a_, :], in1=t2[:, b_, :], op=ALU.add),
                                  R=[("t1", a_), ("t2", b_)], W=ost_k[os_])
                        if "outs" in T:
                            for tl_, r0_, nr_ in T["outs"]:
                                dma(sc_pad[tl_, r0_:r0_ + nr_, cs], ost[r0_:r0_ + nr_, os_, :], R=ost_k[os_], W=[("scp", tl_, c)])
                        else:
                            dma(sc_fm[T["sc"], :, cs], ost[:, os_, :], R=ost_k[os_], W=[("scfm", T["sc"], c)])
                    SKC.push(stageC)
                SKB.push(stageB)
        SKB.flush()
        SKC.flush()

        def tm_group(col_segs, ncols, handler):
            o = 0
            for col0, n_ in col_segs:
                load_w(Wv[:, :, o:o + n_], Wv_k, win[:, :, col0:col0 + n_])
                o += n_
            handler()

        def v_pairs(npairs, sc0):
            for a in range(npairs):
                for jg in range(4):
                    tv, tvk = bank((6, 7))
                    for jj in range(4):
                        j = jg * 4 + jj
                        for kc in range(8):
                            mm(tv[:, jj * 128:(jj + 1) * 128], hT[:, kc, j * 128:(j + 1) * 128], Wv[:, kc, a * 128:(a + 1) * 128],
                               kc == 0, kc == 7, R=Wv_k + [("hT", kc, jg)], W=[tvk])
                    os_ = slot("ost", 4)
                    act(ost[:, os_, :], tv[:, :], AF.Copy, R=[tvk], W=ost_k[os_])
                    dma(sc_v[sc0 + a, :, jg * 4:(jg + 1) * 4, :], ost[:, os_, :].rearrange("p (j n) -> p j n", j=4),
                        R=ost_k[os_], W=[("scv", sc0 + a, jg)])

        tm_group([(O_FV, 128), (O_FV + 128, 128), (O_FV + 256, 128)], 384, lambda: v_pairs(3, 0))
        tm_group([(O_DV, 128), (O_DV + 128, 128)], 256, lambda: v_pairs(2, 3))

        def small():
            for jg in range(4):
                tv, tvk = bank((6, 7))
                for jj in range(4):
                    j = jg * 4 + jj
                    for kc in range(8):
                        mm(tv[:, jj * 74:(jj + 1) * 74], hT[:, kc, j * 128:(j + 1) * 128], Wv[:, kc, 0:74],
                           kc == 0, kc == 7, R=Wv_k + [("hT", kc, jg)], W=[tvk])
                tv3 = tv[:, 0:296].rearrange("p (j n) -> p j n", j=4)
                os_ = slot("ost", 4)
                act(ost[:, os_, 0:256].rearrange("p (j n) -> p j n", j=4), tv3[:, :, 0:64], AF.Copy, R=[tvk], W=ost_k[os_])
                dma(sc_sv[:, jg * 4:(jg + 1) * 4, :], ost[:, os_, 0:256].rearrange("p (j n) -> p j n", j=4),
                    R=ost_k[os_], W=[("scsv", jg)])
                P.add("dve", lambda e, jg=jg, tv3=tv3: e.tensor_copy(out=FFt[:, jg * 4:(jg + 1) * 4, :], in_=tv3[:, :, 64:70]),
                      R=[tvk], W=[("FFt", jg)])
                P.add("dve", lambda e, jg=jg, tv3=tv3: e.tensor_scalar(
                    out=IWs[:, jg * 4:(jg + 1) * 4, :], in0=tv3[:, :, 70:74], scalar1=0.0625, scalar2=None, op0=ALU.mult),
                    R=[tvk], W=[("IWs", jg)])

        tm_group([(O_SV, 64), (O_FF, 6), (O_IW, 4)], 74, small)

    TL = [[rv(28 * s_ + 4 * i_, 4, BF16) for i_ in range(4)] for s_ in range(2)]
    TL_k = [[rk(28 * s_ + 4 * i_, 4) for i_ in range(4)] for s_ in range(2)]
    Kd = [rv(28 * s_ + 16, 4, BF16) for s_ in range(2)]
    Kd_k = [rk(28 * s_ + 16, 4) for s_ in range(2)]
    Vp = [rv(28 * s_ + 20, 8, BF16).rearrange("p (j s d) -> p j s d", j=16, s=4) for s_ in range(2)]
    Vp_k = [rk(28 * s_ + 20, 8) for s_ in range(2)]
    identb = CST["irep"][:, 0:128]

    def load_v(sl, vi):
        dma(Vp[sl][:, :, 0, :], sc_v[vi][:, :, 0:64], R=[("scv", vi)], W=Vp_k[sl])
        dma(Vp[sl][:, :, 3, :], sc_v[vi][:, :, 64:128], R=[("scv", vi)], W=Vp_k[sl])
        P.add("pool", lambda e: e.memset(Vp[sl][:, :, 1:3, :], 1.0), W=Vp_k[sl])

    def load_fox_pair(sl, p_):
        for i_, t_ in enumerate((2 * p_, 2 * p_ + 1, 6 + 2 * p_, 7 + 2 * p_)):
            dma(TL[sl][i_], sc_pad[t_], R=[("scp", t_)], W=TL_k[sl][i_])
        load_v(sl, p_)

    def load_diff_pair(sl, d_):
        for i_ in range(4):
            dma(TL[sl][i_], sc_pad[12 + 4 * d_ + i_], R=[("scp", 12 + 4 * d_ + i_)], W=TL_k[sl][i_])
        dma(Kd[sl], sc_fm[8 + d_], R=[("scfm", 8 + d_)], W=Kd_k[sl])
        load_v(sl, 3 + d_)

    SKA = Skew(3)

    def attn_map(sl, e_, lhs, lhsk, rhs, rhsk, c, h, fox, acc, acck):
        nj = 4 * c + 4
        for j in range(nj):
            n0 = max(j * 128, c * 512)
            wN = (c + 1) * 512 - n0
            off = n0 - c * 512
            diag = j >= 4 * c
            st, stk = bank((4, 5, 6, 7))
            mm(st[:, off:off + wN], lhs[:, j * 128:(j + 1) * 128], rhs[:, n0:n0 + wN],
               True, not diag, R=lhsk + rhsk, W=[stk])
            if diag:
                mm(st[:, off:off + 128], identb, CST["cb"][:, :], False, True, R=[("k", "irep"), ("k", "cb")], W=[stk])
            ps_ = slot("pt", 4)
            act(pt[:, ps_, off:off + wN], st[:, off:off + wN], AF.Exp, R=[stk], W=[("pt", ps_)])
            SKA.push(lambda j=j, off=off, wN=wN, ps_=ps_: mm(
                acc[:, off:off + wN], Vp[sl][:, j, 2 * e_:2 * e_ + 2, :].rearrange("p s d -> p (s d)"), pt[:, ps_, off:off + wN],
                j == 0, j == nj - 1, R=Vp_k[sl] + [("pt", ps_)], W=[acck]))

    CT = rv(56, 4, BF16)
    CT_k = rk(56, 4)
    CS = rv(28, 6).rearrange("p (j n) -> p j n", j=16)
    CS_k = rk(28, 6)

    def fox_phase():
        P.add("dve", lambda e: e.tensor_tensor(out=Lt[:, :, :], in0=FFt[:, :, :], in1=apb(ppt[:, 23:29], 16), op=ALU.add),
              R=[("FFt",), ("ppt",)], W=[("Lt",)])
        act(Lt[:, :, :], Lt[:, :, :], AF.Exp, R=[("Lt",)], W=[("Lt",)], scale=-1.0)
        act(Lt[:, :, :], Lt[:, :, :], AF.Ln, R=[("Lt",)] + KEPS, W=[("Lt",)], bias=one_ap)
        Ltf = Lt[:, :, :].rearrange("p j n -> p (j n)")
        cps, cpsk = bank((0,))
        mm(cps[:, 0:96], CST["negones"][:, :], Ltf, True, True, R=[("Lt",), ("k", "negones")], W=[cpsk])
        cp2, cp2k = bank((1,))
        mm(cp2[:, 0:96], CST["tri"][:, :], Ltf, True, True, R=[("Lt",), ("k", "tri")], W=[cp2k])
        cpsv = cps[:, 0:96].rearrange("p (j n) -> p j n", j=16)
        cp23 = cp2[:, 0:96].rearrange("p (j n) -> p j n", j=16)
        P.add("dve", lambda e: e.tensor_copy(out=scA[:, :, :], in_=cpsv), R=[cpsk], W=[("scA",)])
        bufs = [(scA, ("scA",)), (scB, ("scB",))]
        cur = 0
        for sh in (1, 2, 4, 8):
            (a_, ak), (b_, bk) = bufs[cur], bufs[1 - cur]
            P.add("dve", lambda e, a_=a_, b_=b_, sh=sh: e.tensor_copy(out=b_[:, 0:sh, :], in_=a_[:, 0:sh, :]), R=[ak], W=[bk])
            P.add("dve", lambda e, a_=a_, b_=b_, sh=sh: e.tensor_tensor(out=b_[:, sh:16, :], in0=a_[:, sh:16, :],
                                                                   in1=a_[:, 0:16 - sh, :], op=ALU.add), R=[ak], W=[bk])
            cur = 1 - cur
        inc_, inck = bufs[cur]
        P.add("dve", lambda e: e.tensor_copy(out=csb[:, 0:1, :], in_=cp23[:, 0:1, :]), R=[cp2k], W=[("csb",)])
        P.add("dve", lambda e: e.tensor_tensor(out=csb[:, 1:16, :], in0=cp23[:, 1:16, :], in1=inc_[:, 0:15, :], op=ALU.add),
              R=[cp2k, inck], W=[("csb",)])
        cpsk = ("csb",)
        cps3 = csb[:, :, :]
        P.add("dve", lambda e: e.tensor_scalar(out=negc[:, :, :], in0=cps3, scalar1=-1.0, scalar2=None, op0=ALU.mult),
              R=[cpsk], W=[("negc",)])
        P.add("pool", lambda e: e.memset(CS[:, :, :], 0.0), W=CS_k)
        P.add("dve", lambda e: e.tensor_copy(out=chb[:, :, :], in_=cps3), R=[cpsk], W=[("chb",)])
        P.add("dve", lambda e: e.tensor_copy(out=CS[:, :, 0:6], in_=chb[:, :, :]), R=[("chb",)], W=CS_k)
        P.add("dve", lambda e: e.tensor_tensor(out=r1t[:, :, :], in0=cps3, in1=chb[:, :, :], op=ALU.subtract),
              R=[cpsk, ("chb",)], W=[("r1t",)])
        P.add("dve", lambda e: e.tensor_copy(out=chb[:, :, :], in_=r1t[:, :, :]), R=[("r1t",)], W=[("chb",)])
        P.add("dve", lambda e: e.tensor_copy(out=CS[:, :, 32:38], in_=chb[:, :, :]), R=[("chb",)], W=CS_k)
        P.add("dve", lambda e: e.tensor_tensor(out=r1t[:, :, :], in0=r1t[:, :, :], in1=chb[:, :, :], op=ALU.subtract),
              R=[("r1t",), ("chb",)], W=[("r1t",)])
        P.add("dve", lambda e: e.tensor_copy(out=chb[:, :, :], in_=r1t[:, :, :]), R=[("r1t",)], W=[("chb",)])
        P.add("dve", lambda e: e.tensor_copy(out=CS[:, :, 64:70], in_=chb[:, :, :]), R=[("chb",)], W=CS_k)
        for q_ in range(4):
            ctp, ctpk = bank((1, 2, 3))
            for jj in range(4):
                j = q_ * 4 + jj
                P.add("pe", lambda e, ctp=ctp, jj=jj, j=j: e.transpose(
                    out=ctp[0:96, jj * 128:(jj + 1) * 128], in_=CS[:, j, :], identity=CST["ident"][:, :]),
                    R=CS_k + [("k", "ident")], W=[ctpk])
            act(CT[0:96, q_ * 512:(q_ + 1) * 512], ctp[0:96, :], AF.Copy, R=[ctpk], W=CT_k)
        tap("negc", negc[:, :, :], [("negc",)])
        for h_ in range(6):
            tl_ = 2 * (h_ // 2) + (h_ % 2)
            r0_ = 64 if h_ % 2 == 0 else 0
            for q_ in range(3):
                dma(sc_pad[tl_, r0_ + q_:r0_ + q_ + 1, :], CT[32 * q_ + h_:32 * q_ + h_ + 1, :], R=CT_k, W=[("scp", tl_, "c")])
                dma(sc_pad[6 + tl_, r0_ + 3 + q_:r0_ + 4 + q_, :], CT[32 * q_ + h_:32 * q_ + h_ + 1, :], R=CT_k,
                    W=[("scp", 6 + tl_, "c")])
        load_fox_pair(0, 0)
        for p_ in range(3):
            sl = p_ % 2
            SKA.flush()
            if p_ + 1 < 3:
                load_fox_pair((p_ + 1) % 2, p_ + 1)
            for c in range(NCH):
                cs = slice(c * 512, (c + 1) * 512)
                for e_ in range(2):
                    base = 64 * e_
                    acc, acck = bank((0, 1, 2))
                    attn_map(sl, e_, TL[sl][2 + e_], TL_k[sl][2 + e_], TL[sl][e_], TL_k[sl][e_], c, 2 * p_ + e_, True, acc, acck)
                    def fin(base=base, acc=acc, acck=acck, p_=p_, cs=cs, c=c):
                        O = slice(base, base + 64)
                        Dn = slice(64 - base, 128 - base)
                        rc = slot("rct", 2)
                        recip(rct[O, rc, :], acc[Dn, :], R=[acck], W=[("rct", rc)], on_act=False)
                        P.add("dve", lambda e: e.tensor_tensor(out=hT[O, p_, cs], in0=acc[O, :], in1=rct[O, rc, :], op=ALU.mult),
                              R=[acck, ("rct", rc)], W=[("hT", p_, c)])
                    SKA.push(fin)
        SKA.flush()

    def diff_phase():
        load_diff_pair(0, 0)
        for d_ in range(2):
            sl = d_ % 2
            SKA.flush()
            if d_ == 0:
                load_diff_pair(1, 1)
            for c in range(NCH):
                cs = slice(c * 512, (c + 1) * 512)
                od = slot("t1", 2)
                for e_ in range(2):
                    accs = []
                    for m_ in range(2):
                        base = 64 * e_ + 32 * m_
                        acc, acck = bank((0, 1, 2))
                        attn_map(sl, e_, Kd[sl], Kd_k[sl], TL[sl][2 * e_ + m_], TL_k[sl][2 * e_ + m_], c, 0, False, acc, acck)
                        accs.append((acc, acck))
                    def comb(e_=e_, accs=accs, od=od):
                        O = slice(64 * e_, 64 * e_ + 64)
                        Dn = slice(64 - 64 * e_, 128 - 64 * e_)
                        (a1, a1k), (a2, a2k) = accs
                        ra = slot("rct", 2)
                        rb = slot("rct", 2)
                        ob = slot("t2", 2)
                        recip(rct[O, ra, :], a1[Dn, :], R=[a1k], W=[("rct", ra)], on_act=False)
                        recip(rct[O, rb, :], a2[Dn, :], R=[a2k], W=[("rct", rb)], on_act=False)
                        P.add("dve", lambda e: e.tensor_scalar(out=rct[O, rb, :], in0=rct[O, rb, :], scalar1=der[O, 4:5],
                                                               scalar2=None, op0=ALU.mult),
                              R=[("rct", rb), ("der", 4)], W=[("rct", rb)])
                        P.add("dve", lambda e: e.tensor_tensor(out=t1[O, od, :], in0=a1[O, :], in1=rct[O, ra, :], op=ALU.mult),
                              R=[a1k, ("rct", ra)], W=[("t1", od)])
                        P.add("dve", lambda e: e.tensor_tensor(out=t2[O, ob, :], in0=a2[O, :], in1=rct[O, rb, :], op=ALU.mult),
                              R=[a2k, ("rct", rb)], W=[("t2", ob)])
                        P.add("pool", lambda e: e.tensor_tensor(out=t1[O, od, :], in0=t1[O, od, :], in1=t2[O, ob, :], op=ALU.add),
                              R=[("t1", od), ("t2", ob)], W=[("t1", od)])
                    SKA.push(comb)

                def subln(od=od, d_=d_, cs=cs, c=c):
                    sq_ = slot("sqt", 2)
                    act(sqt[:, sq_, :], t1[:, od, :], AF.Square, R=[("t1", od)], W=[("sqt", sq_)])
                    sb_, sbk = bank((3,))
                    mm(sb_[:, :], CST["bones64"][:, :], sqt[:, sq_, :], True, True, R=[("sqt", sq_), ("k", "bones64")], W=[sbk])
                    r_ = slot("rs", 2)
                    act(rs[:, r_, :], sb_[:, :], AF.Ln, R=[sbk] + KEPS, W=[("rs", r_)], bias=eps_ap, scale=1.0 / 64)
                    act(rs[:, r_, :], rs[:, r_, :], AF.Exp, R=[("rs", r_)], W=[("rs", r_)], scale=-0.5)
                    P.add("dve", lambda e: e.scalar_tensor_tensor(
                        out=hT[:, 3 + d_, cs], in0=t1[:, od, :], scalar=der[:, 3:4], in1=rs[:, r_, :], op0=ALU.mult, op1=ALU.mult),
                        R=[("t1", od), ("der", 3), ("rs", r_)], W=[("hT", 3 + d_, c)])
                SKA.push(subln)
        SKA.flush()

    def dsa_phase():
        Qs = rv(0, 12, BF16).rearrange("p (t n) -> p t n", t=3)
        Qs_k = rk(0, 12)
        SKK, SKK_k = rv(12, 4, BF16), rk(12, 4)
        IKK, IKK_k = rv(16, 4, BF16), rk(16, 4)
        IQ = rv(20, 8, BF16).rearrange("p (t n) -> p t n", t=2)
        IQ_k = rk(20, 8)
        SVa = rv(28, 6, BF16).rearrange("p (j s d) -> p j s d", j=16, s=3)
        SVa_k = rk(28, 6)
        accb, acc_k = rv(36, 8), rk(36, 8)
        MBs = [(rv(44, 4, BF16), rk(44, 4)), (rv(56, 4, BF16), rk(56, 4))]
        PERT, PERT_k = rv(48, 8), rk(48, 8)
        junk = t2[:, :, :].rearrange("p a n -> p (a n)").bitcast(BF16)
        junk_k = [("t2",)]
        for t_ in range(3):
            dma(Qs[:, t_, :], sc_fm[10 + t_], R=[("scfm", 10 + t_)], W=Qs_k)
        dma(SKK, sc_fm[13], R=[("scfm", 13)], W=SKK_k)
        dma(IKK, sc_fm[14], R=[("scfm", 14)], W=IKK_k)
        for t_ in range(2):
            dma(IQ[:, t_, :], sc_fm[15 + t_], R=[("scfm", 15 + t_)], W=IQ_k)
        dma(SVa[:, :, 0, :], sc_sv, R=[("scsv",)], W=SVa_k)
        dma(SVa[:, :, 2, :], sc_sv, R=[("scsv",)], W=SVa_k)
        P.add("pool", lambda e: e.memset(SVa[:, :, 1, :], 1.0), W=SVa_k)
        dma(PERT, pert_d, R=[], W=PERT_k)

        accs_ = [(accb, acc_k), (accB_t[:, :], [("accB",)])]
        junks = [(junk, junk_k), (t1[:, :, :].rearrange("p a n -> p (a n)").bitcast(BF16), [("t1",)])]

        def index_block(i):
            q = i % 2
            acc_, acck_ = accs_[q]
            nk = (i + 1) * 128
            nch = (nk + 511) // 512
            for hh in range(4):
                tl, base = hh // 2, 64 * (hh % 2)
                for m_ in range(nch):
                    w_ = min(512, nk - 512 * m_)
                    dp, dpk = bank((0, 1))
                    mm(dp[:, 0:w_], IQ[base:base + 64, tl, i * 128:(i + 1) * 128], IKK[base:base + 64, 512 * m_:512 * m_ + w_],
                       True, True, R=IQ_k + IKK_k, W=[dpk])
                    act(dp[:, 0:w_], dp[:, 0:w_], AF.Relu, R=[dpk], W=[dpk])
                    src = PERT if hh == 0 else acc_
                    srck = PERT_k if hh == 0 else acck_
                    P.add("dve", lambda e, dp=dp, w_=w_, m_=m_, hh=hh, src=src: e.scalar_tensor_tensor(
                        out=acc_[:, 512 * m_:512 * m_ + w_], in0=dp[:, 0:w_], scalar=IWs[:, i, hh:hh + 1],
                        in1=src[:, 512 * m_:512 * m_ + w_], op0=ALU.mult, op1=ALU.add),
                        R=[dpk, ("IWs",)] + srck, W=acck_)
            P.add("dve", lambda e: e.tensor_reduce(out=bis2[:, q, 0:1], in_=acc_[:, 0:nk], axis=AX.X, op=ALU.max,
                                                   apply_absolute_value=True), R=acck_, W=[("bis2", q, 0)])
            P.add("dve", lambda e: e.tensor_tensor(out=acc_[:, i * 128:(i + 1) * 128], in0=acc_[:, i * 128:(i + 1) * 128],
                                                   in1=CST["cbt"][:, :], op=ALU.add), R=acck_ + [("k", "cbt")], W=acck_)
            P.add("dve", lambda e: e.tensor_scalar(out=STt2[:, q, :], in0=CST["pow2"][:, :], scalar1=bis2[:, q, 0:1], scalar2=None,
                                                   op0=ALU.mult), R=[("bis2", q, 0), ("k", "pow2")], W=[("STt2", q)])
            P.add("dve", lambda e: e.memset(bis2[:, q, 1:2], 0.0), W=[("bis2", q, 1)])

        def bisect_pair(iA, iB, zsteps):
            blocks = [b_ for b_ in (iA, iB) if b_ is not None]
            zsteps = list(zsteps)
            per_it = (len(zsteps) + KBIS - 1) // KBIS
            for k in range(KBIS):
                for _ in range(per_it):
                    if zsteps:
                        zsteps.pop(0)()
                for i in blocks:
                    q = i % 2
                    acc_, acck_ = accs_[q]
                    jk, jkk = junks[q]
                    nk = (i + 1) * 128
                    if q == 0:
                        P.add("dve", lambda e, acc_=acc_, jk=jk, nk=nk, q=q: e.tensor_scalar(
                            out=jk[:, 0:nk], in0=acc_[:, 0:nk], scalar1=bis2[:, q, 1:2], scalar2=None,
                            op0=ALU.is_gt, op1=ALU.add, accum_out=bis2[:, q, 2:3]),
                            R=acck_ + [("bis2", q, 1)], W=jkk + [("bis2", q, 2)])
                    else:
                        act(jk[:, 0:nk], acc_[:, 0:nk], AF.Sign, R=acck_ + [("bis2", q, 1)], W=jkk + [("bis2", q, 2)],
                            bias=bis2[:, q, 1:2], scale=-1.0, accum_out=bis2[:, q, 2:3])
                for i in blocks:
                    q = i % 2
                    nk = (i + 1) * 128
                    if q == 0:
                        P.add("dve", lambda e, k=k, q=q: e.tensor_scalar(
                            out=bis2[:, q, 3:4], in0=bis2[:, q, 2:3], scalar1=TOPK - 0.5, scalar2=STt2[:, q, 32 + k:33 + k],
                            op0=ALU.is_gt, op1=ALU.mult), R=[("bis2", q, 2), ("STt2", q)], W=[("bis2", q, 3)])
                    else:
                        P.add("dve", lambda e, k=k, q=q, nk=nk: e.tensor_scalar(
                            out=bis2[:, q, 3:4], in0=bis2[:, q, 2:3], scalar1=float(nk - 2 * TOPK + 1),
                            scalar2=STt2[:, q, 32 + k:33 + k], op0=ALU.is_lt, op1=ALU.mult),
                            R=[("bis2", q, 2), ("STt2", q)], W=[("bis2", q, 3)])
                    P.add("dve", lambda e, k=k, q=q: e.scalar_tensor_tensor(
                        out=bis2[:, q, 1:2], in0=bis2[:, q, 3:4], scalar=STt2[:, q, k:k + 1], in1=bis2[:, q, 1:2],
                        op0=ALU.subtract, op1=ALU.add),
                        R=[("bis2", q, 3), ("bis2", q, 1), ("STt2", q)], W=[("bis2", q, 1)])
            while zsteps:
                zsteps.pop(0)()
            for i in blocks:
                q = i % 2
                acc_, acck_ = accs_[q]
                MB, MB_k = MBs[q]
                nk = (i + 1) * 128
                P.add("dve", lambda e, acc_=acc_, MB=MB, nk=nk, q=q: e.tensor_scalar(
                    out=MB[:, 0:nk], in0=acc_[:, 0:nk], scalar1=bis2[:, q, 1:2], scalar2=NEG, op0=ALU.is_le, op1=ALU.mult),
                    R=acck_ + [("bis2", q, 1)], W=MB_k)

        SKD = Skew(1)
        accE, accEk = banks[6], ("ps", 6)
        accO, accOk = banks[7], ("ps", 7)

        def attend_steps(i):
            MB, MB_k = MBs[i % 2]
            c = i // 4
            qs_ = slice(i * 128, (i + 1) * 128)
            steps = []

            def step(j):
                ks_ = slice(j * 128, (j + 1) * 128)
                sts = []
                for par in range(2):
                    st, stk = bank((2, 3, 4, 5))
                    pr = slice(64 * par, 64 * par + 64)
                    for hi in range(3):
                        mm(st[:, hi * 128:(hi + 1) * 128], SKK[pr, ks_], Qs[pr, hi, qs_], hi == 0, False,
                           R=SKK_k + Qs_k, W=[stk])
                    mm(st[:, 0:384], MB[:, ks_], CST["irep"][:, 0:384], False, True, R=MB_k + [("k", "irep")], W=[stk])
                    sts.append((st, stk))
                pss = []
                for par in range(2):
                    ps_ = slot("pt", 4)
                    act(pt[:, ps_, 0:384], sts[par][0][:, 0:384], AF.Exp, R=[sts[par][1]], W=[("pt", ps_)])
                    pss.append(ps_)

                def pv():
                    mm(accE[:, 0:384], SVa[:, j, 0:2, :].rearrange("p s d -> p (s d)"), pt[:, pss[0], 0:384], j == 0, j == i,
                       R=SVa_k + [("pt", pss[0])], W=[accEk])
                    mm(accO[:, 0:384], SVa[:, j, 1:3, :].rearrange("p s d -> p (s d)"), pt[:, pss[1], 0:384], j == 0, j == i,
                       R=SVa_k + [("pt", pss[1])], W=[accOk])
                SKD.push(pv)

            def fin():
                for par, (acc, acck) in enumerate(((accE, accEk), (accO, accOk))):
                    O = slice(64 * par, 64 * par + 64)
                    Dn = slice(64 - 64 * par, 128 - 64 * par)
                    rc = slot("rct", 2)
                    recip(rct[O, rc, 0:384], acc[Dn, 0:384], R=[acck], W=[("rct", rc)])
                    P.add("dve", lambda e, O=O, rc=rc, acc=acc: e.tensor_tensor(
                        out=hT[O, 5:8, qs_], in0=acc[O, 0:384].rearrange("p (h n) -> p h n", h=3),
                        in1=rct[O, rc, 0:384].rearrange("p (h n) -> p h n", h=3), op=ALU.mult),
                        R=[acck, ("rct", rc)], W=[("hT", 5, c), ("hT", 6, c), ("hT", 7, c)])

            for j in range(i + 1):
                steps.append(lambda j=j: step(j))
            steps.append(lambda: SKD.push(fin))
            return steps

        index_block(0)
        index_block(1)
        prev_steps = []
        for m_ in range(8):
            bisect_pair(2 * m_, 2 * m_ + 1, prev_steps)
            prev_steps = attend_steps(2 * m_) + attend_steps(2 * m_ + 1)
            if m_ + 1 < 8:
                index_block(2 * m_ + 2)
                index_block(2 * m_ + 3)
        for st_ in prev_steps:
            st_()
        SKD.flush()
        tap("acc15", accB_t[:, :], [("accB",)])
        tap("MB15", MBs[1][0], MBs[1][1])
        tap("bis15", bis2[:, 1, :], [("bis2",)])
        tap("STt", STt2[:, 1, :], [("STt2",)])

    def wout_phase(li):
        Wo = [rv(56, 2, BF16).rearrange("p (k n) -> p k n", k=8), rv(58, 2, BF16).rearrange("p (k n) -> p k n", k=8)]
        Wo_k = [[("R", 14, 0)], [("R", 14, 1)]]
        wsrc = w_out_d[li].rearrange("(kt p) n -> p kt n", p=128)
        load_w(Wo[0], Wo_k[0], wsrc[:, :, 0:128])
        for d_ in range(8):
            sl = d_ % 2
            if d_ + 1 < 8:
                load_w(Wo[1 - sl], Wo_k[1 - sl], wsrc[:, :, (d_ + 1) * 128:(d_ + 2) * 128])
            for c in range(NCH):
                cs = slice(c * 512, (c + 1) * 512)
                bk, bkk = bank((0, 1, 2, 3, 4, 5, 6, 7))
                for kt in range(8):
                    mm(bk[:, :], Wo[sl][:, kt, :], hT[:, kt, cs], kt == 0, kt == 7, R=Wo_k[sl] + [("hT", kt, c)], W=[bkk])
                P.add("dve", lambda e, d_=d_, cs=cs, bk=bk: e.tensor_tensor(
                    out=xT[:, d_, cs], in0=bk[:, :], in1=xT[:, d_, cs], op=ALU.add),
                    R=[bkk, ("xT", d_, c)], W=[("xT", d_, c)])

    for li, labs in enumerate(layer_ids):
        dma(ppt[:, :], pp_d[li], R=[], W=[("ppt",)])
        derive_phase(labs)
        norm_phase(7)
        proj_phase(li)
        if li == 0:
            tap("sc_fm", sc_fm, [("scfm",)])
            tap("sc_v", sc_v, [("scv",)])
            tap("sc_sv", sc_sv, [("scsv",)])
            tap("FFt", FFt[:, :, :], [("FFt",)])
            tap("IWs", IWs[:, :, :], [("IWs",)])
        fox_phase()
        diff_phase()
        dsa_phase()
        if li == 0:
            tap("cat", hT[:, :, :], [("hT",)])
        wout_phase(li)
        if li == 0:
            tap("x1", xT[:, :, :], [("xT",)])
        norm_phase(15)
        ffn_phase(li)

    for c in range(NCH):
        for kc in range(8):
            dma(y_d[kc * 128:(kc + 1) * 128, c * 512:(c + 1) * 512], xT[:, kc, c * 512:(c + 1) * 512],
                R=[("xT", kc, c)], W=[("y", kc, c)])

    P.emit(nc, stack)
    stack.close()
    return nc, P.stats


_NC_CACHE = {}


def _get_nc(layer_ids):
    key = tuple(layer_ids)
    if key not in _NC_CACHE:
        _NC_CACHE[key] = build_nc(list(layer_ids))[0]
    return _NC_CACHE[key]


def kernel(**inputs):
    inp = {k: np.asarray(v) for k, v in inputs.items()}
    x = inp["x"].astype(np.float32, copy=False)
    B = x.shape[0]
    cst = _consts()
    base = {}
    for nm, w in _CONST_SHAPES:
        base["c_" + nm] = np.ascontiguousarray(cst[nm], dtype=np.float32)
    base["c_rope"] = cst["rope"]
    base["c_pert"] = cst["pert"]
    layer_ids = list(range(DEPTH))
    nc = _get_nc(layer_ids)
    base["w_in"] = np.ascontiguousarray(inp["w_in"], dtype=np.float32)
    base["w_out"] = np.ascontiguousarray(inp["w_out"], dtype=np.float32)
    base["w_gu"] = np.ascontiguousarray(inp["w_gate_up"], dtype=np.float32)
    base["w_dn"] = np.ascontiguousarray(inp["w_down"], dtype=np.float32)
    base["pp"] = np.stack([_pack_pp(inp, l) for l in layer_ids]).astype(np.float32)
    in_maps = []
    for b in range(B):
        m = dict(base)
        m["xT"] = np.ascontiguousarray(x[b].T)
        in_maps.append(m)
    res = run_bass_kernel_spmd(nc, in_maps, core_ids=list(range(B)))
    out = np.stack([np.asarray(r["yT"]).T for r in res.results]).astype(np.float32)
    return out
```

```python
import math
from contextlib import ExitStack
import numpy as np
import concourse.bass as bass
import concourse.mybir as mybir
from concourse.bass_utils import run_bass_kernel_spmd

F32 = mybir.dt.float32
BF16 = mybir.dt.bfloat16
ALU = mybir.AluOpType
AF = mybir.ActivationFunctionType
AX = mybir.AxisListType

D = 1024
S = 2048
DEPTH = 4
NCH = 4
FFN_H = 2816
INW = 2762
EPS = 1e-6
NEG = -30000.0
KBIS = 14
TOPK = 256
PERT_EPS = 2.0 ** -20
NPP = 157

O_FQ, O_FK, O_FV, O_FF = 0, 384, 768, 1152
O_DQ, O_DK, O_DV = 1158, 1414, 1670
O_SQ, O_SK, O_SV, O_IQ, O_IK, O_IW = 1926, 2310, 2374, 2438, 2694, 2758


class Prog:
    ENGS = ("pe", "act", "dve", "pool", "sp")
    NDMA = 24

    def __init__(self):
        self.ops = []

    def add(self, eng, fn, R=(), W=(), dma=False):
        R = tuple(R)
        W = tuple(W) + tuple(k for k in R if k[0] == "ps" and k not in W)
        R = tuple(k for k in R if k[0] != "ps")
        self.ops.append((eng, fn, R, W, dma))

    def _deps(self):
        lastw = {}
        readers = {}
        desc = {}
        deps_all = []

        def related(k):
            out = []
            for i in range(1, len(k)):
                p = k[:i]
                if p in lastw or p in readers:
                    out.append(p)
            out.extend(desc.get(k, ()))
            return out

        def register(k):
            if k in lastw or k in readers:
                return
            for i in range(1, len(k) + 1):
                desc.setdefault(k[:i], set()).add(k)

        for i, (eng, fn, R, W, dma) in enumerate(self.ops):
            d = set()
            for k in R:
                register(k)
                lastw.setdefault(k, None)
                for r in related(k):
                    w = lastw.get(r)
                    if w is not None:
                        d.add(w)
            for k in W:
                register(k)
                lastw.setdefault(k, None)
                for r in related(k):
                    w = lastw.get(r)
                    if w is not None:
                        d.add(w)
                    d.update(readers.get(r, ()))
            d.discard(i)
            for k in R:
                readers.setdefault(k, []).append(i)
            for k in W:
                lastw[k] = i
                for r in desc.get(k, ()):
                    if r in readers:
                        readers[r] = []
                    lastw[r] = i
            deps_all.append(d)
        return deps_all

    def emit(self, nc, stack):
        ops = self.ops
        deps_all = self._deps()
        n = len(ops)
        sig = [False] * n
        for i, d in enumerate(deps_all):
            e_i = ops[i][0]
            for j in d:
                if ops[j][0] == "pe" and e_i == "pe":
                    continue
                sig[j] = True
        dma_prev = {}
        dma_slot = {}
        nd = 0
        for i, op in enumerate(ops):
            if op[4]:
                s = nd % self.NDMA
                nd += 1
                dma_slot[i] = s
                if s in dma_prev:
                    deps_all[i].add(dma_prev[s])
                dma_prev[s] = i
                sig[i] = True
        EPOCH = 1000
        esems = {e: [] for e in ("pe", "act", "dve", "pool")}
        dsem = [stack.enter_context(nc.semaphore("dsem%d" % k)) for k in range(min(self.NDMA, max(nd, 1)))]
        count = {e: 0 for e in esems}
        dcount = [0] * self.NDMA
        known = {e: {} for e in self.ENGS}
        event = [None] * n
        vc = [None] * n
        plan = {e: [] for e in self.ENGS}
        nwaits = 0
        Z = (0, 0)

        def sem_of(src, ep):
            if isinstance(src, tuple):
                return dsem[src[1]]
            lst = esems[src]
            while len(lst) <= ep:
                lst.append(stack.enter_context(nc.semaphore("sem_%s_%d" % (src, len(lst)))))
            return lst[ep]

        for i, (eng, fn, R, W, dma) in enumerate(ops):
            kn = known[eng]
            wm = {}
            for j in sorted(deps_all[i]):
                if ops[j][0] == "pe" and eng == "pe":
                    continue
                src, val = event[j]
                if kn.get(src, Z) >= val:
                    continue
                if wm.get(src, Z) < val:
                    wm[src] = val
                for s2, v2 in vc[j].items():
                    if kn.get(s2, Z) < v2:
                        kn[s2] = v2
            nwaits += len(wm)
            inc = None
            if sig[i]:
                if dma:
                    s = dma_slot[i]
                    dcount[s] += 16
                    event[i] = (("d", s), (0, dcount[s]))
                    inc = (dsem[s], 16)
                else:
                    ep, cn = divmod(count[eng], EPOCH)
                    count[eng] += 1
                    event[i] = (eng, (ep, cn + 1))
                    inc = (sem_of(eng, ep), 1)
                v = dict(kn)
                v[event[i][0]] = event[i][1]
                vc[i] = v
            plan[eng].append((fn, [(sem_of(src, val[0]), val[1]) for src, val in wm.items()], inc))
        self.stats = dict(n_ops=n, n_waits=nwaits, counts=dict(count), n_dma=nd,
                          n_sems=len(dsem) + sum(len(v) for v in esems.values()))

        block = stack.enter_context(nc.Block())

        def run(engine, items):
            for fn, waits, inc in items:
                for sem, val in waits:
                    engine.wait_ge(sem, val)
                ins = fn(engine)
                if inc is not None:
                    ins.then_inc(inc[0], inc[1])

        @block.tensor
        def _(e):
            run(e, plan["pe"])

        @block.scalar
        def _(e):
            run(e, plan["act"])

        @block.vector
        def _(e):
            run(e, plan["dve"])

        @block.gpsimd
        def _(e):
            run(e, plan["pool"])

        @block.sync
        def _(e):
            run(e, plan["sp"])
            for s in range(len(dsem)):
                if dcount[s] > 0:
                    e.wait_ge(dsem[s], dcount[s])


class Skew:
    def __init__(self, lag):
        self.q = []
        self.lag = lag

    def push(self, fn):
        self.q.append(fn)
        while len(self.q) > self.lag:
            self.q.pop(0)()

    def flush(self):
        while self.q:
            self.q.pop(0)()


def _rope_tab(head_dim, rows_rep):
    rot = head_dim // 4
    half = rot // 2
    inv = (1.0 / (np.float32(500000.0) ** (np.arange(0, rot, 2, dtype=np.float32) / np.float32(rot)))).astype(np.float32)
    ang = np.arange(S, dtype=np.float32)[:, None] * inv[None, :]
    cos = np.cos(ang).astype(np.float32).T
    sin = np.sin(ang).astype(np.float32).T
    C = np.ones((128, S), np.float32)
    Sn = np.zeros((128, S), np.float32)
    for b in range(128 // head_dim):
        o = b * head_dim
        C[o:o + half] = cos
        C[o + half:o + rot] = cos
        Sn[o:o + half] = sin
        Sn[o + half:o + rot] = sin
    P = np.zeros((128, 128), np.float32)
    for b in range(128 // head_dim):
        o = b * head_dim
        for r in range(half):
            P[o + r + half, o + r] = -1.0
            P[o + r, o + r + half] = 1.0
    return C, Sn, P


def _consts():
    c = {}
    eye = np.eye(128, dtype=np.float32)
    idx = np.arange(128)
    c["ident"] = eye
    c["tri"] = -(idx[:, None] <= idx[None, :]).astype(np.float32)
    c["negones"] = -np.ones((128, 128), np.float32)
    c["ones"] = np.ones((128, 128), np.float32)
    b64 = np.zeros((128, 128), np.float32)
    b64[:64, :64] = 1
    b64[64:, 64:] = 1
    c["bones64"] = b64
    b32 = np.zeros((128, 128), np.float32)
    for b in range(4):
        b32[32 * b:32 * b + 32, 32 * b:32 * b + 32] = 1
    c["bones32"] = b32
    sel = np.zeros((128, 6, 128), np.float32)
    for h in range(6):
        sel[h, h, :] = 1
        sel[32 + h, h, :] = 1
        sel[64 + h, h, :] = 1
    c["sel"] = sel.reshape(128, 768)
    c["cb"] = np.where(idx[:, None] <= idx[None, :], 0.0, NEG).astype(np.float32)
    c["cbt"] = np.where(idx[None, :] <= idx[:, None], 0.0, NEG).astype(np.float32)
    c["irep"] = np.tile(eye, (1, 4))
    C64, S64, P64 = _rope_tab(64, 2)
    C32, S32, P32 = _rope_tab(32, 4)
    c["prot64"] = P64
    c["prot32"] = P32
    c["rope"] = np.stack([C32, S32, C64, S64]).astype(np.float32)
    c["pert"] = np.tile((-PERT_EPS * np.arange(S, dtype=np.float32))[None, :], (128, 1)).astype(np.float32)
    p2 = np.zeros((128, 64), np.float32)
    for k in range(32):
        p2[:, k] = 2.0 ** (-k)
        p2[:, 32 + k] = 2.0 ** (1 - k)
    c["pow2"] = p2
    return c


_CONST_SHAPES = [("ident", 128), ("tri", 128), ("negones", 128), ("ones", 128), ("bones64", 128), ("bones32", 128),
                 ("cb", 128), ("cbt", 128), ("irep", 512), ("prot64", 128), ("prot32", 128), ("pow2", 64)]


def _pack_pp(inp, l):
    pp = np.zeros((128, NPP), np.float32)
    p = np.arange(128)
    pp[:, 0] = inp["fox_qn"][l][p % 64]
    pp[:, 1] = inp["fox_kn"][l][p % 64]
    pp[:, 2] = inp["diff_qn"][l][p % 32]
    pp[:, 3] = inp["diff_kn"][l][p % 32]
    pp[:, 4] = inp["dsa_qn"][l][p % 64]
    pp[:, 5] = inp["dsa_kn"][l][p % 64]
    pp[:, 6] = inp["diff_subln"][l][p % 64]
    pp[:, 7:15] = inp["attn_norm"][l].reshape(8, 128).T
    pp[:, 15:23] = inp["ffn_norm"][l].reshape(8, 128).T
    pp[:, 23:29] = inp["fox_fb"][l][None, :]
    pp[:, 29:61] = inp["diff_lq1"][l][None, :]
    pp[:, 61:93] = inp["diff_lk1"][l][None, :]
    pp[:, 93:125] = inp["diff_lq2"][l][None, :]
    pp[:, 125:157] = inp["diff_lk2"][l][None, :]
    return pp


def build_nc(layer_ids, debug=None):
    NL = len(layer_ids)
    nc = bass.Bass("TRN2", target_bir_lowering=False)
    stack = ExitStack()
    P = Prog()

    def dram(name, shape, dt=F32, kind="ExternalInput"):
        return nc.dram_tensor(name, list(shape), dt, kind=kind).ap()

    x_d = dram("xT", [D, S])
    y_d = dram("yT", [D, S], kind="ExternalOutput")
    w_in_d = dram("w_in", [NL, D, INW])
    w_out_d = dram("w_out", [NL, D, D])
    w_gu_d = dram("w_gu", [NL, D, 2 * FFN_H])
    w_dn_d = dram("w_dn", [NL, FFN_H, D])
    pp_d = dram("pp", [NL, 128, NPP])
    cst_d = {nm: dram("c_" + nm, [128, w]) for nm, w in _CONST_SHAPES}
    rope_d = dram("c_rope", [4, 128, S])
    pert_d = dram("c_pert", [128, S])
    sc_fm = dram("sc_fm", [17, 128, S], BF16, kind="Internal")
    sc_v = dram("sc_v", [5, 128, 16, 128], BF16, kind="Internal")
    sc_sv = dram("sc_sv", [128, 16, 64], BF16, kind="Internal")
    sc_pad = dram("sc_pad", [20, 128, S], BF16, kind="Internal")
    dbg = {}
    if debug:
        for nm, shape, dt in debug:
            dbg[nm] = dram("dbg_" + nm, shape, dt, kind="ExternalOutput")

    def sb(name, shape, dt=F32):
        return stack.enter_context(nc.sbuf_tensor(name, list(shape), dt))

    def ps(name, shape, dt=F32):
        return stack.enter_context(nc.psum_tensor(name, list(shape), dt))

    xT = sb("xT_sb", [128, 8, S])
    hT = sb("hT_sb", [128, 8, S], BF16)
    RW = 15 * 1024
    Rg = sb("R_sb", [128, RW])
    stg = sb("stg_sb", [128, 2, 1024])
    banks = [ps("bank%d" % b, [128, 512]) for b in range(8)]

    def rv(off_kb, size_kb, dt=F32):
        a = Rg[:, off_kb * 256:(off_kb + size_kb) * 256]
        if dt == BF16:
            a = a.bitcast(BF16)
        return a

    def rk(off_kb, size_kb):
        return [("R", pg) for pg in range(off_kb // 4, (off_kb + size_kb + 3) // 4)]

    sqt = sb("sqt", [128, 2, 512], BF16)
    rs = sb("rs", [128, 2, 512])
    qn = sb("qn", [128, 3, 512], BF16)
    t1 = sb("t1", [128, 2, 512])
    t2 = sb("t2", [128, 2, 512])
    pt = sb("pt", [128, 4, 512], BF16)
    rct = sb("rct", [128, 2, 512])
    ppt = sb("ppt", [128, NPP])
    der = sb("der", [128, 16])
    FFt = sb("FFt", [128, 16, 6])
    IWs = sb("IWs", [128, 16, 4])
    Lt = sb("Lt", [128, 16, 6])
    negc = sb("negc", [128, 16, 6])
    chb = sb("chb", [128, 16, 6], BF16)
    r1t = sb("r1t", [128, 16, 6])
    scA = sb("scA", [128, 16, 6])
    scB = sb("scB", [128, 16, 6])
    csb = sb("csb", [128, 16, 6])
    lam4 = sb("lam4", [128, 4, 32])
    CST = {}
    for nm, w in _CONST_SHAPES:
        f32c = nm in ("ident", "tri", "negones", "cbt", "pow2")
        CST[nm] = sb("k_" + nm, [128, w], F32 if f32c else BF16)
    epsc = sb("epsc", [128, 2])
    accB_t = sb("accB_t", [128, 2048])
    bis2 = sb("bis2", [128, 2, 4])
    STt2 = sb("STt2", [128, 2, 64])

    bank_rr = {}

    def bank(pool):
        i = bank_rr.get(pool, 0)
        bank_rr[pool] = i + 1
        b = pool[i % len(pool)]
        return banks[b], ("ps", b)

    slot_rr = {}

    def slot(name, nslots):
        i = slot_rr.get(name, 0)
        slot_rr[name] = i + 1
        return i % nslots

    def dma(out, in_, R, W):
        P.add("sp", lambda e: e.dma_start(out=out, in_=in_), R=R, W=W, dma=True)

    def mm(out, lhsT, rhs, start, stop, R, W, **kw):
        P.add("pe", lambda e: e.matmul(out, lhsT=lhsT, rhs=rhs, start=start, stop=stop, **kw), R=R, W=W)

    def act(out, in_, func, R, W, bias=0.0, scale=1.0, accum_out=None):
        if accum_out is None:
            P.add("act", lambda e: e.activation(out=out, in_=in_, func=func, bias=bias, scale=scale), R=R, W=W)
        else:
            P.add("act", lambda e: e.activation(out=out, in_=in_, func=func, bias=bias, scale=scale, accum_out=accum_out),
                  R=R, W=W)

    def recip(out, in_, R, W, on_act=True):
        if on_act:
            act(out, in_, AF.Ln, R=R, W=W)
            act(out, out, AF.Exp, R=W, W=W, scale=-1.0)
        else:
            P.add("dve", lambda e: e.reciprocal(out=out, in_=in_), R=R, W=W)

    def load_w(dst, dst_keys, src_ap):
        s = slot("stg", 2)
        n = 1
        for d_ in src_ap.shape[1:]:
            n *= d_
        sv = stg[:, s, 0:n]
        if len(src_ap.shape) == 3:
            sv = sv.rearrange("p (a b) -> p a b", a=src_ap.shape[1])
        dma(sv, src_ap, R=[], W=[("stg", s)])
        P.add("pool", lambda e: e.tensor_copy(out=dst, in_=sv), R=[("stg", s)], W=dst_keys)

    for nm, w in _CONST_SHAPES:
        if CST[nm].dtype == F32:
            dma(CST[nm][:, :], cst_d[nm][:, :], R=[], W=[("k", nm)])
        else:
            for o in range(0, w, 512):
                ww = min(512, w - o)
                s = slot("t1", 2)
                dma(t1[:, s, 0:ww], cst_d[nm][:, o:o + ww], R=[], W=[("t1", s)])
                P.add("pool", lambda e, nm=nm, o=o, ww=ww, s=s: e.tensor_copy(out=CST[nm][:, o:o + ww], in_=t1[:, s, 0:ww]),
                      R=[("t1", s)], W=[("k", nm)])
    P.add("pool", lambda e: e.memset(epsc[:, 0:1], EPS), W=[("epsc",)])
    P.add("pool", lambda e: e.memset(epsc[:, 1:2], 1.0), W=[("epsc",)])
    KEPS = [("epsc",)]
    for c in range(NCH):
        for kc in range(8):
            dma(xT[:, kc, c * 512:(c + 1) * 512], x_d[kc * 128:(kc + 1) * 128, c * 512:(c + 1) * 512], R=[], W=[("xT", kc, c)])
    P.add("pool", lambda e: e.memset(hT[:, 0, :], 0.0), W=[("hT", 0)])
    P.add("pool", lambda e: e.memset(hT[:, 1, :], 1.0), W=[("hT", 1)])
    for t_ in range(20):
        dma(sc_pad[t_], hT[:, 0, :], R=[("hT", 0)], W=[("scp", t_)])
    P.add("pool", lambda e: e.memset(hT[:, 2, :], -1.0), W=[("hT", 2)])
    for p_ in range(3):
        dma(sc_pad[6 + 2 * p_, 64:67, :], hT[64:67, 1, :], R=[("hT", 1)], W=[("scp", 6 + 2 * p_)])
        dma(sc_pad[7 + 2 * p_, 0:3, :], hT[0:3, 1, :], R=[("hT", 1)], W=[("scp", 7 + 2 * p_)])
        dma(sc_pad[2 * p_, 67:70, :], hT[64:67, 2, :], R=[("hT", 2)], W=[("scp", 2 * p_)])
        dma(sc_pad[2 * p_ + 1, 3:6, :], hT[0:3, 2, :], R=[("hT", 2)], W=[("scp", 2 * p_ + 1)])
    eps_ap = epsc[:, 0:1]
    one_ap = epsc[:, 1:2]

    def norm_phase(gcol0):
        for c in range(NCH):
            cs = slice(c * 512, (c + 1) * 512)
            bk, bkk = bank((0, 1))
            for kc in range(8):
                s = slot("sqt", 2)
                act(sqt[:, s, :], xT[:, kc, cs], AF.Square, R=[("xT", kc, c)], W=[("sqt", s)])
                mm(bk[:, :], CST["ones"][:, :], sqt[:, s, :], kc == 0, kc == 7, R=[("sqt", s), ("k", "ones")], W=[bkk])
            s = slot("rs", 2)
            act(rs[:, s, :], bk[:, :], AF.Ln, R=[bkk] + KEPS, W=[("rs", s)], bias=eps_ap, scale=1.0 / D)
            act(rs[:, s, :], rs[:, s, :], AF.Exp, R=[("rs", s)], W=[("rs", s)], scale=-0.5)
            for kc in range(8):
                P.add("dve", lambda e, kc=kc, cs=cs, s=s: e.scalar_tensor_tensor(
                    out=hT[:, kc, cs], in0=xT[:, kc, cs], scalar=ppt[:, gcol0 + kc:gcol0 + kc + 1], in1=rs[:, s, :],
                    op0=ALU.mult, op1=ALU.mult), R=[("xT", kc, c), ("ppt",), ("rs", s)], W=[("hT", kc, c)])

    def ffn_phase(li):
        groups = [(g * 512, 4) for g in range(5)] + [(2560, 2)]
        Wgu = [rv(0, 16, BF16).rearrange("p (k n) -> p k n", k=8), rv(16, 16, BF16).rearrange("p (k n) -> p k n", k=8)]
        Wgu_k = [rk(0, 16), rk(16, 16)]
        Wd = [rv(32, 8, BF16).rearrange("p (k n) -> p k n", k=4), rv(40, 8, BF16).rearrange("p (k n) -> p k n", k=4)]
        Wd_k = [rk(32, 8), rk(40, 8)]
        actT = rv(48, 8, BF16).rearrange("p (s k n) -> p s k n", s=2, k=4)
        actT_k = [rk(48, 4), rk(52, 4)]
        win = w_gu_d[li].rearrange("(kc p) n -> p kc n", p=128)

        def load_group(gi):
            h0, nt = groups[gi]
            sl = gi % 2
            for t in range(nt):
                load_w(Wgu[sl][:, :, t * 128:(t + 1) * 128], Wgu_k[sl], win[:, :, h0 + t * 128:h0 + (t + 1) * 128])
                load_w(Wgu[sl][:, :, 512 + t * 128:512 + (t + 1) * 128], Wgu_k[sl],
                       win[:, :, FFN_H + h0 + t * 128:FFN_H + h0 + (t + 1) * 128])
                load_w(Wd[sl][:, t, :], Wd_k[sl], w_dn_d[li, h0 + t * 128:h0 + (t + 1) * 128, :])

        load_group(0)
        for gi in range(len(groups)):
            if gi + 1 < len(groups):
                load_group(gi + 1)
            h0, nt = groups[gi]
            sl = gi % 2
            for c in range(NCH):
                cs = slice(c * 512, (c + 1) * 512)
                asl = slot("actT", 2)
                for t in range(nt):
                    gb, gbk = bank((0, 1, 2, 3))
                    ub, ubk = bank((0, 1, 2, 3))
                    for kc in range(8):
                        mm(gb[:, :], Wgu[sl][:, kc, t * 128:(t + 1) * 128], hT[:, kc, cs], kc == 0, kc == 7,
                           R=Wgu_k[sl] + [("hT", kc, c)], W=[gbk])
                    for kc in range(8):
                        mm(ub[:, :], Wgu[sl][:, kc, 512 + t * 128:512 + (t + 1) * 128], hT[:, kc, cs], kc == 0, kc == 7,
                           R=Wgu_k[sl] + [("hT", kc, c)], W=[ubk])
                    s = slot("t1", 2)
                    act(t1[:, s, :], gb[:, :], AF.Silu, R=[gbk], W=[("t1", s)])
                    P.add("dve", lambda e, s=s, ub=ub, asl=asl, t=t: e.tensor_tensor(
                        out=actT[:, asl, t, :], in0=ub[:, :], in1=t1[:, s, :], op=ALU.mult),
                        R=[ubk, ("t1", s)], W=actT_k[asl])
                for d_ in range(8):
                    db, dbk = bank((4, 5, 6, 7))
                    for t in range(nt):
                        mm(db[:, :], Wd[sl][:, t, d_ * 128:(d_ + 1) * 128], actT[:, asl, t, :], t == 0, t == nt - 1,
                           R=Wd_k[sl] + actT_k[asl], W=[dbk])
                    P.add("dve", lambda e, d_=d_, cs=cs, db=db: e.tensor_tensor(
                        out=xT[:, d_, cs], in0=db[:, :], in1=xT[:, d_, cs], op=ALU.add),
                        R=[dbk, ("xT", d_, c)], W=[("xT", d_, c)])


    def apb(base_ap, mid):
        a = base_ap.ap
        return bass.AP(base_ap.tensor, base_ap.offset, [list(a[0]), [0, mid], list(a[-1])])

    def tap(name, src, R):
        if name in dbg:
            dma(dbg[name], src, R=R, W=[("dbg", name)])

    def derive_phase(labs):
        lam_init = 0.8 - 0.6 * math.exp(-0.3 * labs)
        for col, src, mul in ((0, 0, 0.125), (1, 2, 32.0 ** -0.5), (2, 4, 0.125), (3, 6, 1.0 - lam_init)):
            P.add("dve", lambda e, col=col, src=src, mul=mul: e.tensor_scalar(
                out=der[:, col:col + 1], in0=ppt[:, src:src + 1], scalar1=mul, scalar2=None, op0=ALU.mult),
                R=[("ppt",)], W=[("der", col)])
        for q, (a, b) in enumerate(((29, 61), (93, 125))):
            P.add("dve", lambda e, q=q, a=a, b=b: e.tensor_tensor(
                out=lam4[:, q, :], in0=ppt[:, a:a + 32], in1=ppt[:, b:b + 32], op=ALU.mult), R=[("ppt",)], W=[("lam4", q)])
            P.add("dve", lambda e, q=q: e.tensor_reduce(out=der[:, 5 + q:6 + q], in_=lam4[:, q, :], axis=AX.X, op=ALU.add),
                  R=[("lam4", q)], W=[("der", 5 + q)])
            act(der[:, 5 + q:6 + q], der[:, 5 + q:6 + q], AF.Exp, R=[("der", 5 + q)], W=[("der", 5 + q)])
        P.add("dve", lambda e: e.tensor_tensor(out=der[:, 4:5], in0=der[:, 6:7], in1=der[:, 5:6], op=ALU.subtract),
              R=[("der", 5), ("der", 6)], W=[("der", 4)])
        P.add("dve", lambda e: e.tensor_scalar(out=der[:, 4:5], in0=der[:, 4:5], scalar1=-lam_init, scalar2=None, op0=ALU.add),
              R=[("der", 4)], W=[("der", 4)])

    def proj_phase(li):
        win = w_in_d[li].rearrange("(kc p) n -> p kc n", p=128)
        WT = [rv(0, 2, BF16).rearrange("p (k n) -> p k n", k=8), rv(2, 2, BF16).rearrange("p (k n) -> p k n", k=8)]
        WT_k = [[("R", 0, 0)], [("R", 0, 1)]]
        Wv = rv(4, 6, BF16).rearrange("p (k n) -> p k n", k=8)
        Wv_k = rk(4, 6)
        ost = rv(12, 4, BF16).rearrange("p (s n) -> p s n", s=4)
        ost_k = [[("R", 3, s_)] for s_ in range(4)]
        Ct = rv(36, 8)
        St = rv(44, 8)
        Ct_k, St_k = rk(36, 8), rk(44, 8)
        FM = []
        for p_ in range(3):
            FM.append(dict(sc=p_, segs=[(O_FQ + 128 * p_, 128)], norm=(64, "bones64", der, 0, ("der", 0)), rope=None,
                           outs=[(2 * p_, 0, 64), (2 * p_ + 1, 64, 64)]))
        for p_ in range(3):
            FM.append(dict(sc=3 + p_, segs=[(O_FK + 128 * p_, 128)], norm=(64, "bones64", ppt, 1, ("ppt",)), rope=None,
                           outs=[(6 + 2 * p_, 0, 64), (7 + 2 * p_, 64, 64)]))
        for p_ in range(2):
            FM.append(dict(sc=6 + p_, segs=[(O_DQ + 128 * p_, 128)], norm=(32, "bones32", der, 1, ("der", 1)), rope=32,
                           outs=[(12 + 4 * p_ + q_, 32 * q_, 32) for q_ in range(4)]))
        for p_ in range(2):
            FM.append(dict(sc=8 + p_, segs=[(O_DK + 128 * p_, 128)], norm=(32, "bones32", ppt, 3, ("ppt",)), rope=32))
        for p_ in range(3):
            FM.append(dict(sc=10 + p_, segs=[(O_SQ + 128 * p_, 128)], norm=(64, "bones64", der, 2, ("der", 2)), rope=64))
        FM.append(dict(sc=13, segs=[(O_SK, 64), (O_SK, 64)], norm=(64, "bones64", ppt, 5, ("ppt",)), rope=64))
        FM.append(dict(sc=14, segs=[(O_IK, 64), (O_IK, 64)], norm=None, rope=64))
        for p_ in range(2):
            FM.append(dict(sc=15 + p_, segs=[(O_IQ + 128 * p_, 128)], norm=None, rope=64))

        def load_tile(ti):
            sl = ti % 2
            o = 0
            for col0, n_ in FM[ti]["segs"]:
                load_w(WT[sl][:, :, o:o + n_], WT_k[sl], win[:, :, col0:col0 + n_])
                o += n_

        cur_rope = None
        SKB = Skew(1)
        SKC = Skew(2)
        load_tile(0)
        for ti, T in enumerate(FM):
            if ti + 1 < len(FM):
                load_tile(ti + 1)
            sl = ti % 2
            if T["rope"] is not None and T["rope"] != cur_rope:
                SKB.flush()
                SKC.flush()
                cur_rope = T["rope"]
                ro = 0 if cur_rope == 32 else 2
                dma(Ct, rope_d[ro], R=[], W=Ct_k)
                dma(St, rope_d[ro + 1], R=[], W=St_k)
            for c in range(NCH):
                cs = slice(c * 512, (c + 1) * 512)
                pj, pjk = bank((0, 1, 2))
                for kc in range(8):
                    mm(pj[:, :], WT[sl][:, kc, :], hT[:, kc, cs], kc == 0, kc == 7, R=WT_k[sl] + [("hT", kc, c)], W=[pjk])
                sq_ = None
                if T["norm"] is not None:
                    sq_ = slot("sqt", 2)
                    act(sqt[:, sq_, :], pj[:, :], AF.Square, R=[pjk], W=[("sqt", sq_)])

                def stageB(T=T, c=c, cs=cs, pj=pj, pjk=pjk, sq_=sq_):
                    os_ = slot("ost", 4)
                    qs_ = None
                    if T["rope"] is not None:
                        qs_ = slot("qn", 3)
                        tgt, tgtk = qn[:, qs_, :], [("qn", qs_)]
                    else:
                        tgt, tgtk = ost[:, os_, :], ost_k[os_]
                    if T["norm"] is not None:
                        bs_, bname, gt, gcol, gkey = T["norm"]
                        sb_, sbk = bank((3, 4))
                        mm(sb_[:, :], CST[bname][:, :], sqt[:, sq_, :], True, True, R=[("sqt", sq_), ("k", bname)], W=[sbk])
                        r_ = slot("rs", 2)
                        act(rs[:, r_, :], sb_[:, :], AF.Ln, R=[sbk] + KEPS, W=[("rs", r_)], bias=eps_ap, scale=1.0 / bs_)
                        act(rs[:, r_, :], rs[:, r_, :], AF.Exp, R=[("rs", r_)], W=[("rs", r_)], scale=-0.5)
                        P.add("dve", lambda e: e.scalar_tensor_tensor(
                            out=tgt, in0=pj[:, :], scalar=gt[:, gcol:gcol + 1], in1=rs[:, r_, :], op0=ALU.mult, op1=ALU.mult),
                            R=[pjk, gkey, ("rs", r_)], W=tgtk)
                    else:
                        act(tgt, pj[:, :], AF.Copy, R=[pjk], W=tgtk)

                    def stageC():
                        if T["rope"] is not None:
                            pname = "prot%d" % T["rope"]
                            rp, rpk = bank((5, 6))
                            mm(rp[:, :], CST[pname][:, :], qn[:, qs_, :], True, True, R=[("qn", qs_), ("k", pname)], W=[rpk])
                            a_ = slot("t1", 2)
                            b_ = slot("t2", 2)
                            P.add("dve", lambda e: e.tensor_tensor(out=t1[:, a_, :], in0=rp[:, :], in1=St[:, cs], op=ALU.mult),
                                  R=[rpk] + St_k, W=[("t1", a_)])
                            P.add("pool", lambda e: e.tensor_tensor(out=t2[:, b_, :], in0=qn[:, qs_, :], in1=Ct[:, cs], op=ALU.mult),
                                  R=[("qn", qs_)] + Ct_k, W=[("t2", b_)])
                            P.add("dve", lambda e: e.tensor_tensor(out=ost[:, os_, :], in0=t1[:, a_, :], in1=t2[:, b_, :], op=ALU.add),
                                  R=[("t1", a_), ("t2", b_)], W=ost_k[os_])
                        if "outs" in T:
                            for tl_, r0_, nr_ in T["outs"]:
                                dma(sc_pad[tl_, r0_:r0_ + nr_, cs], ost[r0_:r0_ + nr_, os_, :], R=ost_k[os_], W=[("scp", tl_, c)])
                        else:
                            dma(sc_fm[T["sc"], :, cs], ost[:, os_, :], R=ost_k[os_], W=[("scfm", T["sc"], c)])
                    SKC.push(stageC)
                SKB.push(stageB)
        SKB.flush()
        SKC.flush()

        def tm_group(col_segs, ncols, handler):
            o = 0
            for col0, n_ in col_segs:
                load_w(Wv[:, :, o:o + n_], Wv_k, win[:, :, col0:col0 + n_])
                o += n_
            handler()

        def v_pairs(npairs, sc0):
            for a in range(npairs):
                for jg in range(4):
                    tv, tvk = bank((6, 7))
                    for jj in range(4):
                        j = jg * 4 + jj
                        for kc in range(8):
                            mm(tv[:, jj * 128:(jj + 1) * 128], hT[:, kc, j * 128:(j + 1) * 128], Wv[:, kc, a * 128:(a + 1) * 128],
                               kc == 0, kc == 7, R=Wv_k + [("hT", kc, jg)], W=[tvk])
                    os_ = slot("ost", 4)
                    act(ost[:, os_, :], tv[:, :], AF.Copy, R=[tvk], W=ost_k[os_])
                    dma(sc_v[sc0 + a, :, jg * 4:(jg + 1) * 4, :], ost[:, os_, :].rearrange("p (j n) -> p j n", j=4),
                        R=ost_k[os_], W=[("scv", sc0 + a, jg)])

        tm_group([(O_FV, 128), (O_FV + 128, 128), (O_FV + 256, 128)], 384, lambda: v_pairs(3, 0))
        tm_group([(O_DV, 128), (O_DV + 128, 128)], 256, lambda: v_pairs(2, 3))

        def small():
            for jg in range(4):
                tv, tvk = bank((6, 7))
                for jj in range(4):
                    j = jg * 4 + jj
                    for kc in range(8):
                        mm(tv[:, jj * 74:(jj + 1) * 74], hT[:, kc, j * 128:(j + 1) * 128], Wv[:, kc, 0:74],
                           kc == 0, kc == 7, R=Wv_k + [("hT", kc, jg)], W=[tvk])
                tv3 = tv[:, 0:296].rearrange("p (j n) -> p j n", j=4)
                os_ = slot("ost", 4)
                act(ost[:, os_, 0:256].rearrange("p (j n) -> p j n", j=4), tv3[:, :, 0:64], AF.Copy, R=[tvk], W=ost_k[os_])
                dma(sc_sv[:, jg * 4:(jg + 1) * 4, :], ost[:, os_, 0:256].rearrange("p (j n) -> p j n", j=4),
                    R=ost_k[os_], W=[("scsv", jg)])
                P.add("dve", lambda e, jg=jg, tv3=tv3: e.tensor_copy(out=FFt[:, jg * 4:(jg + 1) * 4, :], in_=tv3[:, :, 64:70]),
                      R=[tvk], W=[("FFt", jg)])
                P.add("dve", lambda e, jg=jg, tv3=tv3: e.tensor_scalar(
                    out=IWs[:, jg * 4:(jg + 1) * 4, :], in0=tv3[:, :, 70:74], scalar1=0.0625, scalar2=None, op0=ALU.mult),
                    R=[tvk], W=[("IWs", jg)])

        tm_group([(O_SV, 64), (O_FF, 6), (O_IW, 4)], 74, small)

    TL = [[rv(28 * s_ + 4 * i_, 4, BF16) for i_ in range(4)] for s_ in range(2)]
    TL_k = [[rk(28 * s_ + 4 * i_, 4) for i_ in range(4)] for s_ in range(2)]
    Kd = [rv(28 * s_ + 16, 4, BF16) for s_ in range(2)]
    Kd_k = [rk(28 * s_ + 16, 4) for s_ in range(2)]
    Vp = [rv(28 * s_ + 20, 8, BF16).rearrange("p (j s d) -> p j s d", j=16, s=4) for s_ in range(2)]
    Vp_k = [rk(28 * s_ + 20, 8) for s_ in range(2)]
    identb = CST["irep"][:, 0:128]

    def load_v(sl, vi):
        dma(Vp[sl][:, :, 0, :], sc_v[vi][:, :, 0:64], R=[("scv", vi)], W=Vp_k[sl])
        dma(Vp[sl][:, :, 3, :], sc_v[vi][:, :, 64:128], R=[("scv", vi)], W=Vp_k[sl])
        P.add("pool", lambda e: e.memset(Vp[sl][:, :, 1:3, :], 1.0), W=Vp_k[sl])

    def load_fox_pair(sl, p_):
        for i_, t_ in enumerate((2 * p_, 2 * p_ + 1, 6 + 2 * p_, 7 + 2 * p_)):
            dma(TL[sl][i_], sc_pad[t_], R=[("scp", t_)], W=TL_k[sl][i_])
        load_v(sl, p_)

    def load_diff_pair(sl, d_):
        for i_ in range(4):
            dma(TL[sl][i_], sc_pad[12 + 4 * d_ + i_], R=[("scp", 12 + 4 * d_ + i_)], W=TL_k[sl][i_])
        dma(Kd[sl], sc_fm[8 + d_], R=[("scfm", 8 + d_)], W=Kd_k[sl])
        load_v(sl, 3 + d_)

    SKA = Skew(3)

    def attn_map(sl, e_, lhs, lhsk, rhs, rhsk, c, h, fox, acc, acck):
        nj = 4 * c + 4
        for j in range(nj):
            n0 = max(j * 128, c * 512)
            wN = (c + 1) * 512 - n0
            off = n0 - c * 512
            diag = j >= 4 * c
            st, stk = bank((4, 5, 6, 7))
            mm(st[:, off:off + wN], lhs[:, j * 128:(j + 1) * 128], rhs[:, n0:n0 + wN],
               True, not diag, R=lhsk + rhsk, W=[stk])
            if diag:
                mm(st[:, off:off + 128], identb, CST["cb"][:, :], False, True, R=[("k", "irep"), ("k", "cb")], W=[stk])
            ps_ = slot("pt", 4)
            act(pt[:, ps_, off:off + wN], st[:, off:off + wN], AF.Exp, R=[stk], W=[("pt", ps_)])
            SKA.push(lambda j=j, off=off, wN=wN, ps_=ps_: mm(
                acc[:, off:off + wN], Vp[sl][:, j, 2 * e_:2 * e_ + 2, :].rearrange("p s d -> p (s d)"), pt[:, ps_, off:off + wN],
                j == 0, j == nj - 1, R=Vp_k[sl] + [("pt", ps_)], W=[acck]))

    CT = rv(56, 4, BF16)
    CT_k = rk(56, 4)
    CS = rv(28, 6).rearrange("p (j n) -> p j n", j=16)
    CS_k = rk(28, 6)

    def fox_phase():
        P.add("dve", lambda e: e.tensor_tensor(out=Lt[:, :, :], in0=FFt[:, :, :], in1=apb(ppt[:, 23:29], 16), op=ALU.add),
              R=[("FFt",), ("ppt",)], W=[("Lt",)])
        act(Lt[:, :, :], Lt[:, :, :], AF.Exp, R=[("Lt",)], W=[("Lt",)], scale=-1.0)
        act(Lt[:, :, :], Lt[:, :, :], AF.Ln, R=[("Lt",)] + KEPS, W=[("Lt",)], bias=one_ap)
        Ltf = Lt[:, :, :].rearrange("p j n -> p (j n)")
        cps, cpsk = bank((0,))
        mm(cps[:, 0:96], CST["negones"][:, :], Ltf, True, True, R=[("Lt",), ("k", "negones")], W=[cpsk])
        cp2, cp2k = bank((1,))
        mm(cp2[:, 0:96], CST["tri"][:, :], Ltf, True, True, R=[("Lt",), ("k", "tri")], W=[cp2k])
        cpsv = cps[:, 0:96].rearrange("p (j n) -> p j n", j=16)
        cp23 = cp2[:, 0:96].rearrange("p (j n) -> p j n", j=16)
        P.add("dve", lambda e: e.tensor_copy(out=scA[:, :, :], in_=cpsv), R=[cpsk], W=[("scA",)])
        bufs = [(scA, ("scA",)), (scB, ("scB",))]
        cur = 0
        for sh in (1, 2, 4, 8):
            (a_, ak), (b_, bk) = bufs[cur], bufs[1 - cur]
            P.add("dve", lambda e, a_=a_, b_=b_, sh=sh: e.tensor_copy(out=b_[:, 0:sh, :], in_=a_[:, 0:sh, :]), R=[ak], W=[bk])
            P.add("dve", lambda e, a_=a_, b_=b_, sh=sh: e.tensor_tensor(out=b_[:, sh:16, :], in0=a_[:, sh:16, :],
                                                                   in1=a_[:, 0:16 - sh, :], op=ALU.add), R=[ak], W=[bk])
            cur = 1 - cur
        inc_, inck = bufs[cur]
        P.add("dve", lambda e: e.tensor_copy(out=csb[:, 0:1, :], in_=cp23[:, 0:1, :]), R=[cp2k], W=[("csb",)])
        P.add("dve", lambda e: e.tensor_tensor(out=csb[:, 1:16, :], in0=cp23[:, 1:16, :], in1=inc_[:, 0:15, :], op=ALU.add),
              R=[cp2k, inck], W=[("csb",)])
        cpsk = ("csb",)
        cps3 = csb[:, :, :]
        P.add("dve", lambda e: e.tensor_scalar(out=negc[:, :, :], in0=cps3, scalar1=-1.0, scalar2=None, op0=ALU.mult),
              R=[cpsk], W=[("negc",)])
        P.add("pool", lambda e: e.memset(CS[:, :, :], 0.0), W=CS_k)
        P.add("dve", lambda e: e.tensor_copy(out=chb[:, :, :], in_=cps3), R=[cpsk], W=[("chb",)])
        P.add("dve", lambda e: e.tensor_copy(out=CS[:, :, 0:6], in_=chb[:, :, :]), R=[("chb",)], W=CS_k)
        P.add("dve", lambda e: e.tensor_tensor(out=r1t[:, :, :], in0=cps3, in1=chb[:, :, :], op=ALU.subtract),
              R=[cpsk, ("chb",)], W=[("r1t",)])
        P.add("dve", lambda e: e.tensor_copy(out=chb[:, :, :], in_=r1t[:, :, :]), R=[("r1t",)], W=[("chb",)])
        P.add("dve", lambda e: e.tensor_copy(out=CS[:, :, 32:38], in_=chb[:, :, :]), R=[("chb",)], W=CS_k)
        P.add("dve", lambda e: e.tensor_tensor(out=r1t[:, :, :], in0=r1t[:, :, :], in1=chb[:, :, :], op=ALU.subtract),
              R=[("r1t",), ("chb",)], W=[("r1t",)])
        P.add("dve", lambda e: e.tensor_copy(out=chb[:, :, :], in_=r1t[:, :, :]), R=[("r1t",)], W=[("chb",)])
        P.add("dve", lambda e: e.tensor_copy(out=CS[:, :, 64:70], in_=chb[:, :, :]), R=[("chb",)], W=CS_k)
        for q_ in range(4):
            ctp, ctpk = bank((1, 2, 3))
            for jj in range(4):
                j = q_ * 4 + jj
                P.add("pe", lambda e, ctp=ctp, jj=jj, j=j: e.transpose(
                    out=ctp[0:96, jj * 128:(jj + 1) * 128], in_=CS[:, j, :], identity=CST["ident"][:, :]),
                    R=CS_k + [("k", "ident")], W=[ctpk])
            act(CT[0:96, q_ * 512:(q_ + 1) * 512], ctp[0:96, :], AF.Copy, R=[ctpk], W=CT_k)
        tap("negc", negc[:, :, :], [("negc",)])
        for h_ in range(6):
            tl_ = 2 * (h_ // 2) + (h_ % 2)
            r0_ = 64 if h_ % 2 == 0 else 0
            for q_ in range(3):
                dma(sc_pad[tl_, r0_ + q_:r0_ + q_ + 1, :], CT[32 * q_ + h_:32 * q_ + h_ + 1, :], R=CT_k, W=[("scp", tl_, "c")])
                dma(sc_pad[6 + tl_, r0_ + 3 + q_:r0_ + 4 + q_, :], CT[32 * q_ + h_:32 * q_ + h_ + 1, :], R=CT_k,
                    W=[("scp", 6 + tl_, "c")])
        load_fox_pair(0, 0)
        for p_ in range(3):
            sl = p_ % 2
            SKA.flush()
            if p_ + 1 < 3:
                load_fox_pair((p_ + 1) % 2, p_ + 1)
            for c in range(NCH):
                cs = slice(c * 512, (c + 1) * 512)
                for e_ in range(2):
                    base = 64 * e_
                    acc, acck = bank((0, 1, 2))
                    attn_map(sl, e_, TL[sl][2 + e_], TL_k[sl][2 + e_], TL[sl][e_], TL_k[sl][e_], c, 2 * p_ + e_, True, acc, acck)
                    def fin(base=base, acc=acc, acck=acck, p_=p_, cs=cs, c=c):
                        O = slice(base, base + 64)
                        Dn = slice(64 - base, 128 - base)
                        rc = slot("rct", 2)
                        recip(rct[O, rc, :], acc[Dn, :], R=[acck], W=[("rct", rc)], on_act=False)
                        P.add("dve", lambda e: e.tensor_tensor(out=hT[O, p_, cs], in0=acc[O, :], in1=rct[O, rc, :], op=ALU.mult),
                              R=[acck, ("rct", rc)], W=[("hT", p_, c)])
                    SKA.push(fin)
        SKA.flush()

    def diff_phase():
        load_diff_pair(0, 0)
        for d_ in range(2):
            sl = d_ % 2
            SKA.flush()
            if d_ == 0:
                load_diff_pair(1, 1)
            for c in range(NCH):
                cs = slice(c * 512, (c + 1) * 512)
                od = slot("t1", 2)
                for e_ in range(2):
                    accs = []
                    for m_ in range(2):
                        base = 64 * e_ + 32 * m_
                        acc, acck = bank((0, 1, 2))
                        attn_map(sl, e_, Kd[sl], Kd_k[sl], TL[sl][2 * e_ + m_], TL_k[sl][2 * e_ + m_], c, 0, False, acc, acck)
                        accs.append((acc, acck))
                    def comb(e_=e_, accs=accs, od=od):
                        O = slice(64 * e_, 64 * e_ + 64)
                        Dn = slice(64 - 64 * e_, 128 - 64 * e_)
                        (a1, a1k), (a2, a2k) = accs
                        ra = slot("rct", 2)
                        rb = slot("rct", 2)
                        ob = slot("t2", 2)
                        recip(rct[O, ra, :], a1[Dn, :], R=[a1k], W=[("rct", ra)], on_act=False)
                        recip(rct[O, rb, :], a2[Dn, :], R=[a2k], W=[("rct", rb)], on_act=False)
                        P.add("dve", lambda e: e.tensor_scalar(out=rct[O, rb, :], in0=rct[O, rb, :], scalar1=der[O, 4:5],
                                                               scalar2=None, op0=ALU.mult),
                              R=[("rct", rb), ("der", 4)], W=[("rct", rb)])
                        P.add("dve", lambda e: e.tensor_tensor(out=t1[O, od, :], in0=a1[O, :], in1=rct[O, ra, :], op=ALU.mult),
                              R=[a1k, ("rct", ra)], W=[("t1", od)])
                        P.add("dve", lambda e: e.tensor_tensor(out=t2[O, ob, :], in0=a2[O, :], in1=rct[O, rb, :], op=ALU.mult),
                              R=[a2k, ("rct", rb)], W=[("t2", ob)])
                        P.add("pool", lambda e: e.tensor_tensor(out=t1[O, od, :], in0=t1[O, od, :], in1=t2[O, ob, :], op=ALU.add),
                              R=[("t1", od), ("t2", ob)], W=[("t1", od)])
                    SKA.push(comb)

                def subln(od=od, d_=d_, cs=cs, c=c):
                    sq_ = slot("sqt", 2)
                    act(sqt[:, sq_, :], t1[:, od, :], AF.Square, R=[("t1", od)], W=[("sqt", sq_)])
                    sb_, sbk = bank((3,))
                    mm(sb_[:, :], CST["bones64"][:, :], sqt[:, sq_, :], True, True, R=[("sqt", sq_), ("k", "bones64")], W=[sbk])
                    r_ = slot("rs", 2)
                    act(rs[:, r_, :], sb_[:, :], AF.Ln, R=[sbk] + KEPS, W=[("rs", r_)], bias=eps_ap, scale=1.0 / 64)
                    act(rs[:, r_, :], rs[:, r_, :], AF.Exp, R=[("rs", r_)], W=[("rs", r_)], scale=-0.5)
                    P.add("dve", lambda e: e.scalar_tensor_tensor(
                        out=hT[:, 3 + d_, cs], in0=t1[:, od, :], scalar=der[:, 3:4], in1=rs[:, r_, :], op0=ALU.mult, op1=ALU.mult),
                        R=[("t1", od), ("der", 3), ("rs", r_)], W=[("hT", 3 + d_, c)])
                SKA.push(subln)
        SKA.flush()

    def dsa_phase():
        Qs = rv(0, 12, BF16).rearrange("p (t n) -> p t n", t=3)
        Qs_k = rk(0, 12)
        SKK, SKK_k = rv(12, 4, BF16), rk(12, 4)
        IKK, IKK_k = rv(16, 4, BF16), rk(16, 4)
        IQ = rv(20, 8, BF16).rearrange("p (t n) -> p t n", t=2)
        IQ_k = rk(20, 8)
        SVa = rv(28, 6, BF16).rearrange("p (j s d) -> p j s d", j=16, s=3)
        SVa_k = rk(28, 6)
        accb, acc_k = rv(36, 8), rk(36, 8)
        MBs = [(rv(44, 4, BF16), rk(44, 4)), (rv(56, 4, BF16), rk(56, 4))]
        PERT, PERT_k = rv(48, 8), rk(48, 8)
        junk = t2[:, :, :].rearrange("p a n -> p (a n)").bitcast(BF16)
        junk_k = [("t2",)]
        for t_ in range(3):
            dma(Qs[:, t_, :], sc_fm[10 + t_], R=[("scfm", 10 + t_)], W=Qs_k)
        dma(SKK, sc_fm[13], R=[("scfm", 13)], W=SKK_k)
        dma(IKK, sc_fm[14], R=[("scfm", 14)], W=IKK_k)
        for t_ in range(2):
            dma(IQ[:, t_, :], sc_fm[15 + t_], R=[("scfm", 15 + t_)], W=IQ_k)
        dma(SVa[:, :, 0, :], sc_sv, R=[("scsv",)], W=SVa_k)
        dma(SVa[:, :, 2, :], sc_sv, R=[("scsv",)], W=SVa_k)
        P.add("pool", lambda e: e.memset(SVa[:, :, 1, :], 1.0), W=SVa_k)
        dma(PERT, pert_d, R=[], W=PERT_k)

        accs_ = [(accb, acc_k), (accB_t[:, :], [("accB",)])]
        junks = [(junk, junk_k), (t1[:, :, :].rearrange("p a n -> p (a n)").bitcast(BF16), [("t1",)])]

        def index_block(i):
            q = i % 2
            acc_, acck_ = accs_[q]
            nk = (i + 1) * 128
            nch = (nk + 511) // 512
            for hh in range(4):
                tl, base = hh // 2, 64 * (hh % 2)
                for m_ in range(nch):
                    w_ = min(512, nk - 512 * m_)
                    dp, dpk = bank((0, 1))
                    mm(dp[:, 0:w_], IQ[base:base + 64, tl, i * 128:(i + 1) * 128], IKK[base:base + 64, 512 * m_:512 * m_ + w_],
                       True, True, R=IQ_k + IKK_k, W=[dpk])
                    act(dp[:, 0:w_], dp[:, 0:w_], AF.Relu, R=[dpk], W=[dpk])
                    src = PERT if hh == 0 else acc_
                    srck = PERT_k if hh == 0 else acck_
                    P.add("dve", lambda e, dp=dp, w_=w_, m_=m_, hh=hh, src=src: e.scalar_tensor_tensor(
                        out=acc_[:, 512 * m_:512 * m_ + w_], in0=dp[:, 0:w_], scalar=IWs[:, i, hh:hh + 1],
                        in1=src[:, 512 * m_:512 * m_ + w_], op0=ALU.mult, op1=ALU.add),
                        R=[dpk, ("IWs",)] + srck, W=acck_)
            P.add("dve", lambda e: e.tensor_reduce(out=bis2[:, q, 0:1], in_=acc_[:, 0:nk], axis=AX.X, op=ALU.max,
                                                   apply_absolute_value=True), R=acck_, W=[("bis2", q, 0)])
            P.add("dve", lambda e: e.tensor_tensor(out=acc_[:, i * 128:(i + 1) * 128], in0=acc_[:, i * 128:(i + 1) * 128],
                                                   in1=CST["cbt"][:, :], op=ALU.add), R=acck_ + [("k", "cbt")], W=acck_)
            P.add("dve", lambda e: e.tensor_scalar(out=STt2[:, q, :], in0=CST["pow2"][:, :], scalar1=bis2[:, q, 0:1], scalar2=None,
                                                   op0=ALU.mult), R=[("bis2", q, 0), ("k", "pow2")], W=[("STt2", q)])
            P.add("dve", lambda e: e.memset(bis2[:, q, 1:2], 0.0), W=[("bis2", q, 1)])

        def bisect_pair(iA, iB, zsteps):
            blocks = [b_ for b_ in (iA, iB) if b_ is not None]
            zsteps = list(zsteps)
            per_it = (len(zsteps) + KBIS - 1) // KBIS
            for k in range(KBIS):
                for _ in range(per_it):
                    if zsteps:
                        zsteps.pop(0)()
                for i in blocks:
                    q = i % 2
                    acc_, acck_ = accs_[q]
                    jk, jkk = junks[q]
                    nk = (i + 1) * 128
                    if q == 0:
                        P.add("dve", lambda e, acc_=acc_, jk=jk, nk=nk, q=q: e.tensor_scalar(
                            out=jk[:, 0:nk], in0=acc_[:, 0:nk], scalar1=bis2[:, q, 1:2], scalar2=None,
                            op0=ALU.is_gt, op1=ALU.add, accum_out=bis2[:, q, 2:3]),
                            R=acck_ + [("bis2", q, 1)], W=jkk + [("bis2", q, 2)])
                    else:
                        act(jk[:, 0:nk], acc_[:, 0:nk], AF.Sign, R=acck_ + [("bis2", q, 1)], W=jkk + [("bis2", q, 2)],
                            bias=bis2[:, q, 1:2], scale=-1.0, accum_out=bis2[:, q, 2:3])
                for i in blocks:
                    q = i % 2
                    nk = (i + 1) * 128
                    if q == 0:
                        P.add("dve", lambda e, k=k, q=q: e.tensor_scalar(
                            out=bis2[:, q, 3:4], in0=bis2[:, q, 2:3], scalar1=TOPK - 0.5, scalar2=STt2[:, q, 32 + k:33 + k],
                            op0=ALU.is_gt, op1=ALU.mult), R=[("bis2", q, 2), ("STt2", q)], W=[("bis2", q, 3)])
                    else:
                        P.add("pool", lambda e, k=k, q=q, nk=nk: e.tensor_scalar(
                            out=bis2[:, q, 3:4], in0=bis2[:, q, 2:3], scalar1=float(nk - 2 * TOPK + 1),
                            scalar2=STt2[:, q, 32 + k:33 + k], op0=ALU.is_lt, op1=ALU.mult),
                            R=[("bis2", q, 2), ("STt2", q)], W=[("bis2", q, 3)])
                        P.add("pool", lambda e, k=k, q=q: e.tensor_scalar(
                            out=bis2[:, q, 1:2], in0=bis2[:, q, 3:4], scalar1=STt2[:, q, k:k + 1], scalar2=bis2[:, q, 1:2],
                            op0=ALU.subtract, op1=ALU.add),
                            R=[("bis2", q, 3), ("bis2", q, 1), ("STt2", q)], W=[("bis2", q, 1)])
                        continue
                    P.add("dve", lambda e, k=k, q=q: e.scalar_tensor_tensor(
                        out=bis2[:, q, 1:2], in0=bis2[:, q, 3:4], scalar=STt2[:, q, k:k + 1], in1=bis2[:, q, 1:2],
                        op0=ALU.subtract, op1=ALU.add),
                        R=[("bis2", q, 3), ("bis2", q, 1), ("STt2", q)], W=[("bis2", q, 1)])
            while zsteps:
                zsteps.pop(0)()
            for i in blocks:
                q = i % 2
                acc_, acck_ = accs_[q]
                MB, MB_k = MBs[q]
                nk = (i + 1) * 128
                P.add("dve", lambda e, acc_=acc_, MB=MB, nk=nk, q=q: e.tensor_scalar(
                    out=MB[:, 0:nk], in0=acc_[:, 0:nk], scalar1=bis2[:, q, 1:2], scalar2=NEG, op0=ALU.is_le, op1=ALU.mult),
                    R=acck_ + [("bis2", q, 1)], W=MB_k)

        SKD = Skew(1)
        accE, accEk = banks[6], ("ps", 6)
        accO, accOk = banks[7], ("ps", 7)

        def attend_steps(i):
            MB, MB_k = MBs[i % 2]
            c = i // 4
            qs_ = slice(i * 128, (i + 1) * 128)
            steps = []

            def step(j):
                ks_ = slice(j * 128, (j + 1) * 128)
                sts = []
                for par in range(2):
                    st, stk = bank((2, 3, 4, 5))
                    pr = slice(64 * par, 64 * par + 64)
                    for hi in range(3):
                        mm(st[:, hi * 128:(hi + 1) * 128], SKK[pr, ks_], Qs[pr, hi, qs_], hi == 0, False,
                           R=SKK_k + Qs_k, W=[stk])
                    mm(st[:, 0:384], MB[:, ks_], CST["irep"][:, 0:384], False, True, R=MB_k + [("k", "irep")], W=[stk])
                    sts.append((st, stk))
                pss = []
                for par in range(2):
                    ps_ = slot("pt", 4)
                    act(pt[:, ps_, 0:384], sts[par][0][:, 0:384], AF.Exp, R=[sts[par][1]], W=[("pt", ps_)])
                    pss.append(ps_)

                def pv():
                    mm(accE[:, 0:384], SVa[:, j, 0:2, :].rearrange("p s d -> p (s d)"), pt[:, pss[0], 0:384], j == 0, j == i,
                       R=SVa_k + [("pt", pss[0])], W=[accEk])
                    mm(accO[:, 0:384], SVa[:, j, 1:3, :].rearrange("p s d -> p (s d)"), pt[:, pss[1], 0:384], j == 0, j == i,
                       R=SVa_k + [("pt", pss[1])], W=[accOk])
                SKD.push(pv)

            def fin():
                for par, (acc, acck) in enumerate(((accE, accEk), (accO, accOk))):
                    O = slice(64 * par, 64 * par + 64)
                    Dn = slice(64 - 64 * par, 128 - 64 * par)
                    rc = slot("rct", 2)
                    recip(rct[O, rc, 0:384], acc[Dn, 0:384], R=[acck], W=[("rct", rc)])
                    P.add("dve", lambda e, O=O, rc=rc, acc=acc: e.tensor_tensor(
                        out=hT[O, 5:8, qs_], in0=acc[O, 0:384].rearrange("p (h n) -> p h n", h=3),
                        in1=rct[O, rc, 0:384].rearrange("p (h n) -> p h n", h=3), op=ALU.mult),
                        R=[acck, ("rct", rc)], W=[("hT", 5, c), ("hT", 6, c), ("hT", 7, c)])

            for j in range(i + 1):
                steps.append(lambda j=j: step(j))
            steps.append(lambda: SKD.push(fin))
            return steps

        index_block(0)
        index_block(1)
        prev_steps = []
        for m_ in range(8):
            bisect_pair(2 * m_, 2 * m_ + 1, prev_steps)
            prev_steps = attend_steps(2 * m_) + attend_steps(2 * m_ + 1)
            if m_ + 1 < 8:
                index_block(2 * m_ + 2)
                index_block(2 * m_ + 3)
        for st_ in prev_steps:
            st_()
        SKD.flush()
        tap("acc15", accB_t[:, :], [("accB",)])
        tap("MB15", MBs[1][0], MBs[1][1])
        tap("bis15", bis2[:, 1, :], [("bis2",)])
        tap("STt", STt2[:, 1, :], [("STt2",)])

    def wout_phase(li):
        Wo = [rv(56, 2, BF16).rearrange("p (k n) -> p k n", k=8), rv(58, 2, BF16).rearrange("p (k n) -> p k n", k=8)]
        Wo_k = [[("R", 14, 0)], [("R", 14, 1)]]
        wsrc = w_out_d[li].rearrange("(kt p) n -> p kt n", p=128)
        load_w(Wo[0], Wo_k[0], wsrc[:, :, 0:128])
        for d_ in range(8):
            sl = d_ % 2
            if d_ + 1 < 8:
                load_w(Wo[1 - sl], Wo_k[1 - sl], wsrc[:, :, (d_ + 1) * 128:(d_ + 2) * 128])
            for c in range(NCH):
                cs = slice(c * 512, (c + 1) * 512)
                bk, bkk = bank((0, 1, 2, 3, 4, 5, 6, 7))
                for kt in range(8):
                    mm(bk[:, :], Wo[sl][:, kt, :], hT[:, kt, cs], kt == 0, kt == 7, R=Wo_k[sl] + [("hT", kt, c)], W=[bkk])
                P.add("dve", lambda e, d_=d_, cs=cs, bk=bk: e.tensor_tensor(
                    out=xT[:, d_, cs], in0=bk[:, :], in1=xT[:, d_, cs], op=ALU.add),
                    R=[bkk, ("xT", d_, c)], W=[("xT", d_, c)])

    for li, labs in enumerate(layer_ids):
        dma(ppt[:, :], pp_d[li], R=[], W=[("ppt",)])
        derive_phase(labs)
        norm_phase(7)
        proj_phase(li)
        if li == 0:
            tap("sc_fm", sc_fm, [("scfm",)])
            tap("sc_v", sc_v, [("scv",)])
            tap("sc_sv", sc_sv, [("scsv",)])
            tap("FFt", FFt[:, :, :], [("FFt",)])
            tap("IWs", IWs[:, :, :], [("IWs",)])
        fox_phase()
        diff_phase()
        dsa_phase()
        if li == 0:
            tap("cat", hT[:, :, :], [("hT",)])
        wout_phase(li)
        if li == 0:
            tap("x1", xT[:, :, :], [("xT",)])
        norm_phase(15)
        ffn_phase(li)

    for c in range(NCH):
        for kc in range(8):
            dma(y_d[kc * 128:(kc + 1) * 128, c * 512:(c + 1) * 512], xT[:, kc, c * 512:(c + 1) * 512],
                R=[("xT", kc, c)], W=[("y", kc, c)])

    P.emit(nc, stack)
    stack.close()
    return nc, P.stats


_NC_CACHE = {}


def _get_nc(layer_ids):
    key = tuple(layer_ids)
    if key not in _NC_CACHE:
        _NC_CACHE[key] = build_nc(list(layer_ids))[0]
    return _NC_CACHE[key]


def kernel(**inputs):
    inp = {k: np.asarray(v) for k, v in inputs.items()}
    x = inp["x"].astype(np.float32, copy=False)
    B = x.shape[0]
    cst = _consts()
    base = {}
    for nm, w in _CONST_SHAPES:
        base["c_" + nm] = np.ascontiguousarray(cst[nm], dtype=np.float32)
    base["c_rope"] = cst["rope"]
    base["c_pert"] = cst["pert"]
    layer_ids = list(range(DEPTH))
    nc = _get_nc(layer_ids)
    base["w_in"] = np.ascontiguousarray(inp["w_in"], dtype=np.float32)
    base["w_out"] = np.ascontiguousarray(inp["w_out"], dtype=np.float32)
    base["w_gu"] = np.ascontiguousarray(inp["w_gate_up"], dtype=np.float32)
    base["w_dn"] = np.ascontiguousarray(inp["w_down"], dtype=np.float32)
    base["pp"] = np.stack([_pack_pp(inp, l) for l in layer_ids]).astype(np.float32)
    in_maps = []
    for b in range(B):
        m = dict(base)
        m["xT"] = np.ascontiguousarray(x[b].T)
        in_maps.append(m)
    res = run_bass_kernel_spmd(nc, in_maps, core_ids=list(range(B)))
    out = np.stack([np.asarray(r["yT"]).T for r in res.results]).astype(np.float32)
    return out
```
